# Optimizing a Trainium2 kernel written in Bass

```python
import jax, jax.numpy as jnp
from jax import lax
import numpy as np

D_MODEL = 1024
BATCH = 8
SEQ = 4096
DEPTH = 2

CTX_LEN = 256
GRID_W = 64
D_RWKV = D_MODEL // 2
RWKV_HEAD = 64
N_RWKV_HEADS = D_RWKV // RWKV_HEAD
LORA_W = 64
LORA_A = 64
LORA_G = 128
N_DIR = 2
RWKV_COLS = 3 * D_RWKV + N_DIR * LORA_W + N_DIR * LORA_A + LORA_G
N_MLA_HEADS = 8
QK_NOPE = 64
QK_ROPE = 32
QK_DIM = QK_NOPE + QK_ROPE
V_HEAD = 64
Q_LORA = 384
KV_LORA = 256
MLA_COLS = Q_LORA + KV_LORA + QK_ROPE
IN_COLS = RWKV_COLS + MLA_COLS
D_MIX = D_RWKV + N_MLA_HEADS * V_HEAD
D_FF = 2816
CONV_W = 3
Q_BLOCK = 128
ROPE_BASE = 10000.0
DEEPNORM_ALPHA = (2 * DEPTH) ** 0.25
DEEPNORM_BETA = (8 * DEPTH) ** -0.25
LN_EPS = 1e-5
RMS_EPS = 1e-6
GN_EPS = 64e-5

kernel_name = "hymba_rwkv7_mla_convffn_deepnorm_dit"

F32 = jnp.float32


def layer_norm(x, g, b):
    xf = x.astype(F32)
    xc = xf - jnp.mean(xf, axis=-1, keepdims=True)
    var = jnp.mean(xc * xc, axis=-1, keepdims=True)
    return (xc * lax.rsqrt(var + LN_EPS) * g + b).astype(x.dtype)


def rms_norm(x, g):
    xf = x.astype(F32)
    return (xf * lax.rsqrt(jnp.mean(xf * xf, axis=-1, keepdims=True) + RMS_EPS) * g).astype(x.dtype)


def dwconv3(x, w, b=None):
    xp = jnp.pad(x, ((0, 0), (1, 1), (0, 0)))
    y = xp[:, :-2] * w[0] + xp[:, 1:-1] * w[1] + xp[:, 2:] * w[2]
    return y if b is None else y + b


def heads(x, n_heads, head_dim):
    return x.reshape(*x.shape[:-1], n_heads, head_dim)


def axial_rope_tables(n_tok):
    rows = n_tok // GRID_W
    row = jnp.repeat(jnp.arange(rows, dtype=F32), GRID_W)
    col = jnp.tile(jnp.arange(GRID_W, dtype=F32), rows)
    n_pairs = QK_ROPE // 4
    inv = ROPE_BASE ** (-jnp.arange(n_pairs, dtype=F32) / n_pairs)
    ang = jnp.concatenate([row[:, None] * inv, col[:, None] * inv], axis=-1)
    return jnp.cos(ang), jnp.sin(ang)


def apply_rope(x, cos, sin):
    x1, x2 = x[..., 0::2], x[..., 1::2]
    o1 = x1 * cos - x2 * sin
    o2 = x1 * sin + x2 * cos
    return jnp.stack([o1, o2], axis=-1).reshape(x.shape).astype(x.dtype)


def rwkv_prepare(z, w0, w_b, a0, a_b, g_b, k_k, k_a):
    bsz, n, _ = z.shape
    idx = [D_RWKV, 2 * D_RWKV, 3 * D_RWKV, 3 * D_RWKV + N_DIR * LORA_W,
           3 * D_RWKV + N_DIR * LORA_W + N_DIR * LORA_A]
    r, k, v, w_lo, a_lo, g_lo = jnp.split(z, idx, axis=-1)
    w_lo = w_lo.reshape(bsz, n, N_DIR, LORA_W)
    a_lo = a_lo.reshape(bsz, n, N_DIR, LORA_A)
    w_log = -jax.nn.softplus(-(w0 + jnp.einsum('btdl,dlc->btdc', jnp.tanh(w_lo), w_b))) - 0.5
    decay = jnp.exp(-jnp.exp(w_log.astype(F32)))
    a = jax.nn.sigmoid(a0 + jnp.einsum('btdl,dlc->btdc', a_lo, a_b))
    g = jax.nn.sigmoid(g_lo) @ g_b
    kk = heads((k * k_k).astype(F32), N_RWKV_HEADS, RWKV_HEAD)
    kk = kk / jnp.maximum(jnp.sqrt(jnp.sum(kk * kk, axis=-1, keepdims=True)), 1e-12)
    k_dir = k[:, :, None, :] * (1.0 + (a - 1.0) * k_a)
    hd = lambda t: heads(t, N_RWKV_HEADS, RWKV_HEAD)
    return hd(r), hd(v), kk, g, hd(decay), hd(a), hd(k_dir)


def wkv_scan(r, decay, k, v, kk, a, state0, reverse):
    xs = tuple(jnp.moveaxis(t.astype(F32), 1, 0) for t in (r, decay, k, v, kk, a))

    def step(s, inp):
        r_t, w_t, k_t, v_t, kk_t, a_t = inp
        sa = jnp.einsum('bhvk,bhk->bhv', s, kk_t)
        s = (s * w_t[:, :, None, :]
             - sa[..., None] * (kk_t * a_t)[:, :, None, :]
             + v_t[..., None] * k_t[:, :, None, :])
        return s, jnp.einsum('bhvk,bhk->bhv', s, r_t)

    s_fin, ys = lax.scan(step, state0, xs, reverse=reverse)
    return s_fin, jnp.moveaxis(ys, 0, 1)


def rwkv_output(y, r, v, g, k_dir, gn_g, gn_b, r_k, dtype):
    mu = jnp.mean(y, axis=-1, keepdims=True)
    yc = y - mu
    var = jnp.mean(yc * yc, axis=-1, keepdims=True)
    gn = (yc * lax.rsqrt(var + GN_EPS) * gn_g.reshape(N_RWKV_HEADS, RWKV_HEAD)
          + gn_b.reshape(N_RWKV_HEADS, RWKV_HEAD))
    bonus = jnp.sum(r[:, :, None] * k_dir * r_k, axis=(2, -1))[..., None] * v
    out = (gn + bonus).reshape(*y.shape[:2], D_RWKV) * g
    return out.astype(dtype)


def mla_prepare(z, q_norm_g, w_uq, kv_norm_g, w_ukv, cos, sin):
    bsz, n, _ = z.shape
    c_q, c_kv, k_r = jnp.split(z, [Q_LORA, Q_LORA + KV_LORA], axis=-1)
    q = (rms_norm(c_q, q_norm_g) @ w_uq).reshape(bsz, n, N_MLA_HEADS, QK_DIM)
    kv = (rms_norm(c_kv, kv_norm_g) @ w_ukv).reshape(bsz, n, N_MLA_HEADS, QK_NOPE + V_HEAD)
    q_nope, q_rope = q[..., :QK_NOPE], q[..., QK_NOPE:]
    k_nope, v = kv[..., :QK_NOPE], kv[..., QK_NOPE:]
    if cos is not None:
        q_rope = apply_rope(q_rope, cos[:, None, :], sin[:, None, :])
        k_r = apply_rope(k_r, cos, sin)
    k = jnp.concatenate([k_nope, jnp.broadcast_to(k_r[:, :, None, :], (bsz, n, N_MLA_HEADS, QK_ROPE))], axis=-1)
    q = jnp.concatenate([q_nope, q_rope], axis=-1)
    return q, k, v


def attend(q, k, v):
    s = jnp.einsum('bqhd,bkhd->bhqk', q, k).astype(F32) * (QK_DIM ** -0.5)
    p = jax.nn.softmax(s, axis=-1).astype(v.dtype)
    return jnp.einsum('bhqk,bkhd->bqhd', p, v)


def mla_latent_attention(q, k_lat, v_lat, k_ctx, v_ctx):
    bsz, n, h, dq = q.shape
    k_all = jnp.concatenate([k_lat, k_ctx], axis=1)
    v_all = jnp.concatenate([v_lat, v_ctx], axis=1)
    qb = q.reshape(bsz, n // Q_BLOCK, Q_BLOCK, h, dq).transpose(1, 0, 2, 3, 4)
    ob = lax.map(lambda qi: attend(qi, k_all, v_all), qb)
    return ob.transpose(1, 0, 2, 3, 4).reshape(bsz, n, h * V_HEAD)


def token_mixer(u_lat, u_ctx, cos, sin, need_ctx, w_in, rwkv_conv, w0, w_b, a0, a_b, g_b, k_k, k_a,
                r_k, gn_g, gn_b, q_norm_g, w_uq, kv_norm_g, w_ukv, w_o):
    bsz = u_lat.shape[0]
    p_lat = u_lat @ w_in
    p_ctx = u_ctx @ w_in
    rl = rwkv_prepare(dwconv3(p_lat[..., :RWKV_COLS], rwkv_conv), w0, w_b, a0, a_b, g_b, k_k, k_a)
    rc = rwkv_prepare(dwconv3(p_ctx[..., :RWKV_COLS], rwkv_conv), w0, w_b, a0, a_b, g_b, k_k, k_a)
    r_l, v_l, kk_l, g_l, dec_l, a_l, kd_l = rl
    r_c, v_c, kk_c, g_c, dec_c, a_c, kd_c = rc
    ys_lat, ys_ctx = [], []
    for d in range(N_DIR):
        rev = d == 1
        s0 = jnp.zeros((bsz, N_RWKV_HEADS, RWKV_HEAD, RWKV_HEAD), F32)
        s_ctx, y_c = wkv_scan(r_c, dec_c[:, :, d], kd_c[:, :, d], v_c, kk_c, a_c[:, :, d], s0, rev)
        _, y_l = wkv_scan(r_l, dec_l[:, :, d], kd_l[:, :, d], v_l, kk_l, a_l[:, :, d], s_ctx, rev)
        ys_lat.append(y_l)
        ys_ctx.append(y_c)
    rwkv_lat = rwkv_output(ys_lat[0] + ys_lat[1], r_l, v_l, g_l, kd_l, gn_g, gn_b, r_k, u_lat.dtype)
    q_l, k_l, v_lm = mla_prepare(p_lat[..., RWKV_COLS:], q_norm_g, w_uq, kv_norm_g, w_ukv, cos, sin)
    q_c, k_c, v_cm = mla_prepare(p_ctx[..., RWKV_COLS:], q_norm_g, w_uq, kv_norm_g, w_ukv, None, None)
    mla_lat = mla_latent_attention(q_l, k_l, v_lm, k_c, v_cm)
    m_lat = jnp.concatenate([rwkv_lat, mla_lat], axis=-1) @ w_o
    if not need_ctx:
        return m_lat, None
    rwkv_ctx = rwkv_output(ys_ctx[0] + ys_ctx[1], r_c, v_c, g_c, kd_c, gn_g, gn_b, r_k, u_ctx.dtype)
    mla_ctx = attend(q_c, k_c, v_cm).reshape(bsz, u_ctx.shape[1], N_MLA_HEADS * V_HEAD)
    m_ctx = jnp.concatenate([rwkv_ctx, mla_ctx], axis=-1) @ w_o
    return m_lat, m_ctx


def conv_ffn(u, w_up, conv_w, conv_b, w_down):
    h = dwconv3(u @ w_up, conv_w, conv_b)
    h_gate, h_val = jnp.split(h, 2, axis=-1)
    return (jax.nn.silu(h_gate) * h_val) @ w_down


def setup_inputs(seed: int = 0) -> dict:
    key = jax.random.key(seed)
    ks = iter(jax.random.split(key, 40))
    L = DEPTH

    def nrm(shape, scale):
        return scale * jax.random.normal(next(ks), shape, F32)

    center_tap = jnp.array([0.0, 1.0, 0.0], F32)[None, :, None]
    return {
        "x": nrm((BATCH, SEQ, D_MODEL), 1.0),
        "c": nrm((BATCH, D_MODEL), 1.0),
        "ctx": nrm((BATCH, CTX_LEN, D_MODEL), 1.0),
        "c_ctx": nrm((D_MODEL,), 1.0),
        "w_ada": nrm((L, D_MODEL, 6 * D_MODEL), 0.5 * D_MODEL ** -0.5),
        "b_ada": nrm((L, 6 * D_MODEL), 0.01),
        "w_in": nrm((L, D_MODEL, IN_COLS), D_MODEL ** -0.5),
        "rwkv_conv": center_tap + nrm((L, CONV_W, RWKV_COLS), 0.3),
        "w0": -6.0 + 5.0 * jax.random.uniform(next(ks), (L, N_DIR, D_RWKV), F32),
        "w_b": nrm((L, N_DIR, LORA_W, D_RWKV), 0.1),
        "a0": nrm((L, N_DIR, D_RWKV), 0.5),
        "a_b": nrm((L, N_DIR, LORA_A, D_RWKV), LORA_A ** -0.5),
        "g_b": nrm((L, LORA_G, D_RWKV), LORA_G ** -0.5),
        "k_k": 0.85 + nrm((L, D_RWKV), 0.1),
        "k_a": 1.0 + nrm((L, D_RWKV), 0.1),
        "r_k": nrm((L, N_RWKV_HEADS, RWKV_HEAD), 0.1),
        "gn_g": 1.0 + nrm((L, D_RWKV), 0.1),
        "gn_b": nrm((L, D_RWKV), 0.01),
        "q_norm_g": 1.0 + nrm((L, Q_LORA), 0.1),
        "w_uq": nrm((L, Q_LORA, N_MLA_HEADS * QK_DIM), Q_LORA ** -0.5),
        "kv_norm_g": 1.0 + nrm((L, KV_LORA), 0.1),
        "w_ukv": nrm((L, KV_LORA, N_MLA_HEADS * (QK_NOPE + V_HEAD)), KV_LORA ** -0.5),
        "w_o": nrm((L, D_MIX, D_MODEL), DEEPNORM_BETA * D_MIX ** -0.5),
        "ln1_g": 1.0 + nrm((L, D_MODEL), 0.1),
        "ln1_b": nrm((L, D_MODEL), 0.01),
        "w_up": nrm((L, D_MODEL, 2 * D_FF), D_MODEL ** -0.5),
        "ffn_conv_w": center_tap + nrm((L, CONV_W, 2 * D_FF), 0.3),
        "ffn_conv_b": nrm((L, 2 * D_FF), 0.01),
        "w_down": nrm((L, D_FF, D_MODEL), DEEPNORM_BETA * D_FF ** -0.5),
        "ln2_g": 1.0 + nrm((L, D_MODEL), 0.1),
        "ln2_b": nrm((L, D_MODEL), 0.01),
    }


def reference(x, c, ctx, c_ctx, w_ada, b_ada, w_in, rwkv_conv, w0, w_b, a0, a_b, g_b, k_k, k_a, r_k,
              gn_g, gn_b, q_norm_g, w_uq, kv_norm_g, w_ukv, w_o, ln1_g, ln1_b, w_up, ffn_conv_w,
              ffn_conv_b, w_down, ln2_g, ln2_b):
    alpha = DEEPNORM_ALPHA
    cos, sin = axial_rope_tables(x.shape[1])
    for l in range(DEPTH):
        need_ctx = l < DEPTH - 1
        mod_lat = jax.nn.silu(c) @ w_ada[l] + b_ada[l]
        mod_ctx = jax.nn.silu(c_ctx) @ w_ada[l] + b_ada[l]
        sh1, sc1, g1, sh2, sc2, g2 = jnp.split(mod_lat[:, None, :], 6, axis=-1)
        csh1, csc1, cg1, csh2, csc2, cg2 = jnp.split(mod_ctx, 6, axis=-1)
        m_lat, m_ctx = token_mixer(
            x * (1.0 + sc1) + sh1, ctx * (1.0 + csc1) + csh1, cos, sin, need_ctx,
            w_in[l], rwkv_conv[l], w0[l], w_b[l], a0[l], a_b[l], g_b[l], k_k[l], k_a[l], r_k[l],
            gn_g[l], gn_b[l], q_norm_g[l], w_uq[l], kv_norm_g[l], w_ukv[l], w_o[l])
        x = layer_norm(alpha * x + g1 * m_lat, ln1_g[l], ln1_b[l])
        f_lat = conv_ffn(x * (1.0 + sc2) + sh2, w_up[l], ffn_conv_w[l], ffn_conv_b[l], w_down[l])
        x = layer_norm(alpha * x + g2 * f_lat, ln2_g[l], ln2_b[l])
        if need_ctx:
            ctx = layer_norm(alpha * ctx + cg1 * m_ctx, ln1_g[l], ln1_b[l])
            f_ctx = conv_ffn(ctx * (1.0 + csc2) + csh2, w_up[l], ffn_conv_w[l], ffn_conv_b[l], w_down[l])
            ctx = layer_norm(alpha * ctx + cg2 * f_ctx, ln2_g[l], ln2_b[l])
    return x
```

```python
import numpy as np
import ml_dtypes
from contextlib import ExitStack
import concourse.bass as bass
import concourse.mybir as mybir
from concourse.bass_utils import run_bass_kernel_spmd

F32 = mybir.dt.float32
BF16 = mybir.dt.bfloat16
AF = mybir.ActivationFunctionType
ALU = mybir.AluOpType
AX = mybir.AxisListType

D = 1024
SEQ = 4096
CTXL = 256
NT = SEQ + CTXL
NCOL = NT + 3
NTILE = NT // 128
DEPTH = 2
DFF = 2816
RW = 1920
INC = 2592
ALPHA = (2 * DEPTH) ** 0.25
LN_EPS = 1e-5
RMS_EPS = 1e-6
GN_EPS = 64e-5
CW = float(np.exp(-0.5))
QSCALE = 96 ** -0.5
BLOCKS = [(0, 256)] + [(256 + 512 * j, 512) for j in range(8)]


def tcol(i):
    return 128 * i + (1 if i < 2 else 2)


def gcol(g):
    return g + (1 if g < 256 else 2)


class Dep:
    __slots__ = ("w", "r", "x")

    def __init__(self, x=False):
        self.w = None
        self.r = {}
        self.x = x


class T:
    __slots__ = ("t", "d")

    def __init__(self, t, d=None):
        self.t = t
        self.d = d if d is not None else Dep()


class KB:
    NDMA = 8

    def __init__(self, nc, es):
        self.nc = nc
        self.engs = {"pe": nc.tensor, "act": nc.scalar, "dve": nc.vector,
                     "pool": nc.gpsimd, "sp": nc.sync}
        self.sems = {}
        self.cnt = {}
        for e in self.engs:
            self.sems[e] = es.enter_context(nc.semaphore("s_" + e))
            self.cnt[e] = 0
        self.dq = {}
        for q in ("sp", "pool", "act"):
            self.dq[q] = 0
            for j in range(self.NDMA):
                key = "d_%s%d" % (q, j)
                self.sems[key] = es.enter_context(nc.semaphore(key))
                self.cnt[key] = 0
        self.known = {e: {} for e in self.engs}
        self.pending = {e: False for e in self.engs}
        self.nwait = 0
        self.nins = 0
        self.halt = False
        self.banks = []
        self.banks_bf = []
        self.bdep = []
        for b in range(8):
            t = es.enter_context(nc.psum_tensor("psb%d" % b, [128, 512], F32))
            self.banks.append(t)
            self.banks_bf.append(t.bitcast(BF16))
            bd = Dep(x=True)
            self.bdep.append((bd, bd))
        self.ev_i = 0
        self.sb_i = 0

    def _need(self, reads, writes):
        need = {}
        for d in reads:
            if d.w is not None:
                k, v = d.w
                if need.get(k, 0) < v:
                    need[k] = v
        for d in writes:
            if d.w is not None:
                k, v = d.w
                if need.get(k, 0) < v:
                    need[k] = v
            for k, v in d.r.items():
                if need.get(k, 0) < v:
                    need[k] = v
        return need

    def _waits(self, e, need, skip_own=False):
        eng = self.engs[e]
        kn = self.known[e]
        for k, v in need.items():
            if skip_own and k == e:
                continue
            if kn.get(k, 0) < v:
                eng.wait_ge(self.sems[k], v)
                kn[k] = v
                self.nwait += 1

    def _mark(self, tok, reads, writes):
        k, v = tok
        for d in reads:
            if d.r.get(k, 0) < v:
                d.r[k] = v
        for d in writes:
            d.w = tok
            d.r = {}

    def op(self, e, fn, reads=(), writes=(), inc=True):
        if self.halt:
            return None
        reads = [x.d if isinstance(x, T) else x for x in reads]
        writes = [x.d if isinstance(x, T) else x for x in writes]
        xs = [d for d in reads if d.x]
        if xs:
            reads = [d for d in reads if not d.x]
            writes = writes + xs
        need = self._need(reads, writes)
        self._waits(e, need, skip_own=(e == "pe"))
        ins = fn(self.engs[e])
        self.nins += 1
        if inc:
            self.cnt[e] += 1
            ins.then_inc(self.sems[e], 1)
            tok = (e, self.cnt[e])
            self.pending[e] = False
        else:
            tok = (e, self.cnt[e] + 1)
            self.pending[e] = True
        self._mark(tok, reads, writes)
        return tok

    def dma(self, q, out, in_, reads=(), writes=(), **kw):
        if self.halt:
            return None
        reads = [x.d if isinstance(x, T) else x for x in reads]
        writes = [x.d if isinstance(x, T) else x for x in writes]
        j = self.dq[q] % self.NDMA
        self.dq[q] += 1
        key = "d_%s%d" % (q, j)
        need = self._need(reads, writes)
        if self.cnt[key] > 0:
            need[key] = max(need.get(key, 0), self.cnt[key])
        self._waits(q, need)
        ins = self.engs[q].dma_start(out=out, in_=in_, **kw)
        self.nins += 1
        self.cnt[key] += 16
        ins.then_inc(self.sems[key], 16)
        tok = (key, self.cnt[key])
        self._mark(tok, reads, writes)
        return tok

    def barrier(self, engines=("pe", "act", "dve", "pool", "sp")):
        if self.halt:
            return
        for e in self.engs:
            assert not self.pending[e], e
        need = {k: v for k, v in self.cnt.items() if v > 0}
        for e in engines:
            self._waits(e, dict(need))

    def tile(self, es, shape, dtype, name):
        self.tile_i = getattr(self, "tile_i", 0) + 1
        return T(es.enter_context(self.nc.sbuf_tensor("%s_%d" % (name, self.tile_i), list(shape), dtype)))

    def ev_eng(self):
        self.ev_i += 1
        return "act" if self.ev_i % 2 else "dve"

    def sb_eng(self):
        self.sb_i += 1
        return "pool" if self.sb_i % 2 else "dve"

    def copy(self, e, out, in_, reads, writes, scale=None):
        if e == "act":
            if scale is None:
                return self.op("act", lambda g: g.copy(out, in_), reads, writes)
            return self.op("act", lambda g: g.mul(out, in_, scale), reads, writes)
        if scale is None:
            return self.op(e, lambda g: g.tensor_copy(out, in_), reads, writes)
        return self.op(e, lambda g: g.tensor_scalar_mul(out, in_, scale), reads, writes)


class PsPool:
    def __init__(self, kb, banks):
        self.kb = kb
        self.halves = [(b, h) for b in banks for h in (0, 1)]
        self.i = 0

    def get(self, ncols):
        n = len(self.halves)
        if ncols <= 256:
            b, h = self.halves[self.i % n]
            self.i += 1
            return b, h * 256, [self.kb.bdep[b][h]]
        if self.i % 2:
            self.i += 1
        b, _ = self.halves[self.i % n]
        self.i += 2
        return b, 0, [self.kb.bdep[b][0], self.kb.bdep[b][1]]


def make_consts():
    c = {}
    c["identf"] = np.eye(128, dtype=np.float32)
    c["onesf"] = np.ones((128, 128), dtype=np.float32)
    t = np.arange(128)
    m4 = np.zeros((2, 128, 512), np.float32)
    mt = np.zeros((2, 128, 128), np.float32)
    tri = np.zeros((2, 3, 128, 128), np.float32)
    for d in range(2):
        before = (t[:, None] < t[None, :]) if d == 0 else (t[:, None] > t[None, :])
        beq = before | (t[:, None] == t[None, :])
        m4[d, :, 0:128] = -1.0 * before
        m4[d, :, 128:256] = beq
        m4[d, :, 256:384] = before
        m4[d, :, 384:512] = beq
        mt[d] = -1.0 * before.T
        tri[d, 0] = -CW * beq
        tri[d, 1] = -CW * before
        tri[d, 2] = -CW * (~beq)
    c["mask4"] = m4
    c["maskt"] = mt
    c["tri"] = tri
    c["cvec"] = np.full((128, 1), -CW, np.float32)
    esel = np.zeros((65, 64), np.float32)
    esel[64, :] = 1.0
    c["esel"] = esel
    n_pairs = 8
    inv = 10000.0 ** (-np.arange(n_pairs, dtype=np.float32) / n_pairs)
    row = np.repeat(np.arange(SEQ // 64, dtype=np.float32), 64)
    col = np.tile(np.arange(64, dtype=np.float32), SEQ // 64)
    ang = np.concatenate([row[:, None] * inv, col[:, None] * inv], axis=-1).astype(np.float32)
    cos = np.cos(ang).astype(np.float32)
    sin = np.sin(ang).astype(np.float32)
    cos2 = np.repeat(cos, 2, axis=1).T
    sin2 = np.repeat(sin, 2, axis=1).T
    cosq = np.ones((96, NT), np.float32)
    sinq = np.zeros((96, NT), np.float32)
    cosq[64:96, CTXL:] = cos2
    sinq[64:96, CTXL:] = sin2
    c["cosk"] = cosq.copy()
    c["sink"] = sinq.copy()
    c["cosq"] = (cosq * QSCALE).astype(np.float32)
    c["sinq"] = (sinq * QSCALE).astype(np.float32)
    return c


CONST_SHAPES = {k: v.shape for k, v in make_consts().items()}

W_SHAPES = {
    "w_ada": (2, 1024, 6144), "b_ada": (2, 6144), "w_in": (2, 1024, 2592), "rwkv_conv": (2, 3, 1920),
    "w0": (2, 2, 512), "w_b": (2, 2, 64, 512), "a0": (2, 2, 512), "a_b": (2, 2, 64, 512),
    "g_b": (2, 128, 512), "k_k": (2, 512), "k_a": (2, 512), "r_k": (2, 8, 64), "gn_g": (2, 512),
    "gn_b": (2, 512), "q_norm_g": (2, 384), "w_uq": (2, 384, 768), "kv_norm_g": (2, 256),
    "w_ukv": (2, 256, 1024), "w_o": (2, 1024, 1024), "ln1_g": (2, 1024), "ln1_b": (2, 1024),
    "w_up": (2, 1024, 5632), "ffn_conv_w": (2, 3, 5632), "ffn_conv_b": (2, 5632),
    "w_down": (2, 2816, 1024), "ln2_g": (2, 1024), "ln2_b": (2, 1024),
}


class Env:
    pass


def load_vec_fm(kb, env, es, vec_ap, n, name):
    nc = kb.nc
    rows = kb.tile(es, [n, 128], F32, name + "_r")
    out = kb.tile(es, [128, n], F32, name)
    kb.dma("sp", rows.t[:], vec_ap.rearrange("(n p) -> n p", p=128), writes=[rows])
    b, c0, deps = env.pp.get(n)
    ps = kb.banks[b]
    kb.op("pe", lambda e: e.transpose(ps[:, c0:c0 + n], rows.t[:], env.identf.t[0:n, 0:n]),
          reads=[rows, env.identf], writes=deps)
    kb.copy("dve", out.t[:], ps[:, c0:c0 + n], deps, [out])
    return out


def load_bcast(kb, es, vec_ap, n, name):
    out = kb.tile(es, [128, n], F32, name)
    kb.dma("sp", out.t[:], vec_ap.partition_broadcast(128), writes=[out])
    return out


def load_w_bf16(kb, es, dst, w_ap, kchunks, ncols, stage, col_off=0, scale=None):
    wv = w_ap.rearrange("(k p) n -> k p n", p=128)
    for k in range(kchunks):
        st = stage[k % len(stage)]
        kb.dma("sp", st.t[:, 0:ncols], wv[k], writes=[st])
        e = ("act", "dve", "pool")[k % 3]
        if scale is None:
            kb.copy(e, dst.t[:, k, col_off:col_off + ncols], st.t[:, 0:ncols], [st], [dst])
        else:
            kb.op("dve", lambda g: g.tensor_scalar_mul(dst.t[:, k, col_off:col_off + ncols], st.t[:, 0:ncols], scale.t[:, k:k + 1]),
                  [st, scale], [dst])


def emit_ut(kb, env, xt, s, modf, w_scale, w_shift, dst, i, pad):
    c0 = tcol(i)
    st = env.ut_st[env.ut_i % 2]
    env.ut_i += 1
    for half in range(2):
        b, _, deps = env.pp.get(512)
        ps = kb.banks[b]
        for j in range(4):
            c = half * 4 + j
            kb.op("pe", lambda e: e.transpose(ps[:, j * 128:(j + 1) * 128], xt.t[:, c * 128:(c + 1) * 128], env.identf.t[:]),
                  reads=[xt, env.identf], writes=deps, inc=(j == 3))
        for j in range(4):
            c = half * 4 + j
            kb.op("act", lambda e: e.activation(st.t[:, c, 1:129], ps[:, j * 128:(j + 1) * 128], AF.Identity,
                                                bias=modf.t[:, w_shift * 8 + c, s:s + 1], scale=modf.t[:, w_scale * 8 + c, s:s + 1]),
                  reads=deps + [modf], writes=[st])
    lo, hi = 1, 129
    if pad and i in (0, 2):
        lo = 0
    if pad and i in (1, NTILE - 1):
        hi = 130
    kb.dma("pool", dst[:, c0 - 1 + lo:c0 - 1 + hi].rearrange("(k p) c -> p k c", p=128), st.t[:, :, lo:hi],
           reads=[st], writes=[env.dd[dst.tensor.name]])


def phase_mod(kb, env, l):
    nc = kb.nc
    with ExitStack() as es:
        modf = env.modf[l]
        crow = kb.tile(es, [16, 128], F32, "crow")
        kb.dma("sp", crow.t[0:8, :], env.c.rearrange("(n p) -> n p", p=128), writes=[crow])
        kb.dma("sp", crow.t[8:16, :], env.c_ctx.rearrange("(n p) -> n p", p=128), writes=[crow])
        sct = kb.tile(es, [128, 2, 8], F32, "sct")
        b, c0, deps = env.pp.get(16)
        ps = kb.banks[b]
        kb.op("pe", lambda e: e.transpose(ps[:, c0:c0 + 16], crow.t[:], env.identf.t[0:16, 0:16]), [crow, env.identf], deps)
        kb.op("act", lambda e: e.activation(sct.t[:].rearrange("p s k -> p (s k)"), ps[:, c0:c0 + 16], AF.Silu), deps, [sct])
        bF = load_vec_fm(kb, env, es, env.w["b_ada"][l], 48, "bF")
        stg = [kb.tile(es, [128, 8, 768], F32, "wada%d" % i) for i in range(2)]
        bq, cq0, depq = env.pp.get(96)
        psq = kb.banks[bq]
        wv = env.w["w_ada"][l].rearrange("(k p) n -> p k n", p=128)
        for pc in range(8):
            st = stg[pc % 2]
            kb.dma("sp", st.t[:], wv[:, :, pc * 768:(pc + 1) * 768], writes=[st])
            for jj in range(6):
                j = pc * 6 + jj
                for k in range(8):
                    kb.op("pe", lambda e: e.matmul(psq[:, cq0 + 2 * j:cq0 + 2 * j + 2], st.t[:, k, jj * 128:(jj + 1) * 128], sct.t[:, :, k],
                                                  start=(k == 0), stop=(k == 7)),
                          [st, sct], depq, inc=(k == 7))
        kb.op("dve", lambda e: e.tensor_tensor(modf.t[:], psq[:, cq0:cq0 + 96].rearrange("p (j s) -> p j s", s=2),
                                              bF.t[:].unsqueeze(2).to_broadcast([128, 48, 2]), ALU.add),
              depq + [bF], [modf])
        for wch in (1, 4):
            kb.op("dve", lambda e: e.tensor_scalar_add(modf.t[:, wch * 8:(wch + 1) * 8, :], modf.t[:, wch * 8:(wch + 1) * 8, :], 1.0),
                  [modf], [modf])
        grow = kb.tile(es, [1, 2, 2, 1024], F32, "grow")
        brow = kb.tile(es, [1, 2, 1024], F32, "brow")
        for gi, wch in enumerate((2, 5)):
            kb.dma("sp", brow.t[:, gi, :], env.w["b_ada"][l][wch * 1024:(wch + 1) * 1024].rearrange("(o n) -> o n", o=1), writes=[brow])
        for gi, wch in enumerate((2, 5)):
            st = stg[gi % 2]
            for hh in range(2):
                col = wch * 1024 + hh * 512
                kb.dma("sp", st.t[:, :, 0:512], wv[:, :, col:col + 512], writes=[st])
                for s in range(2):
                    b2, c2, dep2 = env.pp.get(512)
                    ps2 = kb.banks[b2]
                    for k in range(8):
                        kb.op("pe", lambda e: e.matmul(ps2[0:1, :], sct.t[:, s, k:k + 1], st.t[:, k, 0:512], start=(k == 0), stop=(k == 7)),
                              [st, sct], dep2, inc=(k == 7))
                    kb.op("dve", lambda e: e.tensor_tensor(grow.t[:, s, gi, hh * 512:(hh + 1) * 512], ps2[0:1, :], brow.t[:, gi, hh * 512:(hh + 1) * 512], ALU.add),
                          dep2 + [brow], [grow])
        kb.dma("sp", env.MODROW[l:l + 1].rearrange("o s g n -> o (s g n)"), grow.t[:].rearrange("o s g n -> o (s g n)"),
               reads=[grow], writes=[env.dd["MODROW"]])
        kb.barrier()


def xin_ap(env, l, i):
    if l == 0:
        return env.ctx[i * 128:(i + 1) * 128, :] if i < 2 else env.x[(i - 2) * 128:(i - 1) * 128, :]
    return env.X2[i * 128:(i + 1) * 128, :]


def phase_u0(kb, env):
    with ExitStack() as es:
        env.ut_st = [kb.tile(es, [128, 8, 130], BF16, "utst%d" % i) for i in range(2)]
        env.ut_i = 0
        xts = [kb.tile(es, [128, 1024], F32, "xt%d" % i) for i in range(3)]
        for i in range(NTILE):
            xt = xts[i % 3]
            kb.dma("sp", xt.t[:], xin_ap(env, 0, i), writes=[xt])
            emit_ut(kb, env, xt, 1 if i < 2 else 0, env.modf[0], 1, 0, env.UT, i, pad=False)
        kb.barrier()


def phase_p1(kb, env, l):
    nc = kb.nc
    with ExitStack() as es:
        win = kb.tile(es, [128, 8, 2624], BF16, "win")
        stage = [kb.tile(es, [128, 2592], F32, "wst%d" % i) for i in range(2)]
        load_w_bf16(kb, es, win, env.w["w_in"][l], 8, 2592, stage)
        kb.op("dve", lambda e: e.tensor_scalar_mul(win.t[:, :, 2592:2624:2], win.t[:, :, 2561:2592:2], -1.0), [win], [win])
        kb.op("dve", lambda e: e.tensor_copy(win.t[:, :, 2593:2624:2], win.t[:, :, 2560:2592:2]), [win], [win])
        utb = [kb.tile(es, [128, 8, 512], BF16, "utb%d" % i) for i in range(2)]
        pst = [kb.tile(es, [128, 15, 514], BF16, "pst%d" % i) for i in range(2)]
        pmst = [kb.tile(es, [128, 6, 512], F32, "pmst%d" % i) for i in range(2)]
        for t in pst:
            kb.op("pool", lambda e: e.memset(t.t[:], 0.0), [], [t])
        for bi, (g0, n) in enumerate(BLOCKS):
            c0 = gcol(g0)
            ub = utb[bi % 2]
            ps_ = pst[bi % 2]
            pm_ = pmst[bi % 2]
            kb.dma("sp", ub.t[:, :, 0:n], env.UT[:, c0:c0 + n].rearrange("(k p) c -> p k c", p=128),
                   reads=[env.dd["UT"]], writes=[ub])
            for jf in range(21):
                rows = 128 if jf < 20 else 64
                b, _, deps = env.pp.get(512)
                ps = kb.banks[b]
                for k in range(8):
                    kb.op("pe", lambda e: e.matmul(ps[0:rows, 0:n], win.t[:, k, jf * 128:jf * 128 + rows], ub.t[:, k, 0:n],
                                                  start=(k == 0), stop=(k == 7)),
                          [win, ub], deps, inc=(k == 7))
                if jf < 15:
                    kb.copy(kb.ev_eng(), ps_.t[:, jf, 1:1 + n], ps[:, 0:n], deps, [ps_])
                else:
                    kb.copy(kb.ev_eng(), pm_.t[0:rows, jf - 15, 0:n], ps[0:rows, 0:n], deps, [pm_])
            lo, hi = 1, 1 + n
            if bi == 0:
                lo, hi = 0, n + 2
            if bi == len(BLOCKS) - 1:
                hi = n + 2
            kb.dma("pool", env.PT[:, c0 - 1 + lo:c0 - 1 + hi].rearrange("(k p) c -> p k c", p=128), ps_.t[:, :, lo:hi],
                   reads=[ps_], writes=[env.dd["PT"]])
            kb.dma("pool", env.PM[0:640, c0:c0 + n].rearrange("(k p) c -> p k c", p=128), pm_.t[:, 0:5, 0:n],
                   reads=[pm_], writes=[env.dd["PM"]])
            kb.dma("pool", env.PM[640:704, c0:c0 + n], pm_.t[0:64, 5, 0:n], reads=[pm_], writes=[env.dd["PM"]])
        kb.barrier()


SCRATCH = {
    "UT": ([1024, NCOL], BF16), "PT": ([RW, NCOL], BF16), "PM": ([704, NCOL], F32),
    "MIXT": ([1024, NCOL], BF16), "X1": ([NT, 1024], F32), "U2T": ([1024, NCOL], BF16),
    "GT": ([DFF, NCOL], BF16), "X2": ([NT, 1024], F32), "YF": ([NT, 512], F32),
    "MODROW": ([2, 2, 2, 1024], F32), "DBG": ([16, 128, 512], F32),
}


def build(phases=None, debug_out=(), debug_in=(), stop=None):
    nc = bass.Bass("TRN2", target_bir_lowering=False)
    env = Env()
    env.x = nc.dram_tensor("x", [SEQ, D], F32, kind="ExternalInput").ap()
    env.c = nc.dram_tensor("c", [D], F32, kind="ExternalInput").ap()
    env.ctx = nc.dram_tensor("ctx", [CTXL, D], F32, kind="ExternalInput").ap()
    env.c_ctx = nc.dram_tensor("c_ctx", [D], F32, kind="ExternalInput").ap()
    env.w = {k: nc.dram_tensor(k, list(s), F32, kind="ExternalInput").ap() for k, s in W_SHAPES.items()}
    env.cst = {k: nc.dram_tensor("k_" + k, list(s), F32, kind="ExternalInput").ap() for k, s in CONST_SHAPES.items()}
    env.y = nc.dram_tensor("y", [SEQ, D], F32, kind="ExternalOutput").ap()
    env.dd = {}
    for name, (shape, dt_) in SCRATCH.items():
        kind = "ExternalOutput" if name in debug_out else ("ExternalInput" if name in debug_in else "Internal")
        setattr(env, name, nc.dram_tensor(name, shape, dt_, kind=kind).ap())
        env.dd[name] = Dep()
    env.dd["y"] = Dep()
    env.stop = stop
    allp = ["mod", "u0"]
    for l in range(DEPTH):
        allp += ["p1_%d" % l, "mla_%d" % l, "rwkv_%d" % l, "wo_%d" % l, "ffu_%d" % l, "ffd_%d" % l]
    if phases is None:
        phases = allp
    with ExitStack() as es:
        kb = KB(nc, es)
        env.pp = PsPool(kb, list(range(8)))
        env.identf = kb.tile(es, [128, 128], F32, "identf")
        kb.dma("sp", env.identf.t[:], env.cst["identf"], writes=[env.identf])
        env.identb = kb.tile(es, [128, 128], BF16, "identb")
        kb.copy("dve", env.identb.t[:], env.identf.t[:], [env.identf], [env.identb])
        env.onesf = kb.tile(es, [128, 128], F32, "onesf")
        kb.dma("sp", env.onesf.t[:], env.cst["onesf"], writes=[env.onesf])
        env.modf = [kb.tile(es, [128, 48, 2], F32, "modf%d" % l) for l in range(DEPTH)]
        env.epsr = kb.tile(es, [128, 1], F32, "epsr")
        kb.op("pool", lambda e: e.memset(env.epsr.t[:], RMS_EPS), [], [env.epsr])
        env.epsl = kb.tile(es, [128, 1], F32, "epsl")
        kb.op("pool", lambda e: e.memset(env.epsl.t[:], LN_EPS), [], [env.epsl])
        env.epsg = kb.tile(es, [128, 1], F32, "epsg")
        kb.op("pool", lambda e: e.memset(env.epsg.t[:], GN_EPS), [], [env.epsg])
        for ph in phases:
            if ph == "mod":
                for l in range(DEPTH):
                    phase_mod(kb, env, l)
            elif ph == "u0":
                phase_u0(kb, env)
            else:
                name, l = ph.rsplit("_", 1)
                try:
                    PHASES[name](kb, env, int(l))
                except StopPhase:
                    print("stopped at checkpoint", env.stop, flush=True)
                    break
        kb.barrier()
        env.kb = kb
    print("built: %d instructions, %d waits" % (kb.nins, kb.nwait), flush=True)
    return nc, env


PHASES = {"p1": phase_p1}


def phase_mla(kb, env, l):
    nc = kb.nc
    need_ctx = l < DEPTH - 1
    env.pp = PsPool(kb, [0, 1, 2, 3, 4])
    pp = env.pp
    with ExitStack() as es:
        cqn = kb.tile(es, [128, 3, NT], BF16, "cqn")
        ckvn = kb.tile(es, [128, 2, NT], BF16, "ckvn")
        va = kb.tile(es, [128, NTILE, 8, 65], BF16, "va")
        KT = [kb.tile(es, [128, NT], BF16, "kt%d" % i) for i in range(2)]
        wq2 = kb.tile(es, [128, 3, 8, 2, 96], BF16, "wq2")
        wk = kb.tile(es, [128, 2, 512], BF16, "wk")
        wv = kb.tile(es, [128, 2, 512], BF16, "wv")
        esel = kb.tile(es, [65, 64], F32, "esel")
        kb.dma("sp", esel.t[:], env.cst["esel"], writes=[esel])
        with ExitStack() as es2:
            qg = load_vec_fm(kb, env, es2, env.w["q_norm_g"][l], 3, "qg")
            kg = load_vec_fm(kb, env, es2, env.w["kv_norm_g"][l], 2, "kg")
            wq_st = kb.tile(es2, [128, 3, 768], F32, "wq_st")
            wkv_st = kb.tile(es2, [128, 2, 1024], F32, "wkv_st")
            kb.dma("sp", wq_st.t[:], env.w["w_uq"][l].rearrange("(k p) n -> p k n", p=128), writes=[wq_st])
            kb.dma("sp", wkv_st.t[:], env.w["w_ukv"][l].rearrange("(k p) n -> p k n", p=128), writes=[wkv_st])
            kb.op("pool", lambda e: e.memset(wq2.t[:], 0.0), [], [wq2])
            kb.op("pool", lambda e: e.memset(va.t[:], 1.0), [], [va])
            for k in range(3):
                kb.op("dve", lambda e: e.tensor_scalar_mul(wq2.t[:, k, :, 0, :], wq_st.t[:, k, :].rearrange("p (h d) -> p h d", d=96), qg.t[:, k:k + 1]),
                      [wq_st, qg], [wq2])
                kb.op("dve", lambda e: e.tensor_scalar_mul(wq2.t[:, k, :, 1, 64:96:2], wq2.t[:, k, :, 0, 65:96:2], -1.0), [wq2], [wq2])
                kb.op("dve", lambda e: e.tensor_copy(wq2.t[:, k, :, 1, 65:96:2], wq2.t[:, k, :, 0, 64:96:2]), [wq2], [wq2])
            for k in range(2):
                src = wkv_st.t[:, k, :].rearrange("p (h e) -> p h e", e=128)
                kb.op("dve", lambda e: e.tensor_scalar_mul(wk.t[:, k, :].rearrange("p (h d) -> p h d", d=64), src[:, :, 0:64], kg.t[:, k:k + 1]),
                      [wkv_st, kg], [wk])
                kb.op("dve", lambda e: e.tensor_scalar_mul(wv.t[:, k, :].rearrange("p (h d) -> p h d", d=64), src[:, :, 64:128], kg.t[:, k:k + 1]),
                      [wkv_st, kg], [wv])
            cs = [kb.tile(es2, [128, 5, 512], F32, "cs%d" % i) for i in range(2)]
            sq = [kb.tile(es2, [128, 5, 512], F32, "sq%d" % i) for i in range(2)]
            rsd = [kb.tile(es2, [128, 2, 512], F32, "rsd%d" % i) for i in range(2)]
            krt = [kb.tile(es2, [128, 4, 512], F32, "krt%d" % i) for i in range(2)]
            krm = [kb.tile(es2, [128, 2, 512], F32, "krm%d" % i) for i in range(2)]
            for bi, (g0, n) in enumerate(BLOCKS):
                c0 = gcol(g0)
                c_, s_, r_, kr_, km_ = cs[bi % 2], sq[bi % 2], rsd[bi % 2], krt[bi % 2], krm[bi % 2]
                kb.dma("sp", c_.t[:, :, 0:n], env.PM[0:640, c0:c0 + n].rearrange("(k p) c -> p k c", p=128), reads=[env.dd["PM"]], writes=[c_])
                kb.op("act", lambda e: e.activation(s_.t[:, :, 0:n], c_.t[:, :, 0:n], AF.Square), [c_], [s_])
                for wi, (k0, k1, dim) in enumerate(((0, 3, 384.0), (3, 5, 256.0))):
                    b, _, deps = pp.get(512)
                    ps = kb.banks[b]
                    for k in range(k0, k1):
                        kb.op("pe", lambda e: e.matmul(ps[:, 0:n], env.onesf.t[:], s_.t[:, k, 0:n], start=(k == k0), stop=(k == k1 - 1)),
                              [env.onesf, s_], deps, inc=(k == k1 - 1))
                    kb.op("act", lambda e: e.activation(r_.t[:, wi, 0:n], ps[:, 0:n], AF.Sqrt, bias=env.epsr.t[:, 0:1], scale=1.0 / dim), deps + [env.epsr], [r_])
                    kb.op("dve", lambda e: e.reciprocal(r_.t[:, wi, 0:n], r_.t[:, wi, 0:n]), [r_], [r_])
                    dst = cqn if wi == 0 else ckvn
                    for k in range(k0, k1):
                        kb.op(kb.sb_eng(), lambda e: e.tensor_tensor(dst.t[:, k - k0, g0:g0 + n], c_.t[:, k, 0:n], r_.t[:, wi, 0:n], ALU.mult),
                              [c_, r_], [dst])
                kb.dma("sp", kr_.t[64:96, 0, 0:n], env.PM[640:672, c0:c0 + n], reads=[env.dd["PM"]], writes=[kr_])
                kb.dma("sp", kr_.t[64:96, 1, 0:n], env.PM[672:704, c0:c0 + n], reads=[env.dd["PM"]], writes=[kr_])
                kb.dma("sp", kr_.t[64:96, 2, 0:n], env.cst["cosk"][64:96, g0:g0 + n], writes=[kr_])
                kb.dma("sp", kr_.t[64:96, 3, 0:n], env.cst["sink"][64:96, g0:g0 + n], writes=[kr_])
                kb.op("pool", lambda e: e.tensor_tensor(km_.t[64:96, 0, 0:n], kr_.t[64:96, 0, 0:n], kr_.t[64:96, 2, 0:n], ALU.mult), [kr_], [km_])
                kb.op("dve", lambda e: e.tensor_tensor(km_.t[64:96, 1, 0:n], kr_.t[64:96, 1, 0:n], kr_.t[64:96, 3, 0:n], ALU.mult), [kr_], [km_])
                kb.op("pool", lambda e: e.tensor_tensor(KT[0].t[64:96, g0:g0 + n], km_.t[64:96, 0, 0:n], km_.t[64:96, 1, 0:n], ALU.add), [km_], [KT[0]])
            kb.op("pool", lambda e: e.tensor_copy(KT[1].t[64:96, :], KT[0].t[64:96, :]), [KT[0]], [KT[1]])
            for i in range(NTILE):
                b, _, deps = pp.get(512)
                ps = kb.banks[b]
                for k in range(2):
                    kb.op("pe", lambda e: e.matmul(ps[:, :], ckvn.t[:, k, i * 128:(i + 1) * 128], wv.t[:, k, :], start=(k == 0), stop=(k == 1)),
                          [ckvn, wv], deps, inc=(k == 1))
                kb.copy(kb.ev_eng(), va.t[:, i, :, 0:64], ps[:, :].rearrange("p (h d) -> p h d", d=64), deps, [va])
            kb.barrier()
        QT = [kb.tile(es, [128, NT], BF16, "qt%d" % i) for i in range(2)]
        NEGM = [kb.tile(es, [128, 1], F32, "negm%d" % i) for i in range(2)]
        tabs = [kb.tile(es, [128, 2, 512], F32, "tab%d" % i) for i in range(2)]
        tmp = [kb.tile(es, [128, 2, 512], F32, "qtmp%d" % i) for i in range(2)]
        sqq = [kb.tile(es, [128, 512], F32, "sqq%d" % i) for i in range(2)]
        ptt = [kb.tile(es, [128, 512], BF16, "ptt%d" % i) for i in range(6)]
        osb = [kb.tile(es, [128, 512], F32, "osb%d" % i) for i in range(2)]
        rl = [kb.tile(es, [64, 512], F32, "rl%d" % i) for i in range(2)]
        ob = [kb.tile(es, [64, 512], BF16, "ob%d" % i) for i in range(2)]
        nb = kb.tile(es, [1, 2, 16], F32, "nb")
        msc = kb.tile(es, [1, 4], F32, "msc")
        qblocks = BLOCKS if need_ctx else BLOCKS[1:]
        LOOK = 3
        cnt = {"ti": 0, "pi": 0, "qi": 0}

        def setup(h):
            kt = KT[h % 2]
            qt = QT[h % 2]
            negm = NEGM[h % 2]
            for bi, (g0, n) in enumerate(BLOCKS):
                b, _, deps = pp.get(512)
                ps = kb.banks[b]
                for k in range(2):
                    kb.op("pe", lambda e: e.matmul(ps[0:64, 0:n], wk.t[:, k, h * 64:(h + 1) * 64], ckvn.t[:, k, g0:g0 + n], start=(k == 0), stop=(k == 1)),
                          [wk, ckvn], deps, inc=(k == 1))
                kb.copy("dve", kt.t[0:64, g0:g0 + n], ps[0:64, 0:n], deps, [kt])
            for bi, (g0, n) in enumerate(qblocks):
                tb = tabs[cnt["ti"] % 2]
                tm = tmp[cnt["ti"] % 2]
                cnt["ti"] += 1
                kb.dma("sp", tb.t[0:96, 0, 0:n], env.cst["cosq"][:, g0:g0 + n], writes=[tb])
                kb.dma("sp", tb.t[0:96, 1, 0:n], env.cst["sinq"][:, g0:g0 + n], writes=[tb])
                for ab in range(2):
                    b, _, deps = pp.get(512)
                    ps = kb.banks[b]
                    for k in range(3):
                        kb.op("pe", lambda e: e.matmul(ps[0:96, 0:n], wq2.t[:, k, h, ab, :], cqn.t[:, k, g0:g0 + n], start=(k == 0), stop=(k == 2)),
                              [wq2, cqn], deps, inc=(k == 2))
                    kb.op("dve", lambda e: e.tensor_tensor(tm.t[0:96, ab, 0:n], ps[0:96, 0:n], tb.t[0:96, ab, 0:n], ALU.mult), deps + [tb], [tm])
                kb.op("pool", lambda e: e.tensor_tensor(qt.t[0:96, g0:g0 + n], tm.t[0:96, 0, 0:n], tm.t[0:96, 1, 0:n], ALU.add), [tm], [qt])
            for wi, (src, blks) in enumerate(((qt, qblocks), (kt, BLOCKS))):
                for bi, (g0, n) in enumerate(blks):
                    s_ = sqq[(wi + bi) % 2]
                    kb.op("pool", lambda e: e.tensor_tensor(s_.t[0:96, 0:n], src.t[0:96, g0:g0 + n], src.t[0:96, g0:g0 + n], ALU.mult), [src], [s_])
                    b, c0, deps = pp.get(512)
                    ps = kb.banks[b]
                    kb.op("pe", lambda e: e.matmul(ps[0:1, 0:n], env.onesf.t[0:96, 0:1], s_.t[0:96, 0:n], start=True, stop=True), [env.onesf, s_], deps)
                    kb.op("dve", lambda e: e.reduce_max(nb.t[0:1, wi, bi:bi + 1], ps[0:1, 0:n], AX.X), deps, [nb])
                kb.op("dve", lambda e: e.reduce_max(msc.t[0:1, wi:wi + 1], nb.t[0:1, wi, 0:len(blks)], AX.X), [nb], [msc])
            kb.op("dve", lambda e: e.tensor_tensor(msc.t[0:1, 2:3], msc.t[0:1, 0:1], msc.t[0:1, 1:2], ALU.mult), [msc], [msc])
            kb.op("act", lambda e: e.activation(msc.t[0:1, 3:4], msc.t[0:1, 2:3], AF.Sqrt), [msc], [msc])
            kb.op("dve", lambda e: e.tensor_scalar_mul(msc.t[0:1, 3:4], msc.t[0:1, 3:4], -1.0), [msc], [msc])
            b, c0, deps = pp.get(16)
            ps = kb.banks[b]
            kb.op("pe", lambda e: e.matmul(ps[:, c0:c0 + 1], env.onesf.t[0:1, :], msc.t[0:1, 3:4], start=True, stop=True), [env.onesf, msc], deps)
            kb.copy("dve", negm.t[:, 0:1], ps[:, c0:c0 + 1], deps, [negm])

        def attn(h):
            kt = KT[h % 2]
            qt = QT[h % 2]
            negm = NEGM[h % 2]
            items = []
            for bi, (g0, n) in enumerate(qblocks):
                kts = list(range(NTILE)) if g0 >= CTXL else [0, 1]
                qi = cnt["qi"]
                cnt["qi"] += 1
                for ii, ki in enumerate(kts):
                    items.append((g0, n, ii, ki, len(kts), qi))
            inflight = []

            def stage_a(it):
                g0, n, ii, ki, nk, qi = it
                b, _, deps = pp.get(512)
                ps = kb.banks[b]
                kb.op("pe", lambda e: e.matmul(ps[:, 0:n], kt.t[0:96, ki * 128:(ki + 1) * 128], qt.t[0:96, g0:g0 + n], start=True, stop=True),
                      [kt, qt], deps)
                p_ = ptt[cnt["pi"] % len(ptt)]
                cnt["pi"] += 1
                kb.op("act", lambda e: e.activation(p_.t[:, 0:n], ps[:, 0:n], AF.Exp, bias=negm.t[:, 0:1], scale=1.0), deps + [negm], [p_])
                inflight.append(p_)

            def stage_b(it):
                g0, n, ii, ki, nk, qi = it
                p_ = inflight.pop(0)
                pob = 5 + (qi % 2)
                po = kb.banks[pob]
                pod = list(kb.bdep[pob])
                kb.op("pe", lambda e: e.matmul(po[0:65, 0:n], va.t[:, ki, h, :], p_.t[:, 0:n], start=(ii == 0), stop=(ii == nk - 1)),
                      [va, p_], pod, inc=(ii == nk - 1))
                if ii != nk - 1:
                    return
                o_ = osb[qi % 2]
                r_ = rl[qi % 2]
                b_ = ob[qi % 2]
                kb.copy("dve", o_.t[0:65, 0:n], po[0:65, 0:n], pod, [o_])
                b, _, deps = pp.get(512)
                ps = kb.banks[b]
                kb.op("pe", lambda e: e.matmul(ps[0:64, 0:n], esel.t[:, :], o_.t[0:65, 0:n], start=True, stop=True), [esel, o_], deps)
                kb.op("dve", lambda e: e.reciprocal(r_.t[:, 0:n], ps[0:64, 0:n]), deps, [r_])
                kb.op("pool", lambda e: e.tensor_tensor(b_.t[:, 0:n], o_.t[0:64, 0:n], r_.t[:, 0:n], ALU.mult), [o_, r_], [b_])
                c0 = gcol(g0)
                kb.dma("pool", env.MIXT[512 + h * 64:512 + (h + 1) * 64, c0:c0 + n], b_.t[:, 0:n], reads=[b_], writes=[env.dd["MIXT"]])

            for idx in range(len(items) + LOOK):
                if idx < len(items):
                    stage_a(items[idx])
                if idx >= LOOK:
                    stage_b(items[idx - LOOK])

        setup(0)
        for h in range(8):
            if h + 1 < 8:
                setup(h + 1)
            attn(h)
        kb.barrier()
    env.pp = PsPool(kb, list(range(8)))


PHASES["mla"] = phase_mla


NSTEP = 6


def _tt(kb, e, out, in0, in1, op, reads, writes):
    return kb.op(e, lambda g: g.tensor_tensor(out, in0, in1, op), reads, writes)


def _stt(kb, e, out, in0, scalar, in1, op0, op1, reads, writes):
    return kb.op(e, lambda g: g.scalar_tensor_tensor(out, in0, scalar, in1, op0, op1), reads, writes)


def _act(kb, out, in_, func, reads, writes, **kw):
    return kb.op("act", lambda g: g.activation(out, in_, func, **kw), reads, writes)


def bc8(t):
    return t.unsqueeze(2).to_broadcast([128, 8, 64])


def v3(ap):
    return ap.rearrange("p (h d) -> p h d", d=64)


class StopPhase(Exception):
    pass


def ck(kb, env, n, dumps=()):
    if getattr(env, "stop", None) != n:
        return
    for slot, t in dumps:
        if t.t.dtype == BF16:
            n = t.t.shape[-1]
            kb.dma("sp", env.DBG[slot].bitcast(BF16)[0:t.t.shape[0], 0:n], t.t[:], reads=[t], writes=[env.dd["DBG"]])
        else:
            kb.dma("sp", env.DBG[slot][0:t.t.shape[0], 0:t.t.shape[-1]], t.t[:], reads=[t], writes=[env.dd["DBG"]])
    kb.barrier()
    kb.halt = True


def phase_rwkv(kb, env, l):
    nc = kb.nc
    need_ctx = l < DEPTH - 1
    env.pp = PsPool(kb, [0, 1, 2, 3, 4, 5])
    pp = env.pp
    YB, HB = 6, 7
    ybank, hbank = kb.banks[YB], kb.banks[HB]
    ydep, hdep = list(kb.bdep[YB]), list(kb.bdep[HB])
    idb = env.identb
    with ExitStack() as es:
        dg = kb.tile(es, [128, 15, 3, 128], BF16, "dg")
        lw = kb.tile(es, [128, 3, 512], BF16, "lw")
        with ExitStack() as es2:
            for a in range(3):
                cw = load_vec_fm(kb, env, es2, env.w["rwkv_conv"][l][a], 15, "cw%d" % a)
                for j in range(15):
                    kb.op(kb.sb_eng(), lambda e: e.tensor_scalar_mul(dg.t[:, j, a, :], env.identf.t[:], cw.t[:, j:j + 1]), [env.identf, cw], [dg])
            lst = kb.tile(es2, [128, 3, 512], F32, "lst")
            kb.dma("sp", lst.t[:, 0, :], env.w["w_b"][l].rearrange("d r c -> (d r) c"), writes=[lst])
            kb.dma("sp", lst.t[:, 1, :], env.w["a_b"][l].rearrange("d r c -> (d r) c"), writes=[lst])
            kb.dma("sp", lst.t[:, 2, :], env.w["g_b"][l], writes=[lst])
            kb.copy("dve", lw.t[:], lst.t[:], [lst], [lw])
            kb.barrier()
        brow = kb.tile(es, [1, 2, 2, 512], F32, "brow")
        kb.dma("sp", brow.t[:, 0, :, :], env.w["w0"][l].rearrange("(o d) c -> o d c", o=1), writes=[brow])
        kb.dma("sp", brow.t[:, 1, :, :], env.w["a0"][l].rearrange("(o d) c -> o d c", o=1), writes=[brow])
        kkb = load_bcast(kb, es, env.w["k_k"][l], 512, "kkb")
        kab = load_bcast(kb, es, env.w["k_a"][l], 512, "kab")
        rkb = load_bcast(kb, es, env.w["r_k"][l].rearrange("h d -> (h d)"), 512, "rkb")
        gng = load_bcast(kb, es, env.w["gn_g"][l], 512, "gng")
        gnb = load_bcast(kb, es, env.w["gn_b"][l], 512, "gnb")
        m4 = kb.tile(es, [128, 512], F32, "m4")
        mt = kb.tile(es, [128, 128], F32, "mt")
        tri = kb.tile(es, [128, 3, 128], F32, "tri")
        cvec = kb.tile(es, [128, 1], F32, "cvec")
        kb.dma("sp", cvec.t[:], env.cst["cvec"], writes=[cvec])
        hb = kb.tile(es, [64, 512], BF16, "hb")
        pw = [kb.tile(es, [128, 15, 3, 128], BF16, "pw%d" % i) for i in range(2)]
        f32names = ["r", "k", "v", "sw", "a0", "a1", "g", "t1", "sqk", "kk", "am1", "kd0", "kd1", "b", "rk",
                    "eG", "enG", "eGx", "eD", "y", "yf", "u1", "u2"]
        F = {n: kb.tile(es, [128, 512], F32, "f_" + n) for n in f32names}
        FD = [{n: (F[n] if bi_ == 0 else kb.tile(es, [128, 512], F32, "f2_" + n)) for n in ("r", "v", "g", "kd0", "kd1")} for bi_ in range(2)]
        bfnames = ["KKt", "Bt", "Kt", "Rt", "Bh", "Kh", "Vb", "ob"]
        BfD = [{n: kb.tile(es, [128, 512], BF16, "b%d_%s" % (bi_, n)) for n in bfnames} for bi_ in range(2)]
        th = kb.tile(es, [128, 128], BF16, "th")
        alo = kb.tile(es, [128, 128], BF16, "alo")
        sg = kb.tile(es, [128, 128], BF16, "sg")
        s8 = {n: kb.tile(es, [128, 8], F32, "s8_" + n) for n in ("ss", "nrm", "m", "var", "bs")}
        fmaD = [kb.tile(es, [128, 4, 2, 128], BF16, "fma%d" % i) for i in range(2)]
        fmbD = [kb.tile(es, [128, 4, 2, 128], BF16, "fmb%d" % i) for i in range(2)]
        gcfD = [kb.tile(es, [64, 8], F32, "gcf%d" % i) for i in range(2)]
        mst = kb.tile(es, [128, 4, 128], BF16, "mst")
        SB1 = [kb.tile(es, [128, 512], BF16, "sb1_%d" % h) for h in range(8)]
        X0 = [kb.tile(es, [128, 128], BF16, "x0_%d" % h) for h in range(8)]
        XX = [[kb.tile(es, [128, 256], BF16, "xx%d_%d" % (i, h)) for i in range(2)] for h in range(8)]
        ET = [[kb.tile(es, [128, 128], BF16, "et%d_%d" % (i, h)) for i in range(2)] for h in range(8)]
        ZC = [kb.tile(es, [128, 128], BF16, "zc_%d" % h) for h in range(8)]
        TE = [kb.tile(es, [128, 128], F32, "te_%d" % h) for h in range(8)]
        WU = [kb.tile(es, [128, 128], BF16, "wu_%d" % h) for h in range(8)]
        QTb = [kb.tile(es, [64, 128], BF16, "qtb_%d" % h) for h in range(8)]
        MC = [kb.tile(es, [64, 64], BF16, "mc_%d" % h) for h in range(8)]

        def hs(h):
            return slice(h * 64, (h + 1) * 64)

        ck(kb, env, -1)

        for d in range(2):
            kb.dma("sp", m4.t[:], env.cst["mask4"][d], writes=[m4])
            kb.dma("sp", mt.t[:], env.cst["maskt"][d], writes=[mt])
            kb.dma("sp", tri.t[:], env.cst["tri"][d].rearrange("a s t -> s a t"), writes=[tri])
            kb.op("pool", lambda e: e.memset(hb.t[:], 0.0), [], [hb])
            order = list(range(NTILE)) if d == 0 else [1, 0] + list(range(NTILE - 1, 1, -1))
            def prep(ci, i, bi_):
                Fb, Bf, fma, fmb, gcf = FD[bi_], BfD[bi_], fmaD[bi_], fmbD[bi_], gcfD[bi_]
                FF = dict(F)
                FF.update(Fb)
                c0 = tcol(i)
                p_ = pw[ci % 2]
                for a in range(3):
                    kb.dma("sp", p_.t[:, :, a, :], env.PT[:, c0 - 1 + a:c0 + 127 + a].rearrange("(j p) c -> p j c", p=128), reads=[env.dd["PT"]], writes=[p_])
                ck(kb, env, 0)
                yield
                for gi, nm in enumerate(("r", "k", "v")):
                    b, _, deps = pp.get(512)
                    ps = kb.banks[b]
                    for jj in range(4):
                        j = gi * 4 + jj
                        for a in range(3):
                            kb.op("pe", lambda e: e.matmul(ps[:, jj * 128:(jj + 1) * 128], p_.t[:, j, a, :], dg.t[:, j, a, :], start=(a == 0), stop=(a == 2)),
                                  [p_, dg], deps, inc=(jj == 3 and a == 2))
                    kb.copy("act" if gi != 1 else "dve", FF[nm].t[:], ps[:, :], deps, [FF[nm]])
                b, _, deps = pp.get(512)
                ps = kb.banks[b]
                for jj in range(3):
                    j = 12 + jj
                    for a in range(3):
                        kb.op("pe", lambda e: e.matmul(ps[:, jj * 128:(jj + 1) * 128], dg.t[:, j, a, :], p_.t[:, j, a, :], start=(a == 0), stop=(a == 2)),
                              [p_, dg], deps, inc=(jj == 2 and a == 2))
                _act(kb, FF["t1"].t[:, 0:128], ps[:, 0:128], AF.Sigmoid, deps, [FF["t1"]], scale=2.0)
                kb.op("dve", lambda e: e.tensor_scalar(th.t[:], FF["t1"].t[:, 0:128], 2.0, -1.0, ALU.mult, ALU.add), [FF["t1"]], [th])
                kb.copy("dve", alo.t[:], ps[:, 128:256], deps, [alo])
                _act(kb, sg.t[:], ps[:, 256:384], AF.Sigmoid, deps, [sg])
                ck(kb, env, 1, [(0, FF["r"]), (1, FF["k"]), (2, FF["v"])])
                yield
                P0 = 64 * d
                b, _, deps = pp.get(512)
                ps = kb.banks[b]
                kb.op("pe", lambda e: e.matmul(ps[:, :], th.t[P0:P0 + 64, :], lw.t[P0:P0 + 64, 0, :], start=True, stop=False), [th, lw], deps, inc=False)
                kb.op("pe", lambda e: e.matmul(ps[:, :], env.onesf.t[0:1, :], brow.t[0:1, 0, d, :], start=False, stop=True), [env.onesf, brow], deps)
                _act(kb, FF["sw"].t[:], ps[:, :], AF.Sigmoid, deps, [FF["sw"]])
                for dd in range(2):
                    b, _, deps = pp.get(512)
                    ps = kb.banks[b]
                    kb.op("pe", lambda e: e.matmul(ps[:, :], alo.t[64 * dd:64 * dd + 64, :], lw.t[64 * dd:64 * dd + 64, 1, :], start=True, stop=False), [alo, lw], deps, inc=False)
                    kb.op("pe", lambda e: e.matmul(ps[:, :], env.onesf.t[0:1, :], brow.t[0:1, 1, dd, :], start=False, stop=True), [env.onesf, brow], deps)
                    _act(kb, FF["a%d" % dd].t[:], ps[:, :], AF.Sigmoid, deps, [FF["a%d" % dd]])
                b, _, deps = pp.get(512)
                ps = kb.banks[b]
                kb.op("pe", lambda e: e.matmul(ps[:, :], sg.t[:], lw.t[:, 2, :], start=True, stop=True), [sg, lw], deps)
                kb.copy("dve", FF["g"].t[:], ps[:, :], deps, [FF["g"]])
                ck(kb, env, 2, [(0, FF["sw"]), (1, FF["a0"]), (2, FF["a1"]), (3, FF["g"])])
                yield
                _tt(kb, "pool", FF["t1"].t[:], FF["k"].t[:], kkb.t[:], ALU.mult, [FF["k"], kkb], [FF["t1"]])
                _tt(kb, "pool", FF["sqk"].t[:], FF["t1"].t[:], FF["t1"].t[:], ALU.mult, [FF["t1"]], [FF["sqk"]])
                kb.op("dve", lambda e: e.reduce_sum(s8["ss"].t[:], v3(FF["sqk"].t[:]), AX.X), [FF["sqk"]], [s8["ss"]])
                kb.op("dve", lambda e: e.tensor_scalar_max(s8["ss"].t[:], s8["ss"].t[:], 1e-24), [s8["ss"]], [s8["ss"]])
                _act(kb, s8["nrm"].t[:], s8["ss"].t[:], AF.Ln, [s8["ss"]], [s8["nrm"]])
                _act(kb, s8["nrm"].t[:], s8["nrm"].t[:], AF.Exp, [s8["nrm"]], [s8["nrm"]], scale=-0.5)
                _tt(kb, "dve", v3(FF["kk"].t[:]), v3(FF["t1"].t[:]), bc8(s8["nrm"].t[:]), ALU.mult, [FF["t1"], s8["nrm"]], [FF["kk"]])
                for dd in range(2):
                    a_ = FF["a%d" % dd]
                    kd = FF["kd%d" % dd]
                    _stt(kb, "dve", FF["am1"].t[:], a_.t[:], -1.0, kab.t[:], ALU.add, ALU.mult, [a_, kab], [FF["am1"]])
                    _stt(kb, "dve", kd.t[:], FF["am1"].t[:], 1.0, FF["k"].t[:], ALU.add, ALU.mult, [FF["am1"], FF["k"]], [kd])
                a_d = FF["a%d" % d]
                kd_d = FF["kd%d" % d]
                _tt(kb, "dve", FF["b"].t[:], FF["kk"].t[:], a_d.t[:], ALU.mult, [FF["kk"], a_d], [FF["b"]])
                ck(kb, env, 3, [(0, FF["kk"]), (1, FF["kd0"]), (2, FF["kd1"]), (3, FF["b"])])
                yield
                exps = []
                for ti_, (nm, sc) in enumerate((("eG", 1.0), ("eGx", 1.0), ("eD", 1.0))):
                    b, _, deps = pp.get(512)
                    ps = kb.banks[b]
                    kb.op("pe", lambda e: e.matmul(ps[:, :], tri.t[:, ti_, :], FF["sw"].t[:], start=True, stop=True), [tri, FF["sw"]], deps)
                    _act(kb, FF[nm].t[:], ps[:, :], AF.Exp, deps, [FF[nm]])
                    if ti_ == 0:
                        _act(kb, FF["enG"].t[:], ps[:, :], AF.Exp, deps, [FF["enG"]], scale=-1.0)
                b, c8, deps = pp.get(8)
                ps = kb.banks[b]
                for h in range(8):
                    kb.op("pe", lambda e: e.matmul(ps[0:64, c8 + h:c8 + h + 1], FF["sw"].t[:, hs(h)], cvec.t[:, 0:1], start=True, stop=True),
                          [FF["sw"], cvec], deps, inc=(h == 7))
                _act(kb, gcf.t[:], ps[0:64, c8:c8 + 8], AF.Exp, deps, [gcf])
                ck(kb, env, 4, [(0, FF["eG"]), (1, FF["enG"]), (2, FF["eGx"]), (3, FF["eD"])])
                yield
                for nm, x_, e_ in (("KKt", "kk", "eGx"), ("Bt", "b", "enG"), ("Kt", "kd%d" % d, "enG"), ("Rt", "r", "eG"),
                                   ("Bh", "b", "eD"), ("Kh", "kd%d" % d, "eD")):
                    _tt(kb, kb.sb_eng(), Bf[nm].t[:], FF[x_].t[:], FF[e_].t[:], ALU.mult, [FF[x_], FF[e_]], [Bf[nm]])
                kb.copy("pool", Bf["Vb"].t[:], FF["v"].t[:], [FF["v"]], [Bf["Vb"]])
                for nm, dst, slot in (("KKt", fma, 0), ("Rt", fma, 1), ("Bt", fmb, 0), ("Kt", fmb, 1)):
                    b, cc, deps = pp.get(256)
                    psb = kb.banks_bf[b]
                    for jp in range(4):
                        kb.op("pe", lambda e: e.transpose(psb[:, 2 * cc + jp * 128:2 * cc + (jp + 1) * 128], Bf[nm].t[:, jp * 128:(jp + 1) * 128], idb.t[:]),
                              [Bf[nm], idb], deps, inc=(jp == 3))
                    kb.copy(kb.ev_eng(), dst.t[:, :, slot, :], psb[:, 2 * cc:2 * cc + 512].rearrange("p (j t) -> p j t", t=128), deps, [dst])
                yield

            def heads(ci, i, bi_):
                Fb, Bf, fma, fmb, gcf = FD[bi_], BfD[bi_], fmaD[bi_], fmbD[bi_], gcfD[bi_]
                FF = dict(F)
                FF.update(Fb)
                c0 = tcol(i)
                ck(kb, env, 5)
                yield
                R = {}
                for hg in range(2):
                    for h in range(4 * hg, 4 * hg + 4):
                        P = 64 * (h % 2)
                        jp = h // 2
                        b, _, deps = pp.get(512)
                        ps = kb.banks[b]
                        rhs = fma.t[P:P + 64, jp, :, :].rearrange("p a t -> p (a t)")
                        kb.op("pe", lambda e: e.matmul(ps[:, 0:256], fmb.t[P:P + 64, jp, 0, :], rhs, start=True, stop=True), [fma, fmb], deps, inc=False)
                        kb.op("pe", lambda e: e.matmul(ps[:, 256:512], fmb.t[P:P + 64, jp, 1, :], rhs, start=True, stop=True), [fma, fmb], deps)
                        R[h] = (ps, deps)
                    for h in range(4 * hg, 4 * hg + 4):
                        ps, deps = R[h]
                        _tt(kb, "dve", SB1[h].t[:], ps[:, :], m4.t[:], ALU.mult, deps + [m4], [SB1[h]])
                    yield
                for h in range(8):
                    P = 64 * (h % 2)
                    jp = h // 2
                    b2, c2, deps2 = pp.get(256)
                    ps2 = kb.banks[b2]
                    kb.op("pe", lambda e: e.matmul(ps2[:, c2:c2 + 128], fma.t[P:P + 64, jp, 0, :], fmb.t[P:P + 64, jp, 0, :], start=True, stop=True), [fma, fmb], deps2, inc=False)
                    kb.op("pe", lambda e: e.matmul(ps2[:, c2 + 128:c2 + 192], SB1[h].t[:, 256:384], Bf["Vb"].t[:, hs(h)], start=True, stop=True), [SB1[h], Bf["Vb"]], deps2)
                    R[h] = (ps2, c2, deps2)
                for h in range(8):
                    ps2, c2, deps2 = R[h]
                    _tt(kb, "dve", X0[h].t[:], ps2[:, c2:c2 + 128], mt.t[:], ALU.mult, deps2 + [mt], [X0[h]])
                    kb.copy("act", ZC[h].t[:, 64:128], ps2[:, c2 + 128:c2 + 192], deps2, [ZC[h]])
                    kb.copy("pool", ZC[h].t[:, 0:64], Bf["KKt"].t[:, hs(h)], [Bf["KKt"]], [ZC[h]])
                ck(kb, env, 7, [(0, SB1[0]), (1, X0[0]), (2, ZC[0]), (3, SB1[3]), (4, ZC[3])])
                yield
                for st in range(1, NSTEP + 1):
                    for h in range(8):
                        if st == 1:
                            xp, xtp, xd = X0[h].t[:, :], SB1[h].t[:, 0:128], [X0[h], SB1[h]]
                        else:
                            xx = XX[h][(st - 1) % 2]
                            xp, xtp, xd = xx.t[:, 0:128], xx.t[:, 128:256], [xx]
                        b, cc, deps = pp.get(256)
                        ps = kb.banks[b]
                        kb.op("pe", lambda e: e.matmul(ps[:, cc:cc + 128], xtp, xp, start=True, stop=True), xd, deps, inc=False)
                        kb.op("pe", lambda e: e.matmul(ps[:, cc + 128:cc + 256], xp, xtp, start=True, stop=True), xd, deps)
                        R[h] = (ps, cc, deps)
                    for h in range(8):
                        ps, cc, deps = R[h]
                        kb.copy("act", XX[h][st % 2].t[:], ps[:, cc:cc + 256], deps, [XX[h][st % 2]])
                    yield
                    for h in range(8):
                        xx = XX[h][st % 2]
                        if st == 1:
                            etp, ed = SB1[h].t[:, 0:128], [SB1[h]]
                        else:
                            etp, ed = ET[h][(st - 1) % 2].t[:, :], [ET[h][(st - 1) % 2]]
                        _tt(kb, "pool", TE[h].t[:], etp, xx.t[:, 128:256], ALU.add, ed + [xx], [TE[h]])
                        b, cc, deps = pp.get(128)
                        ps = kb.banks[b]
                        kb.op("pe", lambda e: e.matmul(ps[:, cc:cc + 128], xx.t[:, 0:128], etp, start=True, stop=True), ed + [xx], deps)
                        R[h] = (ps, cc, deps)
                    for h in range(8):
                        ps, cc, deps = R[h]
                        _tt(kb, "dve", ET[h][st % 2].t[:], ps[:, cc:cc + 128], TE[h].t[:], ALU.add, deps + [TE[h]], [ET[h][st % 2]])
                    yield
                ck(kb, env, 8, [(0, ET[0][NSTEP % 2]), (1, XX[0][NSTEP % 2]), (2, ET[3][NSTEP % 2])])
                yield
                for h in range(8):
                    et = ET[h][NSTEP % 2]
                    b, cc, deps = pp.get(128)
                    ps = kb.banks[b]
                    kb.op("pe", lambda e: e.matmul(ps[:, cc:cc + 128], et.t[:, :], ZC[h].t[:, :], start=True, stop=True), [et, ZC[h]], deps)
                    R[h] = (ps, cc, deps)
                for h in range(8):
                    ps, cc, deps = R[h]
                    _stt(kb, "dve", WU[h].t[:], ps[:, cc:cc + 128], -1.0, ZC[h].t[:], ALU.mult, ALU.subtract, deps + [ZC[h]], [WU[h]])
                ck(kb, env, 9, [(0, WU[0]), (1, WU[3])])
                yield
                for h in range(8):
                    b, cc, deps = pp.get(256)
                    ps = kb.banks[b]
                    kb.op("pe", lambda e: e.matmul(ps[0:64, cc:cc + 128], Bf["Rt"].t[:, hs(h)], idb.t[:], start=True, stop=False), [Bf["Rt"], idb], deps, inc=False)
                    kb.op("pe", lambda e: e.matmul(ps[0:64, cc:cc + 128], WU[h].t[:, 0:64], SB1[h].t[:, 128:256], start=False, stop=True), [WU[h], SB1[h]], deps, inc=False)
                    kb.op("pe", lambda e: e.matmul(ps[0:64, cc + 128:cc + 192], WU[h].t[:, 0:64], Bf["Bh"].t[:, hs(h)], start=True, stop=True), [WU[h], Bf["Bh"]], deps)
                    R[h] = (ps, cc, deps)
                for h in range(8):
                    ps, cc, deps = R[h]
                    kb.copy("act", QTb[h].t[:], ps[0:64, cc:cc + 128], deps, [QTb[h]])
                    _stt(kb, "dve", MC[h].t[:], env.identf.t[0:64, 0:64], gcf.t[0:64, h:h + 1], ps[0:64, cc + 128:cc + 192], ALU.mult, ALU.add,
                         deps + [env.identf, gcf], [MC[h]])
                ck(kb, env, 10, [(0, QTb[0]), (1, MC[0]), (2, QTb[3]), (3, MC[3])])
                yield
                for h in range(8):
                    yo = ybank[:, hs(h)]
                    kb.op("pe", lambda e: e.matmul(yo, SB1[h].t[:, 128:256], WU[h].t[:, 64:128], start=True, stop=False), [SB1[h], WU[h]], ydep, inc=False)
                    kb.op("pe", lambda e: e.matmul(yo, SB1[h].t[:, 384:512], Bf["Vb"].t[:, hs(h)], start=False, stop=False), [SB1[h], Bf["Vb"]], ydep, inc=False)
                    kb.op("pe", lambda e: e.matmul(yo, QTb[h].t[:, :], hb.t[0:64, hs(h)], start=False, stop=True), [QTb[h], hb], ydep, inc=False)
                    ho = hbank[0:64, hs(h)]
                    kb.op("pe", lambda e: e.matmul(ho, Bf["Bh"].t[:, hs(h)], WU[h].t[:, 64:128], start=True, stop=False), [Bf["Bh"], WU[h]], hdep, inc=False)
                    kb.op("pe", lambda e: e.matmul(ho, Bf["Kh"].t[:, hs(h)], Bf["Vb"].t[:, hs(h)], start=False, stop=False), [Bf["Kh"], Bf["Vb"]], hdep, inc=False)
                    kb.op("pe", lambda e: e.matmul(ho, MC[h].t[:, :], hb.t[0:64, hs(h)], start=False, stop=True), [MC[h], hb], hdep, inc=(h == 7))
                kb.copy("act", hb.t[:, :], hbank[0:64, :], hdep, [hb])
                kb.copy("dve", FF["y"].t[:], ybank[:, :], ydep, [FF["y"]])
                ck(kb, env, 11, [(0, FF["y"])])
                yield
                if getattr(env, "stop", None) == 100 + ci:
                    ck(kb, env, 100 + ci, [(0, FF["y"])])
                    yield
                if d == 0:
                    kb.dma("pool", env.YF[i * 128:(i + 1) * 128, :], FF["y"].t[:], reads=[FF["y"]], writes=[env.dd["YF"]])
                    return
                if i < 2 and not need_ctx:
                    return
                kb.dma("sp", FF["yf"].t[:], env.YF[i * 128:(i + 1) * 128, :], reads=[env.dd["YF"]], writes=[FF["yf"]])
                y = FF["y"]
                _tt(kb, "pool", y.t[:], y.t[:], FF["yf"].t[:], ALU.add, [y, FF["yf"]], [y])
                kb.op("dve", lambda e: e.reduce_sum(s8["m"].t[:], v3(y.t[:]), AX.X), [y], [s8["m"]])
                kb.op("dve", lambda e: e.tensor_scalar_mul(s8["m"].t[:], s8["m"].t[:], -1.0 / 64), [s8["m"]], [s8["m"]])
                _tt(kb, "dve", v3(y.t[:]), v3(y.t[:]), bc8(s8["m"].t[:]), ALU.add, [y, s8["m"]], [y])
                _tt(kb, "pool", FF["u1"].t[:], y.t[:], y.t[:], ALU.mult, [y], [FF["u1"]])
                kb.op("dve", lambda e: e.reduce_sum(s8["var"].t[:], v3(FF["u1"].t[:]), AX.X), [FF["u1"]], [s8["var"]])
                _act(kb, s8["var"].t[:], s8["var"].t[:], AF.Ln, [s8["var"], env.epsg], [s8["var"]], bias=env.epsg.t[:, 0:1], scale=1.0 / 64)
                _act(kb, s8["var"].t[:], s8["var"].t[:], AF.Exp, [s8["var"]], [s8["var"]], scale=-0.5)
                _tt(kb, "dve", v3(y.t[:]), v3(y.t[:]), bc8(s8["var"].t[:]), ALU.mult, [y, s8["var"]], [y])
                _tt(kb, "pool", y.t[:], y.t[:], gng.t[:], ALU.mult, [y, gng], [y])
                _tt(kb, "pool", y.t[:], y.t[:], gnb.t[:], ALU.add, [y, gnb], [y])
                _tt(kb, "pool", FF["rk"].t[:], FF["r"].t[:], rkb.t[:], ALU.mult, [FF["r"], rkb], [FF["rk"]])
                _tt(kb, "pool", FF["u1"].t[:], FF["kd0"].t[:], FF["kd1"].t[:], ALU.add, [FF["kd0"], FF["kd1"]], [FF["u1"]])
                _tt(kb, "pool", FF["u1"].t[:], FF["u1"].t[:], FF["rk"].t[:], ALU.mult, [FF["u1"], FF["rk"]], [FF["u1"]])
                kb.op("dve", lambda e: e.reduce_sum(s8["bs"].t[:], v3(FF["u1"].t[:]), AX.X), [FF["u1"]], [s8["bs"]])
                _tt(kb, "dve", v3(FF["u2"].t[:]), v3(FF["v"].t[:]), bc8(s8["bs"].t[:]), ALU.mult, [FF["v"], s8["bs"]], [FF["u2"]])
                _tt(kb, "pool", y.t[:], y.t[:], FF["u2"].t[:], ALU.add, [y, FF["u2"]], [y])
                _tt(kb, "pool", Bf["ob"].t[:], y.t[:], FF["g"].t[:], ALU.mult, [y, FF["g"]], [Bf["ob"]])
                b, cc, deps = pp.get(256)
                psb = kb.banks_bf[b]
                for jp in range(4):
                    kb.op("pe", lambda e: e.transpose(psb[:, 2 * cc + jp * 128:2 * cc + (jp + 1) * 128], Bf["ob"].t[:, jp * 128:(jp + 1) * 128], idb.t[:]),
                          [Bf["ob"], idb], deps, inc=(jp == 3))
                kb.copy("act", mst.t[:], psb[:, 2 * cc:2 * cc + 512].rearrange("p (j t) -> p j t", t=128), deps, [mst])
                kb.dma("pool", env.MIXT[0:512, c0:c0 + 128].rearrange("(k p) c -> p k c", p=128), mst.t[:], reads=[mst], writes=[env.dd["MIXT"]])
                yield

            def drain(g):
                for _ in g:
                    pass

            drain(prep(0, order[0], 0))
            for ci, i in enumerate(order):
                gh = heads(ci, i, ci % 2)
                gp = prep(ci + 1, order[ci + 1], (ci + 1) % 2) if ci + 1 < len(order) else iter(())
                alive = [gh, gp]
                while alive:
                    for g in list(alive):
                        try:
                            next(g)
                        except StopIteration:
                            alive.remove(g)
            kb.barrier()
    env.pp = PsPool(kb, list(range(8)))


PHASES["rwkv"] = phase_rwkv


def residual_ln(kb, env, es_tiles, xt, ps_list, gate, lng, lnb, out):
    st = es_tiles
    for half, (ps, deps) in enumerate(ps_list):
        sl = slice(half * 512, (half + 1) * 512)
        _tt(kb, "dve", out.t[:, sl], ps, gate.t[:, sl], ALU.mult, deps + [gate], [out])
    _stt(kb, "dve", out.t[:], xt.t[:], ALPHA, out.t[:], ALU.mult, ALU.add, [xt, out], [out])
    _act(kb, st["junk"].t[:], out.t[:], AF.Identity, [out], [st["junk"], st["s1"]], accum_out=st["s1"].t[:, 0:1])
    kb.op("dve", lambda e: e.tensor_scalar_mul(st["s1"].t[:, 0:1], st["s1"].t[:, 0:1], -1.0 / D), [st["s1"]], [st["s1"]])
    kb.op("dve", lambda e: e.tensor_scalar_add(out.t[:], out.t[:], st["s1"].t[:, 0:1]), [out, st["s1"]], [out])
    _act(kb, st["junk"].t[:], out.t[:], AF.Square, [out], [st["junk"], st["s2"]], accum_out=st["s2"].t[:, 0:1])
    _act(kb, st["s2"].t[:, 0:1], st["s2"].t[:, 0:1], AF.Sqrt, [st["s2"], env.epsl], [st["s2"]], bias=env.epsl.t[:, 0:1], scale=1.0 / D)
    kb.op("dve", lambda e: e.reciprocal(st["s2"].t[:, 0:1], st["s2"].t[:, 0:1]), [st["s2"]], [st["s2"]])
    _stt(kb, "dve", out.t[:], out.t[:], st["s2"].t[:, 0:1], lng.t[:], ALU.mult, ALU.mult, [out, st["s2"], lng], [out])
    _tt(kb, "pool", out.t[:], out.t[:], lnb.t[:], ALU.add, [out, lnb], [out])


def load_gate(kb, es, env, l, s, gi, name):
    t = kb.tile(es, [128, 1024], F32, name)
    kb.dma("sp", t.t[:], env.MODROW[l, s, gi].partition_broadcast(128), reads=[env.dd["MODROW"]], writes=[t])
    return t


def phase_wo(kb, env, l):
    need_ctx = l < DEPTH - 1
    env.pp = PsPool(kb, list(range(8)))
    pp = env.pp
    with ExitStack() as es:
        wo = kb.tile(es, [128, 8, 1024], BF16, "wo")
        with ExitStack() as es2:
            stage = [kb.tile(es2, [128, 1024], F32, "wost%d" % i) for i in range(2)]
            load_w_bf16(kb, es2, wo, env.w["w_o"][l], 8, 1024, stage)
            kb.barrier()
        gates = [load_gate(kb, es, env, l, s, 0, "g1_%d" % s) for s in range(2)]
        lng = load_bcast(kb, es, env.w["ln1_g"][l], 1024, "ln1g")
        lnb = load_bcast(kb, es, env.w["ln1_b"][l], 1024, "ln1b")
        st = {"junk": kb.tile(es, [128, 1024], F32, "junk"), "s1": kb.tile(es, [128, 1], F32, "s1"), "s2": kb.tile(es, [128, 1], F32, "s2")}
        env.ut_st = [kb.tile(es, [128, 8, 130], BF16, "utst%d" % i) for i in range(2)]
        env.ut_i = 0
        for t in env.ut_st:
            kb.op("pool", lambda e: e.memset(t.t[:], 0.0), [], [t])
        xts = [kb.tile(es, [128, 1024], F32, "xt%d" % i) for i in range(2)]
        outs = [kb.tile(es, [128, 1024], F32, "xo%d" % i) for i in range(2)]
        mts = [kb.tile(es, [128, 8, 128], BF16, "mt%d" % i) for i in range(2)]
        for ii, i in enumerate(range(NTILE) if need_ctx else range(2, NTILE)):
            s = 1 if i < 2 else 0
            c0 = tcol(i)
            xt, out, mt = xts[ii % 2], outs[ii % 2], mts[ii % 2]
            kb.dma("sp", mt.t[:], env.MIXT[:, c0:c0 + 128].rearrange("(k p) c -> p k c", p=128), reads=[env.dd["MIXT"]], writes=[mt])
            kb.dma("sp", xt.t[:], xin_ap(env, l, i), reads=[env.dd["X2"]], writes=[xt])
            pl = []
            for half in range(2):
                b, _, deps = pp.get(512)
                ps = kb.banks[b]
                for k in range(8):
                    kb.op("pe", lambda e: e.matmul(ps[:, :], mt.t[:, k, :], wo.t[:, k, half * 512:(half + 1) * 512], start=(k == 0), stop=(k == 7)),
                          [mt, wo], deps, inc=(k == 7))
                pl.append((ps[:, :], deps))
            residual_ln(kb, env, st, xt, pl, gates[s], lng, lnb, out)
            kb.dma("pool", env.X1[i * 128:(i + 1) * 128, :], out.t[:], reads=[out], writes=[env.dd["X1"]])
            emit_ut(kb, env, out, s, env.modf[l], 4, 3, env.U2T, i, pad=True)
        kb.barrier()


def phase_ffu(kb, env, l):
    need_ctx = l < DEPTH - 1
    env.pp = PsPool(kb, list(range(8)))
    pp = env.pp
    NF = 2 * DFF // 128
    with ExitStack() as es:
        wup = kb.tile(es, [128, 8, 2 * DFF], BF16, "wup")
        fcw = []
        with ExitStack() as es2:
            stage = [kb.tile(es2, [128, 2816], F32, "wust%d" % i) for i in range(2)]
            for hh in range(2):
                load_w_bf16(kb, es2, wup, env.w["w_up"][l][:, hh * 2816:(hh + 1) * 2816], 8, 2816, stage, col_off=hh * 2816)
            kb.barrier()
        for a in range(3):
            fcw.append(load_vec_fm(kb, env, es, env.w["ffn_conv_w"][l][a], NF, "fcw%d" % a))
        fcb = load_vec_fm(kb, env, es, env.w["ffn_conv_b"][l], NF, "fcb")
        u2b = [kb.tile(es, [128, 8, 514], BF16, "u2b%d" % i) for i in range(2)]
        gst = [kb.tile(es, [128, 22, 512], BF16, "gst%d" % i) for i in range(2)]
        hs_ = [kb.tile(es, [128, 514], F32, "hs%d" % i) for i in range(4)]
        tt_ = [kb.tile(es, [128, 512], F32, "tt%d" % i) for i in range(4)]
        sgt = [kb.tile(es, [128, 512], F32, "sgt%d" % i) for i in range(2)]
        blocks = BLOCKS if need_ctx else BLOCKS[1:]
        hi = 0
        for bi, (g0, n) in enumerate(blocks):
            c0 = gcol(g0)
            ub = u2b[bi % 2]
            gs = gst[bi % 2]
            kb.dma("sp", ub.t[:, :, 0:n + 2], env.U2T[:, c0 - 1:c0 + n + 1].rearrange("(k p) c -> p k c", p=128), reads=[env.dd["U2T"]], writes=[ub])
            for j in range(22):
                tv = []
                for which in range(2):
                    jf = j + 22 * which
                    hs = hs_[hi % 4]
                    tt = tt_[hi % 4]
                    hi += 1
                    b, _, deps = pp.get(512)
                    ps = kb.banks[b]
                    for k in range(8):
                        kb.op("pe", lambda e: e.matmul(ps[:, 0:n], wup.t[:, k, jf * 128:(jf + 1) * 128], ub.t[:, k, 1:n + 1], start=(k == 0), stop=(k == 7)),
                              [wup, ub], deps, inc=(k == 7))
                    b2, c2, deps2 = pp.get(2)
                    ps2 = kb.banks[b2]
                    for k in range(8):
                        kb.op("pe", lambda e: e.matmul(ps2[:, c2:c2 + 2], wup.t[:, k, jf * 128:(jf + 1) * 128], ub.t[:, k, 0:n + 2:n + 1], start=(k == 0), stop=(k == 7)),
                              [wup, ub], deps2, inc=(k == 7))
                    kb.copy("act", hs.t[:, 1:n + 1], ps[:, 0:n], deps, [hs])
                    _act(kb, tt.t[:, 0:n], ps[:, 0:n], AF.Identity, deps + [fcw[1], fcb], [tt], bias=fcb.t[:, jf:jf + 1], scale=fcw[1].t[:, jf:jf + 1])
                    kb.copy("dve", hs.t[:, 0:n + 2:n + 1], ps2[:, c2:c2 + 2], deps2, [hs])
                    _stt(kb, "dve", tt.t[:, 0:n], hs.t[:, 0:n], fcw[0].t[:, jf:jf + 1], tt.t[:, 0:n], ALU.mult, ALU.add, [hs, tt, fcw[0]], [tt])
                    _stt(kb, "dve", tt.t[:, 0:n], hs.t[:, 2:n + 2], fcw[2].t[:, jf:jf + 1], tt.t[:, 0:n], ALU.mult, ALU.add, [hs, tt, fcw[2]], [tt])
                    tv.append(tt)
                sg = sgt[j % 2]
                _act(kb, sg.t[:, 0:n], tv[0].t[:, 0:n], AF.Silu, [tv[0]], [sg])
                _tt(kb, "pool", gs.t[:, j, 0:n], sg.t[:, 0:n], tv[1].t[:, 0:n], ALU.mult, [sg, tv[1]], [gs])
            kb.dma("pool", env.GT[:, c0:c0 + n].rearrange("(k p) c -> p k c", p=128), gs.t[:, :, 0:n], reads=[gs], writes=[env.dd["GT"]])
        kb.barrier()


def phase_ffd(kb, env, l):
    need_ctx = l < DEPTH - 1
    last = l == DEPTH - 1
    env.pp = PsPool(kb, list(range(8)))
    pp = env.pp
    with ExitStack() as es:
        wdn = kb.tile(es, [128, 22, 1024], BF16, "wdn")
        with ExitStack() as es2:
            stage = [kb.tile(es2, [128, 1024], F32, "wdst%d" % i) for i in range(3)]
            load_w_bf16(kb, es2, wdn, env.w["w_down"][l], 22, 1024, stage)
            kb.barrier()
        gates = [load_gate(kb, es, env, l, s, 1, "g2_%d" % s) for s in range(2)]
        lng = load_bcast(kb, es, env.w["ln2_g"][l], 1024, "ln2g")
        lnb = load_bcast(kb, es, env.w["ln2_b"][l], 1024, "ln2b")
        st = {"junk": kb.tile(es, [128, 1024], F32, "junk"), "s1": kb.tile(es, [128, 1], F32, "s1"), "s2": kb.tile(es, [128, 1], F32, "s2")}
        env.ut_st = [kb.tile(es, [128, 8, 130], BF16, "utst%d" % i) for i in range(2)]
        env.ut_i = 0
        xts = [kb.tile(es, [128, 1024], F32, "xt%d" % i) for i in range(2)]
        outs = [kb.tile(es, [128, 1024], F32, "xo%d" % i) for i in range(2)]
        gts = [kb.tile(es, [128, 22, 128], BF16, "gt%d" % i) for i in range(2)]
        for ii, i in enumerate(range(NTILE) if need_ctx else range(2, NTILE)):
            s = 1 if i < 2 else 0
            c0 = tcol(i)
            xt, out, gt = xts[ii % 2], outs[ii % 2], gts[ii % 2]
            kb.dma("sp", gt.t[:], env.GT[:, c0:c0 + 128].rearrange("(k p) c -> p k c", p=128), reads=[env.dd["GT"]], writes=[gt])
            kb.dma("sp", xt.t[:], env.X1[i * 128:(i + 1) * 128, :], reads=[env.dd["X1"]], writes=[xt])
            pl = []
            for half in range(2):
                b, _, deps = pp.get(512)
                ps = kb.banks[b]
                for k in range(22):
                    kb.op("pe", lambda e: e.matmul(ps[:, :], gt.t[:, k, :], wdn.t[:, k, half * 512:(half + 1) * 512], start=(k == 0), stop=(k == 21)),
                          [gt, wdn], deps, inc=(k == 21))
                pl.append((ps[:, :], deps))
            residual_ln(kb, env, st, xt, pl, gates[s], lng, lnb, out)
            if last:
                kb.dma("pool", env.y[(i - 2) * 128:(i - 1) * 128, :], out.t[:], reads=[out], writes=[env.dd["y"]])
            else:
                kb.dma("pool", env.X2[i * 128:(i + 1) * 128, :], out.t[:], reads=[out], writes=[env.dd["X2"]])
                emit_ut(kb, env, out, s, env.modf[l + 1], 1, 0, env.UT, i, pad=False)
        kb.barrier()


PHASES["wo"] = phase_wo
PHASES["ffu"] = phase_ffu
PHASES["ffd"] = phase_ffd


_CACHE = {}


def kernel(**inputs):
    if "nc" not in _CACHE:
        _CACHE["nc"] = build()[0]
        _CACHE["consts"] = make_consts()
    nc = _CACHE["nc"]
    consts = _CACHE["consts"]
    B = inputs["x"].shape[0]
    shared = {k: np.ascontiguousarray(np.asarray(inputs[k], dtype=np.float32)) for k in W_SHAPES}
    shared["c_ctx"] = np.ascontiguousarray(np.asarray(inputs["c_ctx"], dtype=np.float32))
    for k, v in consts.items():
        shared["k_" + k] = v
    in_maps = []
    for b in range(B):
        m = dict(shared)
        m["x"] = np.ascontiguousarray(np.asarray(inputs["x"][b], dtype=np.float32))
        m["c"] = np.ascontiguousarray(np.asarray(inputs["c"][b], dtype=np.float32))
        m["ctx"] = np.ascontiguousarray(np.asarray(inputs["ctx"][b], dtype=np.float32))
        in_maps.append(m)
    res = run_bass_kernel_spmd(nc, in_maps, core_ids=list(range(B)))
    return np.stack([np.asarray(r["y"], dtype=np.float32) for r in res.results], axis=0)
```

```python
import numpy as np
import ml_dtypes
from contextlib import ExitStack
import concourse.bass as bass
import concourse.mybir as mybir
from concourse.bass_utils import run_bass_kernel_spmd

F32 = mybir.dt.float32
BF16 = mybir.dt.bfloat16
AF = mybir.ActivationFunctionType
ALU = mybir.AluOpType
AX = mybir.AxisListType

D = 1024
SEQ = 4096
CTXL = 256
NT = SEQ + CTXL
NCOL = NT + 3
NTILE = NT // 128
DEPTH = 2
DFF = 2816
RW = 1920
INC = 2592
ALPHA = (2 * DEPTH) ** 0.25
LN_EPS = 1e-5
RMS_EPS = 1e-6
GN_EPS = 64e-5
CW = float(np.exp(-0.5))
QSCALE = 96 ** -0.5
BLOCKS = [(0, 256)] + [(256 + 512 * j, 512) for j in range(8)]


def tcol(i):
    return 128 * i + (1 if i < 2 else 2)


def gcol(g):
    return g + (1 if g < 256 else 2)


class Dep:
    __slots__ = ("w", "r", "x")

    def __init__(self, x=False):
        self.w = None
        self.r = {}
        self.x = x


class T:
    __slots__ = ("t", "d")

    def __init__(self, t, d=None):
        self.t = t
        self.d = d if d is not None else Dep()


class KB:
    NDMA = 8

    def __init__(self, nc, es):
        self.nc = nc
        self.engs = {"pe": nc.tensor, "act": nc.scalar, "dve": nc.vector,
                     "pool": nc.gpsimd, "sp": nc.sync}
        self.sems = {}
        self.cnt = {}
        for e in self.engs:
            self.sems[e] = es.enter_context(nc.semaphore("s_" + e))
            self.cnt[e] = 0
        self.dq = {}
        for q in ("sp", "pool", "act"):
            self.dq[q] = 0
            for j in range(self.NDMA):
                key = "d_%s%d" % (q, j)
                self.sems[key] = es.enter_context(nc.semaphore(key))
                self.cnt[key] = 0
        self.known = {e: {} for e in self.engs}
        self.pending = {e: False for e in self.engs}
        self.nwait = 0
        self.nins = 0
        self.halt = False
        self.banks = []
        self.banks_bf = []
        self.bdep = []
        for b in range(8):
            t = es.enter_context(nc.psum_tensor("psb%d" % b, [128, 512], F32))
            self.banks.append(t)
            self.banks_bf.append(t.bitcast(BF16))
            bd = Dep(x=True)
            self.bdep.append((bd, bd))
        self.ev_i = 0
        self.sb_i = 0

    def _need(self, reads, writes):
        need = {}
        for d in reads:
            if d.w is not None:
                k, v = d.w
                if need.get(k, 0) < v:
                    need[k] = v
        for d in writes:
            if d.w is not None:
                k, v = d.w
                if need.get(k, 0) < v:
                    need[k] = v
            for k, v in d.r.items():
                if need.get(k, 0) < v:
                    need[k] = v
        return need

    def _waits(self, e, need, skip_own=False):
        eng = self.engs[e]
        kn = self.known[e]
        for k, v in need.items():
            if skip_own and k == e:
                continue
            if kn.get(k, 0) < v:
                eng.wait_ge(self.sems[k], v)
                kn[k] = v
                self.nwait += 1

    def _mark(self, tok, reads, writes):
        k, v = tok
        for d in reads:
            if d.r.get(k, 0) < v:
                d.r[k] = v
        for d in writes:
            d.w = tok
            d.r = {}

    def op(self, e, fn, reads=(), writes=(), inc=True):
        if self.halt:
            return None
        reads = [x.d if isinstance(x, T) else x for x in reads]
        writes = [x.d if isinstance(x, T) else x for x in writes]
        xs = [d for d in reads if d.x]
        if xs:
            reads = [d for d in reads if not d.x]
            writes = writes + xs
        need = self._need(reads, writes)
        self._waits(e, need, skip_own=(e == "pe"))
        ins = fn(self.engs[e])
        self.nins += 1
        if inc:
            self.cnt[e] += 1
            ins.then_inc(self.sems[e], 1)
            tok = (e, self.cnt[e])
            self.pending[e] = False
        else:
            tok = (e, self.cnt[e] + 1)
            self.pending[e] = True
        self._mark(tok, reads, writes)
        return tok

    def dma(self, q, out, in_, reads=(), writes=(), **kw):
        if self.halt:
            return None
        reads = [x.d if isinstance(x, T) else x for x in reads]
        writes = [x.d if isinstance(x, T) else x for x in writes]
        j = self.dq[q] % self.NDMA
        self.dq[q] += 1
        key = "d_%s%d" % (q, j)
        need = self._need(reads, writes)
        if self.cnt[key] > 0:
            need[key] = max(need.get(key, 0), self.cnt[key])
        self._waits(q, need)
        ins = self.engs[q].dma_start(out=out, in_=in_, **kw)
        self.nins += 1
        self.cnt[key] += 16
        ins.then_inc(self.sems[key], 16)
        tok = (key, self.cnt[key])
        self._mark(tok, reads, writes)
        return tok

    def barrier(self, engines=("pe", "act", "dve", "pool", "sp")):
        if self.halt:
            return
        for e in self.engs:
            assert not self.pending[e], e
        need = {k: v for k, v in self.cnt.items() if v > 0}
        for e in engines:
            self._waits(e, dict(need))

    def tile(self, es, shape, dtype, name):
        self.tile_i = getattr(self, "tile_i", 0) + 1
        return T(es.enter_context(self.nc.sbuf_tensor("%s_%d" % (name, self.tile_i), list(shape), dtype)))

    def ev_eng(self):
        self.ev_i += 1
        return "act" if self.ev_i % 2 else "dve"

    def sb_eng(self):
        self.sb_i += 1
        return "pool" if self.sb_i % 2 else "dve"

    def copy(self, e, out, in_, reads, writes, scale=None):
        if e == "act":
            if scale is None:
                return self.op("act", lambda g: g.copy(out, in_), reads, writes)
            return self.op("act", lambda g: g.mul(out, in_, scale), reads, writes)
        if scale is None:
            return self.op(e, lambda g: g.tensor_copy(out, in_), reads, writes)
        return self.op(e, lambda g: g.tensor_scalar_mul(out, in_, scale), reads, writes)


class PsPool:
    def __init__(self, kb, banks):
        self.kb = kb
        self.halves = [(b, h) for b in banks for h in (0, 1)]
        self.i = 0

    def get(self, ncols):
        n = len(self.halves)
        if ncols <= 256:
            b, h = self.halves[self.i % n]
            self.i += 1
            return b, h * 256, [self.kb.bdep[b][h]]
        if self.i % 2:
            self.i += 1
        b, _ = self.halves[self.i % n]
        self.i += 2
        return b, 0, [self.kb.bdep[b][0], self.kb.bdep[b][1]]


def make_consts():
    c = {}
    c["identf"] = np.eye(128, dtype=np.float32)
    c["onesf"] = np.ones((128, 128), dtype=np.float32)
    t = np.arange(128)
    m4 = np.zeros((2, 128, 512), np.float32)
    mt = np.zeros((2, 128, 128), np.float32)
    tri = np.zeros((2, 3, 128, 128), np.float32)
    for d in range(2):
        before = (t[:, None] < t[None, :]) if d == 0 else (t[:, None] > t[None, :])
        beq = before | (t[:, None] == t[None, :])
        m4[d, :, 0:128] = -1.0 * before
        m4[d, :, 128:256] = beq
        m4[d, :, 256:384] = before
        m4[d, :, 384:512] = beq
        mt[d] = -1.0 * before.T
        tri[d, 0] = -CW * beq
        tri[d, 1] = -CW * before
        tri[d, 2] = -CW * (~beq)
    c["mask4"] = m4
    c["maskt"] = mt
    c["tri"] = tri
    c["cvec"] = np.full((128, 1), -CW, np.float32)
    esel = np.zeros((65, 64), np.float32)
    esel[64, :] = 1.0
    c["esel"] = esel
    n_pairs = 8
    inv = 10000.0 ** (-np.arange(n_pairs, dtype=np.float32) / n_pairs)
    row = np.repeat(np.arange(SEQ // 64, dtype=np.float32), 64)
    col = np.tile(np.arange(64, dtype=np.float32), SEQ // 64)
    ang = np.concatenate([row[:, None] * inv, col[:, None] * inv], axis=-1).astype(np.float32)
    cos = np.cos(ang).astype(np.float32)
    sin = np.sin(ang).astype(np.float32)
    cos2 = np.repeat(cos, 2, axis=1).T
    sin2 = np.repeat(sin, 2, axis=1).T
    cosq = np.ones((96, NT), np.float32)
    sinq = np.zeros((96, NT), np.float32)
    cosq[64:96, CTXL:] = cos2
    sinq[64:96, CTXL:] = sin2
    c["cosk"] = cosq.copy()
    c["sink"] = sinq.copy()
    c["cosq"] = (cosq * QSCALE).astype(np.float32)
    c["sinq"] = (sinq * QSCALE).astype(np.float32)
    return c


CONST_SHAPES = {k: v.shape for k, v in make_consts().items()}

W_SHAPES = {
    "w_ada": (2, 1024, 6144), "b_ada": (2, 6144), "w_in": (2, 1024, 2592), "rwkv_conv": (2, 3, 1920),
    "w0": (2, 2, 512), "w_b": (2, 2, 64, 512), "a0": (2, 2, 512), "a_b": (2, 2, 64, 512),
    "g_b": (2, 128, 512), "k_k": (2, 512), "k_a": (2, 512), "r_k": (2, 8, 64), "gn_g": (2, 512),
    "gn_b": (2, 512), "q_norm_g": (2, 384), "w_uq": (2, 384, 768), "kv_norm_g": (2, 256),
    "w_ukv": (2, 256, 1024), "w_o": (2, 1024, 1024), "ln1_g": (2, 1024), "ln1_b": (2, 1024),
    "w_up": (2, 1024, 5632), "ffn_conv_w": (2, 3, 5632), "ffn_conv_b": (2, 5632),
    "w_down": (2, 2816, 1024), "ln2_g": (2, 1024), "ln2_b": (2, 1024),
}


class Env:
    pass


def load_vec_fm(kb, env, es, vec_ap, n, name):
    nc = kb.nc
    rows = kb.tile(es, [n, 128], F32, name + "_r")
    out = kb.tile(es, [128, n], F32, name)
    kb.dma("sp", rows.t[:], vec_ap.rearrange("(n p) -> n p", p=128), writes=[rows])
    b, c0, deps = env.pp.get(n)
    ps = kb.banks[b]
    kb.op("pe", lambda e: e.transpose(ps[:, c0:c0 + n], rows.t[:], env.identf.t[0:n, 0:n]),
          reads=[rows, env.identf], writes=deps)
    kb.copy("dve", out.t[:], ps[:, c0:c0 + n], deps, [out])
    return out


def load_bcast(kb, es, vec_ap, n, name):
    out = kb.tile(es, [128, n], F32, name)
    kb.dma("sp", out.t[:], vec_ap.partition_broadcast(128), writes=[out])
    return out


def load_w_bf16(kb, es, dst, w_ap, kchunks, ncols, stage, col_off=0, scale=None):
    wv = w_ap.rearrange("(k p) n -> k p n", p=128)
    for k in range(kchunks):
        st = stage[k % len(stage)]
        kb.dma("sp", st.t[:, 0:ncols], wv[k], writes=[st])
        e = ("act", "dve", "pool")[k % 3]
        if scale is None:
            kb.copy(e, dst.t[:, k, col_off:col_off + ncols], st.t[:, 0:ncols], [st], [dst])
        else:
            kb.op("dve", lambda g: g.tensor_scalar_mul(dst.t[:, k, col_off:col_off + ncols], st.t[:, 0:ncols], scale.t[:, k:k + 1]),
                  [st, scale], [dst])


def emit_ut(kb, env, xt, s, modf, w_scale, w_shift, dst, i, pad):
    c0 = tcol(i)
    st = env.ut_st[env.ut_i % 2]
    env.ut_i += 1
    for half in range(2):
        b, _, deps = env.pp.get(512)
        ps = kb.banks[b]
        for j in range(4):
            c = half * 4 + j
            kb.op("pe", lambda e: e.transpose(ps[:, j * 128:(j + 1) * 128], xt.t[:, c * 128:(c + 1) * 128], env.identf.t[:]),
                  reads=[xt, env.identf], writes=deps, inc=(j == 3))
        for j in range(4):
            c = half * 4 + j
            kb.op("act", lambda e: e.activation(st.t[:, c, 1:129], ps[:, j * 128:(j + 1) * 128], AF.Identity,
                                                bias=modf.t[:, w_shift * 8 + c, s:s + 1], scale=modf.t[:, w_scale * 8 + c, s:s + 1]),
                  reads=deps + [modf], writes=[st])
    lo, hi = 1, 129
    if pad and i in (0, 2):
        lo = 0
    if pad and i in (1, NTILE - 1):
        hi = 130
    kb.dma("pool", dst[:, c0 - 1 + lo:c0 - 1 + hi].rearrange("(k p) c -> p k c", p=128), st.t[:, :, lo:hi],
           reads=[st], writes=[env.dd[dst.tensor.name]])


def phase_mod(kb, env, l):
    nc = kb.nc
    with ExitStack() as es:
        modf = env.modf[l]
        crow = kb.tile(es, [16, 128], F32, "crow")
        kb.dma("sp", crow.t[0:8, :], env.c.rearrange("(n p) -> n p", p=128), writes=[crow])
        kb.dma("sp", crow.t[8:16, :], env.c_ctx.rearrange("(n p) -> n p", p=128), writes=[crow])
        sct = kb.tile(es, [128, 2, 8], F32, "sct")
        b, c0, deps = env.pp.get(16)
        ps = kb.banks[b]
        kb.op("pe", lambda e: e.transpose(ps[:, c0:c0 + 16], crow.t[:], env.identf.t[0:16, 0:16]), [crow, env.identf], deps)
        kb.op("act", lambda e: e.activation(sct.t[:].rearrange("p s k -> p (s k)"), ps[:, c0:c0 + 16], AF.Silu), deps, [sct])
        bF = load_vec_fm(kb, env, es, env.w["b_ada"][l], 48, "bF")
        stg = [kb.tile(es, [128, 8, 768], F32, "wada%d" % i) for i in range(2)]
        bq, cq0, depq = env.pp.get(96)
        psq = kb.banks[bq]
        wv = env.w["w_ada"][l].rearrange("(k p) n -> p k n", p=128)
        for pc in range(8):
            st = stg[pc % 2]
            kb.dma("sp", st.t[:], wv[:, :, pc * 768:(pc + 1) * 768], writes=[st])
            for jj in range(6):
                j = pc * 6 + jj
                for k in range(8):
                    kb.op("pe", lambda e: e.matmul(psq[:, cq0 + 2 * j:cq0 + 2 * j + 2], st.t[:, k, jj * 128:(jj + 1) * 128], sct.t[:, :, k],
                                                  start=(k == 0), stop=(k == 7)),
                          [st, sct], depq, inc=(k == 7))
        kb.op("dve", lambda e: e.tensor_tensor(modf.t[:], psq[:, cq0:cq0 + 96].rearrange("p (j s) -> p j s", s=2),
                                              bF.t[:].unsqueeze(2).to_broadcast([128, 48, 2]), ALU.add),
              depq + [bF], [modf])
        for wch in (1, 4):
            kb.op("dve", lambda e: e.tensor_scalar_add(modf.t[:, wch * 8:(wch + 1) * 8, :], modf.t[:, wch * 8:(wch + 1) * 8, :], 1.0),
                  [modf], [modf])
        grow = kb.tile(es, [1, 2, 2, 1024], F32, "grow")
        brow = kb.tile(es, [1, 2, 1024], F32, "brow")
        for gi, wch in enumerate((2, 5)):
            kb.dma("sp", brow.t[:, gi, :], env.w["b_ada"][l][wch * 1024:(wch + 1) * 1024].rearrange("(o n) -> o n", o=1), writes=[brow])
        for gi, wch in enumerate((2, 5)):
            st = stg[gi % 2]
            for hh in range(2):
                col = wch * 1024 + hh * 512
                kb.dma("sp", st.t[:, :, 0:512], wv[:, :, col:col + 512], writes=[st])
                for s in range(2):
                    b2, c2, dep2 = env.pp.get(512)
                    ps2 = kb.banks[b2]
                    for k in range(8):
                        kb.op("pe", lambda e: e.matmul(ps2[0:1, :], sct.t[:, s, k:k + 1], st.t[:, k, 0:512], start=(k == 0), stop=(k == 7)),
                              [st, sct], dep2, inc=(k == 7))
                    kb.op("dve", lambda e: e.tensor_tensor(grow.t[:, s, gi, hh * 512:(hh + 1) * 512], ps2[0:1, :], brow.t[:, gi, hh * 512:(hh + 1) * 512], ALU.add),
                          dep2 + [brow], [grow])
        kb.dma("sp", env.MODROW[l:l + 1].rearrange("o s g n -> o (s g n)"), grow.t[:].rearrange("o s g n -> o (s g n)"),
               reads=[grow], writes=[env.dd["MODROW"]])
        kb.barrier()


def xin_ap(env, l, i):
    if l == 0:
        return env.ctx[i * 128:(i + 1) * 128, :] if i < 2 else env.x[(i - 2) * 128:(i - 1) * 128, :]
    return env.X2[i * 128:(i + 1) * 128, :]


def phase_u0(kb, env):
    with ExitStack() as es:
        env.ut_st = [kb.tile(es, [128, 8, 130], BF16, "utst%d" % i) for i in range(2)]
        env.ut_i = 0
        xts = [kb.tile(es, [128, 1024], F32, "xt%d" % i) for i in range(3)]
        for i in range(NTILE):
            xt = xts[i % 3]
            kb.dma("sp", xt.t[:], xin_ap(env, 0, i), writes=[xt])
            emit_ut(kb, env, xt, 1 if i < 2 else 0, env.modf[0], 1, 0, env.UT, i, pad=False)
        kb.barrier()


def phase_p1(kb, env, l):
    nc = kb.nc
    with ExitStack() as es:
        win = kb.tile(es, [128, 8, 2624], BF16, "win")
        stage = [kb.tile(es, [128, 2592], F32, "wst%d" % i) for i in range(2)]
        load_w_bf16(kb, es, win, env.w["w_in"][l], 8, 2592, stage)
        kb.op("dve", lambda e: e.tensor_scalar_mul(win.t[:, :, 2592:2624:2], win.t[:, :, 2561:2592:2], -1.0), [win], [win])
        kb.op("dve", lambda e: e.tensor_copy(win.t[:, :, 2593:2624:2], win.t[:, :, 2560:2592:2]), [win], [win])
        utb = [kb.tile(es, [128, 8, 512], BF16, "utb%d" % i) for i in range(2)]
        pst = [kb.tile(es, [128, 15, 514], BF16, "pst%d" % i) for i in range(2)]
        pmst = [kb.tile(es, [128, 6, 512], F32, "pmst%d" % i) for i in range(2)]
        for t in pst:
            kb.op("pool", lambda e: e.memset(t.t[:], 0.0), [], [t])
        for bi, (g0, n) in enumerate(BLOCKS):
            c0 = gcol(g0)
            ub = utb[bi % 2]
            ps_ = pst[bi % 2]
            pm_ = pmst[bi % 2]
            kb.dma("sp", ub.t[:, :, 0:n], env.UT[:, c0:c0 + n].rearrange("(k p) c -> p k c", p=128),
                   reads=[env.dd["UT"]], writes=[ub])
            for jf in range(21):
                rows = 128 if jf < 20 else 64
                b, _, deps = env.pp.get(512)
                ps = kb.banks[b]
                for k in range(8):
                    kb.op("pe", lambda e: e.matmul(ps[0:rows, 0:n], win.t[:, k, jf * 128:jf * 128 + rows], ub.t[:, k, 0:n],
                                                  start=(k == 0), stop=(k == 7)),
                          [win, ub], deps, inc=(k == 7))
                if jf < 15:
                    kb.copy(kb.ev_eng(), ps_.t[:, jf, 1:1 + n], ps[:, 0:n], deps, [ps_])
                else:
                    kb.copy(kb.ev_eng(), pm_.t[0:rows, jf - 15, 0:n], ps[0:rows, 0:n], deps, [pm_])
            lo, hi = 1, 1 + n
            if bi == 0:
                lo, hi = 0, n + 2
            if bi == len(BLOCKS) - 1:
                hi = n + 2
            kb.dma("pool", env.PT[:, c0 - 1 + lo:c0 - 1 + hi].rearrange("(k p) c -> p k c", p=128), ps_.t[:, :, lo:hi],
                   reads=[ps_], writes=[env.dd["PT"]])
            kb.dma("pool", env.PM[0:640, c0:c0 + n].rearrange("(k p) c -> p k c", p=128), pm_.t[:, 0:5, 0:n],
                   reads=[pm_], writes=[env.dd["PM"]])
            kb.dma("pool", env.PM[640:704, c0:c0 + n], pm_.t[0:64, 5, 0:n], reads=[pm_], writes=[env.dd["PM"]])
        kb.barrier()


SCRATCH = {
    "UT": ([1024, NCOL], BF16), "PT": ([RW, NCOL], BF16), "PM": ([704, NCOL], F32),
    "MIXT": ([1024, NCOL], BF16), "X1": ([NT, 1024], F32), "U2T": ([1024, NCOL], BF16),
    "GT": ([DFF, NCOL], BF16), "X2": ([NT, 1024], F32), "YF": ([NT, 512], F32),
    "MODROW": ([2, 2, 2, 1024], F32), "DBG": ([16, 128, 512], F32),
}


def build(phases=None, debug_out=(), debug_in=(), stop=None):
    nc = bass.Bass("TRN2", target_bir_lowering=False)
    env = Env()
    env.x = nc.dram_tensor("x", [SEQ, D], F32, kind="ExternalInput").ap()
    env.c = nc.dram_tensor("c", [D], F32, kind="ExternalInput").ap()
    env.ctx = nc.dram_tensor("ctx", [CTXL, D], F32, kind="ExternalInput").ap()
    env.c_ctx = nc.dram_tensor("c_ctx", [D], F32, kind="ExternalInput").ap()
    env.w = {k: nc.dram_tensor(k, list(s), F32, kind="ExternalInput").ap() for k, s in W_SHAPES.items()}
    env.cst = {k: nc.dram_tensor("k_" + k, list(s), F32, kind="ExternalInput").ap() for k, s in CONST_SHAPES.items()}
    env.y = nc.dram_tensor("y", [SEQ, D], F32, kind="ExternalOutput").ap()
    env.dd = {}
    for name, (shape, dt_) in SCRATCH.items():
        kind = "ExternalOutput" if name in debug_out else ("ExternalInput" if name in debug_in else "Internal")
        setattr(env, name, nc.dram_tensor(name, shape, dt_, kind=kind).ap())
        env.dd[name] = Dep()
    env.dd["y"] = Dep()
    env.stop = stop
    allp = ["mod", "u0"]
    for l in range(DEPTH):
        allp += ["p1_%d" % l, "mla_%d" % l, "rwkv_%d" % l, "wo_%d" % l, "ffu_%d" % l, "ffd_%d" % l]
    if phases is None:
        phases = allp
    with ExitStack() as es:
        kb = KB(nc, es)
        env.pp = PsPool(kb, list(range(8)))
        env.identf = kb.tile(es, [128, 128], F32, "identf")
        kb.dma("sp", env.identf.t[:], env.cst["identf"], writes=[env.identf])
        env.identb = kb.tile(es, [128, 128], BF16, "identb")
        kb.copy("dve", env.identb.t[:], env.identf.t[:], [env.identf], [env.identb])
        env.onesf = kb.tile(es, [128, 128], F32, "onesf")
        kb.dma("sp", env.onesf.t[:], env.cst["onesf"], writes=[env.onesf])
        env.modf = [kb.tile(es, [128, 48, 2], F32, "modf%d" % l) for l in range(DEPTH)]
        env.epsr = kb.tile(es, [128, 1], F32, "epsr")
        kb.op("pool", lambda e: e.memset(env.epsr.t[:], RMS_EPS), [], [env.epsr])
        env.epsl = kb.tile(es, [128, 1], F32, "epsl")
        kb.op("pool", lambda e: e.memset(env.epsl.t[:], LN_EPS), [], [env.epsl])
        env.epsg = kb.tile(es, [128, 1], F32, "epsg")
        kb.op("pool", lambda e: e.memset(env.epsg.t[:], GN_EPS), [], [env.epsg])
        for ph in phases:
            if ph == "mod":
                for l in range(DEPTH):
                    phase_mod(kb, env, l)
            elif ph == "u0":
                phase_u0(kb, env)
            else:
                name, l = ph.rsplit("_", 1)
                try:
                    PHASES[name](kb, env, int(l))
                except StopPhase:
                    print("stopped at checkpoint", env.stop, flush=True)
                    break
        kb.barrier()
        env.kb = kb
    print("built: %d instructions, %d waits" % (kb.nins, kb.nwait), flush=True)
    return nc, env


PHASES = {"p1": phase_p1}


def phase_mla(kb, env, l):
    nc = kb.nc
    need_ctx = l < DEPTH - 1
    env.pp = PsPool(kb, [0, 1, 2, 3, 4])
    pp = env.pp
    with ExitStack() as es:
        cqn = kb.tile(es, [128, 3, NT], BF16, "cqn")
        ckvn = kb.tile(es, [128, 2, NT], BF16, "ckvn")
        va = kb.tile(es, [128, NTILE, 8, 65], BF16, "va")
        KT = [kb.tile(es, [128, NT], BF16, "kt%d" % i) for i in range(2)]
        wq2 = kb.tile(es, [128, 3, 8, 2, 96], BF16, "wq2")
        wk = kb.tile(es, [128, 2, 512], BF16, "wk")
        wv = kb.tile(es, [128, 2, 512], BF16, "wv")
        esel = kb.tile(es, [65, 64], F32, "esel")
        kb.dma("sp", esel.t[:], env.cst["esel"], writes=[esel])
        with ExitStack() as es2:
            qg = load_vec_fm(kb, env, es2, env.w["q_norm_g"][l], 3, "qg")
            kg = load_vec_fm(kb, env, es2, env.w["kv_norm_g"][l], 2, "kg")
            wq_st = kb.tile(es2, [128, 3, 768], F32, "wq_st")
            wkv_st = kb.tile(es2, [128, 2, 1024], F32, "wkv_st")
            kb.dma("sp", wq_st.t[:], env.w["w_uq"][l].rearrange("(k p) n -> p k n", p=128), writes=[wq_st])
            kb.dma("sp", wkv_st.t[:], env.w["w_ukv"][l].rearrange("(k p) n -> p k n", p=128), writes=[wkv_st])
            kb.op("pool", lambda e: e.memset(wq2.t[:], 0.0), [], [wq2])
            kb.op("pool", lambda e: e.memset(va.t[:], 1.0), [], [va])
            for k in range(3):
                kb.op("dve", lambda e: e.tensor_scalar_mul(wq2.t[:, k, :, 0, :], wq_st.t[:, k, :].rearrange("p (h d) -> p h d", d=96), qg.t[:, k:k + 1]),
                      [wq_st, qg], [wq2])
                kb.op("dve", lambda e: e.tensor_scalar_mul(wq2.t[:, k, :, 1, 64:96:2], wq2.t[:, k, :, 0, 65:96:2], -1.0), [wq2], [wq2])
                kb.op("dve", lambda e: e.tensor_copy(wq2.t[:, k, :, 1, 65:96:2], wq2.t[:, k, :, 0, 64:96:2]), [wq2], [wq2])
            for k in range(2):
                src = wkv_st.t[:, k, :].rearrange("p (h e) -> p h e", e=128)
                kb.op("dve", lambda e: e.tensor_scalar_mul(wk.t[:, k, :].rearrange("p (h d) -> p h d", d=64), src[:, :, 0:64], kg.t[:, k:k + 1]),
                      [wkv_st, kg], [wk])
                kb.op("dve", lambda e: e.tensor_scalar_mul(wv.t[:, k, :].rearrange("p (h d) -> p h d", d=64), src[:, :, 64:128], kg.t[:, k:k + 1]),
                      [wkv_st, kg], [wv])
            cs = [kb.tile(es2, [128, 5, 512], F32, "cs%d" % i) for i in range(2)]
            sq = [kb.tile(es2, [128, 5, 512], F32, "sq%d" % i) for i in range(2)]
            rsd = [kb.tile(es2, [128, 2, 512], F32, "rsd%d" % i) for i in range(2)]
            krt = [kb.tile(es2, [128, 4, 512], F32, "krt%d" % i) for i in range(2)]
            krm = [kb.tile(es2, [128, 2, 512], F32, "krm%d" % i) for i in range(2)]
            for bi, (g0, n) in enumerate(BLOCKS):
                c0 = gcol(g0)
                c_, s_, r_, kr_, km_ = cs[bi % 2], sq[bi % 2], rsd[bi % 2], krt[bi % 2], krm[bi % 2]
                kb.dma("sp", c_.t[:, :, 0:n], env.PM[0:640, c0:c0 + n].rearrange("(k p) c -> p k c", p=128), reads=[env.dd["PM"]], writes=[c_])
                kb.op("act", lambda e: e.activation(s_.t[:, :, 0:n], c_.t[:, :, 0:n], AF.Square), [c_], [s_])
                for wi, (k0, k1, dim) in enumerate(((0, 3, 384.0), (3, 5, 256.0))):
                    b, _, deps = pp.get(512)
                    ps = kb.banks[b]
                    for k in range(k0, k1):
                        kb.op("pe", lambda e: e.matmul(ps[:, 0:n], env.onesf.t[:], s_.t[:, k, 0:n], start=(k == k0), stop=(k == k1 - 1)),
                              [env.onesf, s_], deps, inc=(k == k1 - 1))
                    kb.op("act", lambda e: e.activation(r_.t[:, wi, 0:n], ps[:, 0:n], AF.Sqrt, bias=env.epsr.t[:, 0:1], scale=1.0 / dim), deps + [env.epsr], [r_])
                    kb.op("dve", lambda e: e.reciprocal(r_.t[:, wi, 0:n], r_.t[:, wi, 0:n]), [r_], [r_])
                    dst = cqn if wi == 0 else ckvn
                    for k in range(k0, k1):
                        kb.op(kb.sb_eng(), lambda e: e.tensor_tensor(dst.t[:, k - k0, g0:g0 + n], c_.t[:, k, 0:n], r_.t[:, wi, 0:n], ALU.mult),
                              [c_, r_], [dst])
                kb.dma("sp", kr_.t[64:96, 0, 0:n], env.PM[640:672, c0:c0 + n], reads=[env.dd["PM"]], writes=[kr_])
                kb.dma("sp", kr_.t[64:96, 1, 0:n], env.PM[672:704, c0:c0 + n], reads=[env.dd["PM"]], writes=[kr_])
                kb.dma("sp", kr_.t[64:96, 2, 0:n], env.cst["cosk"][64:96, g0:g0 + n], writes=[kr_])
                kb.dma("sp", kr_.t[64:96, 3, 0:n], env.cst["sink"][64:96, g0:g0 + n], writes=[kr_])
                kb.op("pool", lambda e: e.tensor_tensor(km_.t[64:96, 0, 0:n], kr_.t[64:96, 0, 0:n], kr_.t[64:96, 2, 0:n], ALU.mult), [kr_], [km_])
                kb.op("dve", lambda e: e.tensor_tensor(km_.t[64:96, 1, 0:n], kr_.t[64:96, 1, 0:n], kr_.t[64:96, 3, 0:n], ALU.mult), [kr_], [km_])
                kb.op("pool", lambda e: e.tensor_tensor(KT[0].t[64:96, g0:g0 + n], km_.t[64:96, 0, 0:n], km_.t[64:96, 1, 0:n], ALU.add), [km_], [KT[0]])
            kb.op("pool", lambda e: e.tensor_copy(KT[1].t[64:96, :], KT[0].t[64:96, :]), [KT[0]], [KT[1]])
            for i in range(NTILE):
                b, _, deps = pp.get(512)
                ps = kb.banks[b]
                for k in range(2):
                    kb.op("pe", lambda e: e.matmul(ps[:, :], ckvn.t[:, k, i * 128:(i + 1) * 128], wv.t[:, k, :], start=(k == 0), stop=(k == 1)),
                          [ckvn, wv], deps, inc=(k == 1))
                kb.copy(kb.ev_eng(), va.t[:, i, :, 0:64], ps[:, :].rearrange("p (h d) -> p h d", d=64), deps, [va])
            kb.barrier()
        QT = [kb.tile(es, [128, NT], BF16, "qt%d" % i) for i in range(2)]
        NEGM = [kb.tile(es, [128, 1], F32, "negm%d" % i) for i in range(2)]
        tabs = [kb.tile(es, [128, 2, 512], F32, "tab%d" % i) for i in range(2)]
        tmp = [kb.tile(es, [128, 2, 512], F32, "qtmp%d" % i) for i in range(2)]
        sqq = [kb.tile(es, [128, 512], F32, "sqq%d" % i) for i in range(2)]
        ptt = [kb.tile(es, [128, 512], BF16, "ptt%d" % i) for i in range(6)]
        osb = [kb.tile(es, [128, 512], F32, "osb%d" % i) for i in range(2)]
        rl = [kb.tile(es, [64, 512], F32, "rl%d" % i) for i in range(2)]
        ob = [kb.tile(es, [64, 512], BF16, "ob%d" % i) for i in range(2)]
        nb = kb.tile(es, [1, 2, 16], F32, "nb")
        msc = kb.tile(es, [1, 4], F32, "msc")
        qblocks = BLOCKS if need_ctx else BLOCKS[1:]
        LOOK = 3
        cnt = {"ti": 0, "pi": 0, "qi": 0}

        def setup(h):
            kt = KT[h % 2]
            qt = QT[h % 2]
            negm = NEGM[h % 2]
            for bi, (g0, n) in enumerate(BLOCKS):
                b, _, deps = pp.get(512)
                ps = kb.banks[b]
                for k in range(2):
                    kb.op("pe", lambda e: e.matmul(ps[0:64, 0:n], wk.t[:, k, h * 64:(h + 1) * 64], ckvn.t[:, k, g0:g0 + n], start=(k == 0), stop=(k == 1)),
                          [wk, ckvn], deps, inc=(k == 1))
                kb.copy("dve", kt.t[0:64, g0:g0 + n], ps[0:64, 0:n], deps, [kt])
            for bi, (g0, n) in enumerate(qblocks):
                tb = tabs[cnt["ti"] % 2]
                tm = tmp[cnt["ti"] % 2]
                cnt["ti"] += 1
                kb.dma("sp", tb.t[0:96, 0, 0:n], env.cst["cosq"][:, g0:g0 + n], writes=[tb])
                kb.dma("sp", tb.t[0:96, 1, 0:n], env.cst["sinq"][:, g0:g0 + n], writes=[tb])
                for ab in range(2):
                    b, _, deps = pp.get(512)
                    ps = kb.banks[b]
                    for k in range(3):
                        kb.op("pe", lambda e: e.matmul(ps[0:96, 0:n], wq2.t[:, k, h, ab, :], cqn.t[:, k, g0:g0 + n], start=(k == 0), stop=(k == 2)),
                              [wq2, cqn], deps, inc=(k == 2))
                    kb.op("dve", lambda e: e.tensor_tensor(tm.t[0:96, ab, 0:n], ps[0:96, 0:n], tb.t[0:96, ab, 0:n], ALU.mult), deps + [tb], [tm])
                kb.op("pool", lambda e: e.tensor_tensor(qt.t[0:96, g0:g0 + n], tm.t[0:96, 0, 0:n], tm.t[0:96, 1, 0:n], ALU.add), [tm], [qt])
            for wi, (src, blks) in enumerate(((qt, qblocks), (kt, BLOCKS))):
                for bi, (g0, n) in enumerate(blks):
                    s_ = sqq[(wi + bi) % 2]
                    kb.op("pool", lambda e: e.tensor_tensor(s_.t[0:96, 0:n], src.t[0:96, g0:g0 + n], src.t[0:96, g0:g0 + n], ALU.mult), [src], [s_])
                    b, c0, deps = pp.get(512)
                    ps = kb.banks[b]
                    kb.op("pe", lambda e: e.matmul(ps[0:1, 0:n], env.onesf.t[0:96, 0:1], s_.t[0:96, 0:n], start=True, stop=True), [env.onesf, s_], deps)
                    kb.op("dve", lambda e: e.reduce_max(nb.t[0:1, wi, bi:bi + 1], ps[0:1, 0:n], AX.X), deps, [nb])
                kb.op("dve", lambda e: e.reduce_max(msc.t[0:1, wi:wi + 1], nb.t[0:1, wi, 0:len(blks)], AX.X), [nb], [msc])
            kb.op("dve", lambda e: e.tensor_tensor(msc.t[0:1, 2:3], msc.t[0:1, 0:1], msc.t[0:1, 1:2], ALU.mult), [msc], [msc])
            kb.op("act", lambda e: e.activation(msc.t[0:1, 3:4], msc.t[0:1, 2:3], AF.Sqrt), [msc], [msc])
            kb.op("dve", lambda e: e.tensor_scalar_mul(msc.t[0:1, 3:4], msc.t[0:1, 3:4], -1.0), [msc], [msc])
            b, c0, deps = pp.get(16)
            ps = kb.banks[b]
            kb.op("pe", lambda e: e.matmul(ps[:, c0:c0 + 1], env.onesf.t[0:1, :], msc.t[0:1, 3:4], start=True, stop=True), [env.onesf, msc], deps)
            kb.copy("dve", negm.t[:, 0:1], ps[:, c0:c0 + 1], deps, [negm])

        def attn(h):
            kt = KT[h % 2]
            qt = QT[h % 2]
            negm = NEGM[h % 2]
            items = []
            for bi, (g0, n) in enumerate(qblocks):
                kts = list(range(NTILE)) if g0 >= CTXL else [0, 1]
                qi = cnt["qi"]
                cnt["qi"] += 1
                for ii, ki in enumerate(kts):
                    items.append((g0, n, ii, ki, len(kts), qi))
            inflight = []

            def stage_a(it):
                g0, n, ii, ki, nk, qi = it
                b, _, deps = pp.get(512)
                ps = kb.banks[b]
                kb.op("pe", lambda e: e.matmul(ps[:, 0:n], kt.t[0:96, ki * 128:(ki + 1) * 128], qt.t[0:96, g0:g0 + n], start=True, stop=True),
                      [kt, qt], deps)
                p_ = ptt[cnt["pi"] % len(ptt)]
                cnt["pi"] += 1
                kb.op("act", lambda e: e.activation(p_.t[:, 0:n], ps[:, 0:n], AF.Exp, bias=negm.t[:, 0:1], scale=1.0), deps + [negm], [p_])
                inflight.append(p_)

            def stage_b(it):
                g0, n, ii, ki, nk, qi = it
                p_ = inflight.pop(0)
                pob = 5 + (qi % 2)
                po = kb.banks[pob]
                pod = list(kb.bdep[pob])
                kb.op("pe", lambda e: e.matmul(po[0:65, 0:n], va.t[:, ki, h, :], p_.t[:, 0:n], start=(ii == 0), stop=(ii == nk - 1)),
                      [va, p_], pod, inc=(ii == nk - 1))
                if ii != nk - 1:
                    return
                o_ = osb[qi % 2]
                r_ = rl[qi % 2]
                b_ = ob[qi % 2]
                kb.copy("dve", o_.t[0:65, 0:n], po[0:65, 0:n], pod, [o_])
                b, _, deps = pp.get(512)
                ps = kb.banks[b]
                kb.op("pe", lambda e: e.matmul(ps[0:64, 0:n], esel.t[:, :], o_.t[0:65, 0:n], start=True, stop=True), [esel, o_], deps)
                kb.op("dve", lambda e: e.reciprocal(r_.t[:, 0:n], ps[0:64, 0:n]), deps, [r_])
                kb.op("pool", lambda e: e.tensor_tensor(b_.t[:, 0:n], o_.t[0:64, 0:n], r_.t[:, 0:n], ALU.mult), [o_, r_], [b_])
                c0 = gcol(g0)
                kb.dma("pool", env.MIXT[512 + h * 64:512 + (h + 1) * 64, c0:c0 + n], b_.t[:, 0:n], reads=[b_], writes=[env.dd["MIXT"]])

            for idx in range(len(items) + LOOK):
                if idx < len(items):
                    stage_a(items[idx])
                if idx >= LOOK:
                    stage_b(items[idx - LOOK])

        setup(0)
        for h in range(8):
            if h + 1 < 8:
                setup(h + 1)
            attn(h)
        kb.barrier()
    env.pp = PsPool(kb, list(range(8)))


PHASES["mla"] = phase_mla


NSTEP = 6


def _tt(kb, e, out, in0, in1, op, reads, writes):
    return kb.op(e, lambda g: g.tensor_tensor(out, in0, in1, op), reads, writes)


def _stt(kb, e, out, in0, scalar, in1, op0, op1, reads, writes):
    return kb.op(e, lambda g: g.scalar_tensor_tensor(out, in0, scalar, in1, op0, op1), reads, writes)


def _act(kb, out, in_, func, reads, writes, **kw):
    return kb.op("act", lambda g: g.activation(out, in_, func, **kw), reads, writes)


def bc8(t):
    return t.unsqueeze(2).to_broadcast([128, 8, 64])


def v3(ap):
    return ap.rearrange("p (h d) -> p h d", d=64)


class StopPhase(Exception):
    pass


def ck(kb, env, n, dumps=()):
    if getattr(env, "stop", None) != n:
        return
    for slot, t in dumps:
        if t.t.dtype == BF16:
            n = t.t.shape[-1]
            kb.dma("sp", env.DBG[slot].bitcast(BF16)[0:t.t.shape[0], 0:n], t.t[:], reads=[t], writes=[env.dd["DBG"]])
        else:
            kb.dma("sp", env.DBG[slot][0:t.t.shape[0], 0:t.t.shape[-1]], t.t[:], reads=[t], writes=[env.dd["DBG"]])
    kb.barrier()
    kb.halt = True


def phase_rwkv(kb, env, l):
    nc = kb.nc
    need_ctx = l < DEPTH - 1
    env.pp = PsPool(kb, [0, 1, 2, 3, 4, 5])
    pp = env.pp
    YB, HB = 6, 7
    ybank, hbank = kb.banks[YB], kb.banks[HB]
    ydep, hdep = list(kb.bdep[YB]), list(kb.bdep[HB])
    idb = env.identb
    with ExitStack() as es:
        dg = kb.tile(es, [128, 15, 3, 128], BF16, "dg")
        lw = kb.tile(es, [128, 3, 512], BF16, "lw")
        with ExitStack() as es2:
            for a in range(3):
                cw = load_vec_fm(kb, env, es2, env.w["rwkv_conv"][l][a], 15, "cw%d" % a)
                for j in range(15):
                    kb.op(kb.sb_eng(), lambda e: e.tensor_scalar_mul(dg.t[:, j, a, :], env.identf.t[:], cw.t[:, j:j + 1]), [env.identf, cw], [dg])
            lst = kb.tile(es2, [128, 3, 512], F32, "lst")
            kb.dma("sp", lst.t[:, 0, :], env.w["w_b"][l].rearrange("d r c -> (d r) c"), writes=[lst])
            kb.dma("sp", lst.t[:, 1, :], env.w["a_b"][l].rearrange("d r c -> (d r) c"), writes=[lst])
            kb.dma("sp", lst.t[:, 2, :], env.w["g_b"][l], writes=[lst])
            kb.copy("dve", lw.t[:], lst.t[:], [lst], [lw])
            kb.barrier()
        brow = kb.tile(es, [1, 2, 2, 512], F32, "brow")
        kb.dma("sp", brow.t[:, 0, :, :], env.w["w0"][l].rearrange("(o d) c -> o d c", o=1), writes=[brow])
        kb.dma("sp", brow.t[:, 1, :, :], env.w["a0"][l].rearrange("(o d) c -> o d c", o=1), writes=[brow])
        kkb = load_bcast(kb, es, env.w["k_k"][l], 512, "kkb")
        kab = load_bcast(kb, es, env.w["k_a"][l], 512, "kab")
        rkb = load_bcast(kb, es, env.w["r_k"][l].rearrange("h d -> (h d)"), 512, "rkb")
        gng = load_bcast(kb, es, env.w["gn_g"][l], 512, "gng")
        gnb = load_bcast(kb, es, env.w["gn_b"][l], 512, "gnb")
        m4 = kb.tile(es, [128, 512], F32, "m4")
        mt = kb.tile(es, [128, 128], F32, "mt")
        tri = kb.tile(es, [128, 3, 128], F32, "tri")
        cvec = kb.tile(es, [128, 1], F32, "cvec")
        kb.dma("sp", cvec.t[:], env.cst["cvec"], writes=[cvec])
        hb = kb.tile(es, [64, 512], BF16, "hb")
        pw = [kb.tile(es, [128, 15, 3, 128], BF16, "pw%d" % i) for i in range(2)]
        f32names = ["r", "k", "v", "sw", "a0", "a1", "g", "t1", "sqk", "kk", "am1", "kd0", "kd1", "b", "rk",
                    "eG", "enG", "eGx", "eD", "y", "yf", "u1", "u2"]
        F = {n: kb.tile(es, [128, 512], F32, "f_" + n) for n in f32names}
        FD = [{n: (F[n] if bi_ == 0 else kb.tile(es, [128, 512], F32, "f%d_%s" % (bi_, n))) for n in ("r", "v", "g", "kd0", "kd1", "y")} for bi_ in range(3)]
        bfnames = ["KKt", "Bt", "Kt", "Rt", "Bh", "Kh", "Vb", "ob"]
        BfD = [{n: kb.tile(es, [128, 512], BF16, "b%d_%s" % (bi_, n)) for n in bfnames} for bi_ in range(2)]
        th = kb.tile(es, [128, 128], BF16, "th")
        alo = kb.tile(es, [128, 128], BF16, "alo")
        sg = kb.tile(es, [128, 128], BF16, "sg")
        s8 = {n: kb.tile(es, [128, 8], F32, "s8_" + n) for n in ("ss", "nrm", "m", "var", "bs")}
        fmaD = [kb.tile(es, [128, 4, 2, 128], BF16, "fma%d" % i) for i in range(2)]
        fmbD = [kb.tile(es, [128, 4, 2, 128], BF16, "fmb%d" % i) for i in range(2)]
        gcfD = [kb.tile(es, [64, 8], F32, "gcf%d" % i) for i in range(2)]
        mst = kb.tile(es, [128, 4, 128], BF16, "mst")
        SB1 = [kb.tile(es, [128, 512], BF16, "sb1_%d" % h) for h in range(8)]
        X0 = [kb.tile(es, [128, 128], BF16, "x0_%d" % h) for h in range(8)]
        XX = [[kb.tile(es, [128, 256], BF16, "xx%d_%d" % (i, h)) for i in range(3)] for h in range(8)]
        ET = [[kb.tile(es, [128, 128], BF16, "et%d_%d" % (i, h)) for i in range(2)] for h in range(8)]
        ZC = [kb.tile(es, [128, 128], BF16, "zc_%d" % h) for h in range(8)]
        WU = [kb.tile(es, [128, 128], BF16, "wu_%d" % h) for h in range(8)]
        QTb = [kb.tile(es, [64, 128], BF16, "qtb_%d" % h) for h in range(8)]
        MC = [kb.tile(es, [64, 64], BF16, "mc_%d" % h) for h in range(8)]

        def hs(h):
            return slice(h * 64, (h + 1) * 64)

        ck(kb, env, -1)

        for d in range(2):
            kb.dma("sp", m4.t[:], env.cst["mask4"][d], writes=[m4])
            kb.dma("sp", mt.t[:], env.cst["maskt"][d], writes=[mt])
            kb.dma("sp", tri.t[:], env.cst["tri"][d].rearrange("a s t -> s a t"), writes=[tri])
            kb.op("pool", lambda e: e.memset(hb.t[:], 0.0), [], [hb])
            order = list(range(NTILE)) if d == 0 else [1, 0] + list(range(NTILE - 1, 1, -1))
            def prep(ci, i, bi_):
                Fb, Bf, fma, fmb, gcf = FD[ci % 3], BfD[bi_], fmaD[bi_], fmbD[bi_], gcfD[bi_]
                FF = dict(F)
                FF.update(Fb)
                c0 = tcol(i)
                p_ = pw[ci % 2]
                for a in range(3):
                    kb.dma("sp", p_.t[:, :, a, :], env.PT[:, c0 - 1 + a:c0 + 127 + a].rearrange("(j p) c -> p j c", p=128), reads=[env.dd["PT"]], writes=[p_])
                ck(kb, env, 0)
                yield
                for gi, nm in enumerate(("r", "k", "v")):
                    b, _, deps = pp.get(512)
                    ps = kb.banks[b]
                    for jj in range(4):
                        j = gi * 4 + jj
                        for a in range(3):
                            kb.op("pe", lambda e: e.matmul(ps[:, jj * 128:(jj + 1) * 128], p_.t[:, j, a, :], dg.t[:, j, a, :], start=(a == 0), stop=(a == 2)),
                                  [p_, dg], deps, inc=(jj == 3 and a == 2))
                    kb.copy("act" if gi != 1 else "dve", FF[nm].t[:], ps[:, :], deps, [FF[nm]])
                b, _, deps = pp.get(512)
                ps = kb.banks[b]
                for jj in range(3):
                    j = 12 + jj
                    for a in range(3):
                        kb.op("pe", lambda e: e.matmul(ps[:, jj * 128:(jj + 1) * 128], dg.t[:, j, a, :], p_.t[:, j, a, :], start=(a == 0), stop=(a == 2)),
                              [p_, dg], deps, inc=(jj == 2 and a == 2))
                _act(kb, FF["t1"].t[:, 0:128], ps[:, 0:128], AF.Sigmoid, deps, [FF["t1"]], scale=2.0)
                kb.op("dve", lambda e: e.tensor_scalar(th.t[:], FF["t1"].t[:, 0:128], 2.0, -1.0, ALU.mult, ALU.add), [FF["t1"]], [th])
                kb.copy("dve", alo.t[:], ps[:, 128:256], deps, [alo])
                _act(kb, sg.t[:], ps[:, 256:384], AF.Sigmoid, deps, [sg])
                ck(kb, env, 1, [(0, FF["r"]), (1, FF["k"]), (2, FF["v"])])
                yield
                P0 = 64 * d
                b, _, deps = pp.get(512)
                ps = kb.banks[b]
                kb.op("pe", lambda e: e.matmul(ps[:, :], th.t[P0:P0 + 64, :], lw.t[P0:P0 + 64, 0, :], start=True, stop=False), [th, lw], deps, inc=False)
                kb.op("pe", lambda e: e.matmul(ps[:, :], env.onesf.t[0:1, :], brow.t[0:1, 0, d, :], start=False, stop=True), [env.onesf, brow], deps)
                _act(kb, FF["sw"].t[:], ps[:, :], AF.Sigmoid, deps, [FF["sw"]])
                for dd in range(2):
                    b, _, deps = pp.get(512)
                    ps = kb.banks[b]
                    kb.op("pe", lambda e: e.matmul(ps[:, :], alo.t[64 * dd:64 * dd + 64, :], lw.t[64 * dd:64 * dd + 64, 1, :], start=True, stop=False), [alo, lw], deps, inc=False)
                    kb.op("pe", lambda e: e.matmul(ps[:, :], env.onesf.t[0:1, :], brow.t[0:1, 1, dd, :], start=False, stop=True), [env.onesf, brow], deps)
                    _act(kb, FF["a%d" % dd].t[:], ps[:, :], AF.Sigmoid, deps, [FF["a%d" % dd]])
                b, _, deps = pp.get(512)
                ps = kb.banks[b]
                kb.op("pe", lambda e: e.matmul(ps[:, :], sg.t[:], lw.t[:, 2, :], start=True, stop=True), [sg, lw], deps)
                kb.copy("dve", FF["g"].t[:], ps[:, :], deps, [FF["g"]])
                ck(kb, env, 2, [(0, FF["sw"]), (1, FF["a0"]), (2, FF["a1"]), (3, FF["g"])])
                yield
                _tt(kb, "pool", FF["t1"].t[:], FF["k"].t[:], kkb.t[:], ALU.mult, [FF["k"], kkb], [FF["t1"]])
                _tt(kb, "pool", FF["sqk"].t[:], FF["t1"].t[:], FF["t1"].t[:], ALU.mult, [FF["t1"]], [FF["sqk"]])
                kb.op("dve", lambda e: e.reduce_sum(s8["ss"].t[:], v3(FF["sqk"].t[:]), AX.X), [FF["sqk"]], [s8["ss"]])
                kb.op("dve", lambda e: e.tensor_scalar_max(s8["ss"].t[:], s8["ss"].t[:], 1e-24), [s8["ss"]], [s8["ss"]])
                _act(kb, s8["nrm"].t[:], s8["ss"].t[:], AF.Ln, [s8["ss"]], [s8["nrm"]])
                _act(kb, s8["nrm"].t[:], s8["nrm"].t[:], AF.Exp, [s8["nrm"]], [s8["nrm"]], scale=-0.5)
                _tt(kb, "dve", v3(FF["kk"].t[:]), v3(FF["t1"].t[:]), bc8(s8["nrm"].t[:]), ALU.mult, [FF["t1"], s8["nrm"]], [FF["kk"]])
                for dd in range(2):
                    a_ = FF["a%d" % dd]
                    kd = FF["kd%d" % dd]
                    _stt(kb, "dve", FF["am1"].t[:], a_.t[:], -1.0, kab.t[:], ALU.add, ALU.mult, [a_, kab], [FF["am1"]])
                    _stt(kb, "dve", kd.t[:], FF["am1"].t[:], 1.0, FF["k"].t[:], ALU.add, ALU.mult, [FF["am1"], FF["k"]], [kd])
                a_d = FF["a%d" % d]
                kd_d = FF["kd%d" % d]
                _tt(kb, "dve", FF["b"].t[:], FF["kk"].t[:], a_d.t[:], ALU.mult, [FF["kk"], a_d], [FF["b"]])
                ck(kb, env, 3, [(0, FF["kk"]), (1, FF["kd0"]), (2, FF["kd1"]), (3, FF["b"])])
                yield
                exps = []
                for ti_, (nm, sc) in enumerate((("eG", 1.0), ("eGx", 1.0), ("eD", 1.0))):
                    b, _, deps = pp.get(512)
                    ps = kb.banks[b]
                    kb.op("pe", lambda e: e.matmul(ps[:, :], tri.t[:, ti_, :], FF["sw"].t[:], start=True, stop=True), [tri, FF["sw"]], deps)
                    _act(kb, FF[nm].t[:], ps[:, :], AF.Exp, deps, [FF[nm]])
                    if ti_ == 0:
                        _act(kb, FF["enG"].t[:], ps[:, :], AF.Exp, deps, [FF["enG"]], scale=-1.0)
                b, c8, deps = pp.get(8)
                ps = kb.banks[b]
                for h in range(8):
                    kb.op("pe", lambda e: e.matmul(ps[0:64, c8 + h:c8 + h + 1], FF["sw"].t[:, hs(h)], cvec.t[:, 0:1], start=True, stop=True),
                          [FF["sw"], cvec], deps, inc=(h == 7))
                _act(kb, gcf.t[:], ps[0:64, c8:c8 + 8], AF.Exp, deps, [gcf])
                ck(kb, env, 4, [(0, FF["eG"]), (1, FF["enG"]), (2, FF["eGx"]), (3, FF["eD"])])
                yield
                for nm, x_, e_ in (("KKt", "kk", "eGx"), ("Bt", "b", "enG"), ("Kt", "kd%d" % d, "enG"), ("Rt", "r", "eG"),
                                   ("Bh", "b", "eD"), ("Kh", "kd%d" % d, "eD")):
                    _tt(kb, kb.sb_eng(), Bf[nm].t[:], FF[x_].t[:], FF[e_].t[:], ALU.mult, [FF[x_], FF[e_]], [Bf[nm]])
                kb.copy("pool", Bf["Vb"].t[:], FF["v"].t[:], [FF["v"]], [Bf["Vb"]])
                for nm, dst, slot in (("KKt", fma, 0), ("Rt", fma, 1), ("Bt", fmb, 0), ("Kt", fmb, 1)):
                    b, cc, deps = pp.get(256)
                    psb = kb.banks_bf[b]
                    for jp in range(4):
                        kb.op("pe", lambda e: e.transpose(psb[:, 2 * cc + jp * 128:2 * cc + (jp + 1) * 128], Bf[nm].t[:, jp * 128:(jp + 1) * 128], idb.t[:]),
                              [Bf[nm], idb], deps, inc=(jp == 3))
                    kb.copy(kb.ev_eng(), dst.t[:, :, slot, :], psb[:, 2 * cc:2 * cc + 512].rearrange("p (j t) -> p j t", t=128), deps, [dst])
                yield

            def heads(ci, i, bi_):
                Fb, Bf, fma, fmb, gcf = FD[ci % 3], BfD[bi_], fmaD[bi_], fmbD[bi_], gcfD[bi_]
                FF = dict(F)
                FF.update(Fb)
                c0 = tcol(i)
                ck(kb, env, 5)
                yield
                R = {}
                for hg in range(2):
                    for h in range(4 * hg, 4 * hg + 4):
                        P = 64 * (h % 2)
                        jp = h // 2
                        b, _, deps = pp.get(512)
                        ps = kb.banks[b]
                        rhs = fma.t[P:P + 64, jp, :, :].rearrange("p a t -> p (a t)")
                        kb.op("pe", lambda e: e.matmul(ps[:, 0:256], fmb.t[P:P + 64, jp, 0, :], rhs, start=True, stop=True), [fma, fmb], deps, inc=False)
                        kb.op("pe", lambda e: e.matmul(ps[:, 256:512], fmb.t[P:P + 64, jp, 1, :], rhs, start=True, stop=True), [fma, fmb], deps)
                        R[h] = (ps, deps)
                    for h in range(4 * hg, 4 * hg + 4):
                        ps, deps = R[h]
                        _tt(kb, "dve", SB1[h].t[:], ps[:, :], m4.t[:], ALU.mult, deps + [m4], [SB1[h]])
                    yield
                for h in range(8):
                    P = 64 * (h % 2)
                    jp = h // 2
                    b2, c2, deps2 = pp.get(256)
                    ps2 = kb.banks[b2]
                    kb.op("pe", lambda e: e.matmul(ps2[:, c2:c2 + 128], fma.t[P:P + 64, jp, 0, :], fmb.t[P:P + 64, jp, 0, :], start=True, stop=True), [fma, fmb], deps2, inc=False)
                    kb.op("pe", lambda e: e.matmul(ps2[:, c2 + 128:c2 + 192], SB1[h].t[:, 256:384], Bf["Vb"].t[:, hs(h)], start=True, stop=True), [SB1[h], Bf["Vb"]], deps2)
                    R[h] = (ps2, c2, deps2)
                for h in range(8):
                    ps2, c2, deps2 = R[h]
                    _tt(kb, "dve", X0[h].t[:], ps2[:, c2:c2 + 128], mt.t[:], ALU.mult, deps2 + [mt], [X0[h]])
                    kb.copy("act", ZC[h].t[:, 64:128], ps2[:, c2 + 128:c2 + 192], deps2, [ZC[h]])
                    kb.copy("pool", ZC[h].t[:, 0:64], Bf["KKt"].t[:, hs(h)], [Bf["KKt"]], [ZC[h]])
                ck(kb, env, 7, [(0, SB1[0]), (1, X0[0]), (2, ZC[0]), (3, SB1[3]), (4, ZC[3])])
                yield
                def sq_stage(st):
                    for h in range(8):
                        if st == 1:
                            xp, xtp, xd = X0[h].t[:, :], SB1[h].t[:, 0:128], [X0[h], SB1[h]]
                        else:
                            xx = XX[h][(st - 1) % 3]
                            xp, xtp, xd = xx.t[:, 0:128], xx.t[:, 128:256], [xx]
                        b, cc, deps = pp.get(256)
                        ps = kb.banks[b]
                        kb.op("pe", lambda e: e.matmul(ps[:, cc:cc + 128], xtp, xp, start=True, stop=True), xd, deps, inc=False)
                        kb.op("pe", lambda e: e.matmul(ps[:, cc + 128:cc + 256], xp, xtp, start=True, stop=True), xd, deps)
                        R[h] = (ps, cc, deps)
                    for h in range(8):
                        ps, cc, deps = R[h]
                        kb.copy("dve" if h in (2, 5, 7) else "act", XX[h][st % 3].t[:], ps[:, cc:cc + 256], deps, [XX[h][st % 3]])

                def chain_stage(st):
                    R2 = {}
                    for h in range(8):
                        xx = XX[h][st % 3]
                        if st == 1:
                            etp, ed = SB1[h].t[:, 0:128], [SB1[h]]
                        else:
                            etp, ed = ET[h][(st - 1) % 2].t[:, :], [ET[h][(st - 1) % 2]]
                        b, cc, deps = pp.get(128)
                        ps = kb.banks[b]
                        kb.op("pe", lambda e: e.matmul(ps[:, cc:cc + 128], idb.t[:], xx.t[:, 128:256], start=True, stop=False), [xx, idb], deps, inc=False)
                        kb.op("pe", lambda e: e.matmul(ps[:, cc:cc + 128], xx.t[:, 0:128], etp, start=False, stop=True), ed + [xx], deps)
                        R2[h] = (ps, cc, deps, etp, ed)
                    for h in range(8):
                        ps, cc, deps, etp, ed = R2[h]
                        _tt(kb, "dve", ET[h][st % 2].t[:], ps[:, cc:cc + 128], etp, ALU.add, deps + ed, [ET[h][st % 2]])

                sq_stage(1)
                yield
                for st in range(1, NSTEP + 1):
                    if st + 1 <= NSTEP:
                        sq_stage(st + 1)
                        yield
                    chain_stage(st)
                    yield
                ck(kb, env, 8, [(0, ET[0][NSTEP % 2]), (1, XX[0][NSTEP % 3]), (2, ET[3][NSTEP % 2])])
                yield
                for h in range(8):
                    et = ET[h][NSTEP % 2]
                    b, cc, deps = pp.get(128)
                    ps = kb.banks[b]
                    kb.op("pe", lambda e: e.matmul(ps[:, cc:cc + 128], et.t[:, :], ZC[h].t[:, :], start=True, stop=True), [et, ZC[h]], deps)
                    R[h] = (ps, cc, deps)
                for h in range(8):
                    ps, cc, deps = R[h]
                    _stt(kb, "dve", WU[h].t[:], ps[:, cc:cc + 128], -1.0, ZC[h].t[:], ALU.mult, ALU.subtract, deps + [ZC[h]], [WU[h]])
                ck(kb, env, 9, [(0, WU[0]), (1, WU[3])])
                yield
                for h in range(8):
                    b, cc, deps = pp.get(256)
                    ps = kb.banks[b]
                    kb.op("pe", lambda e: e.matmul(ps[0:64, cc:cc + 128], Bf["Rt"].t[:, hs(h)], idb.t[:], start=True, stop=False), [Bf["Rt"], idb], deps, inc=False)
                    kb.op("pe", lambda e: e.matmul(ps[0:64, cc:cc + 128], WU[h].t[:, 0:64], SB1[h].t[:, 128:256], start=False, stop=True), [WU[h], SB1[h]], deps, inc=False)
                    kb.op("pe", lambda e: e.matmul(ps[0:64, cc + 128:cc + 192], WU[h].t[:, 0:64], Bf["Bh"].t[:, hs(h)], start=True, stop=True), [WU[h], Bf["Bh"]], deps)
                    R[h] = (ps, cc, deps)
                for h in range(8):
                    ps, cc, deps = R[h]
                    kb.copy("act", QTb[h].t[:], ps[0:64, cc:cc + 128], deps, [QTb[h]])
                    _stt(kb, "dve", MC[h].t[:], env.identf.t[0:64, 0:64], gcf.t[0:64, h:h + 1], ps[0:64, cc + 128:cc + 192], ALU.mult, ALU.add,
                         deps + [env.identf, gcf], [MC[h]])
                ck(kb, env, 10, [(0, QTb[0]), (1, MC[0]), (2, QTb[3]), (3, MC[3])])
                yield
                for h in range(8):
                    yo = ybank[:, hs(h)]
                    kb.op("pe", lambda e: e.matmul(yo, SB1[h].t[:, 128:256], WU[h].t[:, 64:128], start=True, stop=False), [SB1[h], WU[h]], ydep, inc=False)
                    kb.op("pe", lambda e: e.matmul(yo, SB1[h].t[:, 384:512], Bf["Vb"].t[:, hs(h)], start=False, stop=False), [SB1[h], Bf["Vb"]], ydep, inc=False)
                    kb.op("pe", lambda e: e.matmul(yo, QTb[h].t[:, :], hb.t[0:64, hs(h)], start=False, stop=True), [QTb[h], hb], ydep, inc=False)
                    ho = hbank[0:64, hs(h)]
                    kb.op("pe", lambda e: e.matmul(ho, Bf["Bh"].t[:, hs(h)], WU[h].t[:, 64:128], start=True, stop=False), [Bf["Bh"], WU[h]], hdep, inc=False)
                    kb.op("pe", lambda e: e.matmul(ho, Bf["Kh"].t[:, hs(h)], Bf["Vb"].t[:, hs(h)], start=False, stop=False), [Bf["Kh"], Bf["Vb"]], hdep, inc=False)
                    kb.op("pe", lambda e: e.matmul(ho, MC[h].t[:, :], hb.t[0:64, hs(h)], start=False, stop=True), [MC[h], hb], hdep, inc=(h == 7))
                kb.copy("act", hb.t[:, :], hbank[0:64, :], hdep, [hb])
                kb.copy("dve", FF["y"].t[:], ybank[:, :], ydep, [FF["y"]])
                ck(kb, env, 11, [(0, FF["y"])])
                yield
                if getattr(env, "stop", None) == 100 + ci:
                    ck(kb, env, 100 + ci, [(0, FF["y"])])
                    yield
                if d == 0:
                    kb.dma("pool", env.YF[i * 128:(i + 1) * 128, :], FF["y"].t[:], reads=[FF["y"]], writes=[env.dd["YF"]])
                return

            def outp(ci, i):
                if d == 0 or (i < 2 and not need_ctx):
                    return
                FF = dict(F)
                FF.update(FD[ci % 3])
                Bf = BfD[0]
                c0 = tcol(i)
                yield
                kb.dma("sp", FF["yf"].t[:], env.YF[i * 128:(i + 1) * 128, :], reads=[env.dd["YF"]], writes=[FF["yf"]])
                y = FF["y"]
                _tt(kb, "pool", y.t[:], y.t[:], FF["yf"].t[:], ALU.add, [y, FF["yf"]], [y])
                yield
                kb.op("dve", lambda e: e.reduce_sum(s8["m"].t[:], v3(y.t[:]), AX.X), [y], [s8["m"]])
                kb.op("dve", lambda e: e.tensor_scalar_mul(s8["m"].t[:], s8["m"].t[:], -1.0 / 64), [s8["m"]], [s8["m"]])
                _tt(kb, "dve", v3(y.t[:]), v3(y.t[:]), bc8(s8["m"].t[:]), ALU.add, [y, s8["m"]], [y])
                yield
                _tt(kb, "pool", FF["u1"].t[:], y.t[:], y.t[:], ALU.mult, [y], [FF["u1"]])
                yield
                kb.op("dve", lambda e: e.reduce_sum(s8["var"].t[:], v3(FF["u1"].t[:]), AX.X), [FF["u1"]], [s8["var"]])
                _act(kb, s8["var"].t[:], s8["var"].t[:], AF.Ln, [s8["var"], env.epsg], [s8["var"]], bias=env.epsg.t[:, 0:1], scale=1.0 / 64)
                _act(kb, s8["var"].t[:], s8["var"].t[:], AF.Exp, [s8["var"]], [s8["var"]], scale=-0.5)
                _tt(kb, "dve", v3(y.t[:]), v3(y.t[:]), bc8(s8["var"].t[:]), ALU.mult, [y, s8["var"]], [y])
                yield
                _tt(kb, "pool", y.t[:], y.t[:], gng.t[:], ALU.mult, [y, gng], [y])
                yield
                _tt(kb, "pool", y.t[:], y.t[:], gnb.t[:], ALU.add, [y, gnb], [y])
                yield
                _tt(kb, "pool", FF["rk"].t[:], FF["r"].t[:], rkb.t[:], ALU.mult, [FF["r"], rkb], [FF["rk"]])
                yield
                _tt(kb, "pool", FF["u1"].t[:], FF["kd0"].t[:], FF["kd1"].t[:], ALU.add, [FF["kd0"], FF["kd1"]], [FF["u1"]])
                yield
                _tt(kb, "pool", FF["u1"].t[:], FF["u1"].t[:], FF["rk"].t[:], ALU.mult, [FF["u1"], FF["rk"]], [FF["u1"]])
                yield
                kb.op("dve", lambda e: e.reduce_sum(s8["bs"].t[:], v3(FF["u1"].t[:]), AX.X), [FF["u1"]], [s8["bs"]])
                _tt(kb, "dve", v3(FF["u2"].t[:]), v3(FF["v"].t[:]), bc8(s8["bs"].t[:]), ALU.mult, [FF["v"], s8["bs"]], [FF["u2"]])
                yield
                _tt(kb, "pool", y.t[:], y.t[:], FF["u2"].t[:], ALU.add, [y, FF["u2"]], [y])
                yield
                _tt(kb, "pool", Bf["ob"].t[:], y.t[:], FF["g"].t[:], ALU.mult, [y, FF["g"]], [Bf["ob"]])
                yield
                b, cc, deps = pp.get(256)
                psb = kb.banks_bf[b]
                for jp in range(4):
                    kb.op("pe", lambda e: e.transpose(psb[:, 2 * cc + jp * 128:2 * cc + (jp + 1) * 128], Bf["ob"].t[:, jp * 128:(jp + 1) * 128], idb.t[:]),
                          [Bf["ob"], idb], deps, inc=(jp == 3))
                kb.copy("act", mst.t[:], psb[:, 2 * cc:2 * cc + 512].rearrange("p (j t) -> p j t", t=128), deps, [mst])
                kb.dma("pool", env.MIXT[0:512, c0:c0 + 128].rearrange("(k p) c -> p k c", p=128), mst.t[:], reads=[mst], writes=[env.dd["MIXT"]])
                yield

            def drain(g):
                for _ in g:
                    pass

            drain(prep(0, order[0], 0))
            for ci in range(len(order) + 1):
                alive = []
                if ci < len(order):
                    alive.append(heads(ci, order[ci], ci % 2))
                if ci + 1 < len(order):
                    alive.append(prep(ci + 1, order[ci + 1], (ci + 1) % 2))
                if ci >= 1:
                    alive.append(outp(ci - 1, order[ci - 1]))
                while alive:
                    for g in list(alive):
                        try:
                            next(g)
                        except StopIteration:
                            alive.remove(g)
            kb.barrier()
    env.pp = PsPool(kb, list(range(8)))


PHASES["rwkv"] = phase_rwkv


def residual_ln(kb, env, es_tiles, xt, ps_list, gate, lng, lnb, out):
    st = es_tiles
    for half, (ps, deps) in enumerate(ps_list):
        sl = slice(half * 512, (half + 1) * 512)
        _tt(kb, "dve", out.t[:, sl], ps, gate.t[:, sl], ALU.mult, deps + [gate], [out])
    _stt(kb, "dve", out.t[:], xt.t[:], ALPHA, out.t[:], ALU.mult, ALU.add, [xt, out], [out])
    _act(kb, st["junk"].t[:], out.t[:], AF.Identity, [out], [st["junk"], st["s1"]], accum_out=st["s1"].t[:, 0:1])
    kb.op("dve", lambda e: e.tensor_scalar_mul(st["s1"].t[:, 0:1], st["s1"].t[:, 0:1], -1.0 / D), [st["s1"]], [st["s1"]])
    kb.op("dve", lambda e: e.tensor_scalar_add(out.t[:], out.t[:], st["s1"].t[:, 0:1]), [out, st["s1"]], [out])
    _act(kb, st["junk"].t[:], out.t[:], AF.Square, [out], [st["junk"], st["s2"]], accum_out=st["s2"].t[:, 0:1])
    _act(kb, st["s2"].t[:, 0:1], st["s2"].t[:, 0:1], AF.Sqrt, [st["s2"], env.epsl], [st["s2"]], bias=env.epsl.t[:, 0:1], scale=1.0 / D)
    kb.op("dve", lambda e: e.reciprocal(st["s2"].t[:, 0:1], st["s2"].t[:, 0:1]), [st["s2"]], [st["s2"]])
    _stt(kb, "dve", out.t[:], out.t[:], st["s2"].t[:, 0:1], lng.t[:], ALU.mult, ALU.mult, [out, st["s2"], lng], [out])
    _tt(kb, "pool", out.t[:], out.t[:], lnb.t[:], ALU.add, [out, lnb], [out])


def load_gate(kb, es, env, l, s, gi, name):
    t = kb.tile(es, [128, 1024], F32, name)
    kb.dma("sp", t.t[:], env.MODROW[l, s, gi].partition_broadcast(128), reads=[env.dd["MODROW"]], writes=[t])
    return t


def phase_wo(kb, env, l):
    need_ctx = l < DEPTH - 1
    env.pp = PsPool(kb, list(range(8)))
    pp = env.pp
    with ExitStack() as es:
        wo = kb.tile(es, [128, 8, 1024], BF16, "wo")
        with ExitStack() as es2:
            stage = [kb.tile(es2, [128, 1024], F32, "wost%d" % i) for i in range(2)]
            load_w_bf16(kb, es2, wo, env.w["w_o"][l], 8, 1024, stage)
            kb.barrier()
        gates = [load_gate(kb, es, env, l, s, 0, "g1_%d" % s) for s in range(2)]
        lng = load_bcast(kb, es, env.w["ln1_g"][l], 1024, "ln1g")
        lnb = load_bcast(kb, es, env.w["ln1_b"][l], 1024, "ln1b")
        st = {"junk": kb.tile(es, [128, 1024], F32, "junk"), "s1": kb.tile(es, [128, 1], F32, "s1"), "s2": kb.tile(es, [128, 1], F32, "s2")}
        env.ut_st = [kb.tile(es, [128, 8, 130], BF16, "utst%d" % i) for i in range(2)]
        env.ut_i = 0
        for t in env.ut_st:
            kb.op("pool", lambda e: e.memset(t.t[:], 0.0), [], [t])
        xts = [kb.tile(es, [128, 1024], F32, "xt%d" % i) for i in range(3)]
        outs = [kb.tile(es, [128, 1024], F32, "xo%d" % i) for i in range(2)]
        mts = [kb.tile(es, [128, 8, 128], BF16, "mt%d" % i) for i in range(3)]
        tiles = list(range(NTILE) if need_ctx else range(2, NTILE))

        def mm_stage(ii, i):
            c0 = tcol(i)
            xt, mt = xts[ii % 3], mts[ii % 3]
            kb.dma("sp", mt.t[:], env.MIXT[:, c0:c0 + 128].rearrange("(k p) c -> p k c", p=128), reads=[env.dd["MIXT"]], writes=[mt])
            kb.dma("sp", xt.t[:], xin_ap(env, l, i), reads=[env.dd["X2"]], writes=[xt])
            pl = []
            for half in range(2):
                b, _, deps = pp.get(512)
                ps = kb.banks[b]
                for k in range(8):
                    kb.op("pe", lambda e: e.matmul(ps[:, :], mt.t[:, k, :], wo.t[:, k, half * 512:(half + 1) * 512], start=(k == 0), stop=(k == 7)),
                          [mt, wo], deps, inc=(k == 7))
                pl.append((ps[:, :], deps))
            return pl

        def epi_stage(ii, i, pl):
            s = 1 if i < 2 else 0
            xt, out = xts[ii % 3], outs[ii % 2]
            residual_ln(kb, env, st, xt, pl, gates[s], lng, lnb, out)
            kb.dma("pool", env.X1[i * 128:(i + 1) * 128, :], out.t[:], reads=[out], writes=[env.dd["X1"]])
            emit_ut(kb, env, out, s, env.modf[l], 4, 3, env.U2T, i, pad=True)

        pend = mm_stage(0, tiles[0])
        for ii, i in enumerate(tiles):
            nxt = mm_stage(ii + 1, tiles[ii + 1]) if ii + 1 < len(tiles) else None
            epi_stage(ii, i, pend)
            pend = nxt
        kb.barrier()


def phase_ffu(kb, env, l):
    need_ctx = l < DEPTH - 1
    env.pp = PsPool(kb, list(range(8)))
    pp = env.pp
    NF = 2 * DFF // 128
    with ExitStack() as es:
        wup = kb.tile(es, [128, 8, 2 * DFF], BF16, "wup")
        fcw = []
        with ExitStack() as es2:
            stage = [kb.tile(es2, [128, 2816], F32, "wust%d" % i) for i in range(2)]
            for hh in range(2):
                load_w_bf16(kb, es2, wup, env.w["w_up"][l][:, hh * 2816:(hh + 1) * 2816], 8, 2816, stage, col_off=hh * 2816)
            kb.barrier()
        for a in range(3):
            fcw.append(load_vec_fm(kb, env, es, env.w["ffn_conv_w"][l][a], NF, "fcw%d" % a))
        fcb = load_vec_fm(kb, env, es, env.w["ffn_conv_b"][l], NF, "fcb")
        u2b = [kb.tile(es, [128, 8, 514], BF16, "u2b%d" % i) for i in range(2)]
        gst = [kb.tile(es, [128, 22, 512], BF16, "gst%d" % i) for i in range(2)]
        hs_ = [kb.tile(es, [128, 514], F32, "hs%d" % i) for i in range(4)]
        tt_ = [kb.tile(es, [128, 512], F32, "tt%d" % i) for i in range(4)]
        sgt = [kb.tile(es, [128, 512], F32, "sgt%d" % i) for i in range(2)]
        blocks = BLOCKS if need_ctx else BLOCKS[1:]
        hi = 0
        for bi, (g0, n) in enumerate(blocks):
            c0 = gcol(g0)
            ub = u2b[bi % 2]
            gs = gst[bi % 2]
            kb.dma("sp", ub.t[:, :, 0:n + 2], env.U2T[:, c0 - 1:c0 + n + 1].rearrange("(k p) c -> p k c", p=128), reads=[env.dd["U2T"]], writes=[ub])
            for j in range(22):
                tv = []
                for which in range(2):
                    jf = j + 22 * which
                    hs = hs_[hi % 4]
                    tt = tt_[hi % 4]
                    hi += 1
                    b, _, deps = pp.get(512)
                    ps = kb.banks[b]
                    for k in range(8):
                        kb.op("pe", lambda e: e.matmul(ps[:, 0:n], wup.t[:, k, jf * 128:(jf + 1) * 128], ub.t[:, k, 1:n + 1], start=(k == 0), stop=(k == 7)),
                              [wup, ub], deps, inc=(k == 7))
                    b2, c2, deps2 = pp.get(2)
                    ps2 = kb.banks[b2]
                    for k in range(8):
                        kb.op("pe", lambda e: e.matmul(ps2[:, c2:c2 + 2], wup.t[:, k, jf * 128:(jf + 1) * 128], ub.t[:, k, 0:n + 2:n + 1], start=(k == 0), stop=(k == 7)),
                              [wup, ub], deps2, inc=(k == 7))
                    kb.copy("act", hs.t[:, 1:n + 1], ps[:, 0:n], deps, [hs])
                    _act(kb, tt.t[:, 0:n], ps[:, 0:n], AF.Identity, deps + [fcw[1], fcb], [tt], bias=fcb.t[:, jf:jf + 1], scale=fcw[1].t[:, jf:jf + 1])
                    kb.copy("dve", hs.t[:, 0:n + 2:n + 1], ps2[:, c2:c2 + 2], deps2, [hs])
                    _stt(kb, "dve", tt.t[:, 0:n], hs.t[:, 0:n], fcw[0].t[:, jf:jf + 1], tt.t[:, 0:n], ALU.mult, ALU.add, [hs, tt, fcw[0]], [tt])
                    _stt(kb, "dve", tt.t[:, 0:n], hs.t[:, 2:n + 2], fcw[2].t[:, jf:jf + 1], tt.t[:, 0:n], ALU.mult, ALU.add, [hs, tt, fcw[2]], [tt])
                    tv.append(tt)
                sg = sgt[j % 2]
                _act(kb, sg.t[:, 0:n], tv[0].t[:, 0:n], AF.Silu, [tv[0]], [sg])
                _tt(kb, "pool", gs.t[:, j, 0:n], sg.t[:, 0:n], tv[1].t[:, 0:n], ALU.mult, [sg, tv[1]], [gs])
            kb.dma("pool", env.GT[:, c0:c0 + n].rearrange("(k p) c -> p k c", p=128), gs.t[:, :, 0:n], reads=[gs], writes=[env.dd["GT"]])
        kb.barrier()


def phase_ffd(kb, env, l):
    need_ctx = l < DEPTH - 1
    last = l == DEPTH - 1
    env.pp = PsPool(kb, list(range(8)))
    pp = env.pp
    with ExitStack() as es:
        wdn = kb.tile(es, [128, 22, 1024], BF16, "wdn")
        with ExitStack() as es2:
            stage = [kb.tile(es2, [128, 1024], F32, "wdst%d" % i) for i in range(3)]
            load_w_bf16(kb, es2, wdn, env.w["w_down"][l], 22, 1024, stage)
            kb.barrier()
        gates = [load_gate(kb, es, env, l, s, 1, "g2_%d" % s) for s in range(2)]
        lng = load_bcast(kb, es, env.w["ln2_g"][l], 1024, "ln2g")
        lnb = load_bcast(kb, es, env.w["ln2_b"][l], 1024, "ln2b")
        st = {"junk": kb.tile(es, [128, 1024], F32, "junk"), "s1": kb.tile(es, [128, 1], F32, "s1"), "s2": kb.tile(es, [128, 1], F32, "s2")}
        env.ut_st = [kb.tile(es, [128, 8, 130], BF16, "utst%d" % i) for i in range(2)]
        env.ut_i = 0
        xts = [kb.tile(es, [128, 1024], F32, "xt%d" % i) for i in range(3)]
        outs = [kb.tile(es, [128, 1024], F32, "xo%d" % i) for i in range(2)]
        gts = [kb.tile(es, [128, 22, 128], BF16, "gt%d" % i) for i in range(3)]
        tiles = list(range(NTILE) if need_ctx else range(2, NTILE))

        def mm_stage(ii, i):
            c0 = tcol(i)
            xt, gt = xts[ii % 3], gts[ii % 3]
            kb.dma("sp", gt.t[:], env.GT[:, c0:c0 + 128].rearrange("(k p) c -> p k c", p=128), reads=[env.dd["GT"]], writes=[gt])
            kb.dma("sp", xt.t[:], env.X1[i * 128:(i + 1) * 128, :], reads=[env.dd["X1"]], writes=[xt])
            pl = []
            for half in range(2):
                b, _, deps = pp.get(512)
                ps = kb.banks[b]
                for k in range(22):
                    kb.op("pe", lambda e: e.matmul(ps[:, :], gt.t[:, k, :], wdn.t[:, k, half * 512:(half + 1) * 512], start=(k == 0), stop=(k == 21)),
                          [gt, wdn], deps, inc=(k == 21))
                pl.append((ps[:, :], deps))
            return pl

        def epi_stage(ii, i, pl):
            s = 1 if i < 2 else 0
            xt, out = xts[ii % 3], outs[ii % 2]
            residual_ln(kb, env, st, xt, pl, gates[s], lng, lnb, out)
            if last:
                kb.dma("pool", env.y[(i - 2) * 128:(i - 1) * 128, :], out.t[:], reads=[out], writes=[env.dd["y"]])
            else:
                kb.dma("pool", env.X2[i * 128:(i + 1) * 128, :], out.t[:], reads=[out], writes=[env.dd["X2"]])
                emit_ut(kb, env, out, s, env.modf[l + 1], 1, 0, env.UT, i, pad=False)

        pend = mm_stage(0, tiles[0])
        for ii, i in enumerate(tiles):
            nxt = mm_stage(ii + 1, tiles[ii + 1]) if ii + 1 < len(tiles) else None
            epi_stage(ii, i, pend)
            pend = nxt
        kb.barrier()


PHASES["wo"] = phase_wo
PHASES["ffu"] = phase_ffu
PHASES["ffd"] = phase_ffd


_CACHE = {}


def kernel(**inputs):
    if "nc" not in _CACHE:
        _CACHE["nc"] = build()[0]
        _CACHE["consts"] = make_consts()
    nc = _CACHE["nc"]
    consts = _CACHE["consts"]
    B = inputs["x"].shape[0]
    shared = {k: np.ascontiguousarray(np.asarray(inputs[k], dtype=np.float32)) for k in W_SHAPES}
    shared["c_ctx"] = np.ascontiguousarray(np.asarray(inputs["c_ctx"], dtype=np.float32))
    for k, v in consts.items():
        shared["k_" + k] = v
    in_maps = []
    for b in range(B):
        m = dict(shared)
        m["x"] = np.ascontiguousarray(np.asarray(inputs["x"][b], dtype=np.float32))
        m["c"] = np.ascontiguousarray(np.asarray(inputs["c"][b], dtype=np.float32))
        m["ctx"] = np.ascontiguousarray(np.asarray(inputs["ctx"][b], dtype=np.float32))
        in_maps.append(m)
    res = run_bass_kernel_spmd(nc, in_maps, core_ids=list(range(B)))
    return np.stack([np.asarray(r["y"], dtype=np.float32) for r in res.results], axis=0)
```

```python
import numpy as np
import ml_dtypes
from contextlib import ExitStack
import concourse.bass as bass
import concourse.mybir as mybir
from concourse.bass_utils import run_bass_kernel_spmd

F32 = mybir.dt.float32
BF16 = mybir.dt.bfloat16
AF = mybir.ActivationFunctionType
ALU = mybir.AluOpType
AX = mybir.AxisListType

D = 1024
SEQ = 4096
CTXL = 256
NT = SEQ + CTXL
NCOL = NT + 3
NTILE = NT // 128
DEPTH = 2
DFF = 2816
RW = 1920
INC = 2592
ALPHA = (2 * DEPTH) ** 0.25
LN_EPS = 1e-5
RMS_EPS = 1e-6
GN_EPS = 64e-5
CW = float(np.exp(-0.5))
QSCALE = 96 ** -0.5
BLOCKS = [(0, 256)] + [(256 + 512 * j, 512) for j in range(8)]


def tcol(i):
    return 128 * i + (1 if i < 2 else 2)


def gcol(g):
    return g + (1 if g < 256 else 2)


class Dep:
    __slots__ = ("w", "r", "x")

    def __init__(self, x=False):
        self.w = None
        self.r = {}
        self.x = x


class T:
    __slots__ = ("t", "d")

    def __init__(self, t, d=None):
        self.t = t
        self.d = d if d is not None else Dep()


class KB:
    NDMA = 8

    def __init__(self, nc, es):
        self.nc = nc
        self.engs = {"pe": nc.tensor, "act": nc.scalar, "dve": nc.vector,
                     "pool": nc.gpsimd, "sp": nc.sync}
        self.sems = {}
        self.cnt = {}
        for e in self.engs:
            self.sems[e] = es.enter_context(nc.semaphore("s_" + e))
            self.cnt[e] = 0
        self.dq = {}
        for q in ("sp", "pool", "act"):
            self.dq[q] = 0
            for j in range(self.NDMA):
                key = "d_%s%d" % (q, j)
                self.sems[key] = es.enter_context(nc.semaphore(key))
                self.cnt[key] = 0
        self.known = {e: {} for e in self.engs}
        self.pending = {e: False for e in self.engs}
        self.nwait = 0
        self.nins = 0
        self.halt = False
        self.banks = []
        self.banks_bf = []
        self.bdep = []
        for b in range(8):
            t = es.enter_context(nc.psum_tensor("psb%d" % b, [128, 512], F32))
            self.banks.append(t)
            self.banks_bf.append(t.bitcast(BF16))
            bd = Dep(x=True)
            self.bdep.append((bd, bd))
        self.ev_i = 0
        self.sb_i = 0

    def _need(self, reads, writes):
        need = {}
        for d in reads:
            if d.w is not None:
                k, v = d.w
                if need.get(k, 0) < v:
                    need[k] = v
        for d in writes:
            if d.w is not None:
                k, v = d.w
                if need.get(k, 0) < v:
                    need[k] = v
            for k, v in d.r.items():
                if need.get(k, 0) < v:
                    need[k] = v
        return need

    def _waits(self, e, need, skip_own=False):
        eng = self.engs[e]
        kn = self.known[e]
        for k, v in need.items():
            if skip_own and k == e:
                continue
            if kn.get(k, 0) < v:
                eng.wait_ge(self.sems[k], v)
                kn[k] = v
                self.nwait += 1

    def _mark(self, tok, reads, writes):
        k, v = tok
        for d in reads:
            if d.r.get(k, 0) < v:
                d.r[k] = v
        for d in writes:
            d.w = tok
            d.r = {}

    def op(self, e, fn, reads=(), writes=(), inc=True):
        if self.halt:
            return None
        reads = [x.d if isinstance(x, T) else x for x in reads]
        writes = [x.d if isinstance(x, T) else x for x in writes]
        xs = [d for d in reads if d.x]
        if xs:
            reads = [d for d in reads if not d.x]
            writes = writes + xs
        need = self._need(reads, writes)
        self._waits(e, need, skip_own=(e == "pe"))
        ins = fn(self.engs[e])
        self.nins += 1
        if inc:
            self.cnt[e] += 1
            ins.then_inc(self.sems[e], 1)
            tok = (e, self.cnt[e])
            self.pending[e] = False
        else:
            tok = (e, self.cnt[e] + 1)
            self.pending[e] = True
        self._mark(tok, reads, writes)
        return tok

    def dma(self, q, out, in_, reads=(), writes=(), **kw):
        if self.halt:
            return None
        reads = [x.d if isinstance(x, T) else x for x in reads]
        writes = [x.d if isinstance(x, T) else x for x in writes]
        j = self.dq[q] % self.NDMA
        self.dq[q] += 1
        key = "d_%s%d" % (q, j)
        need = self._need(reads, writes)
        if self.cnt[key] > 0:
            need[key] = max(need.get(key, 0), self.cnt[key])
        self._waits(q, need)
        ins = self.engs[q].dma_start(out=out, in_=in_, **kw)
        self.nins += 1
        self.cnt[key] += 16
        ins.then_inc(self.sems[key], 16)
        tok = (key, self.cnt[key])
        self._mark(tok, reads, writes)
        return tok

    def barrier(self, engines=("pe", "act", "dve", "pool", "sp")):
        if self.halt:
            return
        for e in self.engs:
            assert not self.pending[e], e
        need = {k: v for k, v in self.cnt.items() if v > 0}
        for e in engines:
            self._waits(e, dict(need))

    def tile(self, es, shape, dtype, name):
        self.tile_i = getattr(self, "tile_i", 0) + 1
        return T(es.enter_context(self.nc.sbuf_tensor("%s_%d" % (name, self.tile_i), list(shape), dtype)))

    def ev_eng(self):
        self.ev_i += 1
        return "act" if self.ev_i % 2 else "dve"

    def sb_eng(self):
        self.sb_i += 1
        return "pool" if self.sb_i % 2 else "dve"

    def copy(self, e, out, in_, reads, writes, scale=None):
        if e == "act":
            if scale is None:
                return self.op("act", lambda g: g.copy(out, in_), reads, writes)
            return self.op("act", lambda g: g.mul(out, in_, scale), reads, writes)
        if scale is None:
            return self.op(e, lambda g: g.tensor_copy(out, in_), reads, writes)
        return self.op(e, lambda g: g.tensor_scalar_mul(out, in_, scale), reads, writes)


class PsPool:
    def __init__(self, kb, banks):
        self.kb = kb
        self.halves = [(b, h) for b in banks for h in (0, 1)]
        self.i = 0

    def get(self, ncols):
        n = len(self.halves)
        if ncols <= 256:
            b, h = self.halves[self.i % n]
            self.i += 1
            return b, h * 256, [self.kb.bdep[b][h]]
        if self.i % 2:
            self.i += 1
        b, _ = self.halves[self.i % n]
        self.i += 2
        return b, 0, [self.kb.bdep[b][0], self.kb.bdep[b][1]]


def make_consts():
    c = {}
    c["identf"] = np.eye(128, dtype=np.float32)
    c["onesf"] = np.ones((128, 128), dtype=np.float32)
    t = np.arange(128)
    m4 = np.zeros((2, 128, 512), np.float32)
    mt = np.zeros((2, 128, 128), np.float32)
    tri = np.zeros((2, 3, 128, 128), np.float32)
    for d in range(2):
        before = (t[:, None] < t[None, :]) if d == 0 else (t[:, None] > t[None, :])
        beq = before | (t[:, None] == t[None, :])
        m4[d, :, 0:128] = -1.0 * before
        m4[d, :, 128:256] = beq
        m4[d, :, 256:384] = before
        m4[d, :, 384:512] = beq
        mt[d] = -1.0 * before.T
        tri[d, 0] = -CW * beq
        tri[d, 1] = -CW * before
        tri[d, 2] = -CW * (~beq)
    c["mask4"] = m4
    c["maskt"] = mt
    c["tri"] = tri
    c["cvec"] = np.full((128, 1), -CW, np.float32)
    esel = np.zeros((65, 64), np.float32)
    esel[64, :] = 1.0
    c["esel"] = esel
    n_pairs = 8
    inv = 10000.0 ** (-np.arange(n_pairs, dtype=np.float32) / n_pairs)
    row = np.repeat(np.arange(SEQ // 64, dtype=np.float32), 64)
    col = np.tile(np.arange(64, dtype=np.float32), SEQ // 64)
    ang = np.concatenate([row[:, None] * inv, col[:, None] * inv], axis=-1).astype(np.float32)
    cos = np.cos(ang).astype(np.float32)
    sin = np.sin(ang).astype(np.float32)
    cos2 = np.repeat(cos, 2, axis=1).T
    sin2 = np.repeat(sin, 2, axis=1).T
    cosq = np.ones((96, NT), np.float32)
    sinq = np.zeros((96, NT), np.float32)
    cosq[64:96, CTXL:] = cos2
    sinq[64:96, CTXL:] = sin2
    c["cosk"] = cosq.copy()
    c["sink"] = sinq.copy()
    c["cosq"] = (cosq * QSCALE).astype(np.float32)
    c["sinq"] = (sinq * QSCALE).astype(np.float32)
    return c


CONST_SHAPES = {k: v.shape for k, v in make_consts().items()}

W_SHAPES = {
    "w_ada": (2, 1024, 6144), "b_ada": (2, 6144), "w_in": (2, 1024, 2592), "rwkv_conv": (2, 3, 1920),
    "w0": (2, 2, 512), "w_b": (2, 2, 64, 512), "a0": (2, 2, 512), "a_b": (2, 2, 64, 512),
    "g_b": (2, 128, 512), "k_k": (2, 512), "k_a": (2, 512), "r_k": (2, 8, 64), "gn_g": (2, 512),
    "gn_b": (2, 512), "q_norm_g": (2, 384), "w_uq": (2, 384, 768), "kv_norm_g": (2, 256),
    "w_ukv": (2, 256, 1024), "w_o": (2, 1024, 1024), "ln1_g": (2, 1024), "ln1_b": (2, 1024),
    "w_up": (2, 1024, 5632), "ffn_conv_w": (2, 3, 5632), "ffn_conv_b": (2, 5632),
    "w_down": (2, 2816, 1024), "ln2_g": (2, 1024), "ln2_b": (2, 1024),
}


class Env:
    pass


def load_vec_fm(kb, env, es, vec_ap, n, name):
    nc = kb.nc
    rows = kb.tile(es, [n, 128], F32, name + "_r")
    out = kb.tile(es, [128, n], F32, name)
    kb.dma("sp", rows.t[:], vec_ap.rearrange("(n p) -> n p", p=128), writes=[rows])
    b, c0, deps = env.pp.get(n)
    ps = kb.banks[b]
    kb.op("pe", lambda e: e.transpose(ps[:, c0:c0 + n], rows.t[:], env.identf.t[0:n, 0:n]),
          reads=[rows, env.identf], writes=deps)
    kb.copy("dve", out.t[:], ps[:, c0:c0 + n], deps, [out])
    return out


def load_bcast(kb, es, vec_ap, n, name):
    out = kb.tile(es, [128, n], F32, name)
    kb.dma("sp", out.t[:], vec_ap.partition_broadcast(128), writes=[out])
    return out


def load_w_bf16(kb, es, dst, w_ap, kchunks, ncols, stage, col_off=0, scale=None):
    wv = w_ap.rearrange("(k p) n -> k p n", p=128)
    for k in range(kchunks):
        st = stage[k % len(stage)]
        kb.dma("sp", st.t[:, 0:ncols], wv[k], writes=[st])
        e = ("act", "dve", "pool")[k % 3]
        if scale is None:
            kb.copy(e, dst.t[:, k, col_off:col_off + ncols], st.t[:, 0:ncols], [st], [dst])
        else:
            kb.op("dve", lambda g: g.tensor_scalar_mul(dst.t[:, k, col_off:col_off + ncols], st.t[:, 0:ncols], scale.t[:, k:k + 1]),
                  [st, scale], [dst])


def emit_ut(kb, env, xt, s, modf, w_scale, w_shift, dst, i, pad):
    c0 = tcol(i)
    st = env.ut_st[env.ut_i % 2]
    env.ut_i += 1
    for half in range(2):
        b, _, deps = env.pp.get(512)
        ps = kb.banks[b]
        for j in range(4):
            c = half * 4 + j
            kb.op("pe", lambda e: e.transpose(ps[:, j * 128:(j + 1) * 128], xt.t[:, c * 128:(c + 1) * 128], env.identf.t[:]),
                  reads=[xt, env.identf], writes=deps, inc=(j == 3))
        for j in range(4):
            c = half * 4 + j
            kb.op("act", lambda e: e.activation(st.t[:, c, 1:129], ps[:, j * 128:(j + 1) * 128], AF.Identity,
                                                bias=modf.t[:, w_shift * 8 + c, s:s + 1], scale=modf.t[:, w_scale * 8 + c, s:s + 1]),
                  reads=deps + [modf], writes=[st])
    lo, hi = 1, 129
    if pad and i in (0, 2):
        lo = 0
    if pad and i in (1, NTILE - 1):
        hi = 130
    kb.dma("pool", dst[:, c0 - 1 + lo:c0 - 1 + hi].rearrange("(k p) c -> p k c", p=128), st.t[:, :, lo:hi],
           reads=[st], writes=[env.dd[dst.tensor.name]])


def phase_mod(kb, env, l):
    nc = kb.nc
    with ExitStack() as es:
        modf = env.modf[l]
        crow = kb.tile(es, [16, 128], F32, "crow")
        kb.dma("sp", crow.t[0:8, :], env.c.rearrange("(n p) -> n p", p=128), writes=[crow])
        kb.dma("sp", crow.t[8:16, :], env.c_ctx.rearrange("(n p) -> n p", p=128), writes=[crow])
        sct = kb.tile(es, [128, 2, 8], F32, "sct")
        b, c0, deps = env.pp.get(16)
        ps = kb.banks[b]
        kb.op("pe", lambda e: e.transpose(ps[:, c0:c0 + 16], crow.t[:], env.identf.t[0:16, 0:16]), [crow, env.identf], deps)
        kb.op("act", lambda e: e.activation(sct.t[:].rearrange("p s k -> p (s k)"), ps[:, c0:c0 + 16], AF.Silu), deps, [sct])
        bF = load_vec_fm(kb, env, es, env.w["b_ada"][l], 48, "bF")
        stg = [kb.tile(es, [128, 8, 768], F32, "wada%d" % i) for i in range(2)]
        bq, cq0, depq = env.pp.get(96)
        psq = kb.banks[bq]
        wv = env.w["w_ada"][l].rearrange("(k p) n -> p k n", p=128)
        for pc in range(8):
            st = stg[pc % 2]
            kb.dma("sp", st.t[:], wv[:, :, pc * 768:(pc + 1) * 768], writes=[st])
            for jj in range(6):
                j = pc * 6 + jj
                for k in range(8):
                    kb.op("pe", lambda e: e.matmul(psq[:, cq0 + 2 * j:cq0 + 2 * j + 2], st.t[:, k, jj * 128:(jj + 1) * 128], sct.t[:, :, k],
                                                  start=(k == 0), stop=(k == 7)),
                          [st, sct], depq, inc=(k == 7))
        kb.op("dve", lambda e: e.tensor_tensor(modf.t[:], psq[:, cq0:cq0 + 96].rearrange("p (j s) -> p j s", s=2),
                                              bF.t[:].unsqueeze(2).to_broadcast([128, 48, 2]), ALU.add),
              depq + [bF], [modf])
        for wch in (1, 4):
            kb.op("dve", lambda e: e.tensor_scalar_add(modf.t[:, wch * 8:(wch + 1) * 8, :], modf.t[:, wch * 8:(wch + 1) * 8, :], 1.0),
                  [modf], [modf])
        grow = kb.tile(es, [1, 2, 2, 1024], F32, "grow")
        brow = kb.tile(es, [1, 2, 1024], F32, "brow")
        for gi, wch in enumerate((2, 5)):
            kb.dma("sp", brow.t[:, gi, :], env.w["b_ada"][l][wch * 1024:(wch + 1) * 1024].rearrange("(o n) -> o n", o=1), writes=[brow])
        for gi, wch in enumerate((2, 5)):
            st = stg[gi % 2]
            for hh in range(2):
                col = wch * 1024 + hh * 512
                kb.dma("sp", st.t[:, :, 0:512], wv[:, :, col:col + 512], writes=[st])
                for s in range(2):
                    b2, c2, dep2 = env.pp.get(512)
                    ps2 = kb.banks[b2]
                    for k in range(8):
                        kb.op("pe", lambda e: e.matmul(ps2[0:1, :], sct.t[:, s, k:k + 1], st.t[:, k, 0:512], start=(k == 0), stop=(k == 7)),
                              [st, sct], dep2, inc=(k == 7))
                    kb.op("dve", lambda e: e.tensor_tensor(grow.t[:, s, gi, hh * 512:(hh + 1) * 512], ps2[0:1, :], brow.t[:, gi, hh * 512:(hh + 1) * 512], ALU.add),
                          dep2 + [brow], [grow])
        kb.dma("sp", env.MODROW[l:l + 1].rearrange("o s g n -> o (s g n)"), grow.t[:].rearrange("o s g n -> o (s g n)"),
               reads=[grow], writes=[env.dd["MODROW"]])
        kb.barrier()


def xin_ap(env, l, i):
    if l == 0:
        return env.ctx[i * 128:(i + 1) * 128, :] if i < 2 else env.x[(i - 2) * 128:(i - 1) * 128, :]
    return env.X2[i * 128:(i + 1) * 128, :]


def phase_u0(kb, env):
    with ExitStack() as es:
        env.ut_st = [kb.tile(es, [128, 8, 130], BF16, "utst%d" % i) for i in range(2)]
        env.ut_i = 0
        xts = [kb.tile(es, [128, 1024], F32, "xt%d" % i) for i in range(3)]
        for i in range(NTILE):
            xt = xts[i % 3]
            kb.dma("sp", xt.t[:], xin_ap(env, 0, i), writes=[xt])
            emit_ut(kb, env, xt, 1 if i < 2 else 0, env.modf[0], 1, 0, env.UT, i, pad=False)
        kb.barrier()


def phase_p1(kb, env, l):
    nc = kb.nc
    with ExitStack() as es:
        win = kb.tile(es, [128, 8, 2624], BF16, "win")
        stage = [kb.tile(es, [128, 2592], F32, "wst%d" % i) for i in range(2)]
        load_w_bf16(kb, es, win, env.w["w_in"][l], 8, 2592, stage)
        kb.op("dve", lambda e: e.tensor_scalar_mul(win.t[:, :, 2592:2624:2], win.t[:, :, 2561:2592:2], -1.0), [win], [win])
        kb.op("dve", lambda e: e.tensor_copy(win.t[:, :, 2593:2624:2], win.t[:, :, 2560:2592:2]), [win], [win])
        utb = [kb.tile(es, [128, 8, 512], BF16, "utb%d" % i) for i in range(2)]
        pst = [kb.tile(es, [128, 15, 514], BF16, "pst%d" % i) for i in range(2)]
        pmst = [kb.tile(es, [128, 6, 512], F32, "pmst%d" % i) for i in range(2)]
        for t in pst:
            kb.op("pool", lambda e: e.memset(t.t[:], 0.0), [], [t])
        for bi, (g0, n) in enumerate(BLOCKS):
            c0 = gcol(g0)
            ub = utb[bi % 2]
            ps_ = pst[bi % 2]
            pm_ = pmst[bi % 2]
            kb.dma("sp", ub.t[:, :, 0:n], env.UT[:, c0:c0 + n].rearrange("(k p) c -> p k c", p=128),
                   reads=[env.dd["UT"]], writes=[ub])
            for jf in range(21):
                rows = 128 if jf < 20 else 64
                b, _, deps = env.pp.get(512)
                ps = kb.banks[b]
                for k in range(8):
                    kb.op("pe", lambda e: e.matmul(ps[0:rows, 0:n], win.t[:, k, jf * 128:jf * 128 + rows], ub.t[:, k, 0:n],
                                                  start=(k == 0), stop=(k == 7)),
                          [win, ub], deps, inc=(k == 7))
                if jf < 15:
                    kb.copy(kb.ev_eng(), ps_.t[:, jf, 1:1 + n], ps[:, 0:n], deps, [ps_])
                else:
                    kb.copy(kb.ev_eng(), pm_.t[0:rows, jf - 15, 0:n], ps[0:rows, 0:n], deps, [pm_])
            lo, hi = 1, 1 + n
            if bi == 0:
                lo, hi = 0, n + 2
            if bi == len(BLOCKS) - 1:
                hi = n + 2
            kb.dma("pool", env.PT[:, c0 - 1 + lo:c0 - 1 + hi].rearrange("(k p) c -> p k c", p=128), ps_.t[:, :, lo:hi],
                   reads=[ps_], writes=[env.dd["PT"]])
            kb.dma("pool", env.PM[0:640, c0:c0 + n].rearrange("(k p) c -> p k c", p=128), pm_.t[:, 0:5, 0:n],
                   reads=[pm_], writes=[env.dd["PM"]])
            kb.dma("pool", env.PM[640:704, c0:c0 + n], pm_.t[0:64, 5, 0:n], reads=[pm_], writes=[env.dd["PM"]])
        kb.barrier()


SCRATCH = {
    "UT": ([1024, NCOL], BF16), "PT": ([RW, NCOL], BF16), "PM": ([704, NCOL], F32),
    "MIXT": ([1024, NCOL], BF16), "X1": ([NT, 1024], F32), "U2T": ([1024, NCOL], BF16),
    "GT": ([DFF, NCOL], BF16), "X2": ([NT, 1024], F32), "YF": ([NT, 512], F32),
    "MODROW": ([2, 2, 2, 1024], F32), "DBG": ([16, 128, 512], F32),
}


def build(phases=None, debug_out=(), debug_in=(), stop=None):
    nc = bass.Bass("TRN2", target_bir_lowering=False)
    env = Env()
    env.x = nc.dram_tensor("x", [SEQ, D], F32, kind="ExternalInput").ap()
    env.c = nc.dram_tensor("c", [D], F32, kind="ExternalInput").ap()
    env.ctx = nc.dram_tensor("ctx", [CTXL, D], F32, kind="ExternalInput").ap()
    env.c_ctx = nc.dram_tensor("c_ctx", [D], F32, kind="ExternalInput").ap()
    env.w = {k: nc.dram_tensor(k, list(s), F32, kind="ExternalInput").ap() for k, s in W_SHAPES.items()}
    env.cst = {k: nc.dram_tensor("k_" + k, list(s), F32, kind="ExternalInput").ap() for k, s in CONST_SHAPES.items()}
    env.y = nc.dram_tensor("y", [SEQ, D], F32, kind="ExternalOutput").ap()
    env.dd = {}
    for name, (shape, dt_) in SCRATCH.items():
        kind = "ExternalOutput" if name in debug_out else ("ExternalInput" if name in debug_in else "Internal")
        setattr(env, name, nc.dram_tensor(name, shape, dt_, kind=kind).ap())
        env.dd[name] = Dep()
    env.dd["y"] = Dep()
    env.stop = stop
    allp = ["mod", "u0"]
    for l in range(DEPTH):
        allp += ["p1_%d" % l, "mla_%d" % l, "rwkv_%d" % l, "wo_%d" % l, "ffu_%d" % l, "ffd_%d" % l]
    if phases is None:
        phases = allp
    with ExitStack() as es:
        kb = KB(nc, es)
        env.pp = PsPool(kb, list(range(8)))
        env.identf = kb.tile(es, [128, 128], F32, "identf")
        kb.dma("sp", env.identf.t[:], env.cst["identf"], writes=[env.identf])
        env.identb = kb.tile(es, [128, 128], BF16, "identb")
        kb.copy("dve", env.identb.t[:], env.identf.t[:], [env.identf], [env.identb])
        env.onesf = kb.tile(es, [128, 128], F32, "onesf")
        kb.dma("sp", env.onesf.t[:], env.cst["onesf"], writes=[env.onesf])
        env.modf = [kb.tile(es, [128, 48, 2], F32, "modf%d" % l) for l in range(DEPTH)]
        env.epsr = kb.tile(es, [128, 1], F32, "epsr")
        kb.op("pool", lambda e: e.memset(env.epsr.t[:], RMS_EPS), [], [env.epsr])
        env.epsl = kb.tile(es, [128, 1], F32, "epsl")
        kb.op("pool", lambda e: e.memset(env.epsl.t[:], LN_EPS), [], [env.epsl])
        env.epsg = kb.tile(es, [128, 1], F32, "epsg")
        kb.op("pool", lambda e: e.memset(env.epsg.t[:], GN_EPS), [], [env.epsg])
        for ph in phases:
            if ph == "mod":
                for l in range(DEPTH):
                    phase_mod(kb, env, l)
            elif ph == "u0":
                phase_u0(kb, env)
            else:
                name, l = ph.rsplit("_", 1)
                try:
                    PHASES[name](kb, env, int(l))
                except StopPhase:
                    print("stopped at checkpoint", env.stop, flush=True)
                    break
        kb.barrier()
        env.kb = kb
    print("built: %d instructions, %d waits" % (kb.nins, kb.nwait), flush=True)
    return nc, env


PHASES = {"p1": phase_p1}


def phase_mla(kb, env, l):
    nc = kb.nc
    need_ctx = l < DEPTH - 1
    env.pp = PsPool(kb, [0, 1, 2, 3, 4])
    pp = env.pp
    with ExitStack() as es:
        cqn = kb.tile(es, [128, 3, NT], BF16, "cqn")
        ckvn = kb.tile(es, [128, 2, NT], BF16, "ckvn")
        va = kb.tile(es, [128, NTILE, 8, 65], BF16, "va")
        KT = [kb.tile(es, [128, NT], BF16, "kt%d" % i) for i in range(2)]
        wq2 = kb.tile(es, [128, 3, 8, 2, 96], BF16, "wq2")
        wk = kb.tile(es, [128, 2, 512], BF16, "wk")
        wv = kb.tile(es, [128, 2, 512], BF16, "wv")
        esel = kb.tile(es, [65, 64], F32, "esel")
        kb.dma("sp", esel.t[:], env.cst["esel"], writes=[esel])
        with ExitStack() as es2:
            qg = load_vec_fm(kb, env, es2, env.w["q_norm_g"][l], 3, "qg")
            kg = load_vec_fm(kb, env, es2, env.w["kv_norm_g"][l], 2, "kg")
            wq_st = kb.tile(es2, [128, 3, 768], F32, "wq_st")
            wkv_st = kb.tile(es2, [128, 2, 1024], F32, "wkv_st")
            kb.dma("sp", wq_st.t[:], env.w["w_uq"][l].rearrange("(k p) n -> p k n", p=128), writes=[wq_st])
            kb.dma("sp", wkv_st.t[:], env.w["w_ukv"][l].rearrange("(k p) n -> p k n", p=128), writes=[wkv_st])
            kb.op("pool", lambda e: e.memset(wq2.t[:], 0.0), [], [wq2])
            kb.op("pool", lambda e: e.memset(va.t[:], 1.0), [], [va])
            for k in range(3):
                kb.op("dve", lambda e: e.tensor_scalar_mul(wq2.t[:, k, :, 0, :], wq_st.t[:, k, :].rearrange("p (h d) -> p h d", d=96), qg.t[:, k:k + 1]),
                      [wq_st, qg], [wq2])
                kb.op("dve", lambda e: e.tensor_scalar_mul(wq2.t[:, k, :, 1, 64:96:2], wq2.t[:, k, :, 0, 65:96:2], -1.0), [wq2], [wq2])
                kb.op("dve", lambda e: e.tensor_copy(wq2.t[:, k, :, 1, 65:96:2], wq2.t[:, k, :, 0, 64:96:2]), [wq2], [wq2])
            for k in range(2):
                src = wkv_st.t[:, k, :].rearrange("p (h e) -> p h e", e=128)
                kb.op("dve", lambda e: e.tensor_scalar_mul(wk.t[:, k, :].rearrange("p (h d) -> p h d", d=64), src[:, :, 0:64], kg.t[:, k:k + 1]),
                      [wkv_st, kg], [wk])
                kb.op("dve", lambda e: e.tensor_scalar_mul(wv.t[:, k, :].rearrange("p (h d) -> p h d", d=64), src[:, :, 64:128], kg.t[:, k:k + 1]),
                      [wkv_st, kg], [wv])
            cs = [kb.tile(es2, [128, 5, 512], F32, "cs%d" % i) for i in range(2)]
            sq = [kb.tile(es2, [128, 5, 512], F32, "sq%d" % i) for i in range(2)]
            rsd = [kb.tile(es2, [128, 2, 512], F32, "rsd%d" % i) for i in range(2)]
            krt = [kb.tile(es2, [128, 4, 512], F32, "krt%d" % i) for i in range(2)]
            krm = [kb.tile(es2, [128, 2, 512], F32, "krm%d" % i) for i in range(2)]
            for bi, (g0, n) in enumerate(BLOCKS):
                c0 = gcol(g0)
                c_, s_, r_, kr_, km_ = cs[bi % 2], sq[bi % 2], rsd[bi % 2], krt[bi % 2], krm[bi % 2]
                kb.dma("sp", c_.t[:, :, 0:n], env.PM[0:640, c0:c0 + n].rearrange("(k p) c -> p k c", p=128), reads=[env.dd["PM"]], writes=[c_])
                kb.op("act", lambda e: e.activation(s_.t[:, :, 0:n], c_.t[:, :, 0:n], AF.Square), [c_], [s_])
                for wi, (k0, k1, dim) in enumerate(((0, 3, 384.0), (3, 5, 256.0))):
                    b, _, deps = pp.get(512)
                    ps = kb.banks[b]
                    for k in range(k0, k1):
                        kb.op("pe", lambda e: e.matmul(ps[:, 0:n], env.onesf.t[:], s_.t[:, k, 0:n], start=(k == k0), stop=(k == k1 - 1)),
                              [env.onesf, s_], deps, inc=(k == k1 - 1))
                    kb.op("act", lambda e: e.activation(r_.t[:, wi, 0:n], ps[:, 0:n], AF.Sqrt, bias=env.epsr.t[:, 0:1], scale=1.0 / dim), deps + [env.epsr], [r_])
                    kb.op("dve", lambda e: e.reciprocal(r_.t[:, wi, 0:n], r_.t[:, wi, 0:n]), [r_], [r_])
                    dst = cqn if wi == 0 else ckvn
                    for k in range(k0, k1):
                        kb.op(kb.sb_eng(), lambda e: e.tensor_tensor(dst.t[:, k - k0, g0:g0 + n], c_.t[:, k, 0:n], r_.t[:, wi, 0:n], ALU.mult),
                              [c_, r_], [dst])
                kb.dma("sp", kr_.t[64:96, 0, 0:n], env.PM[640:672, c0:c0 + n], reads=[env.dd["PM"]], writes=[kr_])
                kb.dma("sp", kr_.t[64:96, 1, 0:n], env.PM[672:704, c0:c0 + n], reads=[env.dd["PM"]], writes=[kr_])
                kb.dma("sp", kr_.t[64:96, 2, 0:n], env.cst["cosk"][64:96, g0:g0 + n], writes=[kr_])
                kb.dma("sp", kr_.t[64:96, 3, 0:n], env.cst["sink"][64:96, g0:g0 + n], writes=[kr_])
                kb.op("pool", lambda e: e.tensor_tensor(km_.t[64:96, 0, 0:n], kr_.t[64:96, 0, 0:n], kr_.t[64:96, 2, 0:n], ALU.mult), [kr_], [km_])
                kb.op("dve", lambda e: e.tensor_tensor(km_.t[64:96, 1, 0:n], kr_.t[64:96, 1, 0:n], kr_.t[64:96, 3, 0:n], ALU.mult), [kr_], [km_])
                kb.op("pool", lambda e: e.tensor_tensor(KT[0].t[64:96, g0:g0 + n], km_.t[64:96, 0, 0:n], km_.t[64:96, 1, 0:n], ALU.add), [km_], [KT[0]])
            kb.op("pool", lambda e: e.tensor_copy(KT[1].t[64:96, :], KT[0].t[64:96, :]), [KT[0]], [KT[1]])
            for i in range(NTILE):
                b, _, deps = pp.get(512)
                ps = kb.banks[b]
                for k in range(2):
                    kb.op("pe", lambda e: e.matmul(ps[:, :], ckvn.t[:, k, i * 128:(i + 1) * 128], wv.t[:, k, :], start=(k == 0), stop=(k == 1)),
                          [ckvn, wv], deps, inc=(k == 1))
                kb.copy(kb.ev_eng(), va.t[:, i, :, 0:64], ps[:, :].rearrange("p (h d) -> p h d", d=64), deps, [va])
            kb.barrier()
        QT = [kb.tile(es, [128, NT], BF16, "qt%d" % i) for i in range(2)]
        NEGM = [kb.tile(es, [128, 1], F32, "negm%d" % i) for i in range(2)]
        tabs = [kb.tile(es, [128, 2, 512], F32, "tab%d" % i) for i in range(2)]
        tmp = [kb.tile(es, [128, 2, 512], F32, "qtmp%d" % i) for i in range(2)]
        sqq = [kb.tile(es, [128, 512], F32, "sqq%d" % i) for i in range(2)]
        ptt = [kb.tile(es, [128, 512], BF16, "ptt%d" % i) for i in range(6)]
        osb = [kb.tile(es, [128, 512], F32, "osb%d" % i) for i in range(2)]
        rl = [kb.tile(es, [64, 512], F32, "rl%d" % i) for i in range(2)]
        ob = [kb.tile(es, [64, 512], BF16, "ob%d" % i) for i in range(2)]
        nb = kb.tile(es, [1, 2, 16], F32, "nb")
        msc = kb.tile(es, [1, 4], F32, "msc")
        qblocks = BLOCKS if need_ctx else BLOCKS[1:]
        LOOK = 3
        cnt = {"ti": 0, "pi": 0, "qi": 0}

        def setup(h):
            kt = KT[h % 2]
            qt = QT[h % 2]
            negm = NEGM[h % 2]
            for bi, (g0, n) in enumerate(BLOCKS):
                b, _, deps = pp.get(512)
                ps = kb.banks[b]
                for k in range(2):
                    kb.op("pe", lambda e: e.matmul(ps[0:64, 0:n], wk.t[:, k, h * 64:(h + 1) * 64], ckvn.t[:, k, g0:g0 + n], start=(k == 0), stop=(k == 1)),
                          [wk, ckvn], deps, inc=(k == 1))
                kb.copy("dve", kt.t[0:64, g0:g0 + n], ps[0:64, 0:n], deps, [kt])
            for bi, (g0, n) in enumerate(qblocks):
                tb = tabs[cnt["ti"] % 2]
                tm = tmp[cnt["ti"] % 2]
                cnt["ti"] += 1
                kb.dma("sp", tb.t[0:96, 0, 0:n], env.cst["cosq"][:, g0:g0 + n], writes=[tb])
                kb.dma("sp", tb.t[0:96, 1, 0:n], env.cst["sinq"][:, g0:g0 + n], writes=[tb])
                for ab in range(2):
                    b, _, deps = pp.get(512)
                    ps = kb.banks[b]
                    for k in range(3):
                        kb.op("pe", lambda e: e.matmul(ps[0:96, 0:n], wq2.t[:, k, h, ab, :], cqn.t[:, k, g0:g0 + n], start=(k == 0), stop=(k == 2)),
                              [wq2, cqn], deps, inc=(k == 2))
                    kb.op("dve", lambda e: e.tensor_tensor(tm.t[0:96, ab, 0:n], ps[0:96, 0:n], tb.t[0:96, ab, 0:n], ALU.mult), deps + [tb], [tm])
                kb.op("pool", lambda e: e.tensor_tensor(qt.t[0:96, g0:g0 + n], tm.t[0:96, 0, 0:n], tm.t[0:96, 1, 0:n], ALU.add), [tm], [qt])
            for wi, (src, blks) in enumerate(((qt, qblocks), (kt, BLOCKS))):
                for bi, (g0, n) in enumerate(blks):
                    s_ = sqq[(wi + bi) % 2]
                    kb.op("pool", lambda e: e.tensor_tensor(s_.t[0:96, 0:n], src.t[0:96, g0:g0 + n], src.t[0:96, g0:g0 + n], ALU.mult), [src], [s_])
                    b, c0, deps = pp.get(512)
                    ps = kb.banks[b]
                    kb.op("pe", lambda e: e.matmul(ps[0:1, 0:n], env.onesf.t[0:96, 0:1], s_.t[0:96, 0:n], start=True, stop=True), [env.onesf, s_], deps)
                    kb.op("dve", lambda e: e.reduce_max(nb.t[0:1, wi, bi:bi + 1], ps[0:1, 0:n], AX.X), deps, [nb])
                kb.op("dve", lambda e: e.reduce_max(msc.t[0:1, wi:wi + 1], nb.t[0:1, wi, 0:len(blks)], AX.X), [nb], [msc])
            kb.op("dve", lambda e: e.tensor_tensor(msc.t[0:1, 2:3], msc.t[0:1, 0:1], msc.t[0:1, 1:2], ALU.mult), [msc], [msc])
            kb.op("act", lambda e: e.activation(msc.t[0:1, 3:4], msc.t[0:1, 2:3], AF.Sqrt), [msc], [msc])
            kb.op("dve", lambda e: e.tensor_scalar_mul(msc.t[0:1, 3:4], msc.t[0:1, 3:4], -1.0), [msc], [msc])
            b, c0, deps = pp.get(16)
            ps = kb.banks[b]
            kb.op("pe", lambda e: e.matmul(ps[:, c0:c0 + 1], env.onesf.t[0:1, :], msc.t[0:1, 3:4], start=True, stop=True), [env.onesf, msc], deps)
            kb.copy("dve", negm.t[:, 0:1], ps[:, c0:c0 + 1], deps, [negm])

        def attn(h):
            kt = KT[h % 2]
            qt = QT[h % 2]
            negm = NEGM[h % 2]
            items = []
            for bi, (g0, n) in enumerate(qblocks):
                kts = list(range(NTILE)) if g0 >= CTXL else [0, 1]
                qi = cnt["qi"]
                cnt["qi"] += 1
                for ii, ki in enumerate(kts):
                    items.append((g0, n, ii, ki, len(kts), qi))
            inflight = []

            def stage_a(it):
                g0, n, ii, ki, nk, qi = it
                b, _, deps = pp.get(512)
                ps = kb.banks[b]
                kb.op("pe", lambda e: e.matmul(ps[:, 0:n], kt.t[0:96, ki * 128:(ki + 1) * 128], qt.t[0:96, g0:g0 + n], start=True, stop=True),
                      [kt, qt], deps)
                p_ = ptt[cnt["pi"] % len(ptt)]
                cnt["pi"] += 1
                kb.op("act", lambda e: e.activation(p_.t[:, 0:n], ps[:, 0:n], AF.Exp, bias=negm.t[:, 0:1], scale=1.0), deps + [negm], [p_])
                inflight.append(p_)

            def stage_b(it):
                g0, n, ii, ki, nk, qi = it
                p_ = inflight.pop(0)
                pob = 5 + (qi % 2)
                po = kb.banks[pob]
                pod = list(kb.bdep[pob])
                kb.op("pe", lambda e: e.matmul(po[0:65, 0:n], va.t[:, ki, h, :], p_.t[:, 0:n], start=(ii == 0), stop=(ii == nk - 1)),
                      [va, p_], pod, inc=(ii == nk - 1))
                if ii != nk - 1:
                    return
                o_ = osb[qi % 2]
                r_ = rl[qi % 2]
                b_ = ob[qi % 2]
                kb.copy("dve", o_.t[0:65, 0:n], po[0:65, 0:n], pod, [o_])
                b, _, deps = pp.get(512)
                ps = kb.banks[b]
                kb.op("pe", lambda e: e.matmul(ps[0:64, 0:n], esel.t[:, :], o_.t[0:65, 0:n], start=True, stop=True), [esel, o_], deps)
                kb.op("dve", lambda e: e.reciprocal(r_.t[:, 0:n], ps[0:64, 0:n]), deps, [r_])
                kb.op("pool", lambda e: e.tensor_tensor(b_.t[:, 0:n], o_.t[0:64, 0:n], r_.t[:, 0:n], ALU.mult), [o_, r_], [b_])
                c0 = gcol(g0)
                kb.dma("pool", env.MIXT[512 + h * 64:512 + (h + 1) * 64, c0:c0 + n], b_.t[:, 0:n], reads=[b_], writes=[env.dd["MIXT"]])

            for idx in range(len(items) + LOOK):
                if idx < len(items):
                    stage_a(items[idx])
                if idx >= LOOK:
                    stage_b(items[idx - LOOK])

        setup(0)
        for h in range(8):
            if h + 1 < 8:
                setup(h + 1)
            attn(h)
        kb.barrier()
    env.pp = PsPool(kb, list(range(8)))


PHASES["mla"] = phase_mla


NSTEP = 6


def _tt(kb, e, out, in0, in1, op, reads, writes):
    return kb.op(e, lambda g: g.tensor_tensor(out, in0, in1, op), reads, writes)


def _stt(kb, e, out, in0, scalar, in1, op0, op1, reads, writes):
    return kb.op(e, lambda g: g.scalar_tensor_tensor(out, in0, scalar, in1, op0, op1), reads, writes)


def _act(kb, out, in_, func, reads, writes, **kw):
    return kb.op("act", lambda g: g.activation(out, in_, func, **kw), reads, writes)


def bc8(t):
    return t.unsqueeze(2).to_broadcast([128, 8, 64])


def v3(ap):
    return ap.rearrange("p (h d) -> p h d", d=64)


class StopPhase(Exception):
    pass


def ck(kb, env, n, dumps=()):
    if getattr(env, "stop", None) != n:
        return
    for slot, t in dumps:
        if t.t.dtype == BF16:
            n = t.t.shape[-1]
            kb.dma("sp", env.DBG[slot].bitcast(BF16)[0:t.t.shape[0], 0:n], t.t[:], reads=[t], writes=[env.dd["DBG"]])
        else:
            kb.dma("sp", env.DBG[slot][0:t.t.shape[0], 0:t.t.shape[-1]], t.t[:], reads=[t], writes=[env.dd["DBG"]])
    kb.barrier()
    kb.halt = True


def phase_rwkv(kb, env, l):
    nc = kb.nc
    need_ctx = l < DEPTH - 1
    env.pp = PsPool(kb, [0, 1, 2, 3, 4, 5])
    pp = env.pp
    ppp = PsPool(kb, [6, 7])
    idb = env.identb
    with ExitStack() as es:
        dg = kb.tile(es, [128, 15, 3, 128], BF16, "dg")
        lw = kb.tile(es, [128, 3, 512], BF16, "lw")
        with ExitStack() as es2:
            for a in range(3):
                cw = load_vec_fm(kb, env, es2, env.w["rwkv_conv"][l][a], 15, "cw%d" % a)
                for j in range(15):
                    kb.op(kb.sb_eng(), lambda e: e.tensor_scalar_mul(dg.t[:, j, a, :], env.identf.t[:], cw.t[:, j:j + 1]), [env.identf, cw], [dg])
            lst = kb.tile(es2, [128, 3, 512], F32, "lst")
            kb.dma("sp", lst.t[:, 0, :], env.w["w_b"][l].rearrange("d r c -> (d r) c"), writes=[lst])
            kb.dma("sp", lst.t[:, 1, :], env.w["a_b"][l].rearrange("d r c -> (d r) c"), writes=[lst])
            kb.dma("sp", lst.t[:, 2, :], env.w["g_b"][l], writes=[lst])
            kb.copy("dve", lw.t[:], lst.t[:], [lst], [lw])
            kb.barrier()
        brow = kb.tile(es, [1, 2, 2, 512], F32, "brow")
        kb.dma("sp", brow.t[:, 0, :, :], env.w["w0"][l].rearrange("(o d) c -> o d c", o=1), writes=[brow])
        kb.dma("sp", brow.t[:, 1, :, :], env.w["a0"][l].rearrange("(o d) c -> o d c", o=1), writes=[brow])
        kkb = load_bcast(kb, es, env.w["k_k"][l], 512, "kkb")
        kab = load_bcast(kb, es, env.w["k_a"][l], 512, "kab")
        rkb = load_bcast(kb, es, env.w["r_k"][l].rearrange("h d -> (h d)"), 512, "rkb")
        gng = load_bcast(kb, es, env.w["gn_g"][l], 512, "gng")
        gnb = load_bcast(kb, es, env.w["gn_b"][l], 512, "gnb")
        m4 = kb.tile(es, [128, 512], F32, "m4")
        mt = kb.tile(es, [128, 128], F32, "mt")
        tri = kb.tile(es, [128, 3, 128], F32, "tri")
        cvec = kb.tile(es, [128, 1], F32, "cvec")
        kb.dma("sp", cvec.t[:], env.cst["cvec"], writes=[cvec])
        hb = kb.tile(es, [64, 512], BF16, "hb")
        pw = [kb.tile(es, [128, 15, 3, 128], BF16, "pw%d" % i) for i in range(2)]
        f32names = ["r", "k", "v", "sw", "a0", "a1", "g", "t1", "sqk", "kk", "am1", "kd0", "kd1", "b", "rk",
                    "eG", "enG", "eGx", "eD", "y", "yf", "u1", "u2"]
        F = {n: kb.tile(es, [128, 512], F32, "f_" + n) for n in f32names}
        FD = [{n: (F[n] if bi_ == 0 else kb.tile(es, [128, 512], F32, "f%d_%s" % (bi_, n))) for n in ("r", "v", "g", "kd0", "kd1", "y")} for bi_ in range(3)]
        bfnames = ["KKt", "Bt", "Kt", "Rt", "Bh", "Kh", "Vb", "ob"]
        BfD = [{n: kb.tile(es, [128, 512], BF16, "b%d_%s" % (bi_, n)) for n in bfnames} for bi_ in range(2)]
        th = kb.tile(es, [128, 128], BF16, "th")
        alo = kb.tile(es, [128, 128], BF16, "alo")
        sg = kb.tile(es, [128, 128], BF16, "sg")
        s8 = {n: kb.tile(es, [128, 8], F32, "s8_" + n) for n in ("ss", "nrm", "m", "var", "bs")}
        fmaD = [kb.tile(es, [128, 4, 2, 128], BF16, "fma%d" % i) for i in range(2)]
        fmbD = [kb.tile(es, [128, 4, 2, 128], BF16, "fmb%d" % i) for i in range(2)]
        gcfD = [kb.tile(es, [64, 8], F32, "gcf%d" % i) for i in range(2)]
        mst = kb.tile(es, [128, 4, 128], BF16, "mst")
        SB1 = [kb.tile(es, [128, 512], BF16, "sb1_%d" % h) for h in range(8)]
        X0 = [kb.tile(es, [128, 128], BF16, "x0_%d" % h) for h in range(8)]
        XX = [[kb.tile(es, [128, 256], BF16, "xx%d_%d" % (i, h)) for i in range(3)] for h in range(8)]
        ET = [[kb.tile(es, [128, 128], BF16, "et%d_%d" % (i, h)) for i in range(2)] for h in range(8)]
        ZC = [kb.tile(es, [128, 128], BF16, "zc_%d" % h) for h in range(8)]
        WU = [kb.tile(es, [128, 128], BF16, "wu_%d" % h) for h in range(8)]
        QTb = [kb.tile(es, [64, 128], BF16, "qtb_%d" % h) for h in range(8)]
        MC = [kb.tile(es, [64, 64], BF16, "mc_%d" % h) for h in range(8)]

        def hs(h):
            return slice(h * 64, (h + 1) * 64)

        ck(kb, env, -1)

        for d in range(2):
            kb.dma("sp", m4.t[:], env.cst["mask4"][d], writes=[m4])
            kb.dma("sp", mt.t[:], env.cst["maskt"][d], writes=[mt])
            kb.dma("sp", tri.t[:], env.cst["tri"][d].rearrange("a s t -> s a t"), writes=[tri])
            kb.op("pool", lambda e: e.memset(hb.t[:], 0.0), [], [hb])
            order = list(range(NTILE)) if d == 0 else [1, 0] + list(range(NTILE - 1, 1, -1))
            def prep(ci, i, bi_):
                Fb, Bf, fma, fmb, gcf = FD[ci % 3], BfD[bi_], fmaD[bi_], fmbD[bi_], gcfD[bi_]
                FF = dict(F)
                FF.update(Fb)
                c0 = tcol(i)
                p_ = pw[ci % 2]
                for a in range(3):
                    kb.dma("sp", p_.t[:, :, a, :], env.PT[:, c0 - 1 + a:c0 + 127 + a].rearrange("(j p) c -> p j c", p=128), reads=[env.dd["PT"]], writes=[p_])
                ck(kb, env, 0)
                yield
                for gi, nm in enumerate(("r", "k", "v")):
                    b, _, deps = ppp.get(512)
                    ps = kb.banks[b]
                    for jj in range(4):
                        j = gi * 4 + jj
                        for a in range(3):
                            kb.op("pe", lambda e: e.matmul(ps[:, jj * 128:(jj + 1) * 128], p_.t[:, j, a, :], dg.t[:, j, a, :], start=(a == 0), stop=(a == 2)),
                                  [p_, dg], deps, inc=(jj == 3 and a == 2))
                    kb.copy("act" if gi != 1 else "dve", FF[nm].t[:], ps[:, :], deps, [FF[nm]])
                b, _, deps = ppp.get(512)
                ps = kb.banks[b]
                for jj in range(3):
                    j = 12 + jj
                    for a in range(3):
                        kb.op("pe", lambda e: e.matmul(ps[:, jj * 128:(jj + 1) * 128], dg.t[:, j, a, :], p_.t[:, j, a, :], start=(a == 0), stop=(a == 2)),
                              [p_, dg], deps, inc=(jj == 2 and a == 2))
                _act(kb, FF["t1"].t[:, 0:128], ps[:, 0:128], AF.Sigmoid, deps, [FF["t1"]], scale=2.0)
                kb.op("dve", lambda e: e.tensor_scalar(th.t[:], FF["t1"].t[:, 0:128], 2.0, -1.0, ALU.mult, ALU.add), [FF["t1"]], [th])
                kb.copy("dve", alo.t[:], ps[:, 128:256], deps, [alo])
                _act(kb, sg.t[:], ps[:, 256:384], AF.Sigmoid, deps, [sg])
                ck(kb, env, 1, [(0, FF["r"]), (1, FF["k"]), (2, FF["v"])])
                yield
                P0 = 64 * d
                b, _, deps = ppp.get(512)
                ps = kb.banks[b]
                kb.op("pe", lambda e: e.matmul(ps[:, :], th.t[P0:P0 + 64, :], lw.t[P0:P0 + 64, 0, :], start=True, stop=False), [th, lw], deps, inc=False)
                kb.op("pe", lambda e: e.matmul(ps[:, :], env.onesf.t[0:1, :], brow.t[0:1, 0, d, :], start=False, stop=True), [env.onesf, brow], deps)
                _act(kb, FF["sw"].t[:], ps[:, :], AF.Sigmoid, deps, [FF["sw"]])
                for dd in range(2):
                    b, _, deps = ppp.get(512)
                    ps = kb.banks[b]
                    kb.op("pe", lambda e: e.matmul(ps[:, :], alo.t[64 * dd:64 * dd + 64, :], lw.t[64 * dd:64 * dd + 64, 1, :], start=True, stop=False), [alo, lw], deps, inc=False)
                    kb.op("pe", lambda e: e.matmul(ps[:, :], env.onesf.t[0:1, :], brow.t[0:1, 1, dd, :], start=False, stop=True), [env.onesf, brow], deps)
                    _act(kb, FF["a%d" % dd].t[:], ps[:, :], AF.Sigmoid, deps, [FF["a%d" % dd]])
                b, _, deps = ppp.get(512)
                ps = kb.banks[b]
                kb.op("pe", lambda e: e.matmul(ps[:, :], sg.t[:], lw.t[:, 2, :], start=True, stop=True), [sg, lw], deps)
                kb.copy("dve", FF["g"].t[:], ps[:, :], deps, [FF["g"]])
                ck(kb, env, 2, [(0, FF["sw"]), (1, FF["a0"]), (2, FF["a1"]), (3, FF["g"])])
                yield
                _tt(kb, "pool", FF["t1"].t[:], FF["k"].t[:], kkb.t[:], ALU.mult, [FF["k"], kkb], [FF["t1"]])
                _tt(kb, "pool", FF["sqk"].t[:], FF["t1"].t[:], FF["t1"].t[:], ALU.mult, [FF["t1"]], [FF["sqk"]])
                kb.op("dve", lambda e: e.reduce_sum(s8["ss"].t[:], v3(FF["sqk"].t[:]), AX.X), [FF["sqk"]], [s8["ss"]])
                kb.op("dve", lambda e: e.tensor_scalar_max(s8["ss"].t[:], s8["ss"].t[:], 1e-24), [s8["ss"]], [s8["ss"]])
                _act(kb, s8["nrm"].t[:], s8["ss"].t[:], AF.Ln, [s8["ss"]], [s8["nrm"]])
                _act(kb, s8["nrm"].t[:], s8["nrm"].t[:], AF.Exp, [s8["nrm"]], [s8["nrm"]], scale=-0.5)
                _tt(kb, "dve", v3(FF["kk"].t[:]), v3(FF["t1"].t[:]), bc8(s8["nrm"].t[:]), ALU.mult, [FF["t1"], s8["nrm"]], [FF["kk"]])
                for dd in range(2):
                    a_ = FF["a%d" % dd]
                    kd = FF["kd%d" % dd]
                    _stt(kb, "dve", FF["am1"].t[:], a_.t[:], -1.0, kab.t[:], ALU.add, ALU.mult, [a_, kab], [FF["am1"]])
                    _stt(kb, "dve", kd.t[:], FF["am1"].t[:], 1.0, FF["k"].t[:], ALU.add, ALU.mult, [FF["am1"], FF["k"]], [kd])
                a_d = FF["a%d" % d]
                kd_d = FF["kd%d" % d]
                _tt(kb, "dve", FF["b"].t[:], FF["kk"].t[:], a_d.t[:], ALU.mult, [FF["kk"], a_d], [FF["b"]])
                ck(kb, env, 3, [(0, FF["kk"]), (1, FF["kd0"]), (2, FF["kd1"]), (3, FF["b"])])
                yield
                exps = []
                for ti_, (nm, sc) in enumerate((("eG", 1.0), ("eGx", 1.0), ("eD", 1.0))):
                    b, _, deps = ppp.get(512)
                    ps = kb.banks[b]
                    kb.op("pe", lambda e: e.matmul(ps[:, :], tri.t[:, ti_, :], FF["sw"].t[:], start=True, stop=True), [tri, FF["sw"]], deps)
                    _act(kb, FF[nm].t[:], ps[:, :], AF.Exp, deps, [FF[nm]])
                    if ti_ == 0:
                        _act(kb, FF["enG"].t[:], ps[:, :], AF.Exp, deps, [FF["enG"]], scale=-1.0)
                b, c8, deps = ppp.get(8)
                ps = kb.banks[b]
                for h in range(8):
                    kb.op("pe", lambda e: e.matmul(ps[0:64, c8 + h:c8 + h + 1], FF["sw"].t[:, hs(h)], cvec.t[:, 0:1], start=True, stop=True),
                          [FF["sw"], cvec], deps, inc=(h == 7))
                _act(kb, gcf.t[:], ps[0:64, c8:c8 + 8], AF.Exp, deps, [gcf])
                ck(kb, env, 4, [(0, FF["eG"]), (1, FF["enG"]), (2, FF["eGx"]), (3, FF["eD"])])
                yield
                for nm, x_, e_ in (("KKt", "kk", "eGx"), ("Bt", "b", "enG"), ("Kt", "kd%d" % d, "enG"), ("Rt", "r", "eG"),
                                   ("Bh", "b", "eD"), ("Kh", "kd%d" % d, "eD")):
                    _tt(kb, kb.sb_eng(), Bf[nm].t[:], FF[x_].t[:], FF[e_].t[:], ALU.mult, [FF[x_], FF[e_]], [Bf[nm]])
                kb.copy("pool", Bf["Vb"].t[:], FF["v"].t[:], [FF["v"]], [Bf["Vb"]])
                for nm, dst, slot in (("KKt", fma, 0), ("Rt", fma, 1), ("Bt", fmb, 0), ("Kt", fmb, 1)):
                    b, cc, deps = ppp.get(256)
                    psb = kb.banks_bf[b]
                    for jp in range(4):
                        kb.op("pe", lambda e: e.transpose(psb[:, 2 * cc + jp * 128:2 * cc + (jp + 1) * 128], Bf[nm].t[:, jp * 128:(jp + 1) * 128], idb.t[:]),
                              [Bf[nm], idb], deps, inc=(jp == 3))
                    kb.copy(kb.ev_eng(), dst.t[:, :, slot, :], psb[:, 2 * cc:2 * cc + 512].rearrange("p (j t) -> p j t", t=128), deps, [dst])
                yield

            def heads(ci, i, bi_):
                Fb, Bf, fma, fmb, gcf = FD[ci % 3], BfD[bi_], fmaD[bi_], fmbD[bi_], gcfD[bi_]
                FF = dict(F)
                FF.update(Fb)
                c0 = tcol(i)
                ck(kb, env, 5)
                yield
                R = {}
                for hg in range(2):
                    for h in range(4 * hg, 4 * hg + 4):
                        P = 64 * (h % 2)
                        jp = h // 2
                        b, _, deps = pp.get(512)
                        ps = kb.banks[b]
                        rhs = fma.t[P:P + 64, jp, :, :].rearrange("p a t -> p (a t)")
                        kb.op("pe", lambda e: e.matmul(ps[:, 0:256], fmb.t[P:P + 64, jp, 0, :], rhs, start=True, stop=True), [fma, fmb], deps, inc=False)
                        kb.op("pe", lambda e: e.matmul(ps[:, 256:512], fmb.t[P:P + 64, jp, 1, :], rhs, start=True, stop=True), [fma, fmb], deps)
                        R[h] = (ps, deps)
                    for h in range(4 * hg, 4 * hg + 4):
                        ps, deps = R[h]
                        _tt(kb, "dve", SB1[h].t[:], ps[:, :], m4.t[:], ALU.mult, deps + [m4], [SB1[h]])
                    yield
                for h in range(8):
                    P = 64 * (h % 2)
                    jp = h // 2
                    b2, c2, deps2 = pp.get(256)
                    ps2 = kb.banks[b2]
                    kb.op("pe", lambda e: e.matmul(ps2[:, c2:c2 + 128], fma.t[P:P + 64, jp, 0, :], fmb.t[P:P + 64, jp, 0, :], start=True, stop=True), [fma, fmb], deps2, inc=False)
                    kb.op("pe", lambda e: e.matmul(ps2[:, c2 + 128:c2 + 192], SB1[h].t[:, 256:384], Bf["Vb"].t[:, hs(h)], start=True, stop=True), [SB1[h], Bf["Vb"]], deps2)
                    R[h] = (ps2, c2, deps2)
                for h in range(8):
                    ps2, c2, deps2 = R[h]
                    _tt(kb, "dve", X0[h].t[:], ps2[:, c2:c2 + 128], mt.t[:], ALU.mult, deps2 + [mt], [X0[h]])
                    kb.copy("act", ZC[h].t[:, 64:128], ps2[:, c2 + 128:c2 + 192], deps2, [ZC[h]])
                    kb.copy("pool", ZC[h].t[:, 0:64], Bf["KKt"].t[:, hs(h)], [Bf["KKt"]], [ZC[h]])
                ck(kb, env, 7, [(0, SB1[0]), (1, X0[0]), (2, ZC[0]), (3, SB1[3]), (4, ZC[3])])
                yield
                def sq_stage(st):
                    for h in range(8):
                        if st == 1:
                            xp, xtp, xd = X0[h].t[:, :], SB1[h].t[:, 0:128], [X0[h], SB1[h]]
                        else:
                            xx = XX[h][(st - 1) % 3]
                            xp, xtp, xd = xx.t[:, 0:128], xx.t[:, 128:256], [xx]
                        b, cc, deps = h // 2, (h % 2) * 256, list(kb.bdep[h // 2])
                        ps = kb.banks[b]
                        kb.op("pe", lambda e: e.matmul(ps[:, cc:cc + 128], xtp, xp, start=True, stop=True), xd, deps, inc=False)
                        kb.op("pe", lambda e: e.matmul(ps[:, cc + 128:cc + 256], xp, xtp, start=True, stop=True), xd, deps)
                        R[h] = (ps, cc, deps)
                    for h in range(8):
                        ps, cc, deps = R[h]
                        kb.copy("dve" if h in (2, 5, 7) else "act", XX[h][st % 3].t[:], ps[:, cc:cc + 256], deps, [XX[h][st % 3]])

                def chain_stage(st):
                    R2 = {}
                    for h in range(8):
                        xx = XX[h][st % 3]
                        if st == 1:
                            etp, ed = SB1[h].t[:, 0:128], [SB1[h]]
                        else:
                            etp, ed = ET[h][(st - 1) % 2].t[:, :], [ET[h][(st - 1) % 2]]
                        b, cc, deps = 4 + h // 4, (h % 4) * 128, list(kb.bdep[4 + h // 4])
                        ps = kb.banks[b]
                        kb.op("pe", lambda e: e.matmul(ps[:, cc:cc + 128], idb.t[:], xx.t[:, 128:256], start=True, stop=False), [xx, idb], deps, inc=False)
                        kb.op("pe", lambda e: e.matmul(ps[:, cc:cc + 128], xx.t[:, 0:128], etp, start=False, stop=True), ed + [xx], deps)
                        R2[h] = (ps, cc, deps, etp, ed)
                    for h in range(8):
                        ps, cc, deps, etp, ed = R2[h]
                        _tt(kb, "dve", ET[h][st % 2].t[:], ps[:, cc:cc + 128], etp, ALU.add, deps + ed, [ET[h][st % 2]])

                sq_stage(1)
                yield
                for st in range(1, NSTEP + 1):
                    if st + 1 <= NSTEP:
                        sq_stage(st + 1)
                        yield
                    chain_stage(st)
                    yield
                ck(kb, env, 8, [(0, ET[0][NSTEP % 2]), (1, XX[0][NSTEP % 3]), (2, ET[3][NSTEP % 2])])
                yield
                for h in range(8):
                    et = ET[h][NSTEP % 2]
                    b, cc, deps = pp.get(128)
                    ps = kb.banks[b]
                    kb.op("pe", lambda e: e.matmul(ps[:, cc:cc + 128], et.t[:, :], ZC[h].t[:, :], start=True, stop=True), [et, ZC[h]], deps)
                    R[h] = (ps, cc, deps)
                for h in range(8):
                    ps, cc, deps = R[h]
                    _stt(kb, "dve", WU[h].t[:], ps[:, cc:cc + 128], -1.0, ZC[h].t[:], ALU.mult, ALU.subtract, deps + [ZC[h]], [WU[h]])
                ck(kb, env, 9, [(0, WU[0]), (1, WU[3])])
                yield
                for h in range(8):
                    b, cc, deps = pp.get(256)
                    ps = kb.banks[b]
                    kb.op("pe", lambda e: e.matmul(ps[0:64, cc:cc + 128], Bf["Rt"].t[:, hs(h)], idb.t[:], start=True, stop=False), [Bf["Rt"], idb], deps, inc=False)
                    kb.op("pe", lambda e: e.matmul(ps[0:64, cc:cc + 128], WU[h].t[:, 0:64], SB1[h].t[:, 128:256], start=False, stop=True), [WU[h], SB1[h]], deps, inc=False)
                    kb.op("pe", lambda e: e.matmul(ps[0:64, cc + 128:cc + 192], WU[h].t[:, 0:64], Bf["Bh"].t[:, hs(h)], start=True, stop=True), [WU[h], Bf["Bh"]], deps)
                    R[h] = (ps, cc, deps)
                for h in range(8):
                    ps, cc, deps = R[h]
                    kb.copy("act", QTb[h].t[:], ps[0:64, cc:cc + 128], deps, [QTb[h]])
                    _stt(kb, "dve", MC[h].t[:], env.identf.t[0:64, 0:64], gcf.t[0:64, h:h + 1], ps[0:64, cc + 128:cc + 192], ALU.mult, ALU.add,
                         deps + [env.identf, gcf], [MC[h]])
                ck(kb, env, 10, [(0, QTb[0]), (1, MC[0]), (2, QTb[3]), (3, MC[3])])
                yield
                yb_, _, ydep = pp.get(512)
                hb_, _, hdep = pp.get(512)
                ybank, hbank = kb.banks[yb_], kb.banks[hb_]
                for h in range(8):
                    yo = ybank[:, hs(h)]
                    kb.op("pe", lambda e: e.matmul(yo, SB1[h].t[:, 128:256], WU[h].t[:, 64:128], start=True, stop=False), [SB1[h], WU[h]], ydep, inc=False)
                    kb.op("pe", lambda e: e.matmul(yo, SB1[h].t[:, 384:512], Bf["Vb"].t[:, hs(h)], start=False, stop=False), [SB1[h], Bf["Vb"]], ydep, inc=False)
                    kb.op("pe", lambda e: e.matmul(yo, QTb[h].t[:, :], hb.t[0:64, hs(h)], start=False, stop=True), [QTb[h], hb], ydep, inc=False)
                    ho = hbank[0:64, hs(h)]
                    kb.op("pe", lambda e: e.matmul(ho, Bf["Bh"].t[:, hs(h)], WU[h].t[:, 64:128], start=True, stop=False), [Bf["Bh"], WU[h]], hdep, inc=False)
                    kb.op("pe", lambda e: e.matmul(ho, Bf["Kh"].t[:, hs(h)], Bf["Vb"].t[:, hs(h)], start=False, stop=False), [Bf["Kh"], Bf["Vb"]], hdep, inc=False)
                    kb.op("pe", lambda e: e.matmul(ho, MC[h].t[:, :], hb.t[0:64, hs(h)], start=False, stop=True), [MC[h], hb], hdep, inc=(h == 7))
                kb.copy("act", hb.t[:, :], hbank[0:64, :], hdep, [hb])
                kb.copy("dve", FF["y"].t[:], ybank[:, :], ydep, [FF["y"]])
                ck(kb, env, 11, [(0, FF["y"])])
                yield
                if getattr(env, "stop", None) == 100 + ci:
                    ck(kb, env, 100 + ci, [(0, FF["y"])])
                    yield
                if d == 0:
                    kb.dma("pool", env.YF[i * 128:(i + 1) * 128, :], FF["y"].t[:], reads=[FF["y"]], writes=[env.dd["YF"]])
                return

            def outp(ci, i):
                if d == 0 or (i < 2 and not need_ctx):
                    return
                FF = dict(F)
                FF.update(FD[ci % 3])
                Bf = BfD[0]
                c0 = tcol(i)
                yield
                kb.dma("sp", FF["yf"].t[:], env.YF[i * 128:(i + 1) * 128, :], reads=[env.dd["YF"]], writes=[FF["yf"]])
                y = FF["y"]
                _tt(kb, "pool", y.t[:], y.t[:], FF["yf"].t[:], ALU.add, [y, FF["yf"]], [y])
                yield
                kb.op("dve", lambda e: e.reduce_sum(s8["m"].t[:], v3(y.t[:]), AX.X), [y], [s8["m"]])
                kb.op("dve", lambda e: e.tensor_scalar_mul(s8["m"].t[:], s8["m"].t[:], -1.0 / 64), [s8["m"]], [s8["m"]])
                _tt(kb, "dve", v3(y.t[:]), v3(y.t[:]), bc8(s8["m"].t[:]), ALU.add, [y, s8["m"]], [y])
                yield
                _tt(kb, "pool", FF["u1"].t[:], y.t[:], y.t[:], ALU.mult, [y], [FF["u1"]])
                yield
                kb.op("dve", lambda e: e.reduce_sum(s8["var"].t[:], v3(FF["u1"].t[:]), AX.X), [FF["u1"]], [s8["var"]])
                _act(kb, s8["var"].t[:], s8["var"].t[:], AF.Ln, [s8["var"], env.epsg], [s8["var"]], bias=env.epsg.t[:, 0:1], scale=1.0 / 64)
                _act(kb, s8["var"].t[:], s8["var"].t[:], AF.Exp, [s8["var"]], [s8["var"]], scale=-0.5)
                _tt(kb, "dve", v3(y.t[:]), v3(y.t[:]), bc8(s8["var"].t[:]), ALU.mult, [y, s8["var"]], [y])
                yield
                _tt(kb, "pool", y.t[:], y.t[:], gng.t[:], ALU.mult, [y, gng], [y])
                yield
                _tt(kb, "pool", y.t[:], y.t[:], gnb.t[:], ALU.add, [y, gnb], [y])
                yield
                _tt(kb, "pool", FF["rk"].t[:], FF["r"].t[:], rkb.t[:], ALU.mult, [FF["r"], rkb], [FF["rk"]])
                yield
                _tt(kb, "pool", FF["u1"].t[:], FF["kd0"].t[:], FF["kd1"].t[:], ALU.add, [FF["kd0"], FF["kd1"]], [FF["u1"]])
                yield
                _tt(kb, "pool", FF["u1"].t[:], FF["u1"].t[:], FF["rk"].t[:], ALU.mult, [FF["u1"], FF["rk"]], [FF["u1"]])
                yield
                kb.op("dve", lambda e: e.reduce_sum(s8["bs"].t[:], v3(FF["u1"].t[:]), AX.X), [FF["u1"]], [s8["bs"]])
                _tt(kb, "dve", v3(FF["u2"].t[:]), v3(FF["v"].t[:]), bc8(s8["bs"].t[:]), ALU.mult, [FF["v"], s8["bs"]], [FF["u2"]])
                yield
                _tt(kb, "pool", y.t[:], y.t[:], FF["u2"].t[:], ALU.add, [y, FF["u2"]], [y])
                yield
                _tt(kb, "pool", Bf["ob"].t[:], y.t[:], FF["g"].t[:], ALU.mult, [y, FF["g"]], [Bf["ob"]])
                yield
                b, cc, deps = ppp.get(256)
                psb = kb.banks_bf[b]
                for jp in range(4):
                    kb.op("pe", lambda e: e.transpose(psb[:, 2 * cc + jp * 128:2 * cc + (jp + 1) * 128], Bf["ob"].t[:, jp * 128:(jp + 1) * 128], idb.t[:]),
                          [Bf["ob"], idb], deps, inc=(jp == 3))
                kb.copy("act", mst.t[:], psb[:, 2 * cc:2 * cc + 512].rearrange("p (j t) -> p j t", t=128), deps, [mst])
                kb.dma("pool", env.MIXT[0:512, c0:c0 + 128].rearrange("(k p) c -> p k c", p=128), mst.t[:], reads=[mst], writes=[env.dd["MIXT"]])
                yield

            def drain(g):
                for _ in g:
                    pass

            drain(prep(0, order[0], 0))
            for ci in range(len(order) + 1):
                alive = []
                if ci < len(order):
                    alive.append(heads(ci, order[ci], ci % 2))
                if ci + 1 < len(order):
                    alive.append(prep(ci + 1, order[ci + 1], (ci + 1) % 2))
                if ci >= 1:
                    alive.append(outp(ci - 1, order[ci - 1]))
                while alive:
                    for g in list(alive):
                        try:
                            next(g)
                        except StopIteration:
                            alive.remove(g)
            kb.barrier()
    env.pp = PsPool(kb, list(range(8)))


PHASES["rwkv"] = phase_rwkv


def residual_ln(kb, env, es_tiles, xt, ps_list, gate, lng, lnb, out):
    st = es_tiles
    for half, (ps, deps) in enumerate(ps_list):
        sl = slice(half * 512, (half + 1) * 512)
        _tt(kb, "dve", out.t[:, sl], ps, gate.t[:, sl], ALU.mult, deps + [gate], [out])
    _stt(kb, "dve", out.t[:], xt.t[:], ALPHA, out.t[:], ALU.mult, ALU.add, [xt, out], [out])
    _act(kb, st["junk"].t[:], out.t[:], AF.Identity, [out], [st["junk"], st["s1"]], accum_out=st["s1"].t[:, 0:1])
    kb.op("dve", lambda e: e.tensor_scalar_mul(st["s1"].t[:, 0:1], st["s1"].t[:, 0:1], -1.0 / D), [st["s1"]], [st["s1"]])
    kb.op("dve", lambda e: e.tensor_scalar_add(out.t[:], out.t[:], st["s1"].t[:, 0:1]), [out, st["s1"]], [out])
    _act(kb, st["junk"].t[:], out.t[:], AF.Square, [out], [st["junk"], st["s2"]], accum_out=st["s2"].t[:, 0:1])
    _act(kb, st["s2"].t[:, 0:1], st["s2"].t[:, 0:1], AF.Sqrt, [st["s2"], env.epsl], [st["s2"]], bias=env.epsl.t[:, 0:1], scale=1.0 / D)
    kb.op("dve", lambda e: e.reciprocal(st["s2"].t[:, 0:1], st["s2"].t[:, 0:1]), [st["s2"]], [st["s2"]])
    _stt(kb, "dve", out.t[:], out.t[:], st["s2"].t[:, 0:1], lng.t[:], ALU.mult, ALU.mult, [out, st["s2"], lng], [out])
    _tt(kb, "pool", out.t[:], out.t[:], lnb.t[:], ALU.add, [out, lnb], [out])


def load_gate(kb, es, env, l, s, gi, name):
    t = kb.tile(es, [128, 1024], F32, name)
    kb.dma("sp", t.t[:], env.MODROW[l, s, gi].partition_broadcast(128), reads=[env.dd["MODROW"]], writes=[t])
    return t


def phase_wo(kb, env, l):
    need_ctx = l < DEPTH - 1
    env.pp = PsPool(kb, list(range(8)))
    pp = env.pp
    with ExitStack() as es:
        wo = kb.tile(es, [128, 8, 1024], BF16, "wo")
        with ExitStack() as es2:
            stage = [kb.tile(es2, [128, 1024], F32, "wost%d" % i) for i in range(2)]
            load_w_bf16(kb, es2, wo, env.w["w_o"][l], 8, 1024, stage)
            kb.barrier()
        gates = [load_gate(kb, es, env, l, s, 0, "g1_%d" % s) for s in range(2)]
        lng = load_bcast(kb, es, env.w["ln1_g"][l], 1024, "ln1g")
        lnb = load_bcast(kb, es, env.w["ln1_b"][l], 1024, "ln1b")
        st = {"junk": kb.tile(es, [128, 1024], F32, "junk"), "s1": kb.tile(es, [128, 1], F32, "s1"), "s2": kb.tile(es, [128, 1], F32, "s2")}
        env.ut_st = [kb.tile(es, [128, 8, 130], BF16, "utst%d" % i) for i in range(2)]
        env.ut_i = 0
        for t in env.ut_st:
            kb.op("pool", lambda e: e.memset(t.t[:], 0.0), [], [t])
        xts = [kb.tile(es, [128, 1024], F32, "xt%d" % i) for i in range(3)]
        outs = [kb.tile(es, [128, 1024], F32, "xo%d" % i) for i in range(2)]
        mts = [kb.tile(es, [128, 8, 128], BF16, "mt%d" % i) for i in range(3)]
        tiles = list(range(NTILE) if need_ctx else range(2, NTILE))

        def mm_stage(ii, i):
            c0 = tcol(i)
            xt, mt = xts[ii % 3], mts[ii % 3]
            kb.dma("sp", mt.t[:], env.MIXT[:, c0:c0 + 128].rearrange("(k p) c -> p k c", p=128), reads=[env.dd["MIXT"]], writes=[mt])
            kb.dma("sp", xt.t[:], xin_ap(env, l, i), reads=[env.dd["X2"]], writes=[xt])
            pl = []
            for half in range(2):
                b, _, deps = pp.get(512)
                ps = kb.banks[b]
                for k in range(8):
                    kb.op("pe", lambda e: e.matmul(ps[:, :], mt.t[:, k, :], wo.t[:, k, half * 512:(half + 1) * 512], start=(k == 0), stop=(k == 7)),
                          [mt, wo], deps, inc=(k == 7))
                pl.append((ps[:, :], deps))
            return pl

        def epi_stage(ii, i, pl):
            s = 1 if i < 2 else 0
            xt, out = xts[ii % 3], outs[ii % 2]
            residual_ln(kb, env, st, xt, pl, gates[s], lng, lnb, out)
            kb.dma("pool", env.X1[i * 128:(i + 1) * 128, :], out.t[:], reads=[out], writes=[env.dd["X1"]])
            emit_ut(kb, env, out, s, env.modf[l], 4, 3, env.U2T, i, pad=True)

        pend = mm_stage(0, tiles[0])
        for ii, i in enumerate(tiles):
            nxt = mm_stage(ii + 1, tiles[ii + 1]) if ii + 1 < len(tiles) else None
            epi_stage(ii, i, pend)
            pend = nxt
        kb.barrier()


def phase_ffu(kb, env, l):
    need_ctx = l < DEPTH - 1
    env.pp = PsPool(kb, list(range(8)))
    pp = env.pp
    NF = 2 * DFF // 128
    with ExitStack() as es:
        wup = kb.tile(es, [128, 8, 2 * DFF], BF16, "wup")
        fcw = []
        with ExitStack() as es2:
            stage = [kb.tile(es2, [128, 2816], F32, "wust%d" % i) for i in range(2)]
            for hh in range(2):
                load_w_bf16(kb, es2, wup, env.w["w_up"][l][:, hh * 2816:(hh + 1) * 2816], 8, 2816, stage, col_off=hh * 2816)
            kb.barrier()
        for a in range(3):
            fcw.append(load_vec_fm(kb, env, es, env.w["ffn_conv_w"][l][a], NF, "fcw%d" % a))
        fcb = load_vec_fm(kb, env, es, env.w["ffn_conv_b"][l], NF, "fcb")
        u2b = [kb.tile(es, [128, 8, 514], BF16, "u2b%d" % i) for i in range(2)]
        gst = [kb.tile(es, [128, 22, 512], BF16, "gst%d" % i) for i in range(2)]
        hs_ = [kb.tile(es, [128, 514], F32, "hs%d" % i) for i in range(4)]
        tt_ = [kb.tile(es, [128, 512], F32, "tt%d" % i) for i in range(4)]
        sgt = [kb.tile(es, [128, 512], F32, "sgt%d" % i) for i in range(2)]
        blocks = BLOCKS if need_ctx else BLOCKS[1:]
        hi = 0
        for bi, (g0, n) in enumerate(blocks):
            c0 = gcol(g0)
            ub = u2b[bi % 2]
            gs = gst[bi % 2]
            kb.dma("sp", ub.t[:, :, 0:n + 2], env.U2T[:, c0 - 1:c0 + n + 1].rearrange("(k p) c -> p k c", p=128), reads=[env.dd["U2T"]], writes=[ub])
            for j in range(22):
                tv = []
                for which in range(2):
                    jf = j + 22 * which
                    hs = hs_[hi % 4]
                    tt = tt_[hi % 4]
                    hi += 1
                    b, _, deps = pp.get(512)
                    ps = kb.banks[b]
                    for k in range(8):
                        kb.op("pe", lambda e: e.matmul(ps[:, 0:n], wup.t[:, k, jf * 128:(jf + 1) * 128], ub.t[:, k, 1:n + 1], start=(k == 0), stop=(k == 7)),
                              [wup, ub], deps, inc=(k == 7))
                    b2, c2, deps2 = pp.get(2)
                    ps2 = kb.banks[b2]
                    for k in range(8):
                        kb.op("pe", lambda e: e.matmul(ps2[:, c2:c2 + 2], wup.t[:, k, jf * 128:(jf + 1) * 128], ub.t[:, k, 0:n + 2:n + 1], start=(k == 0), stop=(k == 7)),
                              [wup, ub], deps2, inc=(k == 7))
                    kb.copy("act", hs.t[:, 1:n + 1], ps[:, 0:n], deps, [hs])
                    _act(kb, tt.t[:, 0:n], ps[:, 0:n], AF.Identity, deps + [fcw[1], fcb], [tt], bias=fcb.t[:, jf:jf + 1], scale=fcw[1].t[:, jf:jf + 1])
                    kb.copy("dve", hs.t[:, 0:n + 2:n + 1], ps2[:, c2:c2 + 2], deps2, [hs])
                    _stt(kb, "dve", tt.t[:, 0:n], hs.t[:, 0:n], fcw[0].t[:, jf:jf + 1], tt.t[:, 0:n], ALU.mult, ALU.add, [hs, tt, fcw[0]], [tt])
                    _stt(kb, "dve", tt.t[:, 0:n], hs.t[:, 2:n + 2], fcw[2].t[:, jf:jf + 1], tt.t[:, 0:n], ALU.mult, ALU.add, [hs, tt, fcw[2]], [tt])
                    tv.append(tt)
                sg = sgt[j % 2]
                _act(kb, sg.t[:, 0:n], tv[0].t[:, 0:n], AF.Silu, [tv[0]], [sg])
                _tt(kb, "pool", gs.t[:, j, 0:n], sg.t[:, 0:n], tv[1].t[:, 0:n], ALU.mult, [sg, tv[1]], [gs])
            kb.dma("pool", env.GT[:, c0:c0 + n].rearrange("(k p) c -> p k c", p=128), gs.t[:, :, 0:n], reads=[gs], writes=[env.dd["GT"]])
        kb.barrier()


def phase_ffd(kb, env, l):
    need_ctx = l < DEPTH - 1
    last = l == DEPTH - 1
    env.pp = PsPool(kb, list(range(8)))
    pp = env.pp
    with ExitStack() as es:
        wdn = kb.tile(es, [128, 22, 1024], BF16, "wdn")
        with ExitStack() as es2:
            stage = [kb.tile(es2, [128, 1024], F32, "wdst%d" % i) for i in range(3)]
            load_w_bf16(kb, es2, wdn, env.w["w_down"][l], 22, 1024, stage)
            kb.barrier()
        gates = [load_gate(kb, es, env, l, s, 1, "g2_%d" % s) for s in range(2)]
        lng = load_bcast(kb, es, env.w["ln2_g"][l], 1024, "ln2g")
        lnb = load_bcast(kb, es, env.w["ln2_b"][l], 1024, "ln2b")
        st = {"junk": kb.tile(es, [128, 1024], F32, "junk"), "s1": kb.tile(es, [128, 1], F32, "s1"), "s2": kb.tile(es, [128, 1], F32, "s2")}
        env.ut_st = [kb.tile(es, [128, 8, 130], BF16, "utst%d" % i) for i in range(2)]
        env.ut_i = 0
        xts = [kb.tile(es, [128, 1024], F32, "xt%d" % i) for i in range(3)]
        outs = [kb.tile(es, [128, 1024], F32, "xo%d" % i) for i in range(2)]
        gts = [kb.tile(es, [128, 22, 128], BF16, "gt%d" % i) for i in range(3)]
        tiles = list(range(NTILE) if need_ctx else range(2, NTILE))

        def mm_stage(ii, i):
            c0 = tcol(i)
            xt, gt = xts[ii % 3], gts[ii % 3]
            kb.dma("sp", gt.t[:], env.GT[:, c0:c0 + 128].rearrange("(k p) c -> p k c", p=128), reads=[env.dd["GT"]], writes=[gt])
            kb.dma("sp", xt.t[:], env.X1[i * 128:(i + 1) * 128, :], reads=[env.dd["X1"]], writes=[xt])
            pl = []
            for half in range(2):
                b, _, deps = pp.get(512)
                ps = kb.banks[b]
                for k in range(22):
                    kb.op("pe", lambda e: e.matmul(ps[:, :], gt.t[:, k, :], wdn.t[:, k, half * 512:(half + 1) * 512], start=(k == 0), stop=(k == 21)),
                          [gt, wdn], deps, inc=(k == 21))
                pl.append((ps[:, :], deps))
            return pl

        def epi_stage(ii, i, pl):
            s = 1 if i < 2 else 0
            xt, out = xts[ii % 3], outs[ii % 2]
            residual_ln(kb, env, st, xt, pl, gates[s], lng, lnb, out)
            if last:
                kb.dma("pool", env.y[(i - 2) * 128:(i - 1) * 128, :], out.t[:], reads=[out], writes=[env.dd["y"]])
            else:
                kb.dma("pool", env.X2[i * 128:(i + 1) * 128, :], out.t[:], reads=[out], writes=[env.dd["X2"]])
                emit_ut(kb, env, out, s, env.modf[l + 1], 1, 0, env.UT, i, pad=False)

        pend = mm_stage(0, tiles[0])
        for ii, i in enumerate(tiles):
            nxt = mm_stage(ii + 1, tiles[ii + 1]) if ii + 1 < len(tiles) else None
            epi_stage(ii, i, pend)
            pend = nxt
        kb.barrier()


PHASES["wo"] = phase_wo
PHASES["ffu"] = phase_ffu
PHASES["ffd"] = phase_ffd


_CACHE = {}


def kernel(**inputs):
    if "nc" not in _CACHE:
        _CACHE["nc"] = build()[0]
        _CACHE["consts"] = make_consts()
    nc = _CACHE["nc"]
    consts = _CACHE["consts"]
    B = inputs["x"].shape[0]
    shared = {k: np.ascontiguousarray(np.asarray(inputs[k], dtype=np.float32)) for k in W_SHAPES}
    shared["c_ctx"] = np.ascontiguousarray(np.asarray(inputs["c_ctx"], dtype=np.float32))
    for k, v in consts.items():
        shared["k_" + k] = v
    in_maps = []
    for b in range(B):
        m = dict(shared)
        m["x"] = np.ascontiguousarray(np.asarray(inputs["x"][b], dtype=np.float32))
        m["c"] = np.ascontiguousarray(np.asarray(inputs["c"][b], dtype=np.float32))
        m["ctx"] = np.ascontiguousarray(np.asarray(inputs["ctx"][b], dtype=np.float32))
        in_maps.append(m)
    res = run_bass_kernel_spmd(nc, in_maps, core_ids=list(range(B)))
    return np.stack([np.asarray(r["y"], dtype=np.float32) for r in res.results], axis=0)
```

```python
import numpy as np
import ml_dtypes
from contextlib import ExitStack
import concourse.bass as bass
import concourse.mybir as mybir
from concourse.bass_utils import run_bass_kernel_spmd

F32 = mybir.dt.float32
BF16 = mybir.dt.bfloat16
AF = mybir.ActivationFunctionType
ALU = mybir.AluOpType
AX = mybir.AxisListType

D = 1024
SEQ = 4096
CTXL = 256
NT = SEQ + CTXL
NCOL = NT + 3
NTILE = NT // 128
DEPTH = 2
DFF = 2816
RW = 1920
INC = 2592
ALPHA = (2 * DEPTH) ** 0.25
LN_EPS = 1e-5
RMS_EPS = 1e-6
GN_EPS = 64e-5
CW = float(np.exp(-0.5))
QSCALE = 96 ** -0.5
BLOCKS = [(0, 256)] + [(256 + 512 * j, 512) for j in range(8)]


def tcol(i):
    return 128 * i + (1 if i < 2 else 2)


def gcol(g):
    return g + (1 if g < 256 else 2)


class Dep:
    __slots__ = ("w", "r", "x")

    def __init__(self, x=False):
        self.w = None
        self.r = {}
        self.x = x


class T:
    __slots__ = ("t", "d")

    def __init__(self, t, d=None):
        self.t = t
        self.d = d if d is not None else Dep()


class KB:
    NDMA = 8

    def __init__(self, nc, es):
        self.nc = nc
        self.engs = {"pe": nc.tensor, "act": nc.scalar, "dve": nc.vector,
                     "pool": nc.gpsimd, "sp": nc.sync}
        self.sems = {}
        self.cnt = {}
        for e in self.engs:
            self.sems[e] = es.enter_context(nc.semaphore("s_" + e))
            self.cnt[e] = 0
        self.dq = {}
        for q in ("sp", "pool", "act"):
            self.dq[q] = 0
            for j in range(self.NDMA):
                key = "d_%s%d" % (q, j)
                self.sems[key] = es.enter_context(nc.semaphore(key))
                self.cnt[key] = 0
        self.known = {e: {} for e in self.engs}
        self.pending = {e: False for e in self.engs}
        self.nwait = 0
        self.nins = 0
        self.halt = False
        self.banks = []
        self.banks_bf = []
        self.bdep = []
        for b in range(8):
            t = es.enter_context(nc.psum_tensor("psb%d" % b, [128, 512], F32))
            self.banks.append(t)
            self.banks_bf.append(t.bitcast(BF16))
            bd = Dep(x=True)
            self.bdep.append((bd, bd))
        self.ev_i = 0
        self.sb_i = 0

    def _need(self, reads, writes):
        need = {}
        for d in reads:
            if d.w is not None:
                k, v = d.w
                if need.get(k, 0) < v:
                    need[k] = v
        for d in writes:
            if d.w is not None:
                k, v = d.w
                if need.get(k, 0) < v:
                    need[k] = v
            for k, v in d.r.items():
                if need.get(k, 0) < v:
                    need[k] = v
        return need

    def _waits(self, e, need, skip_own=False):
        eng = self.engs[e]
        kn = self.known[e]
        for k, v in need.items():
            if skip_own and k == e:
                continue
            if kn.get(k, 0) < v:
                eng.wait_ge(self.sems[k], v)
                kn[k] = v
                self.nwait += 1

    def _mark(self, tok, reads, writes):
        k, v = tok
        for d in reads:
            if d.r.get(k, 0) < v:
                d.r[k] = v
        for d in writes:
            d.w = tok
            d.r = {}

    def op(self, e, fn, reads=(), writes=(), inc=True):
        if self.halt:
            return None
        reads = [x.d if isinstance(x, T) else x for x in reads]
        writes = [x.d if isinstance(x, T) else x for x in writes]
        xs = [d for d in reads if d.x]
        if xs:
            reads = [d for d in reads if not d.x]
            writes = writes + xs
        need = self._need(reads, writes)
        self._waits(e, need, skip_own=(e == "pe"))
        ins = fn(self.engs[e])
        self.nins += 1
        if inc:
            self.cnt[e] += 1
            ins.then_inc(self.sems[e], 1)
            tok = (e, self.cnt[e])
            self.pending[e] = False
        else:
            tok = (e, self.cnt[e] + 1)
            self.pending[e] = True
        self._mark(tok, reads, writes)
        return tok

    def dma(self, q, out, in_, reads=(), writes=(), **kw):
        if self.halt:
            return None
        reads = [x.d if isinstance(x, T) else x for x in reads]
        writes = [x.d if isinstance(x, T) else x for x in writes]
        j = self.dq[q] % self.NDMA
        self.dq[q] += 1
        key = "d_%s%d" % (q, j)
        need = self._need(reads, writes)
        if self.cnt[key] > 0:
            need[key] = max(need.get(key, 0), self.cnt[key])
        self._waits(q, need)
        ins = self.engs[q].dma_start(out=out, in_=in_, **kw)
        self.nins += 1
        self.cnt[key] += 16
        ins.then_inc(self.sems[key], 16)
        tok = (key, self.cnt[key])
        self._mark(tok, reads, writes)
        return tok

    def barrier(self, engines=("pe", "act", "dve", "pool", "sp")):
        if self.halt:
            return
        for e in self.engs:
            assert not self.pending[e], e
        need = {k: v for k, v in self.cnt.items() if v > 0}
        for e in engines:
            self._waits(e, dict(need))

    def tile(self, es, shape, dtype, name):
        self.tile_i = getattr(self, "tile_i", 0) + 1
        return T(es.enter_context(self.nc.sbuf_tensor("%s_%d" % (name, self.tile_i), list(shape), dtype)))

    def ev_eng(self):
        self.ev_i += 1
        return "act" if self.ev_i % 2 else "dve"

    def sb_eng(self):
        self.sb_i += 1
        return "pool" if self.sb_i % 2 else "dve"

    def copy(self, e, out, in_, reads, writes, scale=None):
        if e == "act":
            if scale is None:
                return self.op("act", lambda g: g.copy(out, in_), reads, writes)
            return self.op("act", lambda g: g.mul(out, in_, scale), reads, writes)
        if scale is None:
            return self.op(e, lambda g: g.tensor_copy(out, in_), reads, writes)
        return self.op(e, lambda g: g.tensor_scalar_mul(out, in_, scale), reads, writes)


class PsPool:
    def __init__(self, kb, banks):
        self.kb = kb
        self.halves = [(b, h) for b in banks for h in (0, 1)]
        self.i = 0

    def get(self, ncols):
        n = len(self.halves)
        if ncols <= 256:
            b, h = self.halves[self.i % n]
            self.i += 1
            return b, h * 256, [self.kb.bdep[b][h]]
        if self.i % 2:
            self.i += 1
        b, _ = self.halves[self.i % n]
        self.i += 2
        return b, 0, [self.kb.bdep[b][0], self.kb.bdep[b][1]]


def make_consts():
    c = {}
    c["identf"] = np.eye(128, dtype=np.float32)
    c["onesf"] = np.ones((128, 128), dtype=np.float32)
    t = np.arange(128)
    m4 = np.zeros((2, 128, 512), np.float32)
    mt = np.zeros((2, 128, 128), np.float32)
    tri = np.zeros((2, 3, 128, 128), np.float32)
    for d in range(2):
        before = (t[:, None] < t[None, :]) if d == 0 else (t[:, None] > t[None, :])
        beq = before | (t[:, None] == t[None, :])
        m4[d, :, 0:128] = -1.0 * before
        m4[d, :, 128:256] = beq
        m4[d, :, 256:384] = before
        m4[d, :, 384:512] = beq
        mt[d] = -1.0 * before.T
        tri[d, 0] = -CW * beq
        tri[d, 1] = -CW * before
        tri[d, 2] = -CW * (~beq)
    c["mask4"] = m4
    c["maskt"] = mt
    c["tri"] = tri
    c["cvec"] = np.full((128, 1), -CW, np.float32)
    esel = np.zeros((65, 64), np.float32)
    esel[64, :] = 1.0
    c["esel"] = esel
    n_pairs = 8
    inv = 10000.0 ** (-np.arange(n_pairs, dtype=np.float32) / n_pairs)
    row = np.repeat(np.arange(SEQ // 64, dtype=np.float32), 64)
    col = np.tile(np.arange(64, dtype=np.float32), SEQ // 64)
    ang = np.concatenate([row[:, None] * inv, col[:, None] * inv], axis=-1).astype(np.float32)
    cos = np.cos(ang).astype(np.float32)
    sin = np.sin(ang).astype(np.float32)
    cos2 = np.repeat(cos, 2, axis=1).T
    sin2 = np.repeat(sin, 2, axis=1).T
    cosq = np.ones((96, NT), np.float32)
    sinq = np.zeros((96, NT), np.float32)
    cosq[64:96, CTXL:] = cos2
    sinq[64:96, CTXL:] = sin2
    c["cosk"] = cosq.copy()
    c["sink"] = sinq.copy()
    c["cosq"] = (cosq * QSCALE).astype(np.float32)
    c["sinq"] = (sinq * QSCALE).astype(np.float32)
    return c


CONST_SHAPES = {k: v.shape for k, v in make_consts().items()}

W_SHAPES = {
    "w_ada": (2, 1024, 6144), "b_ada": (2, 6144), "w_in": (2, 1024, 2592), "rwkv_conv": (2, 3, 1920),
    "w0": (2, 2, 512), "w_b": (2, 2, 64, 512), "a0": (2, 2, 512), "a_b": (2, 2, 64, 512),
    "g_b": (2, 128, 512), "k_k": (2, 512), "k_a": (2, 512), "r_k": (2, 8, 64), "gn_g": (2, 512),
    "gn_b": (2, 512), "q_norm_g": (2, 384), "w_uq": (2, 384, 768), "kv_norm_g": (2, 256),
    "w_ukv": (2, 256, 1024), "w_o": (2, 1024, 1024), "ln1_g": (2, 1024), "ln1_b": (2, 1024),
    "w_up": (2, 1024, 5632), "ffn_conv_w": (2, 3, 5632), "ffn_conv_b": (2, 5632),
    "w_down": (2, 2816, 1024), "ln2_g": (2, 1024), "ln2_b": (2, 1024),
}


class Env:
    pass


def load_vec_fm(kb, env, es, vec_ap, n, name):
    nc = kb.nc
    rows = kb.tile(es, [n, 128], F32, name + "_r")
    out = kb.tile(es, [128, n], F32, name)
    kb.dma("sp", rows.t[:], vec_ap.rearrange("(n p) -> n p", p=128), writes=[rows])
    b, c0, deps = env.pp.get(n)
    ps = kb.banks[b]
    kb.op("pe", lambda e: e.transpose(ps[:, c0:c0 + n], rows.t[:], env.identf.t[0:n, 0:n]),
          reads=[rows, env.identf], writes=deps)
    kb.copy("dve", out.t[:], ps[:, c0:c0 + n], deps, [out])
    return out


def load_bcast(kb, es, vec_ap, n, name):
    out = kb.tile(es, [128, n], F32, name)
    kb.dma("sp", out.t[:], vec_ap.partition_broadcast(128), writes=[out])
    return out


def load_w_bf16(kb, es, dst, w_ap, kchunks, ncols, stage, col_off=0, scale=None):
    wv = w_ap.rearrange("(k p) n -> k p n", p=128)
    for k in range(kchunks):
        st = stage[k % len(stage)]
        kb.dma(("sp", "act", "pool")[k % 3], st.t[:, 0:ncols], wv[k], writes=[st])
        e = ("act", "dve", "pool")[k % 3]
        if scale is None:
            kb.copy(e, dst.t[:, k, col_off:col_off + ncols], st.t[:, 0:ncols], [st], [dst])
        else:
            kb.op("dve", lambda g: g.tensor_scalar_mul(dst.t[:, k, col_off:col_off + ncols], st.t[:, 0:ncols], scale.t[:, k:k + 1]),
                  [st, scale], [dst])


def emit_ut(kb, env, xt, s, modf, w_scale, w_shift, dst, i, pad):
    for _ in emit_ut_g(kb, env, xt, s, modf, w_scale, w_shift, dst, i, pad):
        pass


def emit_ut_g(kb, env, xt, s, modf, w_scale, w_shift, dst, i, pad):
    c0 = tcol(i)
    st = env.ut_st[env.ut_i % 2]
    env.ut_i += 1
    for half in range(2):
        b, _, deps = env.pp.get(512)
        ps = kb.banks[b]
        for j in range(4):
            c = half * 4 + j
            kb.op("pe", lambda e: e.transpose(ps[:, j * 128:(j + 1) * 128], xt.t[:, c * 128:(c + 1) * 128], env.identf.t[:]),
                  reads=[xt, env.identf], writes=deps, inc=(j == 3))
        for j in range(4):
            c = half * 4 + j
            kb.op("act", lambda e: e.activation(st.t[:, c, 1:129], ps[:, j * 128:(j + 1) * 128], AF.Identity,
                                                bias=modf.t[:, w_shift * 8 + c, s:s + 1], scale=modf.t[:, w_scale * 8 + c, s:s + 1]),
                  reads=deps + [modf], writes=[st])
            if j % 2 == 1:
                yield
    lo, hi = 1, 129
    if pad and i in (0, 2):
        lo = 0
    if pad and i in (1, NTILE - 1):
        hi = 130
    kb.dma("pool", dst[:, c0 - 1 + lo:c0 - 1 + hi].rearrange("(k p) c -> p k c", p=128), st.t[:, :, lo:hi],
           reads=[st], writes=[env.dd[dst.tensor.name]])


def phase_mod(kb, env, l):
    nc = kb.nc
    with ExitStack() as es:
        modf = env.modf[l]
        crow = kb.tile(es, [16, 128], F32, "crow")
        kb.dma("sp", crow.t[0:8, :], env.c.rearrange("(n p) -> n p", p=128), writes=[crow])
        kb.dma("sp", crow.t[8:16, :], env.c_ctx.rearrange("(n p) -> n p", p=128), writes=[crow])
        sct = kb.tile(es, [128, 2, 8], F32, "sct")
        b, c0, deps = env.pp.get(16)
        ps = kb.banks[b]
        kb.op("pe", lambda e: e.transpose(ps[:, c0:c0 + 16], crow.t[:], env.identf.t[0:16, 0:16]), [crow, env.identf], deps)
        kb.op("act", lambda e: e.activation(sct.t[:].rearrange("p s k -> p (s k)"), ps[:, c0:c0 + 16], AF.Silu), deps, [sct])
        bF = load_vec_fm(kb, env, es, env.w["b_ada"][l], 48, "bF")
        stg = [kb.tile(es, [128, 8, 768], F32, "wada%d" % i) for i in range(2)]
        bq, cq0, depq = env.pp.get(96)
        psq = kb.banks[bq]
        wv = env.w["w_ada"][l].rearrange("(k p) n -> p k n", p=128)
        for pc in range(8):
            st = stg[pc % 2]
            kb.dma("sp", st.t[:], wv[:, :, pc * 768:(pc + 1) * 768], writes=[st])
            for jj in range(6):
                j = pc * 6 + jj
                for k in range(8):
                    kb.op("pe", lambda e: e.matmul(psq[:, cq0 + 2 * j:cq0 + 2 * j + 2], st.t[:, k, jj * 128:(jj + 1) * 128], sct.t[:, :, k],
                                                  start=(k == 0), stop=(k == 7)),
                          [st, sct], depq, inc=(k == 7))
        kb.op("dve", lambda e: e.tensor_tensor(modf.t[:], psq[:, cq0:cq0 + 96].rearrange("p (j s) -> p j s", s=2),
                                              bF.t[:].unsqueeze(2).to_broadcast([128, 48, 2]), ALU.add),
              depq + [bF], [modf])
        for wch in (1, 4):
            kb.op("dve", lambda e: e.tensor_scalar_add(modf.t[:, wch * 8:(wch + 1) * 8, :], modf.t[:, wch * 8:(wch + 1) * 8, :], 1.0),
                  [modf], [modf])
        grow = kb.tile(es, [1, 2, 2, 1024], F32, "grow")
        brow = kb.tile(es, [1, 2, 1024], F32, "brow")
        for gi, wch in enumerate((2, 5)):
            kb.dma("sp", brow.t[:, gi, :], env.w["b_ada"][l][wch * 1024:(wch + 1) * 1024].rearrange("(o n) -> o n", o=1), writes=[brow])
        for gi, wch in enumerate((2, 5)):
            st = stg[gi % 2]
            for hh in range(2):
                col = wch * 1024 + hh * 512
                kb.dma("sp", st.t[:, :, 0:512], wv[:, :, col:col + 512], writes=[st])
                for s in range(2):
                    b2, c2, dep2 = env.pp.get(512)
                    ps2 = kb.banks[b2]
                    for k in range(8):
                        kb.op("pe", lambda e: e.matmul(ps2[0:1, :], sct.t[:, s, k:k + 1], st.t[:, k, 0:512], start=(k == 0), stop=(k == 7)),
                              [st, sct], dep2, inc=(k == 7))
                    kb.op("dve", lambda e: e.tensor_tensor(grow.t[:, s, gi, hh * 512:(hh + 1) * 512], ps2[0:1, :], brow.t[:, gi, hh * 512:(hh + 1) * 512], ALU.add),
                          dep2 + [brow], [grow])
        kb.dma("sp", env.MODROW[l:l + 1].rearrange("o s g n -> o (s g n)"), grow.t[:].rearrange("o s g n -> o (s g n)"),
               reads=[grow], writes=[env.dd["MODROW"]])
        kb.barrier()


def xin_ap(env, l, i):
    if l == 0:
        return env.ctx[i * 128:(i + 1) * 128, :] if i < 2 else env.x[(i - 2) * 128:(i - 1) * 128, :]
    return env.X2[i * 128:(i + 1) * 128, :]


def phase_u0(kb, env):
    with ExitStack() as es:
        env.ut_st = [kb.tile(es, [128, 8, 130], BF16, "utst%d" % i) for i in range(2)]
        env.ut_i = 0
        xts = [kb.tile(es, [128, 1024], F32, "xt%d" % i) for i in range(3)]
        for i in range(NTILE):
            xt = xts[i % 3]
            kb.dma("sp", xt.t[:], xin_ap(env, 0, i), writes=[xt])
            emit_ut(kb, env, xt, 1 if i < 2 else 0, env.modf[0], 1, 0, env.UT, i, pad=False)
        kb.barrier()


def phase_p1(kb, env, l):
    nc = kb.nc
    with ExitStack() as es:
        win = kb.tile(es, [128, 8, 2624], BF16, "win")
        stage = [kb.tile(es, [128, 2592], F32, "wst%d" % i) for i in range(2)]
        load_w_bf16(kb, es, win, env.w["w_in"][l], 8, 2592, stage)
        kb.op("dve", lambda e: e.tensor_scalar_mul(win.t[:, :, 2592:2624:2], win.t[:, :, 2561:2592:2], -1.0), [win], [win])
        kb.op("dve", lambda e: e.tensor_copy(win.t[:, :, 2593:2624:2], win.t[:, :, 2560:2592:2]), [win], [win])
        utb = [kb.tile(es, [128, 8, 512], BF16, "utb%d" % i) for i in range(2)]
        pst = [kb.tile(es, [128, 15, 514], BF16, "pst%d" % i) for i in range(2)]
        pmst = [kb.tile(es, [128, 6, 512], F32, "pmst%d" % i) for i in range(2)]
        for t in pst:
            kb.op("pool", lambda e: e.memset(t.t[:], 0.0), [], [t])
        for bi, (g0, n) in enumerate(BLOCKS):
            c0 = gcol(g0)
            ub = utb[bi % 2]
            ps_ = pst[bi % 2]
            pm_ = pmst[bi % 2]
            kb.dma("sp", ub.t[:, :, 0:n], env.UT[:, c0:c0 + n].rearrange("(k p) c -> p k c", p=128),
                   reads=[env.dd["UT"]], writes=[ub])
            for jf in range(21):
                rows = 128 if jf < 20 else 64
                b, _, deps = env.pp.get(512)
                ps = kb.banks[b]
                for k in range(8):
                    kb.op("pe", lambda e: e.matmul(ps[0:rows, 0:n], win.t[:, k, jf * 128:jf * 128 + rows], ub.t[:, k, 0:n],
                                                  start=(k == 0), stop=(k == 7)),
                          [win, ub], deps, inc=(k == 7))
                if jf < 15:
                    kb.copy(kb.ev_eng(), ps_.t[:, jf, 1:1 + n], ps[:, 0:n], deps, [ps_])
                else:
                    kb.copy(kb.ev_eng(), pm_.t[0:rows, jf - 15, 0:n], ps[0:rows, 0:n], deps, [pm_])
            lo, hi = 1, 1 + n
            if bi == 0:
                lo, hi = 0, n + 2
            if bi == len(BLOCKS) - 1:
                hi = n + 2
            kb.dma("pool", env.PT[:, c0 - 1 + lo:c0 - 1 + hi].rearrange("(k p) c -> p k c", p=128), ps_.t[:, :, lo:hi],
                   reads=[ps_], writes=[env.dd["PT"]])
            kb.dma("pool", env.PM[0:640, c0:c0 + n].rearrange("(k p) c -> p k c", p=128), pm_.t[:, 0:5, 0:n],
                   reads=[pm_], writes=[env.dd["PM"]])
            kb.dma("pool", env.PM[640:704, c0:c0 + n], pm_.t[0:64, 5, 0:n], reads=[pm_], writes=[env.dd["PM"]])
        kb.barrier()


SCRATCH = {
    "UT": ([1024, NCOL], BF16), "PT": ([RW, NCOL], BF16), "PM": ([704, NCOL], F32),
    "MIXT": ([1024, NCOL], BF16), "X1": ([NT, 1024], F32), "U2T": ([1024, NCOL], BF16),
    "GT": ([DFF, NCOL], BF16), "X2": ([NT, 1024], F32), "YF": ([NT, 512], F32),
    "MODROW": ([2, 2, 2, 1024], F32), "DBG": ([16, 128, 512], F32),
}


def build(phases=None, debug_out=(), debug_in=(), stop=None):
    nc = bass.Bass("TRN2", target_bir_lowering=False)
    env = Env()
    env.x = nc.dram_tensor("x", [SEQ, D], F32, kind="ExternalInput").ap()
    env.c = nc.dram_tensor("c", [D], F32, kind="ExternalInput").ap()
    env.ctx = nc.dram_tensor("ctx", [CTXL, D], F32, kind="ExternalInput").ap()
    env.c_ctx = nc.dram_tensor("c_ctx", [D], F32, kind="ExternalInput").ap()
    env.w = {k: nc.dram_tensor(k, list(s), F32, kind="ExternalInput").ap() for k, s in W_SHAPES.items()}
    env.cst = {k: nc.dram_tensor("k_" + k, list(s), F32, kind="ExternalInput").ap() for k, s in CONST_SHAPES.items()}
    env.y = nc.dram_tensor("y", [SEQ, D], F32, kind="ExternalOutput").ap()
    env.dd = {}
    for name, (shape, dt_) in SCRATCH.items():
        kind = "ExternalOutput" if name in debug_out else ("ExternalInput" if name in debug_in else "Internal")
        setattr(env, name, nc.dram_tensor(name, shape, dt_, kind=kind).ap())
        env.dd[name] = Dep()
    env.dd["y"] = Dep()
    env.stop = stop
    allp = ["mod", "u0"]
    for l in range(DEPTH):
        allp += ["p1_%d" % l, "mla_%d" % l, "rwkv_%d" % l, "wo_%d" % l, "ffu_%d" % l, "ffd_%d" % l]
    if phases is None:
        phases = allp
    with ExitStack() as es:
        kb = KB(nc, es)
        env.pp = PsPool(kb, list(range(8)))
        env.identf = kb.tile(es, [128, 128], F32, "identf")
        kb.dma("sp", env.identf.t[:], env.cst["identf"], writes=[env.identf])
        env.identb = kb.tile(es, [128, 128], BF16, "identb")
        kb.copy("dve", env.identb.t[:], env.identf.t[:], [env.identf], [env.identb])
        env.onesf = kb.tile(es, [128, 128], F32, "onesf")
        kb.dma("sp", env.onesf.t[:], env.cst["onesf"], writes=[env.onesf])
        env.modf = [kb.tile(es, [128, 48, 2], F32, "modf%d" % l) for l in range(DEPTH)]
        env.epsr = kb.tile(es, [128, 1], F32, "epsr")
        kb.op("pool", lambda e: e.memset(env.epsr.t[:], RMS_EPS), [], [env.epsr])
        env.epsl = kb.tile(es, [128, 1], F32, "epsl")
        kb.op("pool", lambda e: e.memset(env.epsl.t[:], LN_EPS), [], [env.epsl])
        env.epsg = kb.tile(es, [128, 1], F32, "epsg")
        kb.op("pool", lambda e: e.memset(env.epsg.t[:], GN_EPS), [], [env.epsg])
        for ph in phases:
            if ph == "mod":
                for l in range(DEPTH):
                    phase_mod(kb, env, l)
            elif ph == "u0":
                phase_u0(kb, env)
            else:
                name, l = ph.rsplit("_", 1)
                try:
                    PHASES[name](kb, env, int(l))
                except StopPhase:
                    print("stopped at checkpoint", env.stop, flush=True)
                    break
        kb.barrier()
        env.kb = kb
    print("built: %d instructions, %d waits" % (kb.nins, kb.nwait), flush=True)
    return nc, env


PHASES = {"p1": phase_p1}


def phase_mla(kb, env, l):
    nc = kb.nc
    need_ctx = l < DEPTH - 1
    env.pp = PsPool(kb, [0, 1, 2, 3, 4])
    pp = env.pp
    with ExitStack() as es:
        cqn = kb.tile(es, [128, 3, NT], BF16, "cqn")
        ckvn = kb.tile(es, [128, 2, NT], BF16, "ckvn")
        va = kb.tile(es, [128, NTILE, 8, 65], BF16, "va")
        KT = [kb.tile(es, [128, NT], BF16, "kt%d" % i) for i in range(2)]
        wq2 = kb.tile(es, [128, 3, 8, 2, 96], BF16, "wq2")
        wk = kb.tile(es, [128, 2, 512], BF16, "wk")
        wv = kb.tile(es, [128, 2, 512], BF16, "wv")
        esel = kb.tile(es, [65, 64], F32, "esel")
        kb.dma("sp", esel.t[:], env.cst["esel"], writes=[esel])
        with ExitStack() as es2:
            qg = load_vec_fm(kb, env, es2, env.w["q_norm_g"][l], 3, "qg")
            kg = load_vec_fm(kb, env, es2, env.w["kv_norm_g"][l], 2, "kg")
            wq_st = kb.tile(es2, [128, 3, 768], F32, "wq_st")
            wkv_st = kb.tile(es2, [128, 2, 1024], F32, "wkv_st")
            kb.dma("sp", wq_st.t[:], env.w["w_uq"][l].rearrange("(k p) n -> p k n", p=128), writes=[wq_st])
            kb.dma("sp", wkv_st.t[:], env.w["w_ukv"][l].rearrange("(k p) n -> p k n", p=128), writes=[wkv_st])
            kb.op("pool", lambda e: e.memset(wq2.t[:], 0.0), [], [wq2])
            kb.op("pool", lambda e: e.memset(va.t[:], 1.0), [], [va])
            for k in range(3):
                kb.op("dve", lambda e: e.tensor_scalar_mul(wq2.t[:, k, :, 0, :], wq_st.t[:, k, :].rearrange("p (h d) -> p h d", d=96), qg.t[:, k:k + 1]),
                      [wq_st, qg], [wq2])
                kb.op("dve", lambda e: e.tensor_scalar_mul(wq2.t[:, k, :, 1, 64:96:2], wq2.t[:, k, :, 0, 65:96:2], -1.0), [wq2], [wq2])
                kb.op("dve", lambda e: e.tensor_copy(wq2.t[:, k, :, 1, 65:96:2], wq2.t[:, k, :, 0, 64:96:2]), [wq2], [wq2])
            for k in range(2):
                src = wkv_st.t[:, k, :].rearrange("p (h e) -> p h e", e=128)
                kb.op("dve", lambda e: e.tensor_scalar_mul(wk.t[:, k, :].rearrange("p (h d) -> p h d", d=64), src[:, :, 0:64], kg.t[:, k:k + 1]),
                      [wkv_st, kg], [wk])
                kb.op("dve", lambda e: e.tensor_scalar_mul(wv.t[:, k, :].rearrange("p (h d) -> p h d", d=64), src[:, :, 64:128], kg.t[:, k:k + 1]),
                      [wkv_st, kg], [wv])
            cs = [kb.tile(es2, [128, 5, 512], F32, "cs%d" % i) for i in range(2)]
            sq = [kb.tile(es2, [128, 5, 512], F32, "sq%d" % i) for i in range(2)]
            rsd = [kb.tile(es2, [128, 2, 512], F32, "rsd%d" % i) for i in range(2)]
            krt = [kb.tile(es2, [128, 4, 512], F32, "krt%d" % i) for i in range(2)]
            krm = [kb.tile(es2, [128, 2, 512], F32, "krm%d" % i) for i in range(2)]
            for bi, (g0, n) in enumerate(BLOCKS):
                c0 = gcol(g0)
                c_, s_, r_, kr_, km_ = cs[bi % 2], sq[bi % 2], rsd[bi % 2], krt[bi % 2], krm[bi % 2]
                kb.dma("sp", c_.t[:, :, 0:n], env.PM[0:640, c0:c0 + n].rearrange("(k p) c -> p k c", p=128), reads=[env.dd["PM"]], writes=[c_])
                kb.op("act", lambda e: e.activation(s_.t[:, :, 0:n], c_.t[:, :, 0:n], AF.Square), [c_], [s_])
                for wi, (k0, k1, dim) in enumerate(((0, 3, 384.0), (3, 5, 256.0))):
                    b, _, deps = pp.get(512)
                    ps = kb.banks[b]
                    for k in range(k0, k1):
                        kb.op("pe", lambda e: e.matmul(ps[:, 0:n], env.onesf.t[:], s_.t[:, k, 0:n], start=(k == k0), stop=(k == k1 - 1)),
                              [env.onesf, s_], deps, inc=(k == k1 - 1))
                    kb.op("act", lambda e: e.activation(r_.t[:, wi, 0:n], ps[:, 0:n], AF.Sqrt, bias=env.epsr.t[:, 0:1], scale=1.0 / dim), deps + [env.epsr], [r_])
                    kb.op("dve", lambda e: e.reciprocal(r_.t[:, wi, 0:n], r_.t[:, wi, 0:n]), [r_], [r_])
                    dst = cqn if wi == 0 else ckvn
                    for k in range(k0, k1):
                        kb.op(kb.sb_eng(), lambda e: e.tensor_tensor(dst.t[:, k - k0, g0:g0 + n], c_.t[:, k, 0:n], r_.t[:, wi, 0:n], ALU.mult),
                              [c_, r_], [dst])
                kb.dma("sp", kr_.t[64:96, 0, 0:n], env.PM[640:672, c0:c0 + n], reads=[env.dd["PM"]], writes=[kr_])
                kb.dma("sp", kr_.t[64:96, 1, 0:n], env.PM[672:704, c0:c0 + n], reads=[env.dd["PM"]], writes=[kr_])
                kb.dma("sp", kr_.t[64:96, 2, 0:n], env.cst["cosk"][64:96, g0:g0 + n], writes=[kr_])
                kb.dma("sp", kr_.t[64:96, 3, 0:n], env.cst["sink"][64:96, g0:g0 + n], writes=[kr_])
                kb.op("pool", lambda e: e.tensor_tensor(km_.t[64:96, 0, 0:n], kr_.t[64:96, 0, 0:n], kr_.t[64:96, 2, 0:n], ALU.mult), [kr_], [km_])
                kb.op("dve", lambda e: e.tensor_tensor(km_.t[64:96, 1, 0:n], kr_.t[64:96, 1, 0:n], kr_.t[64:96, 3, 0:n], ALU.mult), [kr_], [km_])
                kb.op("pool", lambda e: e.tensor_tensor(KT[0].t[64:96, g0:g0 + n], km_.t[64:96, 0, 0:n], km_.t[64:96, 1, 0:n], ALU.add), [km_], [KT[0]])
            kb.op("pool", lambda e: e.tensor_copy(KT[1].t[64:96, :], KT[0].t[64:96, :]), [KT[0]], [KT[1]])
            for i in range(NTILE):
                b, _, deps = pp.get(512)
                ps = kb.banks[b]
                for k in range(2):
                    kb.op("pe", lambda e: e.matmul(ps[:, :], ckvn.t[:, k, i * 128:(i + 1) * 128], wv.t[:, k, :], start=(k == 0), stop=(k == 1)),
                          [ckvn, wv], deps, inc=(k == 1))
                kb.copy(kb.ev_eng(), va.t[:, i, :, 0:64], ps[:, :].rearrange("p (h d) -> p h d", d=64), deps, [va])
            kb.barrier()
        QT = [kb.tile(es, [128, NT], BF16, "qt%d" % i) for i in range(2)]
        NEGM = [kb.tile(es, [128, 1], F32, "negm%d" % i) for i in range(2)]
        tabs = [kb.tile(es, [128, 2, 512], F32, "tab%d" % i) for i in range(2)]
        tmp = [kb.tile(es, [128, 2, 512], F32, "qtmp%d" % i) for i in range(2)]
        sqq = [kb.tile(es, [128, 512], F32, "sqq%d" % i) for i in range(2)]
        ptt = [kb.tile(es, [128, 512], BF16, "ptt%d" % i) for i in range(8)]
        osb = [kb.tile(es, [128, 512], F32, "osb%d" % i) for i in range(2)]
        rl = [kb.tile(es, [64, 512], F32, "rl%d" % i) for i in range(2)]
        ob = [kb.tile(es, [64, 512], BF16, "ob%d" % i) for i in range(2)]
        nb = kb.tile(es, [1, 2, 16], F32, "nb")
        msc = kb.tile(es, [1, 4], F32, "msc")
        qblocks = BLOCKS if need_ctx else BLOCKS[1:]
        LOOK = 4
        cnt = {"ti": 0, "pi": 0, "qi": 0}

        def setup(h):
            kt = KT[h % 2]
            qt = QT[h % 2]
            negm = NEGM[h % 2]
            for bi, (g0, n) in enumerate(BLOCKS):
                b, _, deps = pp.get(512)
                ps = kb.banks[b]
                for k in range(2):
                    kb.op("pe", lambda e: e.matmul(ps[0:64, 0:n], wk.t[:, k, h * 64:(h + 1) * 64], ckvn.t[:, k, g0:g0 + n], start=(k == 0), stop=(k == 1)),
                          [wk, ckvn], deps, inc=(k == 1))
                kb.copy("dve", kt.t[0:64, g0:g0 + n], ps[0:64, 0:n], deps, [kt])
            for bi, (g0, n) in enumerate(qblocks):
                tb = tabs[cnt["ti"] % 2]
                tm = tmp[cnt["ti"] % 2]
                cnt["ti"] += 1
                kb.dma("sp", tb.t[0:96, 0, 0:n], env.cst["cosq"][:, g0:g0 + n], writes=[tb])
                kb.dma("sp", tb.t[0:96, 1, 0:n], env.cst["sinq"][:, g0:g0 + n], writes=[tb])
                for ab in range(2):
                    b, _, deps = pp.get(512)
                    ps = kb.banks[b]
                    for k in range(3):
                        kb.op("pe", lambda e: e.matmul(ps[0:96, 0:n], wq2.t[:, k, h, ab, :], cqn.t[:, k, g0:g0 + n], start=(k == 0), stop=(k == 2)),
                              [wq2, cqn], deps, inc=(k == 2))
                    kb.op("dve", lambda e: e.tensor_tensor(tm.t[0:96, ab, 0:n], ps[0:96, 0:n], tb.t[0:96, ab, 0:n], ALU.mult), deps + [tb], [tm])
                kb.op("pool", lambda e: e.tensor_tensor(qt.t[0:96, g0:g0 + n], tm.t[0:96, 0, 0:n], tm.t[0:96, 1, 0:n], ALU.add), [tm], [qt])
            for wi, (src, blks) in enumerate(((qt, qblocks), (kt, BLOCKS))):
                for bi, (g0, n) in enumerate(blks):
                    s_ = sqq[(wi + bi) % 2]
                    kb.op("pool", lambda e: e.tensor_tensor(s_.t[0:96, 0:n], src.t[0:96, g0:g0 + n], src.t[0:96, g0:g0 + n], ALU.mult), [src], [s_])
                    b, c0, deps = pp.get(512)
                    ps = kb.banks[b]
                    kb.op("pe", lambda e: e.matmul(ps[0:1, 0:n], env.onesf.t[0:96, 0:1], s_.t[0:96, 0:n], start=True, stop=True), [env.onesf, s_], deps)
                    kb.op("dve", lambda e: e.reduce_max(nb.t[0:1, wi, bi:bi + 1], ps[0:1, 0:n], AX.X), deps, [nb])
                kb.op("dve", lambda e: e.reduce_max(msc.t[0:1, wi:wi + 1], nb.t[0:1, wi, 0:len(blks)], AX.X), [nb], [msc])
            kb.op("dve", lambda e: e.tensor_tensor(msc.t[0:1, 2:3], msc.t[0:1, 0:1], msc.t[0:1, 1:2], ALU.mult), [msc], [msc])
            kb.op("act", lambda e: e.activation(msc.t[0:1, 3:4], msc.t[0:1, 2:3], AF.Sqrt), [msc], [msc])
            kb.op("dve", lambda e: e.tensor_scalar_mul(msc.t[0:1, 3:4], msc.t[0:1, 3:4], -1.0), [msc], [msc])
            b, c0, deps = pp.get(16)
            ps = kb.banks[b]
            kb.op("pe", lambda e: e.matmul(ps[:, c0:c0 + 1], env.onesf.t[0:1, :], msc.t[0:1, 3:4], start=True, stop=True), [env.onesf, msc], deps)
            kb.copy("dve", negm.t[:, 0:1], ps[:, c0:c0 + 1], deps, [negm])

        def attn(h):
            kt = KT[h % 2]
            qt = QT[h % 2]
            negm = NEGM[h % 2]
            items = []
            for bi, (g0, n) in enumerate(qblocks):
                kts = list(range(NTILE)) if g0 >= CTXL else [0, 1]
                qi = cnt["qi"]
                cnt["qi"] += 1
                for ii, ki in enumerate(kts):
                    items.append((g0, n, ii, ki, len(kts), qi))
            inflight = []

            def stage_a(it):
                g0, n, ii, ki, nk, qi = it
                b, _, deps = pp.get(512)
                ps = kb.banks[b]
                kb.op("pe", lambda e: e.matmul(ps[:, 0:n], kt.t[0:96, ki * 128:(ki + 1) * 128], qt.t[0:96, g0:g0 + n], start=True, stop=True),
                      [kt, qt], deps)
                p_ = ptt[cnt["pi"] % len(ptt)]
                cnt["pi"] += 1
                kb.op("act", lambda e: e.activation(p_.t[:, 0:n], ps[:, 0:n], AF.Exp, bias=negm.t[:, 0:1], scale=1.0), deps + [negm], [p_])
                inflight.append(p_)

            def stage_b(it):
                g0, n, ii, ki, nk, qi = it
                p_ = inflight.pop(0)
                pob = 5 + (qi % 2)
                po = kb.banks[pob]
                pod = list(kb.bdep[pob])
                kb.op("pe", lambda e: e.matmul(po[0:65, 0:n], va.t[:, ki, h, :], p_.t[:, 0:n], start=(ii == 0), stop=(ii == nk - 1)),
                      [va, p_], pod, inc=(ii == nk - 1))
                if ii != nk - 1:
                    return
                o_ = osb[qi % 2]
                r_ = rl[qi % 2]
                b_ = ob[qi % 2]
                kb.copy("dve", o_.t[0:65, 0:n], po[0:65, 0:n], pod, [o_])
                b, _, deps = pp.get(512)
                ps = kb.banks[b]
                kb.op("pe", lambda e: e.matmul(ps[0:64, 0:n], esel.t[:, :], o_.t[0:65, 0:n], start=True, stop=True), [esel, o_], deps)
                kb.op("dve", lambda e: e.reciprocal(r_.t[:, 0:n], ps[0:64, 0:n]), deps, [r_])
                kb.op("pool", lambda e: e.tensor_tensor(b_.t[:, 0:n], o_.t[0:64, 0:n], r_.t[:, 0:n], ALU.mult), [o_, r_], [b_])
                c0 = gcol(g0)
                kb.dma("pool", env.MIXT[512 + h * 64:512 + (h + 1) * 64, c0:c0 + n], b_.t[:, 0:n], reads=[b_], writes=[env.dd["MIXT"]])

            for idx in range(len(items) + LOOK):
                if idx < len(items):
                    stage_a(items[idx])
                if idx >= LOOK:
                    stage_b(items[idx - LOOK])

        setup(0)
        for h in range(8):
            if h + 1 < 8:
                setup(h + 1)
            attn(h)
        kb.barrier()
    env.pp = PsPool(kb, list(range(8)))


PHASES["mla"] = phase_mla


NSTEP = 6


def _tt(kb, e, out, in0, in1, op, reads, writes):
    return kb.op(e, lambda g: g.tensor_tensor(out, in0, in1, op), reads, writes)


def _stt(kb, e, out, in0, scalar, in1, op0, op1, reads, writes):
    return kb.op(e, lambda g: g.scalar_tensor_tensor(out, in0, scalar, in1, op0, op1), reads, writes)


def _act(kb, out, in_, func, reads, writes, **kw):
    return kb.op("act", lambda g: g.activation(out, in_, func, **kw), reads, writes)


def bc8(t):
    return t.unsqueeze(2).to_broadcast([128, 8, 64])


def v3(ap):
    return ap.rearrange("p (h d) -> p h d", d=64)


class StopPhase(Exception):
    pass


def ck(kb, env, n, dumps=()):
    if getattr(env, "stop", None) != n:
        return
    for slot, t in dumps:
        if t.t.dtype == BF16:
            n = t.t.shape[-1]
            kb.dma("sp", env.DBG[slot].bitcast(BF16)[0:t.t.shape[0], 0:n], t.t[:], reads=[t], writes=[env.dd["DBG"]])
        else:
            kb.dma("sp", env.DBG[slot][0:t.t.shape[0], 0:t.t.shape[-1]], t.t[:], reads=[t], writes=[env.dd["DBG"]])
    kb.barrier()
    kb.halt = True


def phase_rwkv(kb, env, l):
    nc = kb.nc
    need_ctx = l < DEPTH - 1
    env.pp = PsPool(kb, [0, 1, 2, 3, 4, 5])
    pp = env.pp
    ppp = PsPool(kb, [6, 7])
    idb = env.identb
    with ExitStack() as es:
        dg = kb.tile(es, [128, 15, 3, 128], BF16, "dg")
        lw = kb.tile(es, [128, 3, 512], BF16, "lw")
        with ExitStack() as es2:
            for a in range(3):
                cw = load_vec_fm(kb, env, es2, env.w["rwkv_conv"][l][a], 15, "cw%d" % a)
                for j in range(15):
                    kb.op(kb.sb_eng(), lambda e: e.tensor_scalar_mul(dg.t[:, j, a, :], env.identf.t[:], cw.t[:, j:j + 1]), [env.identf, cw], [dg])
            lst = kb.tile(es2, [128, 3, 512], F32, "lst")
            kb.dma("sp", lst.t[:, 0, :], env.w["w_b"][l].rearrange("d r c -> (d r) c"), writes=[lst])
            kb.dma("sp", lst.t[:, 1, :], env.w["a_b"][l].rearrange("d r c -> (d r) c"), writes=[lst])
            kb.dma("sp", lst.t[:, 2, :], env.w["g_b"][l], writes=[lst])
            kb.copy("dve", lw.t[:], lst.t[:], [lst], [lw])
            kb.barrier()
        brow = kb.tile(es, [1, 2, 2, 512], F32, "brow")
        kb.dma("sp", brow.t[:, 0, :, :], env.w["w0"][l].rearrange("(o d) c -> o d c", o=1), writes=[brow])
        kb.dma("sp", brow.t[:, 1, :, :], env.w["a0"][l].rearrange("(o d) c -> o d c", o=1), writes=[brow])
        kkb = load_bcast(kb, es, env.w["k_k"][l], 512, "kkb")
        kab = load_bcast(kb, es, env.w["k_a"][l], 512, "kab")
        rkb = load_bcast(kb, es, env.w["r_k"][l].rearrange("h d -> (h d)"), 512, "rkb")
        gng = load_bcast(kb, es, env.w["gn_g"][l], 512, "gng")
        gnb = load_bcast(kb, es, env.w["gn_b"][l], 512, "gnb")
        m4 = kb.tile(es, [128, 512], F32, "m4")
        mt = kb.tile(es, [128, 128], F32, "mt")
        tri = kb.tile(es, [128, 3, 128], F32, "tri")
        cvec = kb.tile(es, [128, 1], F32, "cvec")
        kb.dma("sp", cvec.t[:], env.cst["cvec"], writes=[cvec])
        hb = kb.tile(es, [64, 512], BF16, "hb")
        pw = [kb.tile(es, [128, 15, 3, 128], BF16, "pw%d" % i) for i in range(2)]
        f32names = ["r", "k", "v", "sw", "a0", "a1", "g", "t1", "sqk", "kk", "am1", "kd0", "kd1", "b", "rk",
                    "eG", "enG", "eGx", "eD", "y", "yf", "u1", "u2"]
        F = {n: kb.tile(es, [128, 512], F32, "f_" + n) for n in f32names}
        FD = [{n: (F[n] if bi_ == 0 else kb.tile(es, [128, 512], F32, "f%d_%s" % (bi_, n))) for n in ("r", "v", "g", "kd0", "kd1", "y")} for bi_ in range(3)]
        bfnames = ["KKt", "Bt", "Kt", "Rt", "Bh", "Kh", "Vb", "ob"]
        BfD = [{n: kb.tile(es, [128, 512], BF16, "b%d_%s" % (bi_, n)) for n in bfnames} for bi_ in range(2)]
        th = kb.tile(es, [128, 128], BF16, "th")
        alo = kb.tile(es, [128, 128], BF16, "alo")
        sg = kb.tile(es, [128, 128], BF16, "sg")
        s8 = {n: kb.tile(es, [128, 8], F32, "s8_" + n) for n in ("ss", "nrm", "m", "var", "bs")}
        fmaD = [kb.tile(es, [128, 4, 2, 128], BF16, "fma%d" % i) for i in range(2)]
        fmbD = [kb.tile(es, [128, 4, 2, 128], BF16, "fmb%d" % i) for i in range(2)]
        gcfD = [kb.tile(es, [64, 8], F32, "gcf%d" % i) for i in range(2)]
        mst = kb.tile(es, [128, 4, 128], BF16, "mst")
        SB1 = [kb.tile(es, [128, 512], BF16, "sb1_%d" % h) for h in range(8)]
        X0 = [kb.tile(es, [128, 128], BF16, "x0_%d" % h) for h in range(8)]
        XXp = [[kb.tile(es, [128, 512], BF16, "xxp%d_%d" % (i, p)) for i in range(3)] for p in range(4)]
        ETq = [[kb.tile(es, [128, 512], BF16, "etq%d_%d" % (i, q)) for i in range(2)] for q in range(2)]
        ZC = [kb.tile(es, [128, 128], BF16, "zc_%d" % h) for h in range(8)]
        WU = [kb.tile(es, [128, 128], BF16, "wu_%d" % h) for h in range(8)]
        QTb = [kb.tile(es, [64, 128], BF16, "qtb_%d" % h) for h in range(8)]
        MC = [kb.tile(es, [64, 64], BF16, "mc_%d" % h) for h in range(8)]

        def hs(h):
            return slice(h * 64, (h + 1) * 64)

        ck(kb, env, -1)

        for d in range(2):
            kb.dma("sp", m4.t[:], env.cst["mask4"][d], writes=[m4])
            kb.dma("sp", mt.t[:], env.cst["maskt"][d], writes=[mt])
            kb.dma("sp", tri.t[:], env.cst["tri"][d].rearrange("a s t -> s a t"), writes=[tri])
            kb.op("pool", lambda e: e.memset(hb.t[:], 0.0), [], [hb])
            order = list(range(NTILE)) if d == 0 else [1, 0] + list(range(NTILE - 1, 1, -1))
            def prep(ci, i, bi_):
                Fb, Bf, fma, fmb, gcf = FD[ci % 3], BfD[bi_], fmaD[bi_], fmbD[bi_], gcfD[bi_]
                FF = dict(F)
                FF.update(Fb)
                c0 = tcol(i)
                p_ = pw[ci % 2]
                for a in range(3):
                    kb.dma("sp", p_.t[:, :, a, :], env.PT[:, c0 - 1 + a:c0 + 127 + a].rearrange("(j p) c -> p j c", p=128), reads=[env.dd["PT"]], writes=[p_])
                ck(kb, env, 0)
                yield
                for gi, nm in enumerate(("r", "k", "v")):
                    b, _, deps = ppp.get(512)
                    ps = kb.banks[b]
                    for jj in range(4):
                        j = gi * 4 + jj
                        for a in range(3):
                            kb.op("pe", lambda e: e.matmul(ps[:, jj * 128:(jj + 1) * 128], p_.t[:, j, a, :], dg.t[:, j, a, :], start=(a == 0), stop=(a == 2)),
                                  [p_, dg], deps, inc=(jj == 3 and a == 2))
                    kb.copy("act" if gi != 1 else "dve", FF[nm].t[:], ps[:, :], deps, [FF[nm]])
                b, _, deps = ppp.get(512)
                ps = kb.banks[b]
                for jj in range(3):
                    j = 12 + jj
                    for a in range(3):
                        kb.op("pe", lambda e: e.matmul(ps[:, jj * 128:(jj + 1) * 128], dg.t[:, j, a, :], p_.t[:, j, a, :], start=(a == 0), stop=(a == 2)),
                              [p_, dg], deps, inc=(jj == 2 and a == 2))
                _act(kb, FF["t1"].t[:, 0:128], ps[:, 0:128], AF.Sigmoid, deps, [FF["t1"]], scale=2.0)
                kb.op("dve", lambda e: e.tensor_scalar(th.t[:], FF["t1"].t[:, 0:128], 2.0, -1.0, ALU.mult, ALU.add), [FF["t1"]], [th])
                kb.copy("dve", alo.t[:], ps[:, 128:256], deps, [alo])
                _act(kb, sg.t[:], ps[:, 256:384], AF.Sigmoid, deps, [sg])
                ck(kb, env, 1, [(0, FF["r"]), (1, FF["k"]), (2, FF["v"])])
                yield
                P0 = 64 * d
                b, _, deps = ppp.get(512)
                ps = kb.banks[b]
                kb.op("pe", lambda e: e.matmul(ps[:, :], th.t[P0:P0 + 64, :], lw.t[P0:P0 + 64, 0, :], start=True, stop=False), [th, lw], deps, inc=False)
                kb.op("pe", lambda e: e.matmul(ps[:, :], env.onesf.t[0:1, :], brow.t[0:1, 0, d, :], start=False, stop=True), [env.onesf, brow], deps)
                _act(kb, FF["sw"].t[:], ps[:, :], AF.Sigmoid, deps, [FF["sw"]])
                for dd in range(2):
                    b, _, deps = ppp.get(512)
                    ps = kb.banks[b]
                    kb.op("pe", lambda e: e.matmul(ps[:, :], alo.t[64 * dd:64 * dd + 64, :], lw.t[64 * dd:64 * dd + 64, 1, :], start=True, stop=False), [alo, lw], deps, inc=False)
                    kb.op("pe", lambda e: e.matmul(ps[:, :], env.onesf.t[0:1, :], brow.t[0:1, 1, dd, :], start=False, stop=True), [env.onesf, brow], deps)
                    _act(kb, FF["a%d" % dd].t[:], ps[:, :], AF.Sigmoid, deps, [FF["a%d" % dd]])
                b, _, deps = ppp.get(512)
                ps = kb.banks[b]
                kb.op("pe", lambda e: e.matmul(ps[:, :], sg.t[:], lw.t[:, 2, :], start=True, stop=True), [sg, lw], deps)
                kb.copy("dve", FF["g"].t[:], ps[:, :], deps, [FF["g"]])
                ck(kb, env, 2, [(0, FF["sw"]), (1, FF["a0"]), (2, FF["a1"]), (3, FF["g"])])
                yield
                _tt(kb, "pool", FF["t1"].t[:], FF["k"].t[:], kkb.t[:], ALU.mult, [FF["k"], kkb], [FF["t1"]])
                _tt(kb, "pool", FF["sqk"].t[:], FF["t1"].t[:], FF["t1"].t[:], ALU.mult, [FF["t1"]], [FF["sqk"]])
                kb.op("dve", lambda e: e.reduce_sum(s8["ss"].t[:], v3(FF["sqk"].t[:]), AX.X), [FF["sqk"]], [s8["ss"]])
                kb.op("dve", lambda e: e.tensor_scalar_max(s8["ss"].t[:], s8["ss"].t[:], 1e-24), [s8["ss"]], [s8["ss"]])
                _act(kb, s8["nrm"].t[:], s8["ss"].t[:], AF.Ln, [s8["ss"]], [s8["nrm"]])
                _act(kb, s8["nrm"].t[:], s8["nrm"].t[:], AF.Exp, [s8["nrm"]], [s8["nrm"]], scale=-0.5)
                _tt(kb, "dve", v3(FF["kk"].t[:]), v3(FF["t1"].t[:]), bc8(s8["nrm"].t[:]), ALU.mult, [FF["t1"], s8["nrm"]], [FF["kk"]])
                for dd in range(2):
                    a_ = FF["a%d" % dd]
                    kd = FF["kd%d" % dd]
                    _stt(kb, "dve", FF["am1"].t[:], a_.t[:], -1.0, kab.t[:], ALU.add, ALU.mult, [a_, kab], [FF["am1"]])
                    _stt(kb, "dve", kd.t[:], FF["am1"].t[:], 1.0, FF["k"].t[:], ALU.add, ALU.mult, [FF["am1"], FF["k"]], [kd])
                a_d = FF["a%d" % d]
                kd_d = FF["kd%d" % d]
                _tt(kb, "dve", FF["b"].t[:], FF["kk"].t[:], a_d.t[:], ALU.mult, [FF["kk"], a_d], [FF["b"]])
                ck(kb, env, 3, [(0, FF["kk"]), (1, FF["kd0"]), (2, FF["kd1"]), (3, FF["b"])])
                yield
                exps = []
                for ti_, (nm, sc) in enumerate((("eG", 1.0), ("eGx", 1.0), ("eD", 1.0))):
                    b, _, deps = ppp.get(512)
                    ps = kb.banks[b]
                    kb.op("pe", lambda e: e.matmul(ps[:, :], tri.t[:, ti_, :], FF["sw"].t[:], start=True, stop=True), [tri, FF["sw"]], deps)
                    _act(kb, FF[nm].t[:], ps[:, :], AF.Exp, deps, [FF[nm]])
                    if ti_ == 0:
                        _act(kb, FF["enG"].t[:], ps[:, :], AF.Exp, deps, [FF["enG"]], scale=-1.0)
                b, c8, deps = ppp.get(8)
                ps = kb.banks[b]
                for h in range(8):
                    kb.op("pe", lambda e: e.matmul(ps[0:64, c8 + h:c8 + h + 1], FF["sw"].t[:, hs(h)], cvec.t[:, 0:1], start=True, stop=True),
                          [FF["sw"], cvec], deps, inc=(h == 7))
                _act(kb, gcf.t[:], ps[0:64, c8:c8 + 8], AF.Exp, deps, [gcf])
                ck(kb, env, 4, [(0, FF["eG"]), (1, FF["enG"]), (2, FF["eGx"]), (3, FF["eD"])])
                yield
                for nm, x_, e_ in (("KKt", "kk", "eGx"), ("Bt", "b", "enG"), ("Kt", "kd%d" % d, "enG"), ("Rt", "r", "eG"),
                                   ("Bh", "b", "eD"), ("Kh", "kd%d" % d, "eD")):
                    _tt(kb, kb.sb_eng(), Bf[nm].t[:], FF[x_].t[:], FF[e_].t[:], ALU.mult, [FF[x_], FF[e_]], [Bf[nm]])
                kb.copy("pool", Bf["Vb"].t[:], FF["v"].t[:], [FF["v"]], [Bf["Vb"]])
                for nm, dst, slot in (("KKt", fma, 0), ("Rt", fma, 1), ("Bt", fmb, 0), ("Kt", fmb, 1)):
                    b, cc, deps = ppp.get(256)
                    psb = kb.banks_bf[b]
                    for jp in range(4):
                        kb.op("pe", lambda e: e.transpose(psb[:, 2 * cc + jp * 128:2 * cc + (jp + 1) * 128], Bf[nm].t[:, jp * 128:(jp + 1) * 128], idb.t[:]),
                              [Bf[nm], idb], deps, inc=(jp == 3))
                    kb.copy(kb.ev_eng(), dst.t[:, :, slot, :], psb[:, 2 * cc:2 * cc + 512].rearrange("p (j t) -> p j t", t=128), deps, [dst])
                yield

            def heads(ci, i, bi_):
                Fb, Bf, fma, fmb, gcf = FD[ci % 3], BfD[bi_], fmaD[bi_], fmbD[bi_], gcfD[bi_]
                FF = dict(F)
                FF.update(Fb)
                c0 = tcol(i)
                ck(kb, env, 5)
                yield
                R = {}
                for hg in range(2):
                    for h in range(4 * hg, 4 * hg + 4):
                        P = 64 * (h % 2)
                        jp = h // 2
                        b, _, deps = pp.get(512)
                        ps = kb.banks[b]
                        rhs = fma.t[P:P + 64, jp, :, :].rearrange("p a t -> p (a t)")
                        kb.op("pe", lambda e: e.matmul(ps[:, 0:256], fmb.t[P:P + 64, jp, 0, :], rhs, start=True, stop=True), [fma, fmb], deps, inc=False)
                        kb.op("pe", lambda e: e.matmul(ps[:, 256:512], fmb.t[P:P + 64, jp, 1, :], rhs, start=True, stop=True), [fma, fmb], deps)
                        R[h] = (ps, deps)
                    for h in range(4 * hg, 4 * hg + 4):
                        ps, deps = R[h]
                        _tt(kb, "dve", SB1[h].t[:], ps[:, :], m4.t[:], ALU.mult, deps + [m4], [SB1[h]])
                    yield
                for h in range(8):
                    P = 64 * (h % 2)
                    jp = h // 2
                    b2, c2, deps2 = pp.get(256)
                    ps2 = kb.banks[b2]
                    kb.op("pe", lambda e: e.matmul(ps2[:, c2:c2 + 128], fma.t[P:P + 64, jp, 0, :], fmb.t[P:P + 64, jp, 0, :], start=True, stop=True), [fma, fmb], deps2, inc=False)
                    kb.op("pe", lambda e: e.matmul(ps2[:, c2 + 128:c2 + 192], SB1[h].t[:, 256:384], Bf["Vb"].t[:, hs(h)], start=True, stop=True), [SB1[h], Bf["Vb"]], deps2)
                    R[h] = (ps2, c2, deps2)
                for h in range(8):
                    ps2, c2, deps2 = R[h]
                    _tt(kb, "dve", X0[h].t[:], ps2[:, c2:c2 + 128], mt.t[:], ALU.mult, deps2 + [mt], [X0[h]])
                    kb.copy("act", ZC[h].t[:, 64:128], ps2[:, c2 + 128:c2 + 192], deps2, [ZC[h]])
                    kb.copy("pool", ZC[h].t[:, 0:64], Bf["KKt"].t[:, hs(h)], [Bf["KKt"]], [ZC[h]])
                ck(kb, env, 7, [(0, SB1[0]), (1, X0[0]), (2, ZC[0]), (3, SB1[3]), (4, ZC[3])])
                yield
                def xx_ap(h, st):
                    t = XXp[h // 2][st % 3]
                    o = (h % 2) * 256
                    return t, t.t[:, o:o + 128], t.t[:, o + 128:o + 256]

                def et_ap(h, st):
                    t = ETq[h // 4][st % 2]
                    o = (h % 4) * 128
                    return t, t.t[:, o:o + 128]

                def sq_stage(st):
                    for h in range(8):
                        if st == 1:
                            xp, xtp, xd = X0[h].t[:, :], SB1[h].t[:, 0:128], [X0[h], SB1[h]]
                        else:
                            xt_, xp, xtp = xx_ap(h, st - 1)
                            xd = [xt_]
                        b, cc, deps = h // 2, (h % 2) * 256, list(kb.bdep[h // 2])
                        ps = kb.banks[b]
                        kb.op("pe", lambda e: e.matmul(ps[:, cc:cc + 128], xtp, xp, start=True, stop=True), xd, deps, inc=False)
                        kb.op("pe", lambda e: e.matmul(ps[:, cc + 128:cc + 256], xp, xtp, start=True, stop=True), xd, deps)
                    for p in range(4):
                        dst = XXp[p][st % 3]
                        kb.copy("dve" if p == 3 else "act", dst.t[:, :], kb.banks[p][:, :], list(kb.bdep[p]), [dst])

                def chain_stage(st):
                    for h in range(8):
                        xt_, xp, xtp = xx_ap(h, st)
                        if st == 1:
                            etp, ed = SB1[h].t[:, 0:128], [SB1[h]]
                        else:
                            et_, etp = et_ap(h, st - 1)
                            ed = [et_]
                        b, cc, deps = 4 + h // 4, (h % 4) * 128, list(kb.bdep[4 + h // 4])
                        ps = kb.banks[b]
                        kb.op("pe", lambda e: e.matmul(ps[:, cc:cc + 128], idb.t[:], xtp, start=True, stop=False), [xt_, idb], deps, inc=False)
                        kb.op("pe", lambda e: e.matmul(ps[:, cc:cc + 128], xp, etp, start=False, stop=True), ed + [xt_], deps)
                    if st == 1:
                        for h in range(8):
                            b, cc, deps = 4 + h // 4, (h % 4) * 128, list(kb.bdep[4 + h // 4])
                            et_, eo = et_ap(h, st)
                            _tt(kb, "dve", eo, kb.banks[b][:, cc:cc + 128], SB1[h].t[:, 0:128], ALU.add, deps + [SB1[h]], [et_])
                    else:
                        for q in range(2):
                            _tt(kb, "dve", ETq[q][st % 2].t[:, :], kb.banks[4 + q][:, :], ETq[q][(st - 1) % 2].t[:, :], ALU.add,
                                list(kb.bdep[4 + q]) + [ETq[q][(st - 1) % 2]], [ETq[q][st % 2]])

                sq_stage(1)
                yield
                for st in range(1, NSTEP + 1):
                    if st + 1 <= NSTEP:
                        sq_stage(st + 1)
                        yield
                    chain_stage(st)
                    yield
                ck(kb, env, 8)
                yield
                for h in range(8):
                    et, eta = et_ap(h, NSTEP)
                    b, cc, deps = pp.get(128)
                    ps = kb.banks[b]
                    kb.op("pe", lambda e: e.matmul(ps[:, cc:cc + 128], eta, ZC[h].t[:, :], start=True, stop=True), [et, ZC[h]], deps)
                    R[h] = (ps, cc, deps)
                for h in range(8):
                    ps, cc, deps = R[h]
                    _stt(kb, "dve", WU[h].t[:], ps[:, cc:cc + 128], -1.0, ZC[h].t[:], ALU.mult, ALU.subtract, deps + [ZC[h]], [WU[h]])
                ck(kb, env, 9, [(0, WU[0]), (1, WU[3])])
                yield
                for h in range(8):
                    b, cc, deps = pp.get(256)
                    ps = kb.banks[b]
                    kb.op("pe", lambda e: e.matmul(ps[0:64, cc:cc + 128], Bf["Rt"].t[:, hs(h)], idb.t[:], start=True, stop=False), [Bf["Rt"], idb], deps, inc=False)
                    kb.op("pe", lambda e: e.matmul(ps[0:64, cc:cc + 128], WU[h].t[:, 0:64], SB1[h].t[:, 128:256], start=False, stop=True), [WU[h], SB1[h]], deps, inc=False)
                    kb.op("pe", lambda e: e.matmul(ps[0:64, cc + 128:cc + 192], WU[h].t[:, 0:64], Bf["Bh"].t[:, hs(h)], start=True, stop=True), [WU[h], Bf["Bh"]], deps)
                    R[h] = (ps, cc, deps)
                for h in range(8):
                    ps, cc, deps = R[h]
                    kb.copy("act", QTb[h].t[:], ps[0:64, cc:cc + 128], deps, [QTb[h]])
                    _stt(kb, "dve", MC[h].t[:], env.identf.t[0:64, 0:64], gcf.t[0:64, h:h + 1], ps[0:64, cc + 128:cc + 192], ALU.mult, ALU.add,
                         deps + [env.identf, gcf], [MC[h]])
                ck(kb, env, 10, [(0, QTb[0]), (1, MC[0]), (2, QTb[3]), (3, MC[3])])
                yield
                yb_, _, ydep = pp.get(512)
                hb_, _, hdep = pp.get(512)
                ybank, hbank = kb.banks[yb_], kb.banks[hb_]
                for h in range(8):
                    yo = ybank[:, hs(h)]
                    kb.op("pe", lambda e: e.matmul(yo, SB1[h].t[:, 128:256], WU[h].t[:, 64:128], start=True, stop=False), [SB1[h], WU[h]], ydep, inc=False)
                    kb.op("pe", lambda e: e.matmul(yo, SB1[h].t[:, 384:512], Bf["Vb"].t[:, hs(h)], start=False, stop=False), [SB1[h], Bf["Vb"]], ydep, inc=False)
                    kb.op("pe", lambda e: e.matmul(yo, QTb[h].t[:, :], hb.t[0:64, hs(h)], start=False, stop=True), [QTb[h], hb], ydep, inc=False)
                    ho = hbank[0:64, hs(h)]
                    kb.op("pe", lambda e: e.matmul(ho, Bf["Bh"].t[:, hs(h)], WU[h].t[:, 64:128], start=True, stop=False), [Bf["Bh"], WU[h]], hdep, inc=False)
                    kb.op("pe", lambda e: e.matmul(ho, Bf["Kh"].t[:, hs(h)], Bf["Vb"].t[:, hs(h)], start=False, stop=False), [Bf["Kh"], Bf["Vb"]], hdep, inc=False)
                    kb.op("pe", lambda e: e.matmul(ho, MC[h].t[:, :], hb.t[0:64, hs(h)], start=False, stop=True), [MC[h], hb], hdep, inc=(h == 7))
                kb.copy("act", hb.t[:, :], hbank[0:64, :], hdep, [hb])
                kb.copy("dve", FF["y"].t[:], ybank[:, :], ydep, [FF["y"]])
                ck(kb, env, 11, [(0, FF["y"])])
                yield
                if getattr(env, "stop", None) == 100 + ci:
                    ck(kb, env, 100 + ci, [(0, FF["y"])])
                    yield
                if d == 0:
                    kb.dma("pool", env.YF[i * 128:(i + 1) * 128, :], FF["y"].t[:], reads=[FF["y"]], writes=[env.dd["YF"]])
                return

            def outp(ci, i):
                if d == 0 or (i < 2 and not need_ctx):
                    return
                FF = dict(F)
                FF.update(FD[ci % 3])
                Bf = BfD[0]
                c0 = tcol(i)
                yield
                kb.dma("sp", FF["yf"].t[:], env.YF[i * 128:(i + 1) * 128, :], reads=[env.dd["YF"]], writes=[FF["yf"]])
                y = FF["y"]
                _tt(kb, "pool", y.t[:], y.t[:], FF["yf"].t[:], ALU.add, [y, FF["yf"]], [y])
                yield
                kb.op("dve", lambda e: e.reduce_sum(s8["m"].t[:], v3(y.t[:]), AX.X), [y], [s8["m"]])
                kb.op("dve", lambda e: e.tensor_scalar_mul(s8["m"].t[:], s8["m"].t[:], -1.0 / 64), [s8["m"]], [s8["m"]])
                _tt(kb, "dve", v3(y.t[:]), v3(y.t[:]), bc8(s8["m"].t[:]), ALU.add, [y, s8["m"]], [y])
                yield
                _tt(kb, "pool", FF["u1"].t[:], y.t[:], y.t[:], ALU.mult, [y], [FF["u1"]])
                yield
                kb.op("dve", lambda e: e.reduce_sum(s8["var"].t[:], v3(FF["u1"].t[:]), AX.X), [FF["u1"]], [s8["var"]])
                _act(kb, s8["var"].t[:], s8["var"].t[:], AF.Ln, [s8["var"], env.epsg], [s8["var"]], bias=env.epsg.t[:, 0:1], scale=1.0 / 64)
                _act(kb, s8["var"].t[:], s8["var"].t[:], AF.Exp, [s8["var"]], [s8["var"]], scale=-0.5)
                _tt(kb, "dve", v3(y.t[:]), v3(y.t[:]), bc8(s8["var"].t[:]), ALU.mult, [y, s8["var"]], [y])
                yield
                _tt(kb, "pool", y.t[:], y.t[:], gng.t[:], ALU.mult, [y, gng], [y])
                yield
                _tt(kb, "pool", y.t[:], y.t[:], gnb.t[:], ALU.add, [y, gnb], [y])
                yield
                _tt(kb, "pool", FF["rk"].t[:], FF["r"].t[:], rkb.t[:], ALU.mult, [FF["r"], rkb], [FF["rk"]])
                yield
                _tt(kb, "pool", FF["u1"].t[:], FF["kd0"].t[:], FF["kd1"].t[:], ALU.add, [FF["kd0"], FF["kd1"]], [FF["u1"]])
                yield
                _tt(kb, "pool", FF["u1"].t[:], FF["u1"].t[:], FF["rk"].t[:], ALU.mult, [FF["u1"], FF["rk"]], [FF["u1"]])
                yield
                kb.op("dve", lambda e: e.reduce_sum(s8["bs"].t[:], v3(FF["u1"].t[:]), AX.X), [FF["u1"]], [s8["bs"]])
                _tt(kb, "dve", v3(FF["u2"].t[:]), v3(FF["v"].t[:]), bc8(s8["bs"].t[:]), ALU.mult, [FF["v"], s8["bs"]], [FF["u2"]])
                yield
                _tt(kb, "pool", y.t[:], y.t[:], FF["u2"].t[:], ALU.add, [y, FF["u2"]], [y])
                yield
                _tt(kb, "pool", Bf["ob"].t[:], y.t[:], FF["g"].t[:], ALU.mult, [y, FF["g"]], [Bf["ob"]])
                yield
                b, cc, deps = ppp.get(256)
                psb = kb.banks_bf[b]
                for jp in range(4):
                    kb.op("pe", lambda e: e.transpose(psb[:, 2 * cc + jp * 128:2 * cc + (jp + 1) * 128], Bf["ob"].t[:, jp * 128:(jp + 1) * 128], idb.t[:]),
                          [Bf["ob"], idb], deps, inc=(jp == 3))
                kb.copy("act", mst.t[:], psb[:, 2 * cc:2 * cc + 512].rearrange("p (j t) -> p j t", t=128), deps, [mst])
                kb.dma("pool", env.MIXT[0:512, c0:c0 + 128].rearrange("(k p) c -> p k c", p=128), mst.t[:], reads=[mst], writes=[env.dd["MIXT"]])
                yield

            def drain(g):
                for _ in g:
                    pass

            drain(prep(0, order[0], 0))
            for ci in range(len(order) + 1):
                alive = []
                if ci < len(order):
                    alive.append(heads(ci, order[ci], ci % 2))
                if ci + 1 < len(order):
                    alive.append(prep(ci + 1, order[ci + 1], (ci + 1) % 2))
                if ci >= 1:
                    alive.append(outp(ci - 1, order[ci - 1]))
                while alive:
                    for g in list(alive):
                        try:
                            next(g)
                        except StopIteration:
                            alive.remove(g)
            kb.barrier()
    env.pp = PsPool(kb, list(range(8)))


PHASES["rwkv"] = phase_rwkv


def residual_ln(kb, env, es_tiles, xt, ps_list, gate, lng, lnb, out):
    st = es_tiles
    for half, (ps, deps) in enumerate(ps_list):
        sl = slice(half * 512, (half + 1) * 512)
        _tt(kb, "dve", out.t[:, sl], ps, gate.t[:, sl], ALU.mult, deps + [gate], [out])
        yield
    _stt(kb, "dve", out.t[:], xt.t[:], ALPHA, out.t[:], ALU.mult, ALU.add, [xt, out], [out])
    yield
    _act(kb, st["junk"].t[:], out.t[:], AF.Identity, [out], [st["junk"], st["s1"]], accum_out=st["s1"].t[:, 0:1])
    yield
    kb.op("dve", lambda e: e.tensor_scalar_mul(st["s1"].t[:, 0:1], st["s1"].t[:, 0:1], -1.0 / D), [st["s1"]], [st["s1"]])
    kb.op("dve", lambda e: e.tensor_scalar_add(out.t[:], out.t[:], st["s1"].t[:, 0:1]), [out, st["s1"]], [out])
    yield
    _act(kb, st["junk"].t[:], out.t[:], AF.Square, [out], [st["junk"], st["s2"]], accum_out=st["s2"].t[:, 0:1])
    yield
    _act(kb, st["s2"].t[:, 0:1], st["s2"].t[:, 0:1], AF.Sqrt, [st["s2"], env.epsl], [st["s2"]], bias=env.epsl.t[:, 0:1], scale=1.0 / D)
    kb.op("dve", lambda e: e.reciprocal(st["s2"].t[:, 0:1], st["s2"].t[:, 0:1]), [st["s2"]], [st["s2"]])
    yield
    _stt(kb, "dve", out.t[:], out.t[:], st["s2"].t[:, 0:1], lng.t[:], ALU.mult, ALU.mult, [out, st["s2"], lng], [out])
    yield
    _tt(kb, "pool", out.t[:], out.t[:], lnb.t[:], ALU.add, [out, lnb], [out])
    yield


def run_interleaved(gens_iter, depth=2):
    active = []

    def pump():
        for g in list(active):
            try:
                next(g)
            except StopIteration:
                active.remove(g)

    for g in gens_iter:
        active.append(g)
        while len(active) >= depth:
            pump()
    while active:
        pump()


def load_gate(kb, es, env, l, s, gi, name):
    t = kb.tile(es, [128, 1024], F32, name)
    kb.dma("sp", t.t[:], env.MODROW[l, s, gi].partition_broadcast(128), reads=[env.dd["MODROW"]], writes=[t])
    return t


def phase_wo(kb, env, l):
    need_ctx = l < DEPTH - 1
    env.pp = PsPool(kb, [6, 7])
    pp = PsPool(kb, [0, 1, 2, 3, 4, 5])
    with ExitStack() as es:
        wo = kb.tile(es, [128, 8, 1024], BF16, "wo")
        with ExitStack() as es2:
            stage = [kb.tile(es2, [128, 1024], F32, "wost%d" % i) for i in range(2)]
            load_w_bf16(kb, es2, wo, env.w["w_o"][l], 8, 1024, stage)
            kb.barrier()
        gates = [load_gate(kb, es, env, l, s, 0, "g1_%d" % s) for s in range(2)]
        lng = load_bcast(kb, es, env.w["ln1_g"][l], 1024, "ln1g")
        lnb = load_bcast(kb, es, env.w["ln1_b"][l], 1024, "ln1b")
        sts = [{"junk": kb.tile(es, [128, 1024], F32, "junk"), "s1": kb.tile(es, [128, 1], F32, "s1"), "s2": kb.tile(es, [128, 1], F32, "s2")} for _ in range(2)]
        env.ut_st = [kb.tile(es, [128, 8, 130], BF16, "utst%d" % i) for i in range(2)]
        env.ut_i = 0
        for t in env.ut_st:
            kb.op("pool", lambda e: e.memset(t.t[:], 0.0), [], [t])
        xts = [kb.tile(es, [128, 1024], F32, "xt%d" % i) for i in range(3)]
        outs = [kb.tile(es, [128, 1024], F32, "xo%d" % i) for i in range(3)]
        mts = [kb.tile(es, [128, 8, 128], BF16, "mt%d" % i) for i in range(3)]
        tiles = list(range(NTILE) if need_ctx else range(2, NTILE))

        def mm_stage(ii, i):
            c0 = tcol(i)
            xt, mt = xts[ii % 3], mts[ii % 3]
            kb.dma("sp", mt.t[:], env.MIXT[:, c0:c0 + 128].rearrange("(k p) c -> p k c", p=128), reads=[env.dd["MIXT"]], writes=[mt])
            kb.dma("sp", xt.t[:], xin_ap(env, l, i), reads=[env.dd["X2"]], writes=[xt])
            pl = []
            for half in range(2):
                b, _, deps = pp.get(512)
                ps = kb.banks[b]
                for k in range(8):
                    kb.op("pe", lambda e: e.matmul(ps[:, :], mt.t[:, k, :], wo.t[:, k, half * 512:(half + 1) * 512], start=(k == 0), stop=(k == 7)),
                          [mt, wo], deps, inc=(k == 7))
                pl.append((ps[:, :], deps))
            return pl

        def epi_stage(ii, i, pl):
            s = 1 if i < 2 else 0
            xt, out = xts[ii % 3], outs[ii % 3]
            yield from residual_ln(kb, env, sts[ii % 2], xt, pl, gates[s], lng, lnb, out)
            kb.dma("pool", env.X1[i * 128:(i + 1) * 128, :], out.t[:], reads=[out], writes=[env.dd["X1"]])
            yield from emit_ut_g(kb, env, out, s, env.modf[l], 4, 3, env.U2T, i, pad=True)

        def gens():
            pend = mm_stage(0, tiles[0])
            for ii, i in enumerate(tiles):
                nxt = mm_stage(ii + 1, tiles[ii + 1]) if ii + 1 < len(tiles) else None
                yield epi_stage(ii, i, pend)
                pend = nxt

        run_interleaved(gens())
        kb.barrier()


def phase_ffu(kb, env, l):
    need_ctx = l < DEPTH - 1
    env.pp = PsPool(kb, list(range(8)))
    pp = env.pp
    NF = 2 * DFF // 128
    with ExitStack() as es:
        wup = kb.tile(es, [128, 8, 2 * DFF], BF16, "wup")
        fcw = []
        with ExitStack() as es2:
            stage = [kb.tile(es2, [128, 2816], F32, "wust%d" % i) for i in range(2)]
            for hh in range(2):
                load_w_bf16(kb, es2, wup, env.w["w_up"][l][:, hh * 2816:(hh + 1) * 2816], 8, 2816, stage, col_off=hh * 2816)
            kb.barrier()
        for a in range(3):
            fcw.append(load_vec_fm(kb, env, es, env.w["ffn_conv_w"][l][a], NF, "fcw%d" % a))
        fcb = load_vec_fm(kb, env, es, env.w["ffn_conv_b"][l], NF, "fcb")
        u2b = [kb.tile(es, [128, 8, 514], BF16, "u2b%d" % i) for i in range(2)]
        gst = [kb.tile(es, [128, 22, 512], BF16, "gst%d" % i) for i in range(2)]
        hs_ = [kb.tile(es, [128, 514], F32, "hs%d" % i) for i in range(4)]
        tt_ = [kb.tile(es, [128, 512], F32, "tt%d" % i) for i in range(4)]
        sgt = [kb.tile(es, [128, 512], F32, "sgt%d" % i) for i in range(2)]
        blocks = ([(0, 256)] if need_ctx else []) + [(256 + 510 * j, min(510, NT - 256 - 510 * j)) for j in range(9)]
        hi = 0
        for bi, (g0, n) in enumerate(blocks):
            c0 = gcol(g0)
            ub = u2b[bi % 2]
            gs = gst[bi % 2]
            N = n + 2
            kb.dma("sp", ub.t[:, :, 0:N], env.U2T[:, c0 - 1:c0 + n + 1].rearrange("(k p) c -> p k c", p=128), reads=[env.dd["U2T"]], writes=[ub])
            for j in range(22):
                tv = []
                for which in range(2):
                    jf = j + 22 * which
                    hs = hs_[hi % 4]
                    tt = tt_[hi % 4]
                    hi += 1
                    b, _, deps = pp.get(512)
                    ps = kb.banks[b]
                    for k in range(8):
                        kb.op("pe", lambda e: e.matmul(ps[:, 0:N], wup.t[:, k, jf * 128:(jf + 1) * 128], ub.t[:, k, 0:N], start=(k == 0), stop=(k == 7)),
                              [wup, ub], deps, inc=(k == 7))
                    kb.copy("act", hs.t[:, 0:N], ps[:, 0:N], deps, [hs])
                    _act(kb, tt.t[:, 0:n], ps[:, 1:n + 1], AF.Identity, deps + [fcw[1], fcb], [tt], bias=fcb.t[:, jf:jf + 1], scale=fcw[1].t[:, jf:jf + 1])
                    _stt(kb, "dve", tt.t[:, 0:n], hs.t[:, 0:n], fcw[0].t[:, jf:jf + 1], tt.t[:, 0:n], ALU.mult, ALU.add, [hs, tt, fcw[0]], [tt])
                    _stt(kb, "dve", tt.t[:, 0:n], hs.t[:, 2:n + 2], fcw[2].t[:, jf:jf + 1], tt.t[:, 0:n], ALU.mult, ALU.add, [hs, tt, fcw[2]], [tt])
                    tv.append(tt)
                sg = sgt[j % 2]
                _act(kb, sg.t[:, 0:n], tv[0].t[:, 0:n], AF.Silu, [tv[0]], [sg])
                _tt(kb, "pool", gs.t[:, j, 0:n], sg.t[:, 0:n], tv[1].t[:, 0:n], ALU.mult, [sg, tv[1]], [gs])
            kb.dma("pool", env.GT[:, c0:c0 + n].rearrange("(k p) c -> p k c", p=128), gs.t[:, :, 0:n], reads=[gs], writes=[env.dd["GT"]])
        kb.barrier()


def phase_ffd(kb, env, l):
    need_ctx = l < DEPTH - 1
    last = l == DEPTH - 1
    env.pp = PsPool(kb, [6, 7])
    pp = PsPool(kb, [0, 1, 2, 3, 4, 5])
    with ExitStack() as es:
        wdn = kb.tile(es, [128, 22, 1024], BF16, "wdn")
        with ExitStack() as es2:
            stage = [kb.tile(es2, [128, 1024], F32, "wdst%d" % i) for i in range(3)]
            load_w_bf16(kb, es2, wdn, env.w["w_down"][l], 22, 1024, stage)
            kb.barrier()
        gates = [load_gate(kb, es, env, l, s, 1, "g2_%d" % s) for s in range(2)]
        lng = load_bcast(kb, es, env.w["ln2_g"][l], 1024, "ln2g")
        lnb = load_bcast(kb, es, env.w["ln2_b"][l], 1024, "ln2b")
        sts = [{"junk": kb.tile(es, [128, 1024], F32, "junk"), "s1": kb.tile(es, [128, 1], F32, "s1"), "s2": kb.tile(es, [128, 1], F32, "s2")} for _ in range(2)]
        env.ut_st = [kb.tile(es, [128, 8, 130], BF16, "utst%d" % i) for i in range(2)]
        env.ut_i = 0
        xts = [kb.tile(es, [128, 1024], F32, "xt%d" % i) for i in range(3)]
        outs = [kb.tile(es, [128, 1024], F32, "xo%d" % i) for i in range(3)]
        gts = [kb.tile(es, [128, 22, 128], BF16, "gt%d" % i) for i in range(3)]
        tiles = list(range(NTILE) if need_ctx else range(2, NTILE))

        def mm_stage(ii, i):
            c0 = tcol(i)
            xt, gt = xts[ii % 3], gts[ii % 3]
            kb.dma("sp", gt.t[:], env.GT[:, c0:c0 + 128].rearrange("(k p) c -> p k c", p=128), reads=[env.dd["GT"]], writes=[gt])
            kb.dma("sp", xt.t[:], env.X1[i * 128:(i + 1) * 128, :], reads=[env.dd["X1"]], writes=[xt])
            pl = []
            for half in range(2):
                b, _, deps = pp.get(512)
                ps = kb.banks[b]
                for k in range(22):
                    kb.op("pe", lambda e: e.matmul(ps[:, :], gt.t[:, k, :], wdn.t[:, k, half * 512:(half + 1) * 512], start=(k == 0), stop=(k == 21)),
                          [gt, wdn], deps, inc=(k == 21))
                pl.append((ps[:, :], deps))
            return pl

        def epi_stage(ii, i, pl):
            s = 1 if i < 2 else 0
            xt, out = xts[ii % 3], outs[ii % 3]
            yield from residual_ln(kb, env, sts[ii % 2], xt, pl, gates[s], lng, lnb, out)
            if last:
                kb.dma("pool", env.y[(i - 2) * 128:(i - 1) * 128, :], out.t[:], reads=[out], writes=[env.dd["y"]])
            else:
                kb.dma("pool", env.X2[i * 128:(i + 1) * 128, :], out.t[:], reads=[out], writes=[env.dd["X2"]])
                yield from emit_ut_g(kb, env, out, s, env.modf[l + 1], 1, 0, env.UT, i, pad=False)

        def gens():
            pend = mm_stage(0, tiles[0])
            for ii, i in enumerate(tiles):
                nxt = mm_stage(ii + 1, tiles[ii + 1]) if ii + 1 < len(tiles) else None
                yield epi_stage(ii, i, pend)
                pend = nxt

        run_interleaved(gens(), depth=1)
        kb.barrier()


PHASES["wo"] = phase_wo
PHASES["ffu"] = phase_ffu
PHASES["ffd"] = phase_ffd


_CACHE = {}


def kernel(**inputs):
    if "nc" not in _CACHE:
        _CACHE["nc"] = build()[0]
        _CACHE["consts"] = make_consts()
    nc = _CACHE["nc"]
    consts = _CACHE["consts"]
    B = inputs["x"].shape[0]
    shared = {k: np.ascontiguousarray(np.asarray(inputs[k], dtype=np.float32)) for k in W_SHAPES}
    shared["c_ctx"] = np.ascontiguousarray(np.asarray(inputs["c_ctx"], dtype=np.float32))
    for k, v in consts.items():
        shared["k_" + k] = v
    in_maps = []
    for b in range(B):
        m = dict(shared)
        m["x"] = np.ascontiguousarray(np.asarray(inputs["x"][b], dtype=np.float32))
        m["c"] = np.ascontiguousarray(np.asarray(inputs["c"][b], dtype=np.float32))
        m["ctx"] = np.ascontiguousarray(np.asarray(inputs["ctx"][b], dtype=np.float32))
        in_maps.append(m)
    res = run_bass_kernel_spmd(nc, in_maps, core_ids=list(range(B)))
    return np.stack([np.asarray(r["y"], dtype=np.float32) for r in res.results], axis=0)
```

```python
import numpy as np
import ml_dtypes
from contextlib import ExitStack
import concourse.bass as bass
import concourse.mybir as mybir
from concourse.bass_utils import run_bass_kernel_spmd

F32 = mybir.dt.float32
BF16 = mybir.dt.bfloat16
AF = mybir.ActivationFunctionType
ALU = mybir.AluOpType
AX = mybir.AxisListType

D = 1024
SEQ = 4096
CTXL = 256
NT = SEQ + CTXL
NCOL = NT + 3
NTILE = NT // 128
DEPTH = 2
DFF = 2816
RW = 1920
INC = 2592
ALPHA = (2 * DEPTH) ** 0.25
LN_EPS = 1e-5
RMS_EPS = 1e-6
GN_EPS = 64e-5
CW = float(np.exp(-0.5))
QSCALE = 96 ** -0.5
BLOCKS = [(0, 256)] + [(256 + 512 * j, 512) for j in range(8)]


def tcol(i):
    return 128 * i + (1 if i < 2 else 2)


def gcol(g):
    return g + (1 if g < 256 else 2)


class Dep:
    __slots__ = ("w", "r", "x")

    def __init__(self, x=False):
        self.w = None
        self.r = {}
        self.x = x


class T:
    __slots__ = ("t", "d")

    def __init__(self, t, d=None):
        self.t = t
        self.d = d if d is not None else Dep()


class KB:
    NDMA = 8

    def __init__(self, nc, es):
        self.nc = nc
        self.engs = {"pe": nc.tensor, "act": nc.scalar, "dve": nc.vector,
                     "pool": nc.gpsimd, "sp": nc.sync}
        self.sems = {}
        self.cnt = {}
        for e in self.engs:
            self.sems[e] = es.enter_context(nc.semaphore("s_" + e))
            self.cnt[e] = 0
        self.dq = {}
        for q in ("sp", "pool", "act"):
            self.dq[q] = 0
            for j in range(self.NDMA):
                key = "d_%s%d" % (q, j)
                self.sems[key] = es.enter_context(nc.semaphore(key))
                self.cnt[key] = 0
        self.known = {e: {} for e in self.engs}
        self.pending = {e: False for e in self.engs}
        self.nwait = 0
        self.nins = 0
        self.halt = False
        self.banks = []
        self.banks_bf = []
        self.bdep = []
        for b in range(8):
            t = es.enter_context(nc.psum_tensor("psb%d" % b, [128, 512], F32))
            self.banks.append(t)
            self.banks_bf.append(t.bitcast(BF16))
            bd = Dep(x=True)
            self.bdep.append((bd, bd))
        self.ev_i = 0
        self.sb_i = 0

    def _need(self, reads, writes):
        need = {}
        for d in reads:
            if d.w is not None:
                k, v = d.w
                if need.get(k, 0) < v:
                    need[k] = v
        for d in writes:
            if d.w is not None:
                k, v = d.w
                if need.get(k, 0) < v:
                    need[k] = v
            for k, v in d.r.items():
                if need.get(k, 0) < v:
                    need[k] = v
        return need

    def _waits(self, e, need, skip_own=False):
        eng = self.engs[e]
        kn = self.known[e]
        for k, v in need.items():
            if skip_own and k == e:
                continue
            if kn.get(k, 0) < v:
                eng.wait_ge(self.sems[k], v)
                kn[k] = v
                self.nwait += 1

    def _mark(self, tok, reads, writes):
        k, v = tok
        for d in reads:
            if d.r.get(k, 0) < v:
                d.r[k] = v
        for d in writes:
            d.w = tok
            d.r = {}

    def op(self, e, fn, reads=(), writes=(), inc=True):
        if self.halt:
            return None
        reads = [x.d if isinstance(x, T) else x for x in reads]
        writes = [x.d if isinstance(x, T) else x for x in writes]
        xs = [d for d in reads if d.x]
        if xs:
            reads = [d for d in reads if not d.x]
            writes = writes + xs
        need = self._need(reads, writes)
        self._waits(e, need, skip_own=(e == "pe"))
        ins = fn(self.engs[e])
        self.nins += 1
        if inc:
            self.cnt[e] += 1
            ins.then_inc(self.sems[e], 1)
            tok = (e, self.cnt[e])
            self.pending[e] = False
        else:
            tok = (e, self.cnt[e] + 1)
            self.pending[e] = True
        self._mark(tok, reads, writes)
        return tok

    def dma(self, q, out, in_, reads=(), writes=(), **kw):
        if self.halt:
            return None
        reads = [x.d if isinstance(x, T) else x for x in reads]
        writes = [x.d if isinstance(x, T) else x for x in writes]
        j = self.dq[q] % self.NDMA
        self.dq[q] += 1
        key = "d_%s%d" % (q, j)
        need = self._need(reads, writes)
        if self.cnt[key] > 0:
            need[key] = max(need.get(key, 0), self.cnt[key])
        self._waits(q, need)
        ins = self.engs[q].dma_start(out=out, in_=in_, **kw)
        self.nins += 1
        self.cnt[key] += 16
        ins.then_inc(self.sems[key], 16)
        tok = (key, self.cnt[key])
        self._mark(tok, reads, writes)
        return tok

    def barrier(self, engines=("pe", "act", "dve", "pool", "sp")):
        if self.halt:
            return
        for e in self.engs:
            assert not self.pending[e], e
        need = {k: v for k, v in self.cnt.items() if v > 0}
        for e in engines:
            self._waits(e, dict(need))

    def tile(self, es, shape, dtype, name):
        self.tile_i = getattr(self, "tile_i", 0) + 1
        return T(es.enter_context(self.nc.sbuf_tensor("%s_%d" % (name, self.tile_i), list(shape), dtype)))

    def ev_eng(self):
        self.ev_i += 1
        return "act" if self.ev_i % 2 else "dve"

    def sb_eng(self):
        self.sb_i += 1
        return "pool" if self.sb_i % 2 else "dve"

    def copy(self, e, out, in_, reads, writes, scale=None):
        if e == "act":
            if scale is None:
                return self.op("act", lambda g: g.copy(out, in_), reads, writes)
            return self.op("act", lambda g: g.mul(out, in_, scale), reads, writes)
        if scale is None:
            return self.op(e, lambda g: g.tensor_copy(out, in_), reads, writes)
        return self.op(e, lambda g: g.tensor_scalar_mul(out, in_, scale), reads, writes)


class PsPool:
    def __init__(self, kb, banks):
        self.kb = kb
        self.halves = [(b, h) for b in banks for h in (0, 1)]
        self.i = 0

    def get(self, ncols):
        n = len(self.halves)
        if ncols <= 256:
            b, h = self.halves[self.i % n]
            self.i += 1
            return b, h * 256, [self.kb.bdep[b][h]]
        if self.i % 2:
            self.i += 1
        b, _ = self.halves[self.i % n]
        self.i += 2
        return b, 0, [self.kb.bdep[b][0], self.kb.bdep[b][1]]


def make_consts():
    c = {}
    c["identf"] = np.eye(128, dtype=np.float32)
    c["onesf"] = np.ones((128, 128), dtype=np.float32)
    t = np.arange(128)
    m4 = np.zeros((2, 128, 512), np.float32)
    mt = np.zeros((2, 128, 128), np.float32)
    tri = np.zeros((2, 3, 128, 128), np.float32)
    for d in range(2):
        before = (t[:, None] < t[None, :]) if d == 0 else (t[:, None] > t[None, :])
        beq = before | (t[:, None] == t[None, :])
        m4[d, :, 0:128] = -1.0 * before
        m4[d, :, 128:256] = beq
        m4[d, :, 256:384] = before
        m4[d, :, 384:512] = beq
        mt[d] = -1.0 * before.T
        tri[d, 0] = -CW * beq
        tri[d, 1] = -CW * before
        tri[d, 2] = -CW * (~beq)
    c["mask4"] = m4
    c["maskt"] = mt
    c["tri"] = tri
    c["cvec"] = np.full((128, 1), -CW, np.float32)
    esel = np.zeros((65, 64), np.float32)
    esel[64, :] = 1.0
    c["esel"] = esel
    n_pairs = 8
    inv = 10000.0 ** (-np.arange(n_pairs, dtype=np.float32) / n_pairs)
    row = np.repeat(np.arange(SEQ // 64, dtype=np.float32), 64)
    col = np.tile(np.arange(64, dtype=np.float32), SEQ // 64)
    ang = np.concatenate([row[:, None] * inv, col[:, None] * inv], axis=-1).astype(np.float32)
    cos = np.cos(ang).astype(np.float32)
    sin = np.sin(ang).astype(np.float32)
    cos2 = np.repeat(cos, 2, axis=1).T
    sin2 = np.repeat(sin, 2, axis=1).T
    cosq = np.ones((96, NT), np.float32)
    sinq = np.zeros((96, NT), np.float32)
    cosq[64:96, CTXL:] = cos2
    sinq[64:96, CTXL:] = sin2
    c["cosk"] = cosq.copy()
    c["sink"] = sinq.copy()
    c["cosq"] = (cosq * QSCALE).astype(np.float32)
    c["sinq"] = (sinq * QSCALE).astype(np.float32)
    return c


CONST_SHAPES = {k: v.shape for k, v in make_consts().items()}

W_SHAPES = {
    "w_ada": (2, 1024, 6144), "b_ada": (2, 6144), "w_in": (2, 1024, 2592), "rwkv_conv": (2, 3, 1920),
    "w0": (2, 2, 512), "w_b": (2, 2, 64, 512), "a0": (2, 2, 512), "a_b": (2, 2, 64, 512),
    "g_b": (2, 128, 512), "k_k": (2, 512), "k_a": (2, 512), "r_k": (2, 8, 64), "gn_g": (2, 512),
    "gn_b": (2, 512), "q_norm_g": (2, 384), "w_uq": (2, 384, 768), "kv_norm_g": (2, 256),
    "w_ukv": (2, 256, 1024), "w_o": (2, 1024, 1024), "ln1_g": (2, 1024), "ln1_b": (2, 1024),
    "w_up": (2, 1024, 5632), "ffn_conv_w": (2, 3, 5632), "ffn_conv_b": (2, 5632),
    "w_down": (2, 2816, 1024), "ln2_g": (2, 1024), "ln2_b": (2, 1024),
}


class Env:
    pass


def load_vec_fm(kb, env, es, vec_ap, n, name):
    nc = kb.nc
    rows = kb.tile(es, [n, 128], F32, name + "_r")
    out = kb.tile(es, [128, n], F32, name)
    kb.dma("sp", rows.t[:], vec_ap.rearrange("(n p) -> n p", p=128), writes=[rows])
    b, c0, deps = env.pp.get(n)
    ps = kb.banks[b]
    kb.op("pe", lambda e: e.transpose(ps[:, c0:c0 + n], rows.t[:], env.identf.t[0:n, 0:n]),
          reads=[rows, env.identf], writes=deps)
    kb.copy("dve", out.t[:], ps[:, c0:c0 + n], deps, [out])
    return out


def load_bcast(kb, es, vec_ap, n, name):
    out = kb.tile(es, [128, n], F32, name)
    kb.dma("sp", out.t[:], vec_ap.partition_broadcast(128), writes=[out])
    return out


def load_w_bf16(kb, es, dst, w_ap, kchunks, ncols, stage, col_off=0, scale=None):
    wv = w_ap.rearrange("(k p) n -> k p n", p=128)
    for k in range(kchunks):
        st = stage[k % len(stage)]
        kb.dma(("sp", "act", "pool")[k % 3], st.t[:, 0:ncols], wv[k], writes=[st])
        e = ("act", "dve", "pool")[k % 3]
        if scale is None:
            kb.copy(e, dst.t[:, k, col_off:col_off + ncols], st.t[:, 0:ncols], [st], [dst])
        else:
            kb.op("dve", lambda g: g.tensor_scalar_mul(dst.t[:, k, col_off:col_off + ncols], st.t[:, 0:ncols], scale.t[:, k:k + 1]),
                  [st, scale], [dst])


def emit_ut(kb, env, xt, s, modf, w_scale, w_shift, dst, i, pad):
    for _ in emit_ut_g(kb, env, xt, s, modf, w_scale, w_shift, dst, i, pad):
        pass


def emit_ut_g(kb, env, xt, s, modf, w_scale, w_shift, dst, i, pad):
    c0 = tcol(i)
    st = env.ut_st[env.ut_i % 2]
    env.ut_i += 1
    for half in range(2):
        b, _, deps = env.pp.get(512)
        ps = kb.banks[b]
        for j in range(4):
            c = half * 4 + j
            kb.op("pe", lambda e: e.transpose(ps[:, j * 128:(j + 1) * 128], xt.t[:, c * 128:(c + 1) * 128], env.identf.t[:]),
                  reads=[xt, env.identf], writes=deps, inc=(j == 3))
        for j in range(4):
            c = half * 4 + j
            kb.op("act", lambda e: e.activation(st.t[:, c, 1:129], ps[:, j * 128:(j + 1) * 128], AF.Identity,
                                                bias=modf.t[:, w_shift * 8 + c, s:s + 1], scale=modf.t[:, w_scale * 8 + c, s:s + 1]),
                  reads=deps + [modf], writes=[st])
            if j % 2 == 1:
                yield
    lo, hi = 1, 129
    if pad and i in (0, 2):
        lo = 0
    if pad and i in (1, NTILE - 1):
        hi = 130
    kb.dma("pool", dst[:, c0 - 1 + lo:c0 - 1 + hi].rearrange("(k p) c -> p k c", p=128), st.t[:, :, lo:hi],
           reads=[st], writes=[env.dd[dst.tensor.name]])


def phase_mod(kb, env, l):
    nc = kb.nc
    with ExitStack() as es:
        modf = env.modf[l]
        crow = kb.tile(es, [16, 128], F32, "crow")
        kb.dma("sp", crow.t[0:8, :], env.c.rearrange("(n p) -> n p", p=128), writes=[crow])
        kb.dma("sp", crow.t[8:16, :], env.c_ctx.rearrange("(n p) -> n p", p=128), writes=[crow])
        sct = kb.tile(es, [128, 2, 8], F32, "sct")
        b, c0, deps = env.pp.get(16)
        ps = kb.banks[b]
        kb.op("pe", lambda e: e.transpose(ps[:, c0:c0 + 16], crow.t[:], env.identf.t[0:16, 0:16]), [crow, env.identf], deps)
        kb.op("act", lambda e: e.activation(sct.t[:].rearrange("p s k -> p (s k)"), ps[:, c0:c0 + 16], AF.Silu), deps, [sct])
        bF = load_vec_fm(kb, env, es, env.w["b_ada"][l], 48, "bF")
        stg = [kb.tile(es, [128, 8, 768], F32, "wada%d" % i) for i in range(2)]
        bq, cq0, depq = env.pp.get(96)
        psq = kb.banks[bq]
        wv = env.w["w_ada"][l].rearrange("(k p) n -> p k n", p=128)
        for pc in range(8):
            st = stg[pc % 2]
            kb.dma("sp", st.t[:], wv[:, :, pc * 768:(pc + 1) * 768], writes=[st])
            for jj in range(6):
                j = pc * 6 + jj
                for k in range(8):
                    kb.op("pe", lambda e: e.matmul(psq[:, cq0 + 2 * j:cq0 + 2 * j + 2], st.t[:, k, jj * 128:(jj + 1) * 128], sct.t[:, :, k],
                                                  start=(k == 0), stop=(k == 7)),
                          [st, sct], depq, inc=(k == 7))
        kb.op("dve", lambda e: e.tensor_tensor(modf.t[:], psq[:, cq0:cq0 + 96].rearrange("p (j s) -> p j s", s=2),
                                              bF.t[:].unsqueeze(2).to_broadcast([128, 48, 2]), ALU.add),
              depq + [bF], [modf])
        for wch in (1, 4):
            kb.op("dve", lambda e: e.tensor_scalar_add(modf.t[:, wch * 8:(wch + 1) * 8, :], modf.t[:, wch * 8:(wch + 1) * 8, :], 1.0),
                  [modf], [modf])
        grow = kb.tile(es, [1, 2, 2, 1024], F32, "grow")
        brow = kb.tile(es, [1, 2, 1024], F32, "brow")
        for gi, wch in enumerate((2, 5)):
            kb.dma("sp", brow.t[:, gi, :], env.w["b_ada"][l][wch * 1024:(wch + 1) * 1024].rearrange("(o n) -> o n", o=1), writes=[brow])
        for gi, wch in enumerate((2, 5)):
            st = stg[gi % 2]
            for hh in range(2):
                col = wch * 1024 + hh * 512
                kb.dma("sp", st.t[:, :, 0:512], wv[:, :, col:col + 512], writes=[st])
                for s in range(2):
                    b2, c2, dep2 = env.pp.get(512)
                    ps2 = kb.banks[b2]
                    for k in range(8):
                        kb.op("pe", lambda e: e.matmul(ps2[0:1, :], sct.t[:, s, k:k + 1], st.t[:, k, 0:512], start=(k == 0), stop=(k == 7)),
                              [st, sct], dep2, inc=(k == 7))
                    kb.op("dve", lambda e: e.tensor_tensor(grow.t[:, s, gi, hh * 512:(hh + 1) * 512], ps2[0:1, :], brow.t[:, gi, hh * 512:(hh + 1) * 512], ALU.add),
                          dep2 + [brow], [grow])
        kb.dma("sp", env.MODROW[l:l + 1].rearrange("o s g n -> o (s g n)"), grow.t[:].rearrange("o s g n -> o (s g n)"),
               reads=[grow], writes=[env.dd["MODROW"]])
        kb.barrier()


def xin_ap(env, l, i):
    if l == 0:
        return env.ctx[i * 128:(i + 1) * 128, :] if i < 2 else env.x[(i - 2) * 128:(i - 1) * 128, :]
    return env.X2[i * 128:(i + 1) * 128, :]


def phase_u0(kb, env):
    with ExitStack() as es:
        env.ut_st = [kb.tile(es, [128, 8, 130], BF16, "utst%d" % i) for i in range(2)]
        env.ut_i = 0
        xts = [kb.tile(es, [128, 1024], F32, "xt%d" % i) for i in range(3)]
        for i in range(NTILE):
            xt = xts[i % 3]
            kb.dma("sp", xt.t[:], xin_ap(env, 0, i), writes=[xt])
            emit_ut(kb, env, xt, 1 if i < 2 else 0, env.modf[0], 1, 0, env.UT, i, pad=False)
        kb.barrier()


def phase_p1(kb, env, l):
    nc = kb.nc
    with ExitStack() as es:
        win = kb.tile(es, [128, 8, 2624], BF16, "win")
        stage = [kb.tile(es, [128, 2592], F32, "wst%d" % i) for i in range(2)]
        load_w_bf16(kb, es, win, env.w["w_in"][l], 8, 2592, stage)
        kb.op("dve", lambda e: e.tensor_scalar_mul(win.t[:, :, 2592:2624:2], win.t[:, :, 2561:2592:2], -1.0), [win], [win])
        kb.op("dve", lambda e: e.tensor_copy(win.t[:, :, 2593:2624:2], win.t[:, :, 2560:2592:2]), [win], [win])
        utb = [kb.tile(es, [128, 8, 512], BF16, "utb%d" % i) for i in range(2)]
        pst = [kb.tile(es, [128, 15, 514], BF16, "pst%d" % i) for i in range(2)]
        pmst = [kb.tile(es, [128, 6, 512], F32, "pmst%d" % i) for i in range(2)]
        for t in pst:
            kb.op("pool", lambda e: e.memset(t.t[:], 0.0), [], [t])
        for bi, (g0, n) in enumerate(BLOCKS):
            c0 = gcol(g0)
            ub = utb[bi % 2]
            ps_ = pst[bi % 2]
            pm_ = pmst[bi % 2]
            kb.dma("sp", ub.t[:, :, 0:n], env.UT[:, c0:c0 + n].rearrange("(k p) c -> p k c", p=128),
                   reads=[env.dd["UT"]], writes=[ub])
            for jf in range(21):
                rows = 128 if jf < 20 else 64
                b, _, deps = env.pp.get(512)
                ps = kb.banks[b]
                for k in range(8):
                    kb.op("pe", lambda e: e.matmul(ps[0:rows, 0:n], win.t[:, k, jf * 128:jf * 128 + rows], ub.t[:, k, 0:n],
                                                  start=(k == 0), stop=(k == 7)),
                          [win, ub], deps, inc=(k == 7))
                if jf < 15:
                    kb.copy(kb.ev_eng(), ps_.t[:, jf, 1:1 + n], ps[:, 0:n], deps, [ps_])
                else:
                    kb.copy(kb.ev_eng(), pm_.t[0:rows, jf - 15, 0:n], ps[0:rows, 0:n], deps, [pm_])
            lo, hi = 1, 1 + n
            if bi == 0:
                lo, hi = 0, n + 2
            if bi == len(BLOCKS) - 1:
                hi = n + 2
            kb.dma("pool", env.PT[:, c0 - 1 + lo:c0 - 1 + hi].rearrange("(k p) c -> p k c", p=128), ps_.t[:, :, lo:hi],
                   reads=[ps_], writes=[env.dd["PT"]])
            kb.dma("pool", env.PM[0:640, c0:c0 + n].rearrange("(k p) c -> p k c", p=128), pm_.t[:, 0:5, 0:n],
                   reads=[pm_], writes=[env.dd["PM"]])
            kb.dma("pool", env.PM[640:704, c0:c0 + n], pm_.t[0:64, 5, 0:n], reads=[pm_], writes=[env.dd["PM"]])
        kb.barrier()


SCRATCH = {
    "UT": ([1024, NCOL], BF16), "PT": ([RW, NCOL], BF16), "PM": ([704, NCOL], F32),
    "MIXT": ([1024, NCOL], BF16), "X1": ([NT, 1024], F32), "U2T": ([1024, NCOL], BF16),
    "GT": ([DFF, NCOL], BF16), "X2": ([NT, 1024], F32), "YF": ([NT, 512], F32),
    "MODROW": ([2, 2, 2, 1024], F32), "DBG": ([16, 128, 512], F32),
}


def build(phases=None, debug_out=(), debug_in=(), stop=None):
    nc = bass.Bass("TRN2", target_bir_lowering=False)
    env = Env()
    env.x = nc.dram_tensor("x", [SEQ, D], F32, kind="ExternalInput").ap()
    env.c = nc.dram_tensor("c", [D], F32, kind="ExternalInput").ap()
    env.ctx = nc.dram_tensor("ctx", [CTXL, D], F32, kind="ExternalInput").ap()
    env.c_ctx = nc.dram_tensor("c_ctx", [D], F32, kind="ExternalInput").ap()
    env.w = {k: nc.dram_tensor(k, list(s), F32, kind="ExternalInput").ap() for k, s in W_SHAPES.items()}
    env.cst = {k: nc.dram_tensor("k_" + k, list(s), F32, kind="ExternalInput").ap() for k, s in CONST_SHAPES.items()}
    env.y = nc.dram_tensor("y", [SEQ, D], F32, kind="ExternalOutput").ap()
    env.dd = {}
    for name, (shape, dt_) in SCRATCH.items():
        kind = "ExternalOutput" if name in debug_out else ("ExternalInput" if name in debug_in else "Internal")
        setattr(env, name, nc.dram_tensor(name, shape, dt_, kind=kind).ap())
        env.dd[name] = Dep()
    env.dd["y"] = Dep()
    env.stop = stop
    allp = ["mod", "u0"]
    for l in range(DEPTH):
        allp += ["p1_%d" % l, "mla_%d" % l, "rwkv_%d" % l, "wo_%d" % l, "ffu_%d" % l, "ffd_%d" % l]
    if phases is None:
        phases = allp
    with ExitStack() as es:
        kb = KB(nc, es)
        env.pp = PsPool(kb, list(range(8)))
        env.identf = kb.tile(es, [128, 128], F32, "identf")
        kb.dma("sp", env.identf.t[:], env.cst["identf"], writes=[env.identf])
        env.identb = kb.tile(es, [128, 128], BF16, "identb")
        kb.copy("dve", env.identb.t[:], env.identf.t[:], [env.identf], [env.identb])
        env.onesf = kb.tile(es, [128, 128], F32, "onesf")
        kb.dma("sp", env.onesf.t[:], env.cst["onesf"], writes=[env.onesf])
        env.modf = [kb.tile(es, [128, 48, 2], F32, "modf%d" % l) for l in range(DEPTH)]
        env.epsr = kb.tile(es, [128, 1], F32, "epsr")
        kb.op("pool", lambda e: e.memset(env.epsr.t[:], RMS_EPS), [], [env.epsr])
        env.epsl = kb.tile(es, [128, 1], F32, "epsl")
        kb.op("pool", lambda e: e.memset(env.epsl.t[:], LN_EPS), [], [env.epsl])
        env.epsg = kb.tile(es, [128, 1], F32, "epsg")
        kb.op("pool", lambda e: e.memset(env.epsg.t[:], GN_EPS), [], [env.epsg])
        for ph in phases:
            if ph == "mod":
                for l in range(DEPTH):
                    phase_mod(kb, env, l)
            elif ph == "u0":
                phase_u0(kb, env)
            else:
                name, l = ph.rsplit("_", 1)
                try:
                    PHASES[name](kb, env, int(l))
                except StopPhase:
                    print("stopped at checkpoint", env.stop, flush=True)
                    break
        kb.barrier()
        env.kb = kb
    print("built: %d instructions, %d waits" % (kb.nins, kb.nwait), flush=True)
    return nc, env


PHASES = {"p1": phase_p1}


def phase_mla(kb, env, l):
    nc = kb.nc
    need_ctx = l < DEPTH - 1
    env.pp = PsPool(kb, [0, 1, 2, 3, 4])
    pp = env.pp
    with ExitStack() as es:
        cqn = kb.tile(es, [128, 3, NT], BF16, "cqn")
        ckvn = kb.tile(es, [128, 2, NT], BF16, "ckvn")
        va = kb.tile(es, [128, NTILE, 8, 65], BF16, "va")
        KT = [kb.tile(es, [128, NT], BF16, "kt%d" % i) for i in range(2)]
        wq2 = kb.tile(es, [128, 3, 8, 2, 96], BF16, "wq2")
        wk = kb.tile(es, [128, 2, 512], BF16, "wk")
        wv = kb.tile(es, [128, 2, 512], BF16, "wv")
        esel = kb.tile(es, [65, 64], F32, "esel")
        kb.dma("sp", esel.t[:], env.cst["esel"], writes=[esel])
        with ExitStack() as es2:
            qg = load_vec_fm(kb, env, es2, env.w["q_norm_g"][l], 3, "qg")
            kg = load_vec_fm(kb, env, es2, env.w["kv_norm_g"][l], 2, "kg")
            wq_st = kb.tile(es2, [128, 3, 768], F32, "wq_st")
            wkv_st = kb.tile(es2, [128, 2, 1024], F32, "wkv_st")
            kb.dma("sp", wq_st.t[:], env.w["w_uq"][l].rearrange("(k p) n -> p k n", p=128), writes=[wq_st])
            kb.dma("sp", wkv_st.t[:], env.w["w_ukv"][l].rearrange("(k p) n -> p k n", p=128), writes=[wkv_st])
            kb.op("pool", lambda e: e.memset(wq2.t[:], 0.0), [], [wq2])
            kb.op("pool", lambda e: e.memset(va.t[:], 1.0), [], [va])
            for k in range(3):
                kb.op("dve", lambda e: e.tensor_scalar_mul(wq2.t[:, k, :, 0, :], wq_st.t[:, k, :].rearrange("p (h d) -> p h d", d=96), qg.t[:, k:k + 1]),
                      [wq_st, qg], [wq2])
                kb.op("dve", lambda e: e.tensor_scalar_mul(wq2.t[:, k, :, 1, 64:96:2], wq2.t[:, k, :, 0, 65:96:2], -1.0), [wq2], [wq2])
                kb.op("dve", lambda e: e.tensor_copy(wq2.t[:, k, :, 1, 65:96:2], wq2.t[:, k, :, 0, 64:96:2]), [wq2], [wq2])
            for k in range(2):
                src = wkv_st.t[:, k, :].rearrange("p (h e) -> p h e", e=128)
                kb.op("dve", lambda e: e.tensor_scalar_mul(wk.t[:, k, :].rearrange("p (h d) -> p h d", d=64), src[:, :, 0:64], kg.t[:, k:k + 1]),
                      [wkv_st, kg], [wk])
                kb.op("dve", lambda e: e.tensor_scalar_mul(wv.t[:, k, :].rearrange("p (h d) -> p h d", d=64), src[:, :, 64:128], kg.t[:, k:k + 1]),
                      [wkv_st, kg], [wv])
            cs = [kb.tile(es2, [128, 5, 512], F32, "cs%d" % i) for i in range(2)]
            sq = [kb.tile(es2, [128, 5, 512], F32, "sq%d" % i) for i in range(2)]
            rsd = [kb.tile(es2, [128, 2, 512], F32, "rsd%d" % i) for i in range(2)]
            krt = [kb.tile(es2, [128, 4, 512], F32, "krt%d" % i) for i in range(2)]
            krm = [kb.tile(es2, [128, 2, 512], F32, "krm%d" % i) for i in range(2)]
            for bi, (g0, n) in enumerate(BLOCKS):
                c0 = gcol(g0)
                c_, s_, r_, kr_, km_ = cs[bi % 2], sq[bi % 2], rsd[bi % 2], krt[bi % 2], krm[bi % 2]
                kb.dma("sp", c_.t[:, :, 0:n], env.PM[0:640, c0:c0 + n].rearrange("(k p) c -> p k c", p=128), reads=[env.dd["PM"]], writes=[c_])
                kb.op("act", lambda e: e.activation(s_.t[:, :, 0:n], c_.t[:, :, 0:n], AF.Square), [c_], [s_])
                for wi, (k0, k1, dim) in enumerate(((0, 3, 384.0), (3, 5, 256.0))):
                    b, _, deps = pp.get(512)
                    ps = kb.banks[b]
                    for k in range(k0, k1):
                        kb.op("pe", lambda e: e.matmul(ps[:, 0:n], env.onesf.t[:], s_.t[:, k, 0:n], start=(k == k0), stop=(k == k1 - 1)),
                              [env.onesf, s_], deps, inc=(k == k1 - 1))
                    kb.op("act", lambda e: e.activation(r_.t[:, wi, 0:n], ps[:, 0:n], AF.Sqrt, bias=env.epsr.t[:, 0:1], scale=1.0 / dim), deps + [env.epsr], [r_])
                    kb.op("dve", lambda e: e.reciprocal(r_.t[:, wi, 0:n], r_.t[:, wi, 0:n]), [r_], [r_])
                    dst = cqn if wi == 0 else ckvn
                    for k in range(k0, k1):
                        kb.op(kb.sb_eng(), lambda e: e.tensor_tensor(dst.t[:, k - k0, g0:g0 + n], c_.t[:, k, 0:n], r_.t[:, wi, 0:n], ALU.mult),
                              [c_, r_], [dst])
                kb.dma("sp", kr_.t[64:96, 0, 0:n], env.PM[640:672, c0:c0 + n], reads=[env.dd["PM"]], writes=[kr_])
                kb.dma("sp", kr_.t[64:96, 1, 0:n], env.PM[672:704, c0:c0 + n], reads=[env.dd["PM"]], writes=[kr_])
                kb.dma("sp", kr_.t[64:96, 2, 0:n], env.cst["cosk"][64:96, g0:g0 + n], writes=[kr_])
                kb.dma("sp", kr_.t[64:96, 3, 0:n], env.cst["sink"][64:96, g0:g0 + n], writes=[kr_])
                kb.op("pool", lambda e: e.tensor_tensor(km_.t[64:96, 0, 0:n], kr_.t[64:96, 0, 0:n], kr_.t[64:96, 2, 0:n], ALU.mult), [kr_], [km_])
                kb.op("dve", lambda e: e.tensor_tensor(km_.t[64:96, 1, 0:n], kr_.t[64:96, 1, 0:n], kr_.t[64:96, 3, 0:n], ALU.mult), [kr_], [km_])
                kb.op("pool", lambda e: e.tensor_tensor(KT[0].t[64:96, g0:g0 + n], km_.t[64:96, 0, 0:n], km_.t[64:96, 1, 0:n], ALU.add), [km_], [KT[0]])
            kb.op("pool", lambda e: e.tensor_copy(KT[1].t[64:96, :], KT[0].t[64:96, :]), [KT[0]], [KT[1]])
            for i in range(NTILE):
                b, _, deps = pp.get(512)
                ps = kb.banks[b]
                for k in range(2):
                    kb.op("pe", lambda e: e.matmul(ps[:, :], ckvn.t[:, k, i * 128:(i + 1) * 128], wv.t[:, k, :], start=(k == 0), stop=(k == 1)),
                          [ckvn, wv], deps, inc=(k == 1))
                kb.copy(kb.ev_eng(), va.t[:, i, :, 0:64], ps[:, :].rearrange("p (h d) -> p h d", d=64), deps, [va])
            kb.barrier()
        QT = [kb.tile(es, [128, NT], BF16, "qt%d" % i) for i in range(2)]
        NEGM = [kb.tile(es, [128, 1], F32, "negm%d" % i) for i in range(2)]
        tabs = [kb.tile(es, [128, 2, 512], F32, "tab%d" % i) for i in range(2)]
        tmp = [kb.tile(es, [128, 2, 512], F32, "qtmp%d" % i) for i in range(2)]
        sqq = [kb.tile(es, [128, 512], F32, "sqq%d" % i) for i in range(2)]
        ptt = [kb.tile(es, [128, 512], BF16, "ptt%d" % i) for i in range(8)]
        osb = [kb.tile(es, [128, 512], F32, "osb%d" % i) for i in range(2)]
        rl = [kb.tile(es, [64, 512], F32, "rl%d" % i) for i in range(2)]
        ob = [kb.tile(es, [64, 512], BF16, "ob%d" % i) for i in range(2)]
        nb = kb.tile(es, [1, 2, 16], F32, "nb")
        msc = kb.tile(es, [1, 4], F32, "msc")
        qblocks = BLOCKS if need_ctx else BLOCKS[1:]
        LOOK = 4
        cnt = {"ti": 0, "pi": 0, "qi": 0}

        def setup(h):
            kt = KT[h % 2]
            qt = QT[h % 2]
            negm = NEGM[h % 2]
            for bi, (g0, n) in enumerate(BLOCKS):
                b, _, deps = pp.get(512)
                ps = kb.banks[b]
                for k in range(2):
                    kb.op("pe", lambda e: e.matmul(ps[0:64, 0:n], wk.t[:, k, h * 64:(h + 1) * 64], ckvn.t[:, k, g0:g0 + n], start=(k == 0), stop=(k == 1)),
                          [wk, ckvn], deps, inc=(k == 1))
                kb.copy("dve", kt.t[0:64, g0:g0 + n], ps[0:64, 0:n], deps, [kt])
            for bi, (g0, n) in enumerate(qblocks):
                tb = tabs[cnt["ti"] % 2]
                tm = tmp[cnt["ti"] % 2]
                cnt["ti"] += 1
                kb.dma("sp", tb.t[0:96, 0, 0:n], env.cst["cosq"][:, g0:g0 + n], writes=[tb])
                kb.dma("sp", tb.t[0:96, 1, 0:n], env.cst["sinq"][:, g0:g0 + n], writes=[tb])
                for ab in range(2):
                    b, _, deps = pp.get(512)
                    ps = kb.banks[b]
                    for k in range(3):
                        kb.op("pe", lambda e: e.matmul(ps[0:96, 0:n], wq2.t[:, k, h, ab, :], cqn.t[:, k, g0:g0 + n], start=(k == 0), stop=(k == 2)),
                              [wq2, cqn], deps, inc=(k == 2))
                    kb.op("dve", lambda e: e.tensor_tensor(tm.t[0:96, ab, 0:n], ps[0:96, 0:n], tb.t[0:96, ab, 0:n], ALU.mult), deps + [tb], [tm])
                kb.op("pool", lambda e: e.tensor_tensor(qt.t[0:96, g0:g0 + n], tm.t[0:96, 0, 0:n], tm.t[0:96, 1, 0:n], ALU.add), [tm], [qt])
            for wi, (src, blks) in enumerate(((qt, qblocks), (kt, BLOCKS))):
                for bi, (g0, n) in enumerate(blks):
                    s_ = sqq[(wi + bi) % 2]
                    kb.op("pool", lambda e: e.tensor_tensor(s_.t[0:96, 0:n], src.t[0:96, g0:g0 + n], src.t[0:96, g0:g0 + n], ALU.mult), [src], [s_])
                    b, c0, deps = pp.get(512)
                    ps = kb.banks[b]
                    kb.op("pe", lambda e: e.matmul(ps[0:1, 0:n], env.onesf.t[0:96, 0:1], s_.t[0:96, 0:n], start=True, stop=True), [env.onesf, s_], deps)
                    kb.op("dve", lambda e: e.reduce_max(nb.t[0:1, wi, bi:bi + 1], ps[0:1, 0:n], AX.X), deps, [nb])
                kb.op("dve", lambda e: e.reduce_max(msc.t[0:1, wi:wi + 1], nb.t[0:1, wi, 0:len(blks)], AX.X), [nb], [msc])
            kb.op("dve", lambda e: e.tensor_tensor(msc.t[0:1, 2:3], msc.t[0:1, 0:1], msc.t[0:1, 1:2], ALU.mult), [msc], [msc])
            kb.op("act", lambda e: e.activation(msc.t[0:1, 3:4], msc.t[0:1, 2:3], AF.Sqrt), [msc], [msc])
            kb.op("dve", lambda e: e.tensor_scalar_mul(msc.t[0:1, 3:4], msc.t[0:1, 3:4], -1.0), [msc], [msc])
            b, c0, deps = pp.get(16)
            ps = kb.banks[b]
            kb.op("pe", lambda e: e.matmul(ps[:, c0:c0 + 1], env.onesf.t[0:1, :], msc.t[0:1, 3:4], start=True, stop=True), [env.onesf, msc], deps)
            kb.copy("dve", negm.t[:, 0:1], ps[:, c0:c0 + 1], deps, [negm])

        def attn(h):
            kt = KT[h % 2]
            qt = QT[h % 2]
            negm = NEGM[h % 2]
            items = []
            for bi, (g0, n) in enumerate(qblocks):
                kts = list(range(NTILE)) if g0 >= CTXL else [0, 1]
                qi = cnt["qi"]
                cnt["qi"] += 1
                for ii, ki in enumerate(kts):
                    items.append((g0, n, ii, ki, len(kts), qi))
            inflight = []

            def stage_a(it):
                g0, n, ii, ki, nk, qi = it
                b, _, deps = pp.get(512)
                ps = kb.banks[b]
                kb.op("pe", lambda e: e.matmul(ps[:, 0:n], kt.t[0:96, ki * 128:(ki + 1) * 128], qt.t[0:96, g0:g0 + n], start=True, stop=True),
                      [kt, qt], deps)
                p_ = ptt[cnt["pi"] % len(ptt)]
                cnt["pi"] += 1
                kb.op("act", lambda e: e.activation(p_.t[:, 0:n], ps[:, 0:n], AF.Exp, bias=negm.t[:, 0:1], scale=1.0), deps + [negm], [p_])
                inflight.append(p_)

            def stage_b(it):
                g0, n, ii, ki, nk, qi = it
                p_ = inflight.pop(0)
                pob = 5 + (qi % 2)
                po = kb.banks[pob]
                pod = list(kb.bdep[pob])
                kb.op("pe", lambda e: e.matmul(po[0:65, 0:n], va.t[:, ki, h, :], p_.t[:, 0:n], start=(ii == 0), stop=(ii == nk - 1)),
                      [va, p_], pod, inc=(ii == nk - 1))
                if ii != nk - 1:
                    return
                o_ = osb[qi % 2]
                r_ = rl[qi % 2]
                b_ = ob[qi % 2]
                kb.copy("dve", o_.t[0:65, 0:n], po[0:65, 0:n], pod, [o_])
                b, _, deps = pp.get(512)
                ps = kb.banks[b]
                kb.op("pe", lambda e: e.matmul(ps[0:64, 0:n], esel.t[:, :], o_.t[0:65, 0:n], start=True, stop=True), [esel, o_], deps)
                kb.op("dve", lambda e: e.reciprocal(r_.t[:, 0:n], ps[0:64, 0:n]), deps, [r_])
                kb.op("pool", lambda e: e.tensor_tensor(b_.t[:, 0:n], o_.t[0:64, 0:n], r_.t[:, 0:n], ALU.mult), [o_, r_], [b_])
                c0 = gcol(g0)
                kb.dma("pool", env.MIXT[512 + h * 64:512 + (h + 1) * 64, c0:c0 + n], b_.t[:, 0:n], reads=[b_], writes=[env.dd["MIXT"]])

            for idx in range(len(items) + LOOK):
                if idx < len(items):
                    stage_a(items[idx])
                if idx >= LOOK:
                    stage_b(items[idx - LOOK])

        setup(0)
        for h in range(8):
            if h + 1 < 8:
                setup(h + 1)
            attn(h)
        kb.barrier()
    env.pp = PsPool(kb, list(range(8)))


PHASES["mla"] = phase_mla


NSTEP = 6


def _tt(kb, e, out, in0, in1, op, reads, writes):
    return kb.op(e, lambda g: g.tensor_tensor(out, in0, in1, op), reads, writes)


def _stt(kb, e, out, in0, scalar, in1, op0, op1, reads, writes):
    return kb.op(e, lambda g: g.scalar_tensor_tensor(out, in0, scalar, in1, op0, op1), reads, writes)


def _act(kb, out, in_, func, reads, writes, **kw):
    return kb.op("act", lambda g: g.activation(out, in_, func, **kw), reads, writes)


def bc8(t):
    return t.unsqueeze(2).to_broadcast([128, 8, 64])


def v3(ap):
    return ap.rearrange("p (h d) -> p h d", d=64)


class StopPhase(Exception):
    pass


def ck(kb, env, n, dumps=()):
    if getattr(env, "stop", None) != n:
        return
    for slot, t in dumps:
        if t.t.dtype == BF16:
            n = t.t.shape[-1]
            kb.dma("sp", env.DBG[slot].bitcast(BF16)[0:t.t.shape[0], 0:n], t.t[:], reads=[t], writes=[env.dd["DBG"]])
        else:
            kb.dma("sp", env.DBG[slot][0:t.t.shape[0], 0:t.t.shape[-1]], t.t[:], reads=[t], writes=[env.dd["DBG"]])
    kb.barrier()
    kb.halt = True


def phase_rwkv(kb, env, l):
    nc = kb.nc
    need_ctx = l < DEPTH - 1
    env.pp = PsPool(kb, [0, 1, 2, 3, 4, 5])
    pp = env.pp
    ppp = PsPool(kb, [6, 7])
    idb = env.identb
    with ExitStack() as es:
        dg = kb.tile(es, [128, 15, 3, 128], BF16, "dg")
        lw = kb.tile(es, [128, 3, 512], BF16, "lw")
        with ExitStack() as es2:
            for a in range(3):
                cw = load_vec_fm(kb, env, es2, env.w["rwkv_conv"][l][a], 15, "cw%d" % a)
                for j in range(15):
                    kb.op(kb.sb_eng(), lambda e: e.tensor_scalar_mul(dg.t[:, j, a, :], env.identf.t[:], cw.t[:, j:j + 1]), [env.identf, cw], [dg])
            lst = kb.tile(es2, [128, 3, 512], F32, "lst")
            kb.dma("sp", lst.t[:, 0, :], env.w["w_b"][l].rearrange("d r c -> (d r) c"), writes=[lst])
            kb.dma("sp", lst.t[:, 1, :], env.w["a_b"][l].rearrange("d r c -> (d r) c"), writes=[lst])
            kb.dma("sp", lst.t[:, 2, :], env.w["g_b"][l], writes=[lst])
            kb.copy("dve", lw.t[:], lst.t[:], [lst], [lw])
            kb.barrier()
        brow = kb.tile(es, [1, 2, 2, 512], F32, "brow")
        kb.dma("sp", brow.t[:, 0, :, :], env.w["w0"][l].rearrange("(o d) c -> o d c", o=1), writes=[brow])
        kb.dma("sp", brow.t[:, 1, :, :], env.w["a0"][l].rearrange("(o d) c -> o d c", o=1), writes=[brow])
        kkb = load_bcast(kb, es, env.w["k_k"][l], 512, "kkb")
        kab = load_bcast(kb, es, env.w["k_a"][l], 512, "kab")
        rkb = load_bcast(kb, es, env.w["r_k"][l].rearrange("h d -> (h d)"), 512, "rkb")
        gng = load_bcast(kb, es, env.w["gn_g"][l], 512, "gng")
        gnb = load_bcast(kb, es, env.w["gn_b"][l], 512, "gnb")
        m4 = kb.tile(es, [128, 512], F32, "m4")
        mt = kb.tile(es, [128, 128], F32, "mt")
        tri = kb.tile(es, [128, 3, 128], F32, "tri")
        cvec = kb.tile(es, [128, 1], F32, "cvec")
        kb.dma("sp", cvec.t[:], env.cst["cvec"], writes=[cvec])
        hb = kb.tile(es, [64, 512], BF16, "hb")
        pw = [kb.tile(es, [128, 15, 3, 128], BF16, "pw%d" % i) for i in range(2)]
        f32names = ["r", "k", "v", "sw", "a0", "a1", "g", "t1", "sqk", "kk", "am1", "kd0", "kd1", "b", "rk",
                    "eG", "enG", "eGx", "eD", "y", "yf", "u1", "u2"]
        F = {n: kb.tile(es, [128, 512], F32, "f_" + n) for n in f32names}
        FD = [{n: (F[n] if bi_ == 0 else kb.tile(es, [128, 512], F32, "f%d_%s" % (bi_, n))) for n in ("r", "v", "g", "kd0", "kd1", "y")} for bi_ in range(3)]
        bfnames = ["KKt", "Bt", "Kt", "Rt", "Bh", "Kh", "Vb", "ob"]
        BfD = [{n: kb.tile(es, [128, 512], BF16, "b%d_%s" % (bi_, n)) for n in bfnames} for bi_ in range(2)]
        th = kb.tile(es, [128, 128], BF16, "th")
        alo = kb.tile(es, [128, 128], BF16, "alo")
        sg = kb.tile(es, [128, 128], BF16, "sg")
        s8 = {n: kb.tile(es, [128, 8], F32, "s8_" + n) for n in ("ss", "nrm", "m", "var", "bs")}
        fmaD = [kb.tile(es, [128, 4, 2, 128], BF16, "fma%d" % i) for i in range(2)]
        fmbD = [kb.tile(es, [128, 4, 2, 128], BF16, "fmb%d" % i) for i in range(2)]
        gcfD = [kb.tile(es, [64, 8], F32, "gcf%d" % i) for i in range(2)]
        mst = kb.tile(es, [128, 4, 128], BF16, "mst")
        SB1 = [kb.tile(es, [128, 512], BF16, "sb1_%d" % h) for h in range(8)]
        X0 = [kb.tile(es, [128, 128], BF16, "x0_%d" % h) for h in range(8)]
        XXp = [[kb.tile(es, [128, 512], BF16, "xxp%d_%d" % (i, p)) for i in range(3)] for p in range(4)]
        ETq = [[kb.tile(es, [128, 512], BF16, "etq%d_%d" % (i, q)) for i in range(2)] for q in range(2)]
        ZC = [kb.tile(es, [128, 128], BF16, "zc_%d" % h) for h in range(8)]
        WU = [kb.tile(es, [128, 128], BF16, "wu_%d" % h) for h in range(8)]
        QTb = [kb.tile(es, [64, 128], BF16, "qtb_%d" % h) for h in range(8)]
        MC = [kb.tile(es, [64, 64], BF16, "mc_%d" % h) for h in range(8)]

        def hs(h):
            return slice(h * 64, (h + 1) * 64)

        ck(kb, env, -1)

        for d in range(2):
            kb.dma("sp", m4.t[:], env.cst["mask4"][d], writes=[m4])
            kb.dma("sp", mt.t[:], env.cst["maskt"][d], writes=[mt])
            kb.dma("sp", tri.t[:], env.cst["tri"][d].rearrange("a s t -> s a t"), writes=[tri])
            kb.op("pool", lambda e: e.memset(hb.t[:], 0.0), [], [hb])
            order = list(range(NTILE)) if d == 0 else [1, 0] + list(range(NTILE - 1, 1, -1))
            def prep(ci, i, bi_):
                Fb, Bf, fma, fmb, gcf = FD[ci % 3], BfD[bi_], fmaD[bi_], fmbD[bi_], gcfD[bi_]
                FF = dict(F)
                FF.update(Fb)
                c0 = tcol(i)
                p_ = pw[ci % 2]
                for a in range(3):
                    kb.dma("sp", p_.t[:, :, a, :], env.PT[:, c0 - 1 + a:c0 + 127 + a].rearrange("(j p) c -> p j c", p=128), reads=[env.dd["PT"]], writes=[p_])
                ck(kb, env, 0)
                yield
                for gi, nm in enumerate(("r", "k", "v")):
                    b, _, deps = ppp.get(512)
                    ps = kb.banks[b]
                    for jj in range(4):
                        j = gi * 4 + jj
                        for a in range(3):
                            kb.op("pe", lambda e: e.matmul(ps[:, jj * 128:(jj + 1) * 128], p_.t[:, j, a, :], dg.t[:, j, a, :], start=(a == 0), stop=(a == 2)),
                                  [p_, dg], deps, inc=(jj == 3 and a == 2))
                    kb.copy("act" if gi != 1 else "dve", FF[nm].t[:], ps[:, :], deps, [FF[nm]])
                b, _, deps = ppp.get(512)
                ps = kb.banks[b]
                for jj in range(3):
                    j = 12 + jj
                    for a in range(3):
                        kb.op("pe", lambda e: e.matmul(ps[:, jj * 128:(jj + 1) * 128], dg.t[:, j, a, :], p_.t[:, j, a, :], start=(a == 0), stop=(a == 2)),
                              [p_, dg], deps, inc=(jj == 2 and a == 2))
                _act(kb, FF["t1"].t[:, 0:128], ps[:, 0:128], AF.Sigmoid, deps, [FF["t1"]], scale=2.0)
                kb.op("dve", lambda e: e.tensor_scalar(th.t[:], FF["t1"].t[:, 0:128], 2.0, -1.0, ALU.mult, ALU.add), [FF["t1"]], [th])
                kb.copy("dve", alo.t[:], ps[:, 128:256], deps, [alo])
                _act(kb, sg.t[:], ps[:, 256:384], AF.Sigmoid, deps, [sg])
                ck(kb, env, 1, [(0, FF["r"]), (1, FF["k"]), (2, FF["v"])])
                yield
                P0 = 64 * d
                b, _, deps = ppp.get(512)
                ps = kb.banks[b]
                kb.op("pe", lambda e: e.matmul(ps[:, :], th.t[P0:P0 + 64, :], lw.t[P0:P0 + 64, 0, :], start=True, stop=False), [th, lw], deps, inc=False)
                kb.op("pe", lambda e: e.matmul(ps[:, :], env.onesf.t[0:1, :], brow.t[0:1, 0, d, :], start=False, stop=True), [env.onesf, brow], deps)
                _act(kb, FF["sw"].t[:], ps[:, :], AF.Sigmoid, deps, [FF["sw"]])
                for dd in range(2):
                    b, _, deps = ppp.get(512)
                    ps = kb.banks[b]
                    kb.op("pe", lambda e: e.matmul(ps[:, :], alo.t[64 * dd:64 * dd + 64, :], lw.t[64 * dd:64 * dd + 64, 1, :], start=True, stop=False), [alo, lw], deps, inc=False)
                    kb.op("pe", lambda e: e.matmul(ps[:, :], env.onesf.t[0:1, :], brow.t[0:1, 1, dd, :], start=False, stop=True), [env.onesf, brow], deps)
                    _act(kb, FF["a%d" % dd].t[:], ps[:, :], AF.Sigmoid, deps, [FF["a%d" % dd]])
                b, _, deps = ppp.get(512)
                ps = kb.banks[b]
                kb.op("pe", lambda e: e.matmul(ps[:, :], sg.t[:], lw.t[:, 2, :], start=True, stop=True), [sg, lw], deps)
                kb.copy("dve", FF["g"].t[:], ps[:, :], deps, [FF["g"]])
                ck(kb, env, 2, [(0, FF["sw"]), (1, FF["a0"]), (2, FF["a1"]), (3, FF["g"])])
                yield
                _tt(kb, "pool", FF["t1"].t[:], FF["k"].t[:], kkb.t[:], ALU.mult, [FF["k"], kkb], [FF["t1"]])
                _tt(kb, "pool", FF["sqk"].t[:], FF["t1"].t[:], FF["t1"].t[:], ALU.mult, [FF["t1"]], [FF["sqk"]])
                kb.op("dve", lambda e: e.reduce_sum(s8["ss"].t[:], v3(FF["sqk"].t[:]), AX.X), [FF["sqk"]], [s8["ss"]])
                kb.op("dve", lambda e: e.tensor_scalar_max(s8["ss"].t[:], s8["ss"].t[:], 1e-24), [s8["ss"]], [s8["ss"]])
                _act(kb, s8["nrm"].t[:], s8["ss"].t[:], AF.Ln, [s8["ss"]], [s8["nrm"]])
                _act(kb, s8["nrm"].t[:], s8["nrm"].t[:], AF.Exp, [s8["nrm"]], [s8["nrm"]], scale=-0.5)
                _tt(kb, "dve", v3(FF["kk"].t[:]), v3(FF["t1"].t[:]), bc8(s8["nrm"].t[:]), ALU.mult, [FF["t1"], s8["nrm"]], [FF["kk"]])
                for dd in range(2):
                    a_ = FF["a%d" % dd]
                    kd = FF["kd%d" % dd]
                    _stt(kb, "dve", FF["am1"].t[:], a_.t[:], -1.0, kab.t[:], ALU.add, ALU.mult, [a_, kab], [FF["am1"]])
                    _stt(kb, "dve", kd.t[:], FF["am1"].t[:], 1.0, FF["k"].t[:], ALU.add, ALU.mult, [FF["am1"], FF["k"]], [kd])
                a_d = FF["a%d" % d]
                kd_d = FF["kd%d" % d]
                _tt(kb, "dve", FF["b"].t[:], FF["kk"].t[:], a_d.t[:], ALU.mult, [FF["kk"], a_d], [FF["b"]])
                ck(kb, env, 3, [(0, FF["kk"]), (1, FF["kd0"]), (2, FF["kd1"]), (3, FF["b"])])
                yield
                exps = []
                for ti_, (nm, sc) in enumerate((("eG", 1.0), ("eGx", 1.0), ("eD", 1.0))):
                    b, _, deps = ppp.get(512)
                    ps = kb.banks[b]
                    kb.op("pe", lambda e: e.matmul(ps[:, :], tri.t[:, ti_, :], FF["sw"].t[:], start=True, stop=True), [tri, FF["sw"]], deps)
                    _act(kb, FF[nm].t[:], ps[:, :], AF.Exp, deps, [FF[nm]])
                    if ti_ == 0:
                        _act(kb, FF["enG"].t[:], ps[:, :], AF.Exp, deps, [FF["enG"]], scale=-1.0)
                b, c8, deps = ppp.get(8)
                ps = kb.banks[b]
                for h in range(8):
                    kb.op("pe", lambda e: e.matmul(ps[0:64, c8 + h:c8 + h + 1], FF["sw"].t[:, hs(h)], cvec.t[:, 0:1], start=True, stop=True),
                          [FF["sw"], cvec], deps, inc=(h == 7))
                _act(kb, gcf.t[:], ps[0:64, c8:c8 + 8], AF.Exp, deps, [gcf])
                ck(kb, env, 4, [(0, FF["eG"]), (1, FF["enG"]), (2, FF["eGx"]), (3, FF["eD"])])
                yield
                for nm, x_, e_ in (("KKt", "kk", "eGx"), ("Bt", "b", "enG"), ("Kt", "kd%d" % d, "enG"), ("Rt", "r", "eG"),
                                   ("Bh", "b", "eD"), ("Kh", "kd%d" % d, "eD")):
                    _tt(kb, kb.sb_eng(), Bf[nm].t[:], FF[x_].t[:], FF[e_].t[:], ALU.mult, [FF[x_], FF[e_]], [Bf[nm]])
                kb.copy("pool", Bf["Vb"].t[:], FF["v"].t[:], [FF["v"]], [Bf["Vb"]])
                for nm, dst, slot in (("KKt", fma, 0), ("Rt", fma, 1), ("Bt", fmb, 0), ("Kt", fmb, 1)):
                    b, cc, deps = ppp.get(256)
                    psb = kb.banks_bf[b]
                    for jp in range(4):
                        kb.op("pe", lambda e: e.transpose(psb[:, 2 * cc + jp * 128:2 * cc + (jp + 1) * 128], Bf[nm].t[:, jp * 128:(jp + 1) * 128], idb.t[:]),
                              [Bf[nm], idb], deps, inc=(jp == 3))
                    kb.copy(kb.ev_eng(), dst.t[:, :, slot, :], psb[:, 2 * cc:2 * cc + 512].rearrange("p (j t) -> p j t", t=128), deps, [dst])
                yield

            def heads(ci, i, bi_):
                Fb, Bf, fma, fmb, gcf = FD[ci % 3], BfD[bi_], fmaD[bi_], fmbD[bi_], gcfD[bi_]
                FF = dict(F)
                FF.update(Fb)
                c0 = tcol(i)
                ck(kb, env, 5)
                yield
                R = {}
                for hg in range(2):
                    for h in range(4 * hg, 4 * hg + 4):
                        P = 64 * (h % 2)
                        jp = h // 2
                        b, _, deps = pp.get(512)
                        ps = kb.banks[b]
                        rhs = fma.t[P:P + 64, jp, :, :].rearrange("p a t -> p (a t)")
                        kb.op("pe", lambda e: e.matmul(ps[:, 0:256], fmb.t[P:P + 64, jp, 0, :], rhs, start=True, stop=True), [fma, fmb], deps, inc=False)
                        kb.op("pe", lambda e: e.matmul(ps[:, 256:512], fmb.t[P:P + 64, jp, 1, :], rhs, start=True, stop=True), [fma, fmb], deps)
                        R[h] = (ps, deps)
                    for h in range(4 * hg, 4 * hg + 4):
                        ps, deps = R[h]
                        _tt(kb, "dve", SB1[h].t[:], ps[:, :], m4.t[:], ALU.mult, deps + [m4], [SB1[h]])
                    yield
                for h in range(8):
                    P = 64 * (h % 2)
                    jp = h // 2
                    b2, c2, deps2 = pp.get(256)
                    ps2 = kb.banks[b2]
                    kb.op("pe", lambda e: e.matmul(ps2[:, c2:c2 + 128], fma.t[P:P + 64, jp, 0, :], fmb.t[P:P + 64, jp, 0, :], start=True, stop=True), [fma, fmb], deps2, inc=False)
                    kb.op("pe", lambda e: e.matmul(ps2[:, c2 + 128:c2 + 192], SB1[h].t[:, 256:384], Bf["Vb"].t[:, hs(h)], start=True, stop=True), [SB1[h], Bf["Vb"]], deps2)
                    R[h] = (ps2, c2, deps2)
                for h in range(8):
                    ps2, c2, deps2 = R[h]
                    _tt(kb, "dve", X0[h].t[:], ps2[:, c2:c2 + 128], mt.t[:], ALU.mult, deps2 + [mt], [X0[h]])
                    kb.copy("act", ZC[h].t[:, 64:128], ps2[:, c2 + 128:c2 + 192], deps2, [ZC[h]])
                    kb.copy("pool", ZC[h].t[:, 0:64], Bf["KKt"].t[:, hs(h)], [Bf["KKt"]], [ZC[h]])
                ck(kb, env, 7, [(0, SB1[0]), (1, X0[0]), (2, ZC[0]), (3, SB1[3]), (4, ZC[3])])
                yield
                def xx_ap(h, st):
                    t = XXp[h // 2][st % 3]
                    o = (h % 2) * 256
                    return t, t.t[:, o:o + 128], t.t[:, o + 128:o + 256]

                def et_ap(h, st):
                    t = ETq[h // 4][st % 2]
                    o = (h % 4) * 128
                    return t, t.t[:, o:o + 128]

                def sq_stage(st):
                    for h in range(8):
                        if st == 1:
                            xp, xtp, xd = X0[h].t[:, :], SB1[h].t[:, 0:128], [X0[h], SB1[h]]
                        else:
                            xt_, xp, xtp = xx_ap(h, st - 1)
                            xd = [xt_]
                        b, cc, deps = h // 2, (h % 2) * 256, list(kb.bdep[h // 2])
                        ps = kb.banks[b]
                        kb.op("pe", lambda e: e.matmul(ps[:, cc:cc + 128], xtp, xp, start=True, stop=True), xd, deps, inc=False)
                        kb.op("pe", lambda e: e.matmul(ps[:, cc + 128:cc + 256], xp, xtp, start=True, stop=True), xd, deps)
                    for p in range(4):
                        dst = XXp[p][st % 3]
                        kb.copy("act", dst.t[:, :], kb.banks[p][:, :], list(kb.bdep[p]), [dst])

                def chain_stage(st):
                    for h in range(8):
                        xt_, xp, xtp = xx_ap(h, st)
                        if st == 1:
                            etp, ed = SB1[h].t[:, 0:128], [SB1[h]]
                        else:
                            et_, etp = et_ap(h, st - 1)
                            ed = [et_]
                        b, cc, deps = 4 + h // 4, (h % 4) * 128, list(kb.bdep[4 + h // 4])
                        ps = kb.banks[b]
                        kb.op("pe", lambda e: e.matmul(ps[:, cc:cc + 128], idb.t[:], xtp, start=True, stop=False), [xt_, idb], deps, inc=False)
                        kb.op("pe", lambda e: e.matmul(ps[:, cc:cc + 128], xp, etp, start=False, stop=True), ed + [xt_], deps)
                    if st == 1:
                        for h in range(8):
                            b, cc, deps = 4 + h // 4, (h % 4) * 128, list(kb.bdep[4 + h // 4])
                            et_, eo = et_ap(h, st)
                            _tt(kb, "dve", eo, kb.banks[b][:, cc:cc + 128], SB1[h].t[:, 0:128], ALU.add, deps + [SB1[h]], [et_])
                    else:
                        for q in range(2):
                            _tt(kb, "dve", ETq[q][st % 2].t[:, :], kb.banks[4 + q][:, :], ETq[q][(st - 1) % 2].t[:, :], ALU.add,
                                list(kb.bdep[4 + q]) + [ETq[q][(st - 1) % 2]], [ETq[q][st % 2]])

                sq_stage(1)
                yield
                for st in range(1, NSTEP + 1):
                    if st + 1 <= NSTEP:
                        sq_stage(st + 1)
                        yield
                    chain_stage(st)
                    yield
                ck(kb, env, 8)
                yield
                for h in range(8):
                    et, eta = et_ap(h, NSTEP)
                    b, cc, deps = pp.get(128)
                    ps = kb.banks[b]
                    kb.op("pe", lambda e: e.matmul(ps[:, cc:cc + 128], eta, ZC[h].t[:, :], start=True, stop=True), [et, ZC[h]], deps)
                    R[h] = (ps, cc, deps)
                for h in range(8):
                    ps, cc, deps = R[h]
                    _stt(kb, "dve", WU[h].t[:], ps[:, cc:cc + 128], -1.0, ZC[h].t[:], ALU.mult, ALU.subtract, deps + [ZC[h]], [WU[h]])
                ck(kb, env, 9, [(0, WU[0]), (1, WU[3])])
                yield
                for h in range(8):
                    b, cc, deps = pp.get(256)
                    ps = kb.banks[b]
                    kb.op("pe", lambda e: e.matmul(ps[0:64, cc:cc + 128], Bf["Rt"].t[:, hs(h)], idb.t[:], start=True, stop=False), [Bf["Rt"], idb], deps, inc=False)
                    kb.op("pe", lambda e: e.matmul(ps[0:64, cc:cc + 128], WU[h].t[:, 0:64], SB1[h].t[:, 128:256], start=False, stop=True), [WU[h], SB1[h]], deps, inc=False)
                    kb.op("pe", lambda e: e.matmul(ps[0:64, cc + 128:cc + 192], WU[h].t[:, 0:64], Bf["Bh"].t[:, hs(h)], start=True, stop=True), [WU[h], Bf["Bh"]], deps)
                    R[h] = (ps, cc, deps)
                for h in range(8):
                    ps, cc, deps = R[h]
                    kb.copy("act", QTb[h].t[:], ps[0:64, cc:cc + 128], deps, [QTb[h]])
                    _stt(kb, "dve", MC[h].t[:], env.identf.t[0:64, 0:64], gcf.t[0:64, h:h + 1], ps[0:64, cc + 128:cc + 192], ALU.mult, ALU.add,
                         deps + [env.identf, gcf], [MC[h]])
                ck(kb, env, 10, [(0, QTb[0]), (1, MC[0]), (2, QTb[3]), (3, MC[3])])
                yield
                yb_, _, ydep = pp.get(512)
                hb_, _, hdep = pp.get(512)
                ybank, hbank = kb.banks[yb_], kb.banks[hb_]
                for h in range(8):
                    yo = ybank[:, hs(h)]
                    kb.op("pe", lambda e: e.matmul(yo, SB1[h].t[:, 128:256], WU[h].t[:, 64:128], start=True, stop=False), [SB1[h], WU[h]], ydep, inc=False)
                    kb.op("pe", lambda e: e.matmul(yo, SB1[h].t[:, 384:512], Bf["Vb"].t[:, hs(h)], start=False, stop=False), [SB1[h], Bf["Vb"]], ydep, inc=False)
                    kb.op("pe", lambda e: e.matmul(yo, QTb[h].t[:, :], hb.t[0:64, hs(h)], start=False, stop=True), [QTb[h], hb], ydep, inc=False)
                    ho = hbank[0:64, hs(h)]
                    kb.op("pe", lambda e: e.matmul(ho, Bf["Bh"].t[:, hs(h)], WU[h].t[:, 64:128], start=True, stop=False), [Bf["Bh"], WU[h]], hdep, inc=False)
                    kb.op("pe", lambda e: e.matmul(ho, Bf["Kh"].t[:, hs(h)], Bf["Vb"].t[:, hs(h)], start=False, stop=False), [Bf["Kh"], Bf["Vb"]], hdep, inc=False)
                    kb.op("pe", lambda e: e.matmul(ho, MC[h].t[:, :], hb.t[0:64, hs(h)], start=False, stop=True), [MC[h], hb], hdep, inc=(h == 7))
                kb.copy("act", hb.t[:, :], hbank[0:64, :], hdep, [hb])
                kb.copy("dve", FF["y"].t[:], ybank[:, :], ydep, [FF["y"]])
                ck(kb, env, 11, [(0, FF["y"])])
                yield
                if getattr(env, "stop", None) == 100 + ci:
                    ck(kb, env, 100 + ci, [(0, FF["y"])])
                    yield
                if d == 0:
                    kb.dma("pool", env.YF[i * 128:(i + 1) * 128, :], FF["y"].t[:], reads=[FF["y"]], writes=[env.dd["YF"]])
                return

            def outp(ci, i):
                if d == 0 or (i < 2 and not need_ctx):
                    return
                FF = dict(F)
                FF.update(FD[ci % 3])
                Bf = BfD[0]
                c0 = tcol(i)
                yield
                kb.dma("sp", FF["yf"].t[:], env.YF[i * 128:(i + 1) * 128, :], reads=[env.dd["YF"]], writes=[FF["yf"]])
                y = FF["y"]
                _tt(kb, "pool", y.t[:], y.t[:], FF["yf"].t[:], ALU.add, [y, FF["yf"]], [y])
                yield
                kb.op("dve", lambda e: e.reduce_sum(s8["m"].t[:], v3(y.t[:]), AX.X), [y], [s8["m"]])
                kb.op("dve", lambda e: e.tensor_scalar_mul(s8["m"].t[:], s8["m"].t[:], -1.0 / 64), [s8["m"]], [s8["m"]])
                _tt(kb, "dve", v3(y.t[:]), v3(y.t[:]), bc8(s8["m"].t[:]), ALU.add, [y, s8["m"]], [y])
                yield
                _tt(kb, "pool", FF["u1"].t[:], y.t[:], y.t[:], ALU.mult, [y], [FF["u1"]])
                yield
                kb.op("dve", lambda e: e.reduce_sum(s8["var"].t[:], v3(FF["u1"].t[:]), AX.X), [FF["u1"]], [s8["var"]])
                _act(kb, s8["var"].t[:], s8["var"].t[:], AF.Ln, [s8["var"], env.epsg], [s8["var"]], bias=env.epsg.t[:, 0:1], scale=1.0 / 64)
                _act(kb, s8["var"].t[:], s8["var"].t[:], AF.Exp, [s8["var"]], [s8["var"]], scale=-0.5)
                _tt(kb, "dve", v3(y.t[:]), v3(y.t[:]), bc8(s8["var"].t[:]), ALU.mult, [y, s8["var"]], [y])
                yield
                _tt(kb, "pool", y.t[:], y.t[:], gng.t[:], ALU.mult, [y, gng], [y])
                yield
                _tt(kb, "pool", y.t[:], y.t[:], gnb.t[:], ALU.add, [y, gnb], [y])
                yield
                _tt(kb, "pool", FF["rk"].t[:], FF["r"].t[:], rkb.t[:], ALU.mult, [FF["r"], rkb], [FF["rk"]])
                yield
                _tt(kb, "pool", FF["u1"].t[:], FF["kd0"].t[:], FF["kd1"].t[:], ALU.add, [FF["kd0"], FF["kd1"]], [FF["u1"]])
                yield
                _tt(kb, "pool", FF["u1"].t[:], FF["u1"].t[:], FF["rk"].t[:], ALU.mult, [FF["u1"], FF["rk"]], [FF["u1"]])
                yield
                kb.op("dve", lambda e: e.reduce_sum(s8["bs"].t[:], v3(FF["u1"].t[:]), AX.X), [FF["u1"]], [s8["bs"]])
                _tt(kb, "dve", v3(FF["u2"].t[:]), v3(FF["v"].t[:]), bc8(s8["bs"].t[:]), ALU.mult, [FF["v"], s8["bs"]], [FF["u2"]])
                yield
                _tt(kb, "pool", y.t[:], y.t[:], FF["u2"].t[:], ALU.add, [y, FF["u2"]], [y])
                yield
                _tt(kb, "pool", Bf["ob"].t[:], y.t[:], FF["g"].t[:], ALU.mult, [y, FF["g"]], [Bf["ob"]])
                yield
                b, cc, deps = ppp.get(256)
                psb = kb.banks_bf[b]
                for jp in range(4):
                    kb.op("pe", lambda e: e.transpose(psb[:, 2 * cc + jp * 128:2 * cc + (jp + 1) * 128], Bf["ob"].t[:, jp * 128:(jp + 1) * 128], idb.t[:]),
                          [Bf["ob"], idb], deps, inc=(jp == 3))
                kb.copy("act", mst.t[:], psb[:, 2 * cc:2 * cc + 512].rearrange("p (j t) -> p j t", t=128), deps, [mst])
                kb.dma("pool", env.MIXT[0:512, c0:c0 + 128].rearrange("(k p) c -> p k c", p=128), mst.t[:], reads=[mst], writes=[env.dd["MIXT"]])
                yield

            def drain(g):
                for _ in g:
                    pass

            drain(prep(0, order[0], 0))
            for ci in range(len(order) + 1):
                alive = []
                if ci < len(order):
                    alive.append(heads(ci, order[ci], ci % 2))
                if ci + 1 < len(order):
                    alive.append(prep(ci + 1, order[ci + 1], (ci + 1) % 2))
                if ci >= 1:
                    alive.append(outp(ci - 1, order[ci - 1]))
                while alive:
                    for g in list(alive):
                        try:
                            next(g)
                        except StopIteration:
                            alive.remove(g)
            kb.barrier()
    env.pp = PsPool(kb, list(range(8)))


PHASES["rwkv"] = phase_rwkv


def residual_ln(kb, env, es_tiles, xt, ps_list, gate, lng, lnb, out):
    st = es_tiles
    for half, (ps, deps) in enumerate(ps_list):
        sl = slice(half * 512, (half + 1) * 512)
        _tt(kb, "dve", out.t[:, sl], ps, gate.t[:, sl], ALU.mult, deps + [gate], [out])
        yield
    _stt(kb, "dve", out.t[:], xt.t[:], ALPHA, out.t[:], ALU.mult, ALU.add, [xt, out], [out])
    yield
    _act(kb, st["junk"].t[:], out.t[:], AF.Identity, [out], [st["junk"], st["s1"]], accum_out=st["s1"].t[:, 0:1])
    yield
    kb.op("dve", lambda e: e.tensor_scalar_mul(st["s1"].t[:, 0:1], st["s1"].t[:, 0:1], -1.0 / D), [st["s1"]], [st["s1"]])
    kb.op("dve", lambda e: e.tensor_scalar_add(out.t[:], out.t[:], st["s1"].t[:, 0:1]), [out, st["s1"]], [out])
    yield
    _act(kb, st["junk"].t[:], out.t[:], AF.Square, [out], [st["junk"], st["s2"]], accum_out=st["s2"].t[:, 0:1])
    yield
    _act(kb, st["s2"].t[:, 0:1], st["s2"].t[:, 0:1], AF.Sqrt, [st["s2"], env.epsl], [st["s2"]], bias=env.epsl.t[:, 0:1], scale=1.0 / D)
    kb.op("dve", lambda e: e.reciprocal(st["s2"].t[:, 0:1], st["s2"].t[:, 0:1]), [st["s2"]], [st["s2"]])
    yield
    _stt(kb, "dve", out.t[:], out.t[:], st["s2"].t[:, 0:1], lng.t[:], ALU.mult, ALU.mult, [out, st["s2"], lng], [out])
    yield
    _tt(kb, "pool", out.t[:], out.t[:], lnb.t[:], ALU.add, [out, lnb], [out])
    yield


def run_interleaved(gens_iter, depth=2):
    active = []

    def pump():
        for g in list(active):
            try:
                next(g)
            except StopIteration:
                active.remove(g)

    for g in gens_iter:
        active.append(g)
        while len(active) >= depth:
            pump()
    while active:
        pump()


def load_gate(kb, es, env, l, s, gi, name):
    t = kb.tile(es, [128, 1024], F32, name)
    kb.dma("sp", t.t[:], env.MODROW[l, s, gi].partition_broadcast(128), reads=[env.dd["MODROW"]], writes=[t])
    return t


def phase_wo(kb, env, l):
    need_ctx = l < DEPTH - 1
    env.pp = PsPool(kb, [6, 7])
    pp = PsPool(kb, [0, 1, 2, 3, 4, 5])
    with ExitStack() as es:
        wo = kb.tile(es, [128, 8, 1024], BF16, "wo")
        with ExitStack() as es2:
            stage = [kb.tile(es2, [128, 1024], F32, "wost%d" % i) for i in range(2)]
            load_w_bf16(kb, es2, wo, env.w["w_o"][l], 8, 1024, stage)
            kb.barrier()
        gates = [load_gate(kb, es, env, l, s, 0, "g1_%d" % s) for s in range(2)]
        lng = load_bcast(kb, es, env.w["ln1_g"][l], 1024, "ln1g")
        lnb = load_bcast(kb, es, env.w["ln1_b"][l], 1024, "ln1b")
        sts = [{"junk": kb.tile(es, [128, 1024], F32, "junk"), "s1": kb.tile(es, [128, 1], F32, "s1"), "s2": kb.tile(es, [128, 1], F32, "s2")} for _ in range(2)]
        env.ut_st = [kb.tile(es, [128, 8, 130], BF16, "utst%d" % i) for i in range(2)]
        env.ut_i = 0
        for t in env.ut_st:
            kb.op("pool", lambda e: e.memset(t.t[:], 0.0), [], [t])
        xts = [kb.tile(es, [128, 1024], F32, "xt%d" % i) for i in range(3)]
        outs = [kb.tile(es, [128, 1024], F32, "xo%d" % i) for i in range(3)]
        mts = [kb.tile(es, [128, 8, 128], BF16, "mt%d" % i) for i in range(3)]
        tiles = list(range(NTILE) if need_ctx else range(2, NTILE))

        def mm_stage(ii, i):
            c0 = tcol(i)
            xt, mt = xts[ii % 3], mts[ii % 3]
            kb.dma("sp", mt.t[:], env.MIXT[:, c0:c0 + 128].rearrange("(k p) c -> p k c", p=128), reads=[env.dd["MIXT"]], writes=[mt])
            kb.dma("sp", xt.t[:], xin_ap(env, l, i), reads=[env.dd["X2"]], writes=[xt])
            pl = []
            for half in range(2):
                b, _, deps = pp.get(512)
                ps = kb.banks[b]
                for k in range(8):
                    kb.op("pe", lambda e: e.matmul(ps[:, :], mt.t[:, k, :], wo.t[:, k, half * 512:(half + 1) * 512], start=(k == 0), stop=(k == 7)),
                          [mt, wo], deps, inc=(k == 7))
                pl.append((ps[:, :], deps))
            return pl

        def epi_stage(ii, i, pl):
            s = 1 if i < 2 else 0
            xt, out = xts[ii % 3], outs[ii % 3]
            yield from residual_ln(kb, env, sts[ii % 2], xt, pl, gates[s], lng, lnb, out)
            kb.dma("pool", env.X1[i * 128:(i + 1) * 128, :], out.t[:], reads=[out], writes=[env.dd["X1"]])
            yield from emit_ut_g(kb, env, out, s, env.modf[l], 4, 3, env.U2T, i, pad=True)

        def gens():
            pend = mm_stage(0, tiles[0])
            for ii, i in enumerate(tiles):
                nxt = mm_stage(ii + 1, tiles[ii + 1]) if ii + 1 < len(tiles) else None
                yield epi_stage(ii, i, pend)
                pend = nxt

        run_interleaved(gens())
        kb.barrier()


def phase_ffu(kb, env, l):
    need_ctx = l < DEPTH - 1
    env.pp = PsPool(kb, list(range(8)))
    pp = env.pp
    NF = 2 * DFF // 128
    with ExitStack() as es:
        wup = kb.tile(es, [128, 8, 2 * DFF], BF16, "wup")
        fcw = []
        with ExitStack() as es2:
            stage = [kb.tile(es2, [128, 2816], F32, "wust%d" % i) for i in range(2)]
            for hh in range(2):
                load_w_bf16(kb, es2, wup, env.w["w_up"][l][:, hh * 2816:(hh + 1) * 2816], 8, 2816, stage, col_off=hh * 2816)
            kb.barrier()
        for a in range(3):
            fcw.append(load_vec_fm(kb, env, es, env.w["ffn_conv_w"][l][a], NF, "fcw%d" % a))
        fcb = load_vec_fm(kb, env, es, env.w["ffn_conv_b"][l], NF, "fcb")
        u2b = [kb.tile(es, [128, 8, 514], BF16, "u2b%d" % i) for i in range(2)]
        gst = [kb.tile(es, [128, 22, 512], BF16, "gst%d" % i) for i in range(2)]
        hs_ = [kb.tile(es, [128, 514], F32, "hs%d" % i) for i in range(4)]
        tt_ = [kb.tile(es, [128, 512], F32, "tt%d" % i) for i in range(4)]
        sgt = [kb.tile(es, [128, 512], F32, "sgt%d" % i) for i in range(2)]
        blocks = ([(0, 256)] if need_ctx else []) + [(256 + 510 * j, min(510, NT - 256 - 510 * j)) for j in range(9)]
        hi = 0
        for bi, (g0, n) in enumerate(blocks):
            c0 = gcol(g0)
            ub = u2b[bi % 2]
            gs = gst[bi % 2]
            N = n + 2
            kb.dma("sp", ub.t[:, :, 0:N], env.U2T[:, c0 - 1:c0 + n + 1].rearrange("(k p) c -> p k c", p=128), reads=[env.dd["U2T"]], writes=[ub])
            for j in range(22):
                tv = []
                for which in range(2):
                    jf = j + 22 * which
                    hs = hs_[hi % 4]
                    tt = tt_[hi % 4]
                    hi += 1
                    b, _, deps = pp.get(512)
                    ps = kb.banks[b]
                    for k in range(8):
                        kb.op("pe", lambda e: e.matmul(ps[:, 0:N], wup.t[:, k, jf * 128:(jf + 1) * 128], ub.t[:, k, 0:N], start=(k == 0), stop=(k == 7)),
                              [wup, ub], deps, inc=(k == 7))
                    kb.copy("act", hs.t[:, 0:N], ps[:, 0:N], deps, [hs])
                    _act(kb, tt.t[:, 0:n], ps[:, 1:n + 1], AF.Identity, deps + [fcw[1], fcb], [tt], bias=fcb.t[:, jf:jf + 1], scale=fcw[1].t[:, jf:jf + 1])
                    _stt(kb, "dve", tt.t[:, 0:n], hs.t[:, 0:n], fcw[0].t[:, jf:jf + 1], tt.t[:, 0:n], ALU.mult, ALU.add, [hs, tt, fcw[0]], [tt])
                    _stt(kb, "dve", tt.t[:, 0:n], hs.t[:, 2:n + 2], fcw[2].t[:, jf:jf + 1], tt.t[:, 0:n], ALU.mult, ALU.add, [hs, tt, fcw[2]], [tt])
                    tv.append(tt)
                sg = sgt[j % 2]
                _act(kb, sg.t[:, 0:n], tv[0].t[:, 0:n], AF.Silu, [tv[0]], [sg])
                _tt(kb, "pool", gs.t[:, j, 0:n], sg.t[:, 0:n], tv[1].t[:, 0:n], ALU.mult, [sg, tv[1]], [gs])
            kb.dma("pool", env.GT[:, c0:c0 + n].rearrange("(k p) c -> p k c", p=128), gs.t[:, :, 0:n], reads=[gs], writes=[env.dd["GT"]])
        kb.barrier()


def phase_ffd(kb, env, l):
    need_ctx = l < DEPTH - 1
    last = l == DEPTH - 1
    env.pp = PsPool(kb, [6, 7])
    pp = PsPool(kb, [0, 1, 2, 3, 4, 5])
    with ExitStack() as es:
        wdn = kb.tile(es, [128, 22, 1024], BF16, "wdn")
        with ExitStack() as es2:
            stage = [kb.tile(es2, [128, 1024], F32, "wdst%d" % i) for i in range(3)]
            load_w_bf16(kb, es2, wdn, env.w["w_down"][l], 22, 1024, stage)
            kb.barrier()
        gates = [load_gate(kb, es, env, l, s, 1, "g2_%d" % s) for s in range(2)]
        lng = load_bcast(kb, es, env.w["ln2_g"][l], 1024, "ln2g")
        lnb = load_bcast(kb, es, env.w["ln2_b"][l], 1024, "ln2b")
        sts = [{"junk": kb.tile(es, [128, 1024], F32, "junk"), "s1": kb.tile(es, [128, 1], F32, "s1"), "s2": kb.tile(es, [128, 1], F32, "s2")} for _ in range(2)]
        env.ut_st = [kb.tile(es, [128, 8, 130], BF16, "utst%d" % i) for i in range(2)]
        env.ut_i = 0
        xts = [kb.tile(es, [128, 1024], F32, "xt%d" % i) for i in range(3)]
        outs = [kb.tile(es, [128, 1024], F32, "xo%d" % i) for i in range(3)]
        gts = [kb.tile(es, [128, 22, 128], BF16, "gt%d" % i) for i in range(3)]
        tiles = list(range(NTILE) if need_ctx else range(2, NTILE))

        def mm_stage(ii, i):
            c0 = tcol(i)
            xt, gt = xts[ii % 3], gts[ii % 3]
            kb.dma("sp", gt.t[:], env.GT[:, c0:c0 + 128].rearrange("(k p) c -> p k c", p=128), reads=[env.dd["GT"]], writes=[gt])
            kb.dma("sp", xt.t[:], env.X1[i * 128:(i + 1) * 128, :], reads=[env.dd["X1"]], writes=[xt])
            pl = []
            for half in range(2):
                b, _, deps = pp.get(512)
                ps = kb.banks[b]
                for k in range(22):
                    kb.op("pe", lambda e: e.matmul(ps[:, :], gt.t[:, k, :], wdn.t[:, k, half * 512:(half + 1) * 512], start=(k == 0), stop=(k == 21)),
                          [gt, wdn], deps, inc=(k == 21))
                pl.append((ps[:, :], deps))
            return pl

        def epi_stage(ii, i, pl):
            s = 1 if i < 2 else 0
            xt, out = xts[ii % 3], outs[ii % 3]
            yield from residual_ln(kb, env, sts[ii % 2], xt, pl, gates[s], lng, lnb, out)
            if last:
                kb.dma("pool", env.y[(i - 2) * 128:(i - 1) * 128, :], out.t[:], reads=[out], writes=[env.dd["y"]])
            else:
                kb.dma("pool", env.X2[i * 128:(i + 1) * 128, :], out.t[:], reads=[out], writes=[env.dd["X2"]])
                yield from emit_ut_g(kb, env, out, s, env.modf[l + 1], 1, 0, env.UT, i, pad=False)

        def gens():
            pend = mm_stage(0, tiles[0])
            for ii, i in enumerate(tiles):
                nxt = mm_stage(ii + 1, tiles[ii + 1]) if ii + 1 < len(tiles) else None
                yield epi_stage(ii, i, pend)
                pend = nxt

        run_interleaved(gens(), depth=1)
        kb.barrier()


PHASES["wo"] = phase_wo
PHASES["ffu"] = phase_ffu
PHASES["ffd"] = phase_ffd


_CACHE = {}


def kernel(**inputs):
    if "nc" not in _CACHE:
        _CACHE["nc"] = build()[0]
        _CACHE["consts"] = make_consts()
    nc = _CACHE["nc"]
    consts = _CACHE["consts"]
    B = inputs["x"].shape[0]
    shared = {k: np.ascontiguousarray(np.asarray(inputs[k], dtype=np.float32)) for k in W_SHAPES}
    shared["c_ctx"] = np.ascontiguousarray(np.asarray(inputs["c_ctx"], dtype=np.float32))
    for k, v in consts.items():
        shared["k_" + k] = v
    in_maps = []
    for b in range(B):
        m = dict(shared)
        m["x"] = np.ascontiguousarray(np.asarray(inputs["x"][b], dtype=np.float32))
        m["c"] = np.ascontiguousarray(np.asarray(inputs["c"][b], dtype=np.float32))
        m["ctx"] = np.ascontiguousarray(np.asarray(inputs["ctx"][b], dtype=np.float32))
        in_maps.append(m)
    res = run_bass_kernel_spmd(nc, in_maps, core_ids=list(range(B)))
    return np.stack([np.asarray(r["y"], dtype=np.float32) for r in res.results], axis=0)
```
